# Optimizing a Trainium2 kernel written in Bass

```python
import math
import jax, jax.numpy as jnp
from jax import lax
import numpy as np

D_MODEL = 1024
BATCH = 16
SEQ = 2048
DEPTH = 1

ATTN_HEADS = 8
ATTN_HEAD_DIM = 64
ATTN_V_DIM = 2 * ATTN_HEAD_DIM
Q_BLOCK = 128
NUM_BUCKETS = 32
MAX_DISTANCE = 128
SSM_EXPAND = 2
SSM_D_INNER = SSM_EXPAND * D_MODEL
SSM_HEAD_DIM = 64
SSM_HEADS = SSM_D_INNER // SSM_HEAD_DIM
SSM_GROUPS = 4
SSM_HEADS_PER_GROUP = SSM_HEADS // SSM_GROUPS
SSM_STATE = 128
SSM_CONV = 4
SSM_CHUNK = 128
D_FF = 2816
FFN_CONV = 3
RMS_EPS = 1e-6
SUBLN_EPS = 1e-5

Q_COLS = ATTN_HEADS * 2 * ATTN_HEAD_DIM
K_COLS = ATTN_HEADS * 2 * ATTN_HEAD_DIM
V_COLS = ATTN_HEADS * ATTN_V_DIM
Z_COLS = SSM_D_INNER
BC_COLS = SSM_GROUPS * SSM_STATE
XBC_COLS = SSM_D_INNER + 2 * BC_COLS
DT_COLS = SSM_HEADS
GATE_COLS = 2 * D_MODEL
IN_COLS = Q_COLS + K_COLS + V_COLS + Z_COLS + XBC_COLS + DT_COLS + GATE_COLS

kernel_name = "hybrid_diffattn_mamba2_gated_convffn"


def rms_norm(x, g, eps=RMS_EPS):
    xf = x.astype(jnp.float32)
    y = xf * lax.rsqrt(jnp.mean(xf * xf, axis=-1, keepdims=True) + eps)
    return (y * g.astype(jnp.float32)).astype(x.dtype)


def causal_dwconv(x, w, b):
    k = w.shape[0]
    y = lax.conv_general_dilated(
        x, w[:, None, :].astype(x.dtype), window_strides=(1,), padding=[(k - 1, 0)],
        dimension_numbers=('NWC', 'WIO', 'NWC'), feature_group_count=x.shape[-1])
    return y + b.astype(x.dtype)


def t5_causal_bucket(qpos, kpos):
    n = jnp.maximum(qpos[:, None] - kpos[None, :], 0)
    max_exact = NUM_BUCKETS // 2
    nf = jnp.maximum(n, 1).astype(jnp.float32)
    large = max_exact + (jnp.log(nf / max_exact) / math.log(MAX_DISTANCE / max_exact)
                         * (NUM_BUCKETS - max_exact)).astype(jnp.int32)
    large = jnp.minimum(large, NUM_BUCKETS - 1)
    return jnp.where(n < max_exact, n, large)


def diff_attention(q, k, v, lam, rel_bias):
    b, h, _, s, _ = q.shape
    n_blocks = s // Q_BLOCK
    kpos = jnp.arange(s, dtype=jnp.int32)

    def one_block(i):
        start = i * Q_BLOCK
        qb = lax.dynamic_slice_in_dim(q, start, Q_BLOCK, axis=3)
        qpos = start + jnp.arange(Q_BLOCK, dtype=jnp.int32)
        bias = jnp.take(rel_bias, t5_causal_bucket(qpos, kpos), axis=0)
        bias = jnp.transpose(bias.astype(jnp.float32), (2, 0, 1))
        logits = jnp.einsum('bhmqd,bhmkd->bhmqk', qb, k).astype(jnp.float32)
        logits = logits + bias[None, :, None]
        logits = jnp.where((qpos[:, None] >= kpos[None, :])[None, None, None], logits, -jnp.inf)
        p = jax.nn.softmax(logits, axis=-1)
        a = p[:, :, 0] - lam * p[:, :, 1]
        return jnp.einsum('bhqk,bhkv->bhqv', a.astype(v.dtype), v)

    out = lax.map(one_block, jnp.arange(n_blocks, dtype=jnp.int32))
    return jnp.transpose(out, (1, 2, 0, 3, 4)).reshape(b, h, s, v.shape[-1])


def ssd_chunked_scan(x, dt, a, bmat, cmat):
    b, s = x.shape[:2]
    nc = s // SSM_CHUNK
    g, k, p, n = SSM_GROUPS, SSM_HEADS_PER_GROUP, SSM_HEAD_DIM, SSM_STATE

    def to_chunks(t):
        return jnp.swapaxes(t.reshape((b, nc, SSM_CHUNK) + t.shape[2:]), 0, 1)

    xs = to_chunks((x * dt[..., None]).reshape(b, s, g, k, p))
    adts = to_chunks((dt * a).reshape(b, s, g, k))
    bs, cs = to_chunks(bmat), to_chunks(cmat)
    causal = jnp.tril(jnp.ones((SSM_CHUNK, SSM_CHUNK), dtype=bool))[None, :, :, None, None]

    def step(state, inp):
        xc, ac, bc, cc = inp
        cum = jnp.cumsum(ac, axis=1)
        seg = cum[:, :, None] - cum[:, None, :]
        decay = jnp.exp(jnp.where(causal, seg, -jnp.inf))
        cb = jnp.einsum('blgn,bsgn->blsg', cc, bc)
        y_diag = jnp.einsum('blsgk,bsgkp->blgkp', cb[..., None] * decay, xc)
        y_off = jnp.einsum('blgn,bgkpn->blgkp', cc, state) * jnp.exp(cum)[..., None]
        total = cum[:, -1]
        w = jnp.exp(total[:, None] - cum)
        new_state = state * jnp.exp(total)[..., None, None] + jnp.einsum(
            'bsgn,bsgkp->bgkpn', bc, xc * w[..., None])
        return new_state, y_diag + y_off

    init = jnp.zeros((b, g, k, p, n), dtype=jnp.float32)
    _, ys = lax.scan(step, init, (xs, adts, bs, cs))
    return jnp.swapaxes(ys, 0, 1).reshape(b, s, SSM_HEADS, p)


def setup_inputs(seed: int = 0) -> dict:
    key = jax.random.key(seed)
    ks = jax.random.split(key, 32)
    f32 = jnp.float32
    nrm = lambda k, shape, scale: jax.random.normal(k, shape, f32) * scale
    gain = lambda k, shape: 1.0 + 0.02 * jax.random.normal(k, shape, f32)
    L = DEPTH
    dt0 = jnp.exp(jax.random.uniform(ks[10], (L, SSM_HEADS), f32, math.log(1e-3), math.log(1e-1)))
    return {
        "x": nrm(ks[0], (BATCH, SEQ, D_MODEL), 1.0),
        "rel_bias": nrm(ks[1], (NUM_BUCKETS, ATTN_HEADS), 0.5),
        "norm_mix_g": gain(ks[2], (L, D_MODEL)),
        "w_in": nrm(ks[3], (L, D_MODEL, IN_COLS), D_MODEL ** -0.5),
        "q_norm_g": gain(ks[4], (L, ATTN_HEAD_DIM)),
        "k_norm_g": gain(ks[5], (L, ATTN_HEAD_DIM)),
        "lambda_q1": nrm(ks[6], (L, ATTN_HEAD_DIM), 0.1),
        "lambda_k1": nrm(ks[7], (L, ATTN_HEAD_DIM), 0.1),
        "lambda_q2": nrm(ks[8], (L, ATTN_HEAD_DIM), 0.1),
        "lambda_k2": nrm(ks[9], (L, ATTN_HEAD_DIM), 0.1),
        "attn_subln_g": gain(ks[11], (L, ATTN_V_DIM)),
        "conv_ssm_w": nrm(ks[12], (L, SSM_CONV, XBC_COLS), SSM_CONV ** -0.5),
        "conv_ssm_b": nrm(ks[13], (L, XBC_COLS), 0.02),
        "dt_bias": dt0 + jnp.log(-jnp.expm1(-dt0)),
        "a_log": jnp.log(jax.random.uniform(ks[14], (L, SSM_HEADS), f32, 1.0, 16.0)),
        "d_skip": gain(ks[15], (L, SSM_HEADS)),
        "ssm_norm_g": gain(ks[16], (L, SSM_D_INNER)),
        "w_proj_attn": nrm(ks[17], (L, V_COLS, D_MODEL), V_COLS ** -0.5),
        "w_proj_ssm": nrm(ks[18], (L, SSM_D_INNER, D_MODEL), SSM_D_INNER ** -0.5),
        "w_out": nrm(ks[19], (L, D_MODEL, D_MODEL), D_MODEL ** -0.5),
        "norm_ffn_g": gain(ks[20], (L, D_MODEL)),
        "w_up": nrm(ks[21], (L, D_MODEL, 2 * D_FF), D_MODEL ** -0.5),
        "conv_ffn_w": nrm(ks[22], (L, FFN_CONV, 2 * D_FF), FFN_CONV ** -0.5),
        "conv_ffn_b": nrm(ks[23], (L, 2 * D_FF), 0.02),
        "w_down": nrm(ks[24], (L, D_FF, D_MODEL), D_FF ** -0.5),
    }


def reference(x, rel_bias, norm_mix_g, w_in, q_norm_g, k_norm_g, lambda_q1, lambda_k1,
              lambda_q2, lambda_k2, attn_subln_g, conv_ssm_w, conv_ssm_b, dt_bias, a_log,
              d_skip, ssm_norm_g, w_proj_attn, w_proj_ssm, w_out, norm_ffn_g, w_up,
              conv_ffn_w, conv_ffn_b, w_down):
    b, s, _ = x.shape
    f32 = jnp.float32
    sizes = [Q_COLS, K_COLS, V_COLS, Z_COLS, XBC_COLS, DT_COLS]
    offsets = list(np.cumsum(sizes))
    for l in range(DEPTH):
        h = rms_norm(x, norm_mix_g[l])
        proj = h @ w_in[l]
        q, k, v, z, xbc, dt_raw, gate_logits = jnp.split(proj, offsets, axis=-1)

        scale = ATTN_HEAD_DIM ** -0.5
        q = rms_norm(q.reshape(b, s, ATTN_HEADS, 2, ATTN_HEAD_DIM), q_norm_g[l]) * scale
        k = rms_norm(k.reshape(b, s, ATTN_HEADS, 2, ATTN_HEAD_DIM), k_norm_g[l])
        q = jnp.transpose(q, (0, 2, 3, 1, 4))
        k = jnp.transpose(k, (0, 2, 3, 1, 4))
        v = jnp.transpose(v.reshape(b, s, ATTN_HEADS, ATTN_V_DIM), (0, 2, 1, 3))
        lam_init = 0.8 - 0.6 * math.exp(-0.3 * l)
        lam = (jnp.exp(jnp.sum(lambda_q1[l].astype(f32) * lambda_k1[l].astype(f32)))
               - jnp.exp(jnp.sum(lambda_q2[l].astype(f32) * lambda_k2[l].astype(f32)))
               + lam_init)
        o = diff_attention(q, k, v, lam, rel_bias)
        o = rms_norm(o, attn_subln_g[l], SUBLN_EPS) * (1.0 - lam_init)
        y_attn = jnp.transpose(o, (0, 2, 1, 3)).reshape(b, s, V_COLS)

        xbc = jax.nn.silu(causal_dwconv(xbc, conv_ssm_w[l], conv_ssm_b[l]))
        xs, bm, cm = jnp.split(xbc, [SSM_D_INNER, SSM_D_INNER + BC_COLS], axis=-1)
        xs = xs.reshape(b, s, SSM_HEADS, SSM_HEAD_DIM).astype(f32)
        dt = jax.nn.softplus(dt_raw.astype(f32) + dt_bias[l].astype(f32))
        a = -jnp.exp(a_log[l].astype(f32))
        y = ssd_chunked_scan(xs, dt, a,
                             bm.reshape(b, s, SSM_GROUPS, SSM_STATE).astype(f32),
                             cm.reshape(b, s, SSM_GROUPS, SSM_STATE).astype(f32))
        y = y + xs * d_skip[l].astype(f32)[:, None]
        y = y.reshape(b, s, SSM_D_INNER) * jax.nn.silu(z.astype(f32))
        y = rms_norm(y.reshape(b, s, SSM_GROUPS, SSM_D_INNER // SSM_GROUPS),
                     ssm_norm_g[l].reshape(SSM_GROUPS, SSM_D_INNER // SSM_GROUPS), SUBLN_EPS)
        y_ssm = y.reshape(b, s, SSM_D_INNER).astype(x.dtype)

        gates = jax.nn.sigmoid(gate_logits.astype(f32))
        g_attn, g_ssm = gates[..., :D_MODEL], gates[..., D_MODEL:]
        mixed = (g_attn * (y_attn @ w_proj_attn[l]).astype(f32)
                 + g_ssm * (y_ssm @ w_proj_ssm[l]).astype(f32)).astype(x.dtype)
        x = x + mixed @ w_out[l]

        h = rms_norm(x, norm_ffn_g[l])
        u = causal_dwconv(h @ w_up[l], conv_ffn_w[l], conv_ffn_b[l])
        u_gate, u_val = u[..., :D_FF], u[..., D_FF:]
        x = x + (jax.nn.silu(u_gate) * u_val) @ w_down[l]
    return x
```

```python
import contextlib
import numpy as np
import ml_dtypes
import concourse.bass as bass
import concourse.mybir as mybir
from concourse.bass_utils import run_bass_kernel_spmd

F32 = mybir.dt.float32
BF16 = mybir.dt.bfloat16
AF = mybir.ActivationFunctionType
ALU = mybir.AluOpType
AX = mybir.AxisListType

ENGS = ("pe", "act", "dve", "pool", "sp")

S_LEN = 2048
D = 1024
NT = 16
IN_COLS = 10272
C_Q, C_K, C_V, C_Z, C_XS, C_B, C_C, C_DT, C_G = 0, 1024, 2048, 3072, 5120, 7168, 7680, 8192, 8224
D_FF = 2816
NFC = 22


class Op:
    __slots__ = ("eng", "fn", "deps", "is_dma", "sig", "need_sig")

    def __init__(self, eng, fn, is_dma):
        self.eng = eng
        self.fn = fn
        self.is_dma = is_dma
        self.deps = []
        self.sig = None
        self.need_sig = False


class Sched:
    SEM_LIMIT = 30000

    def __init__(self, nc, n_dma_sems=12):
        self.nc = nc
        self.streams = {e: [] for e in ENGS}
        self.last_w = {}
        self.readers = {}
        self.n_dma_sems = n_dma_sems
        self.live_dmas = []

    def op(self, eng, fn, reads=(), writes=(), dma=False):
        o = Op(eng, fn, dma)
        deps = []
        for k in reads:
            w = self.last_w.get(k)
            if w is not None:
                deps.append(w)
        for k in writes:
            w = self.last_w.get(k)
            if w is not None:
                deps.append(w)
            deps.extend(self.readers.get(k, ()))
        seen = set()
        for d in deps:
            if d is o or id(d) in seen:
                continue
            seen.add(id(d))
            if (not d.is_dma) and d.eng == eng and eng == "pe":
                continue
            o.deps.append(d)
            d.need_sig = True
        for k in reads:
            self.readers.setdefault(k, []).append(o)
        for k in writes:
            self.last_w[k] = o
            self.readers[k] = []
        self.streams[eng].append(o)
        if dma:
            self.live_dmas.append(o)
        return o

    def dma(self, q, out, in_, reads=(), writes=(), **kw):
        return self.op(q, lambda e: e.dma_start(out=out, in_=in_, **kw), reads, writes, dma=True)

    def barrier(self):
        lasts = []
        for e in ENGS:
            for o in reversed(self.streams[e]):
                if o.fn is not None and not o.is_dma:
                    lasts.append(o)
                    break
        dmas = list(self.live_dmas)
        for e in ENGS:
            b = Op(e, None, False)
            for d in lasts:
                if d.eng != e:
                    b.deps.append(d)
                    d.need_sig = True
            b.deps.extend(dmas)
            self.streams[e].append(b)
        self.live_dmas = []
        self.last_w = {}
        self.readers = {}

    def emit(self, final_wait_ops=()):
        nc = self.nc
        with contextlib.ExitStack() as es:
            for e in ENGS:
                sigs = [o for o in self.streams[e] if o.need_sig and not o.is_dma]
                n_sems = max(1, (len(sigs) + self.SEM_LIMIT - 1) // self.SEM_LIMIT)
                sems = [es.enter_context(nc.semaphore(f"s_{e}_{i}")) for i in range(n_sems)]
                for cnt, o in enumerate(sigs):
                    o.sig = (sems[cnt // self.SEM_LIMIT], cnt % self.SEM_LIMIT + 1)
            dma_prev = {}
            for e in ENGS:
                dmas = [o for o in self.streams[e] if o.is_dma]
                if not dmas:
                    continue
                k = min(self.n_dma_sems, len(dmas))
                sems = [es.enter_context(nc.semaphore(f"d_{e}_{i}")) for i in range(k)]
                uses = [0] * k
                for cnt, o in enumerate(dmas):
                    s = cnt % k
                    uses[s] += 1
                    o.sig = (sems[s], 16 * uses[s])
                    dma_prev[id(o)] = (sems[s], 16 * (uses[s] - 1)) if uses[s] > 1 else None
            blk = es.enter_context(nc.Block())
            handles = {"pe": blk.tensor, "act": blk.scalar, "dve": blk.vector,
                       "pool": blk.gpsimd, "sp": blk.sync}
            for e in ENGS:
                ops = self.streams[e]

                def body(h, ops=ops, e=e):
                    waited = {}

                    def wait(sem, val):
                        if waited.get(id(sem), 0) >= val:
                            return
                        waited[id(sem)] = val
                        h.wait_ge(sem, val)

                    for o in ops:
                        for d in o.deps:
                            wait(*d.sig)
                        if o.fn is None:
                            continue
                        if o.is_dma:
                            p = dma_prev.get(id(o))
                            if p is not None:
                                wait(*p)
                        inst = o.fn(h)
                        if o.is_dma:
                            inst.then_inc(o.sig[0], 16)
                        elif o.need_sig:
                            inst.then_inc(o.sig[0], 1)
                    if e == "sp":
                        for o in final_wait_ops:
                            wait(*o.sig)

                handles[e](body)


def V(ap, dims, off=0):
    return bass.AP(tensor=ap.tensor, offset=ap.offset + off,
                   ap=[list(ap.ap[0])] + [list(d) for d in dims])


def t5_bucket_table(nmax=256):
    n = np.arange(nmax)
    nf = np.maximum(n, 1).astype(np.float32)
    large = 16 + (np.log(nf / np.float32(16)) / np.float32(np.log(128 / 16)) * np.float32(16)).astype(np.int32)
    large = np.minimum(large, 31)
    return np.where(n < 16, n, large)


def build(nseq=2, stage=99, dbg=False):
    nc = bass.Bass("TRN2", target_bir_lowering=False)

    def din(name, shape):
        return nc.dram_tensor(name, list(shape), F32, kind="ExternalInput").ap()

    x = din("x", [nseq, S_LEN, D])
    rel_bias = din("rel_bias", [32, 8])
    norm_mix_g = din("norm_mix_g", [1, D])
    w_in = din("w_in", [D, IN_COLS])
    q_norm_g = din("q_norm_g", [1, 64])
    k_norm_g = din("k_norm_g", [1, 64])
    lam_q1 = din("lambda_q1", [1, 64])
    lam_k1 = din("lambda_k1", [1, 64])
    lam_q2 = din("lambda_q2", [1, 64])
    lam_k2 = din("lambda_k2", [1, 64])
    subln_g = din("attn_subln_g", [1, 128])
    conv_ssm_w = din("conv_ssm_w", [4, 3072])
    conv_ssm_b = din("conv_ssm_b", [1, 3072])
    dt_bias = din("dt_bias", [1, 32])
    a_log = din("a_log", [1, 32])
    d_skip = din("d_skip", [1, 32])
    ssm_norm_g = din("ssm_norm_g", [1, 2048])
    w_pa = din("w_proj_attn", [1024, 1024])
    w_ps = din("w_proj_ssm", [2048, 1024])
    w_out = din("w_out", [1024, 1024])
    norm_ffn_g = din("norm_ffn_g", [1, D])
    w_up = din("w_up", [D, 2 * D_FF])
    conv_ffn_w = din("conv_ffn_w", [3, 2 * D_FF])
    conv_ffn_b = din("conv_ffn_b", [1, 2 * D_FF])
    w_down = din("w_down", [D_FF, D])
    out = nc.dram_tensor("out", [nseq, S_LEN, D], F32, kind="ExternalOutput").ap()

    skind = "ExternalOutput" if dbg else "Internal"
    ext_d = nc.dram_tensor("ext_d", [8, 384], F32, kind="Internal").ap()
    Zd = nc.dram_tensor("Zd", [8, 128, 384], F32, kind="Internal").ap()
    ya_d = nc.dram_tensor("ya_d", [nseq, NT, 128, 8, 128], BF16, kind=skind).ap()
    yss_d = nc.dram_tensor("yss_d", [nseq, NT, 128, 16, 128], BF16, kind=skind).ap()
    if dbg:
        hT_d = nc.dram_tensor("hT_d", [128, 8, S_LEN], BF16, kind="ExternalOutput").ap()
        dt_d = nc.dram_tensor("dt_d", [128, NT, 32], F32, kind="ExternalOutput").ap()

    w_in_v = w_in.rearrange("(kc p) c -> p kc c", p=128)
    w_up_v = w_up.rearrange("(kc p) c -> p kc c", p=128)
    w_pa_v = w_pa.rearrange("(kc p) c -> p kc c", p=128)
    w_ps_v = w_ps.rearrange("(kc p) c -> p kc c", p=128)
    w_out_v = w_out.rearrange("(kc p) c -> p kc c", p=128)
    w_dn_v = w_down.rearrange("(kc p) c -> p kc c", p=128)

    S = Sched(nc)
    finals = []

    def MM(o, lhsT, rhs, start, stop, r, w):
        S.op("pe", lambda e: e.matmul(out=o, lhsT=lhsT, rhs=rhs, start=start, stop=stop), r, w)

    def TR(o, in_, ident, r, w):
        S.op("pe", lambda e: e.transpose(out=o, in_=in_, identity=ident), r, w)

    def ACT(o, in_, func, r, w, bias=None, scale=None, accum=None):
        kw = {}
        if bias is not None:
            kw["bias"] = bias
        if scale is not None:
            kw["scale"] = scale
        if accum is not None:
            kw["accum_out"] = accum
        S.op("act", lambda e: e.activation(out=o, in_=in_, func=func, **kw), r, w)

    def TT(eng, o, in0, in1, op, r, w):
        S.op(eng, lambda e: e.tensor_tensor(out=o, in0=in0, in1=in1, op=op), r, w)

    def TS(eng, o, in0, s1, op0, r, w, s2=None, op1=None):
        if op1 is None:
            S.op(eng, lambda e: e.tensor_scalar(out=o, in0=in0, scalar1=s1, scalar2=None, op0=op0), r, w)
        else:
            S.op(eng, lambda e: e.tensor_scalar(out=o, in0=in0, scalar1=s1, scalar2=s2, op0=op0, op1=op1), r, w)

    def STT(o, in0, scalar, in1, op0, op1, r, w):
        S.op("dve", lambda e: e.scalar_tensor_tensor(out=o, in0=in0, scalar=scalar, in1=in1, op0=op0, op1=op1), r, w)

    def CP(eng, o, in_, r, w):
        if eng == "act":
            S.op("act", lambda e: e.copy(out=o, in_=in_), r, w)
        else:
            S.op(eng, lambda e: e.tensor_copy(out=o, in_=in_), r, w)

    def RECIP(o, in_, r, w):
        S.op("dve", lambda e: e.reciprocal(out=o, in_=in_), r, w)

    def MEMSET(eng, ap, val, w):
        S.op(eng, lambda e: e.memset(ap, val), (), w)

    def bc(ap, n=128):
        return ap.broadcast_to([n, ap.shape[-1]])

    with contextlib.ExitStack() as G:
        def sbg(name, shape, dt):
            return G.enter_context(nc.sbuf_tensor(name, list(shape), dt))

        ps = G.enter_context(nc.psum_tensor("ps", [128, 8, 512], F32))

        def PB(b):
            return ("ps", b)

        def psbf(b):
            return ps[:, b, :].bitcast(BF16)

        ident = sbg("ident", [128, 128], BF16)
        identf = sbg("identf", [128, 128], F32)
        U = sbg("U", [128, 128], F32)
        Lst = sbg("Lst", [128, 128], F32)
        ones_f = sbg("ones_f", [128, 128], F32)
        gmix_bc = sbg("gmix_bc", [128, D], F32)
        gffn_bc = sbg("gffn_bc", [128, D], F32)
        gqk_bc = sbg("gqk_bc", [128, 256], F32)
        subln_bc = sbg("subln_bc", [128, 128], F32)
        neglam = sbg("neglam", [128, 1], F32)
        lamt = sbg("lamt", [128, 4, 64], F32)
        lamp = sbg("lamp", [128, 2, 64], F32)
        lame = sbg("lame", [128, 2], F32)
        b31 = sbg("b31", [128, 8], F32)
        convw = sbg("convw", [128, 24, 4], F32)
        convb = sbg("convb", [128, 24], F32)
        fconvw = sbg("fconvw", [128, 44, 3], F32)
        fconvb = sbg("fconvb", [128, 44], F32)
        dtb_bc = sbg("dtb_bc", [128, 32], F32)
        a_bc = sbg("a_bc", [128, 32], F32)
        dsk_bc = sbg("dsk_bc", [128, 32], F32)
        RB = sbg("RB", [8, 32], F32)
        ext_sb = sbg("ext_sb", [8, 384], F32)

        MEMSET("pool", ones_f[:], 1.0, ["ones_f"])
        MEMSET("pool", identf[:], 0.0, ["identf"])
        S.op("pool", lambda e: e.affine_select(out=identf[:], in_=identf[:], pattern=[[-1, 128]],
                                               compare_op=ALU.not_equal, fill=1.0, base=0, channel_multiplier=1),
             ["identf"], ["identf"])
        CP("dve", ident[:], identf[:], ["identf"], ["ident"])
        S.op("pool", lambda e: e.affine_select(out=U[:], in_=ones_f[:], pattern=[[1, 128]],
                                               compare_op=ALU.is_ge, fill=0.0, base=0, channel_multiplier=-1),
             ["ones_f"], ["U"])
        S.op("pool", lambda e: e.affine_select(out=Lst[:], in_=ones_f[:], pattern=[[-1, 128]],
                                               compare_op=ALU.is_ge, fill=0.0, base=-1, channel_multiplier=1),
             ["ones_f"], ["Lst"])
        S.dma("sp", gmix_bc[:], bc(norm_mix_g), writes=["gmix"])
        S.dma("sp", gffn_bc[:], bc(norm_ffn_g), writes=["gffn"])
        S.dma("sp", gqk_bc[:, 0:64], bc(q_norm_g), writes=["gqk"])
        S.dma("sp", gqk_bc[:, 64:128], bc(q_norm_g), writes=["gqk"])
        S.dma("sp", gqk_bc[:, 128:192], bc(k_norm_g), writes=["gqk"])
        S.dma("sp", gqk_bc[:, 192:256], bc(k_norm_g), writes=["gqk"])
        ACT(gqk_bc[:, 0:128], gqk_bc[:, 0:128], AF.Identity, ["gqk"], ["gqk"], scale=0.125)
        S.dma("sp", subln_bc[:], bc(subln_g), writes=["subln"])
        ACT(subln_bc[:], subln_bc[:], AF.Identity, ["subln"], ["subln"], scale=0.8)
        for i, a in enumerate((lam_q1, lam_q2, lam_k1, lam_k2)):
            S.dma("sp", lamt[:, i, :], bc(a), writes=["lamt"])
        TT("dve", lamp[:], lamt[:, 0:2, :], lamt[:, 2:4, :], ALU.mult, ["lamt"], ["lamp"])
        S.op("dve", lambda e: e.tensor_reduce(out=lame[:], in_=lamp[:], axis=AX.X, op=ALU.add), ["lamp"], ["lame"])
        ACT(lame[:], lame[:], AF.Exp, ["lame"], ["lame"])
        TT("dve", neglam[:], lame[:, 1:2], lame[:, 0:1], ALU.subtract, ["lame"], ["neglam"])
        TS("dve", neglam[:], neglam[:], -0.2, ALU.add, ["neglam"], ["neglam"])
        S.dma("sp", b31[:], bc(rel_bias[31:32, :]), writes=["b31"])
        S.dma("sp", dtb_bc[:], bc(dt_bias), writes=["dtb"])
        S.dma("sp", dsk_bc[:], bc(d_skip), writes=["dsk"])
        S.dma("sp", a_bc[:], bc(a_log), writes=["a_bc"])
        ACT(a_bc[:], a_bc[:], AF.Exp, ["a_bc"], ["a_bc"])
        ACT(a_bc[:], a_bc[:], AF.Identity, ["a_bc"], ["a_bc"], scale=-1.0)
        stg = sbg("stg", [64, 9, 128], F32)
        rbs = sbg("rbs", [32, 8], F32)
        MEMSET("pool", stg[:], 0.0, ["stg"])
        S.dma("sp", stg[0:24, 0:4, :], conv_ssm_w.rearrange("k (cc p) -> cc k p", p=128), writes=["stg"])
        S.dma("sp", stg[0:24, 4, :], conv_ssm_b[0, :].rearrange("(cc p) -> cc p", p=128), writes=["stg"])
        S.dma("sp", stg[0:44, 5:8, :], conv_ffn_w.rearrange("k (cc p) -> cc k p", p=128), writes=["stg"])
        S.dma("sp", stg[0:44, 8, :], conv_ffn_b[0, :].rearrange("(cc p) -> cc p", p=128), writes=["stg"])
        S.dma("sp", rbs[:], rel_bias, writes=["rbs"])
        def pcol(k):
            return k * 32 if k < 5 else 160 + (k - 5) * 64
        for k in range(9):
            n = 32 if k < 5 else 64
            TR(ps[:, 0, pcol(k):pcol(k) + n], stg[0:n, k, :], identf[0:n, 0:n], ["stg", "identf"], [PB(0)])
        for k in range(4):
            CP("dve", convw[:, :, k], ps[:, 0, pcol(k):pcol(k) + 24], [PB(0)], ["convw"])
        CP("dve", convb[:], ps[:, 0, pcol(4):pcol(4) + 24], [PB(0)], ["convb"])
        for k in range(3):
            CP("dve", fconvw[:, :, k], ps[:, 0, pcol(5 + k):pcol(5 + k) + 44], [PB(0)], ["fconvw"])
        CP("dve", fconvb[:], ps[:, 0, pcol(8):pcol(8) + 44], [PB(0)], ["fconvb"])
        TR(ps[0:8, 1, 0:32], rbs[:], identf[0:32, 0:32], ["rbs", "identf"], [PB(1)])
        CP("dve", RB[:], ps[0:8, 1, 0:32], [PB(1)], ["RB"])
        MEMSET("pool", ext_sb[:], -30000.0, ["ext_sb"])
        CP("dve", ext_sb[:, 127:143], RB[:, 0:16], ["RB", "ext_sb"], ["ext_sb"])
        bt = t5_bucket_table(256)
        assert (bt[113:] == 31).all()
        for bk in range(16, 32):
            idx = np.nonzero(bt == bk)[0]
            if len(idx) == 0:
                continue
            n0, n1 = int(idx[0]), int(idx[-1]) + 1
            assert n1 - n0 == len(idx)
            CP("dve", ext_sb[:, 127 + n0:127 + n1], V(RB[:, bk:bk + 1], [[0, n1 - n0]]), ["RB", "ext_sb"], ["ext_sb"])
        S.dma("sp", ext_d, ext_sb[:, :], reads=["ext_sb"], writes=["ext_d"])
        S.dma("sp", Zd, bass.AP(tensor=ext_d.tensor, offset=ext_d.offset, ap=[[384, 8], [0, 128], [1, 384]]),
              reads=["ext_d"], writes=["Zd"])
        S.barrier()

        for b in range(nseq):
            with contextlib.ExitStack() as Q:
                def sbq(name, shape, dt):
                    return Q.enter_context(nc.sbuf_tensor(f"{name}_{b}", list(shape), dt))

                hT = sbq("hT", [128, 8, S_LEN], BF16)
                HT_ALL = [("hT", t) for t in range(NT)]

                with contextlib.ExitStack() as P:
                    def sbp(name, shape, dt):
                        return P.enter_context(nc.sbuf_tensor(f"{name}_{b}", list(shape), dt))
                    xts = [sbp(f"p0_xt{i}", [128, D], F32) for i in range(2)]
                    hns = [sbp(f"p0_hn{i}", [128, D], BF16) for i in range(2)]
                    junk = sbp("p0_junk", [128, D], BF16)
                    ssq = sbp("p0_ssq", [128, NT], F32)
                    rs = sbp("p0_rs", [128, NT], F32)
                    for t in range(NT):
                        i2 = t % 2
                        xt, hn = xts[i2], hns[i2]
                        S.dma("sp", xt[:], x[b, t * 128:(t + 1) * 128, :], writes=[("xt", i2)])
                        ACT(junk[:], xt[:], AF.Square, [("xt", i2)], ["junk", ("ssq", t)], accum=ssq[:, t:t + 1])
                        ACT(rs[:, t:t + 1], ssq[:, t:t + 1], AF.Sqrt, [("ssq", t)], [("rs", t)], bias=1e-6, scale=1.0 / D)
                        RECIP(rs[:, t:t + 1], rs[:, t:t + 1], [("rs", t)], [("rs", t)])
                        STT(hn[:], xt[:], rs[:, t:t + 1], gmix_bc[:], ALU.mult, ALU.mult,
                            [("xt", i2), ("rs", t), "gmix"], [("hn", i2)])
                        pb = 2 * i2
                        pv = psbf(pb)
                        for c in range(8):
                            TR(pv[:, c * 128:(c + 1) * 128], hn[:, c * 128:(c + 1) * 128], ident[:],
                               [("hn", i2), "ident"], [PB(pb)])
                        CP("act" if i2 else "dve", hT[:, :, t * 128:(t + 1) * 128],
                           V(pv, [[128, 8], [1, 128]]), [PB(pb)], [("hT", t)])
                    if dbg and b == 0:
                        finals.append(S.dma("sp", hT_d, hT[:], reads=HT_ALL))
                    S.barrier()
                if stage <= 0:
                    continue

                with contextlib.ExitStack() as P:
                    def sbp(name, shape, dt):
                        return P.enter_context(nc.sbuf_tensor(f"{name}_{b}", list(shape), dt))
                    NP = 20
                    BT = sbp("p1_BT", [128, 8, 256], F32)
                    wqkv = [sbp(f"p1_w{i}", [128, 8, 384], BF16) for i in range(2)]
                    qkT = [sbp(f"p1_qkT{i}", [128, 2, S_LEN], BF16) for i in range(2)]
                    vaug = [sbp(f"p1_v{i}", [128, NT, 132], BF16) for i in range(2)]
                    PT = [sbp(f"p1_PT{i}", [128, 2, 512], BF16) for i in range(NP)]
                    sq = [sbp(f"p1_sq{i}", [128, 256], F32) for i in range(2)]
                    tmpn = [sbp(f"p1_tmpn{i}", [128, 256], F32) for i in range(2)]
                    qkn = [sbp(f"p1_qkn{i}", [128, 256], BF16) for i in range(2)]
                    ssq4 = sbp("p1_ssq4", [128, NT, 4], F32)
                    rs4 = sbp("p1_rs4", [128, NT, 4], F32)
                    etmp = [sbp(f"p1_et{i}", [128, 2, 256], F32) for i in range(2)]
                    rl = sbp("p1_rl", [128, NT, 2], F32)
                    nrl = sbp("p1_nrl", [128, NT], F32)
                    o1 = [sbp(f"p1_o1{i}", [128, 128], F32) for i in range(2)]
                    oo = [sbp(f"p1_oo{i}", [128, 128], F32) for i in range(2)]
                    junk2 = sbp("p1_junk2", [128, 128], BF16)
                    sso = sbp("p1_sso", [128, NT], F32)
                    rso = sbp("p1_rso", [128, NT], F32)
                    yn = [sbp(f"p1_yn{i}", [128, 128], BF16) for i in range(2)]
                    yst = [sbp(f"p1_yst{i}", [128, 4, 128], BF16) for i in range(2)]

                    for h in range(8):
                        S.dma("sp", BT[:, h, :],
                              bass.AP(tensor=Zd.tensor, offset=Zd.offset + h * 128 * 384 + 127, ap=[[383, 128], [1, 256]]),
                              writes=["BT"])
                    for i in range(2):
                        MEMSET("pool", vaug[i][:, :, 128:129], 1.0, [("vaug", i)])
                    pt_ctr = 0
                    sbuf_ctr = 0
                    yst_ctr = 0
                    for h in range(8):
                        sl = h % 2
                        W = wqkv[sl]
                        for j3, c0 in enumerate((C_Q, C_K, C_V)):
                            S.dma("pool", W[:, :, j3 * 128:(j3 + 1) * 128], w_in_v[:, :, c0 + h * 128:c0 + (h + 1) * 128],
                                  writes=[("wqkv", sl)])
                        for t in range(NT):
                            i2 = t % 2
                            pb = 4 + 2 * i2
                            for kc in range(8):
                                MM(ps[:, pb, 0:384], hT[:, kc, t * 128:(t + 1) * 128], W[:, kc, :], kc == 0, kc == 7,
                                   [("hT", t), ("wqkv", sl)], [PB(pb)])
                            ACT(sq[i2][:], ps[:, pb, 0:256], AF.Square, [PB(pb)], [("sq", i2)])
                            S.op("dve", lambda e, i2=i2, t=t: e.tensor_reduce(
                                out=ssq4[:, t, :], in_=V(sq[i2][:], [[64, 4], [1, 64]]), axis=AX.X, op=ALU.add),
                                [("sq", i2)], [("ssq4", t)])
                            ACT(rs4[:, t, :], ssq4[:, t, :], AF.Sqrt, [("ssq4", t)], [("rs4", t)], bias=1e-6, scale=1.0 / 64)
                            RECIP(rs4[:, t, :], rs4[:, t, :], [("rs4", t)], [("rs4", t)])
                            TT("dve", V(tmpn[i2][:], [[64, 4], [1, 64]]), V(ps[:, pb, 0:256], [[64, 4], [1, 64]]),
                               V(rs4[:, t, :], [[1, 4], [0, 64]]), ALU.mult, [PB(pb), ("rs4", t)], [("tmpn", i2)])
                            TT("pool", qkn[i2][:], tmpn[i2][:], gqk_bc[:], ALU.mult, [("tmpn", i2), "gqk"], [("qkn", i2)])
                            CP("act", vaug[sl][:, t, 0:128], ps[:, pb, 256:384], [PB(pb)], [("vaug", sl)])
                            pv = psbf(2)
                            for m2 in range(2):
                                TR(pv[:, m2 * 128:(m2 + 1) * 128], qkn[i2][:, m2 * 128:(m2 + 1) * 128], ident[:],
                                   [("qkn", i2), "ident"], [PB(2)])
                            CP("act" if i2 else "dve", qkT[sl][:, :, t * 128:(t + 1) * 128], V(pv, [[128, 2], [1, 128]]),
                               [PB(2)], [("qkT", sl)])
                        for c in range(4):
                            PTc = {}
                            for j in range(4 * c + 4):
                                r = j - 4 * c
                                st = max(0, r) * 128
                                b0 = 4 + 2 * (sbuf_ctr % 2)
                                sbuf_ctr += 1
                                for m in range(2):
                                    MM(ps[:, b0 + m, st:512], qkT[sl][64 * m:64 * m + 64, 1, j * 128:(j + 1) * 128],
                                       qkT[sl][64 * m:64 * m + 64, 0, 512 * c + st:512 * c + 512], True, True,
                                       [("qkT", sl)], [PB(b0 + m)])
                                pi = pt_ctr % NP
                                pt_ctr += 1
                                PTc[j] = pi
                                Pt = PT[pi]
                                rb = [PB(b0), PB(b0 + 1)]
                                if r >= -1:
                                    if r >= 0:
                                        nb = min(2, 4 - r)
                                        btsl = BT[:, h, 0:128 * nb]
                                    else:
                                        nb = 1
                                        btsl = BT[:, h, 128:256]
                                    wdt = 128 * nb
                                    ei = sbuf_ctr % 2
                                    TT("dve", etmp[ei][:, :, 0:wdt], ps[:, b0:b0 + 2, st:st + wdt],
                                       V(btsl, [[0, 2], [1, wdt]]), ALU.add, rb + ["BT"], [("etmp", ei)])
                                    ACT(Pt[:, :, st:st + wdt], etmp[ei][:, :, 0:wdt], AF.Exp, [("etmp", ei)], [("PT", pi)])
                                    if st + wdt < 512:
                                        ACT(Pt[:, :, st + wdt:512], ps[:, b0:b0 + 2, st + wdt:512], AF.Exp,
                                            rb + ["b31"], [("PT", pi)], bias=b31[:, h:h + 1])
                                else:
                                    ACT(Pt[:, :, :], ps[:, b0:b0 + 2, :], AF.Exp, rb + ["b31"], [("PT", pi)],
                                        bias=b31[:, h:h + 1])
                            ys = yst[yst_ctr % 2]
                            ysk = ("yst", yst_ctr % 2)
                            yst_ctr += 1
                            for i in range(4 * c, 4 * c + 4):
                                ob = i % 2
                                i2 = i % 2
                                for m in range(2):
                                    for j in range(i + 1):
                                        MM(ps[:, ob, 256 * m:256 * m + 129],
                                           PT[PTc[j]][:, m, (i - 4 * c) * 128:(i - 4 * c + 1) * 128],
                                           vaug[sl][:, j, 0:129], j == 0, j == i,
                                           [("PT", PTc[j]), ("vaug", sl)], [PB(ob)])
                                RECIP(rl[:, i, :], V(ps[:, ob, 128:129], [[256, 2]]), [PB(ob)], [("rl", i)])
                                TS("dve", nrl[:, i:i + 1], rl[:, i, 1:2], neglam[:, 0:1], ALU.mult,
                                   [("rl", i), "neglam"], [("nrl", i)])
                                ACT(o1[i2][:], ps[:, ob, 0:128], AF.Identity, [PB(ob), ("rl", i)], [("o1", i2)],
                                    scale=rl[:, i, 0:1])
                                STT(oo[i2][:], ps[:, ob, 256:384], nrl[:, i:i + 1], o1[i2][:], ALU.mult, ALU.add,
                                    [PB(ob), ("nrl", i), ("o1", i2)], [("oo", i2)])
                                ACT(junk2[:], oo[i2][:], AF.Square, [("oo", i2)], ["junk2", ("sso", i)], accum=sso[:, i:i + 1])
                                ACT(rso[:, i:i + 1], sso[:, i:i + 1], AF.Sqrt, [("sso", i)], [("rso", i)],
                                    bias=1e-5, scale=1.0 / 128)
                                RECIP(rso[:, i:i + 1], rso[:, i:i + 1], [("rso", i)], [("rso", i)])
                                STT(yn[i2][:], oo[i2][:], rso[:, i:i + 1], subln_bc[:], ALU.mult, ALU.mult,
                                    [("oo", i2), ("rso", i), "subln"], [("yn", i2)])
                                pv = psbf(3)
                                TR(pv[:, (i % 4) * 128:(i % 4 + 1) * 128], yn[i2][:], ident[:], [("yn", i2), "ident"], [PB(3)])
                            CP("act", ys[:], V(psbf(3), [[128, 4], [1, 128]]), [PB(3)], [ysk])
                            dst = bass.AP(tensor=ya_d.tensor,
                                          offset=ya_d.offset + ((b * NT + 4 * c) * 128 * 8 + h) * 128,
                                          ap=[[8 * 128, 128], [128 * 8 * 128, 4], [1, 128]])
                            o_ = S.dma("sp", dst, ys[:], reads=[ysk], writes=[("ya_d", b, c)])
                            if dbg:
                                finals.append(o_)
                    S.barrier()
                if stage <= 1:
                    continue

                with contextlib.ExitStack() as P:
                    def sbp(name, shape, dt):
                        return P.enter_context(nc.sbuf_tensor(f"{name}_{b}", list(shape), dt))
                    ssmg_bc = sbp("p2_ssmg", [128, 2048], F32)
                    dt_all = sbp("p2_dt", [128, NT, 32], F32)
                    adt_all = sbp("p2_adt", [128, NT, 32], F32)
                    wdt_ = sbp("p2_wdt", [128, 8, 32], BF16)
                    dtt = [sbp(f"p2_dtt{i}", [128, 32], F32) for i in range(2)]
                    wx = [sbp(f"p2_wx{i}", [128, 8, 128], BF16) for i in range(3)]
                    wz = sbp("p2_wz", [128, 8, 512], BF16)
                    acc = [sbp(f"p2_acc{i}", [128, 1024], F32) for i in range(2)]
                    halo = sbp("p2_halo", [128, 4], F32)
                    fmx = [sbp(f"p2_fmx{i}", [128, S_LEN], BF16) for i in range(2)]
                    BTg = sbp("p2_BTg", [128, S_LEN], BF16)
                    CTg = sbp("p2_CTg", [128, S_LEN], BF16)
                    xs_tok = sbp("p2_xs", [128, NT, 512], BF16)
                    B_tok = sbp("p2_Btok", [128, NT, 128], BF16)
                    state = sbp("p2_state", [128, 512], F32)
                    state_bf = sbp("p2_statebf", [128, 512], BF16)
                    s1 = sbp("p2_s1", [128, 512], F32)
                    cumtot = [sbp(f"p2_ct{i}", [128, 16], F32) for i in range(2)]
                    ecum = [sbp(f"p2_ec{i}", [128, 16], F32) for i in range(2)]
                    w8 = [sbp(f"p2_w8{i}", [128, 8], F32) for i in range(2)]
                    adtU = [sbp(f"p2_adtU{i}", [128, 8, 128], F32) for i in range(2)]
                    eseg = [sbp(f"p2_eseg{i}", [128, 8, 128], BF16) for i in range(2)]
                    cbTm = [sbp(f"p2_cbT{i}", [128, 128], BF16) for i in range(2)]
                    MT = [sbp(f"p2_MT{i}", [128, 8, 128], BF16) for i in range(2)]
                    xdt = [sbp(f"p2_xdt{i}", [128, 8, 64], BF16) for i in range(2)]
                    xw = [sbp(f"p2_xw{i}", [128, 8, 64], BF16) for i in range(2)]
                    t1 = [sbp(f"p2_t1{i}", [128, 512], F32) for i in range(2)]
                    t2 = [sbp(f"p2_t2{i}", [128, 512], F32) for i in range(2)]
                    t3 = [sbp(f"p2_t3{i}", [128, 512], F32) for i in range(2)]
                    zs = [sbp(f"p2_zs{i}", [128, 512], F32) for i in range(2)]
                    junk3 = sbp("p2_junk3", [128, 512], BF16)
                    ssy = sbp("p2_ssy", [128, NT], F32)
                    rsy = sbp("p2_rsy", [128, NT], F32)
                    ynb = [sbp(f"p2_ynb{i}", [128, 512], BF16) for i in range(2)]
                    ysT = [sbp(f"p2_ysT{i}", [128, 4, 128], BF16) for i in range(2)]

                    S.dma("sp", ssmg_bc[:], bc(ssm_norm_g), writes=["ssmg"])
                    S.dma("pool", wdt_[:], w_in_v[:, :, C_DT:C_DT + 32], writes=["wdt"])
                    for t in range(NT):
                        i2 = t % 2
                        for kc in range(8):
                            MM(ps[:, 5, 0:32], hT[:, kc, t * 128:(t + 1) * 128], wdt_[:, kc, :], kc == 0, kc == 7,
                               [("hT", t), "wdt"], [PB(5)])
                        TT("dve", dtt[i2][:], ps[:, 5, 0:32], dtb_bc[:], ALU.add, [PB(5), "dtb"], [("dtt", i2)])
                        ACT(dtt[i2][:], dtt[i2][:], AF.Exp, [("dtt", i2)], [("dtt", i2)])
                        ACT(dt_all[:, t, :], dtt[i2][:], AF.Ln, [("dtt", i2)], [("dt", t)], bias=1.0, scale=1.0)
                        TT("pool", adt_all[:, t, :], dt_all[:, t, :], a_bc[:], ALU.mult, [("dt", t), "a_bc"], [("adt", t)])
                    if dbg and b == 0:
                        finals.append(S.dma("sp", dt_d, dt_all[:], reads=[("dt", t) for t in range(NT)]))

                    wx_ctr = 0
                    pp_ctr = 0
                    ab_ctr = 0
                    for g in range(4):
                        S.dma("pool", wz[:], w_in_v[:, :, C_Z + g * 512:C_Z + (g + 1) * 512], writes=["wz"])
                        for ci in range(6):
                            if ci < 4:
                                col0 = C_XS + g * 512 + ci * 128
                                cch = g * 4 + ci
                                dstT, dkey = fmx[ci % 2], ("fmx", ci % 2)
                            elif ci == 4:
                                col0, cch, dstT, dkey = C_B + g * 128, 16 + g, BTg, "BTg"
                            else:
                                col0, cch, dstT, dkey = C_C + g * 128, 20 + g, CTg, "CTg"
                            wi = wx_ctr % 3
                            wx_ctr += 1
                            S.dma("pool", wx[wi][:], w_in_v[:, :, col0:col0 + 128], writes=[("wx", wi)])
                            for half in range(2):
                                bp = 2 * (pp_ctr % 2)
                                pp_ctr += 1
                                ai = ab_ctr % 2
                                ab_ctr += 1
                                A = acc[ai]
                                ak = ("acc", ai)
                                for tt in range(2):
                                    tok0 = half * 1024 + tt * 512
                                    for kc in range(8):
                                        MM(ps[:, bp + tt, :], wx[wi][:, kc, :], hT[:, kc, tok0:tok0 + 512], kc == 0, kc == 7,
                                           [("wx", wi)] + [("hT", tok0 // 128 + q) for q in range(4)], [PB(bp + tt)])
                                pin = V(ps[:, bp, :], [[1, 1024]])
                                rb = [PB(bp), PB(bp + 1)]
                                ACT(A[:], pin, AF.Identity, rb + ["convw", "convb"], [ak],
                                    bias=convb[:, cch:cch + 1], scale=convw[:, cch, 3:4])
                                for k in (2, 1, 0):
                                    d_ = 3 - k
                                    STT(A[:, d_:1024], V(ps[:, bp, :], [[1, 1024 - d_]]), convw[:, cch, k:k + 1], A[:, d_:1024],
                                        ALU.mult, ALU.add, rb + ["convw", ak], [ak])
                                    if half == 1:
                                        STT(A[:, 0:d_], halo[:, 3 - d_:3], convw[:, cch, k:k + 1], A[:, 0:d_],
                                            ALU.mult, ALU.add, ["halo", "convw", ak], [ak])
                                if half == 0:
                                    CP("act", halo[:, 0:3], ps[:, bp + 1, 509:512], [PB(bp + 1)], ["halo"])
                                ACT(dstT[:, half * 1024:(half + 1) * 1024], A[:], AF.Silu, [ak], [dkey])
                            if ci < 5:
                                for tb in range(2):
                                    pv = psbf(4)
                                    for q in range(8):
                                        t = tb * 8 + q
                                        TR(pv[:, q * 128:(q + 1) * 128], dstT[:, t * 128:(t + 1) * 128], ident[:],
                                           [dkey, "ident"], [PB(4), "ps4a", "ps4b"])
                                    if ci < 4:
                                        CP("act", xs_tok[:, tb * 8:(tb + 1) * 8, ci * 128:(ci + 1) * 128],
                                           V(pv, [[128, 8], [1, 128]]), [PB(4), "ps4a", "ps4b"], ["xs_tok"])
                                    else:
                                        CP("act", B_tok[:, tb * 8:(tb + 1) * 8, :], V(pv, [[128, 8], [1, 128]]), [PB(4), "ps4a", "ps4b"], ["B_tok"])
                        MEMSET("pool", state[:], 0.0, ["state"])
                        MEMSET("pool", state_bf[:], 0.0, ["state_bf"])
                        for c in range(NT):
                            i2 = c % 2
                            cs = slice(c * 128, (c + 1) * 128)
                            adt_c = adt_all[:, c, g * 8:(g + 1) * 8]
                            dt_c = dt_all[:, c, g * 8:(g + 1) * 8]
                            MM(ps[:, 5, 0:8], U[:], adt_c, True, True, ["U", ("adt", c)], [PB(5)])
                            MM(ps[:, 5, 8:16], ones_f[:], adt_c, True, True, ["ones_f", ("adt", c)], [PB(5)])
                            CP("act", cumtot[i2][:], ps[:, 5, 0:16], [PB(5)], [("cumtot", i2)])
                            ACT(ecum[i2][:], cumtot[i2][:], AF.Exp, [("cumtot", i2)], [("ecum", i2)])
                            TT("dve", w8[i2][:], cumtot[i2][:, 8:16], cumtot[i2][:, 0:8], ALU.subtract, [("cumtot", i2)], [("w8", i2)])
                            ACT(w8[i2][:], w8[i2][:], AF.Exp, [("w8", i2)], [("w8", i2)])
                            TT("pool", adtU[i2][:], V(adt_c, [[1, 8], [0, 128]]), V(U[:], [[0, 8], [1, 128]]), ALU.mult,
                               [("adt", c), "U"], [("adtU", i2)])
                            for q in range(2):
                                MM(ps[:, q, :], Lst[:], adtU[i2][:, 4 * q:4 * q + 4, :], True, True, ["Lst", ("adtU", i2)], [PB(q)])
                            ACT(V(eseg[i2][:], [[512, 2], [1, 512]]), ps[:, 0:2, :], AF.Exp, [PB(0), PB(1)], [("eseg", i2)])
                            MM(ps[:, 4, 0:128], BTg[:, cs], CTg[:, cs], True, True, ["BTg", "CTg"], ["ps4a"])
                            TT("dve", cbTm[i2][:], ps[:, 4, 0:128], U[:], ALU.mult, ["ps4a", "U"], [("cbTm", i2)])
                            TT("dve", MT[i2][:], eseg[i2][:], V(cbTm[i2][:], [[0, 8], [1, 128]]), ALU.mult,
                               [("eseg", i2), ("cbTm", i2)], [("MT", i2)])
                            TT("pool", xdt[i2][:], V(xs_tok[:, c, :], [[64, 8], [1, 64]]), V(dt_c, [[1, 8], [0, 64]]), ALU.mult,
                               ["xs_tok", ("dt", c)], [("xdt", i2)])
                            TT("pool", xw[i2][:], xdt[i2][:], V(w8[i2][:], [[1, 8], [0, 64]]), ALU.mult,
                               [("xdt", i2), ("w8", i2)], [("xw", i2)])
                            for hh in range(8):
                                MM(ps[:, 2, hh * 64:(hh + 1) * 64], MT[i2][:, hh, :], xdt[i2][:, hh, :], True, True,
                                   [("MT", i2), ("xdt", i2)], [PB(2)])
                            MM(ps[:, 3, :], CTg[:, cs], state_bf[:], True, True, ["CTg", "state_bf"], [PB(3)])
                            for kc in range(8):
                                MM(ps[:, 6, :], hT[:, kc, cs], wz[:, kc, :], kc == 0, kc == 7, [("hT", c), "wz"], [PB(6)])
                            if c < NT - 1:
                                MM(ps[:, 7, :], B_tok[:, c, :], V(xw[i2][:], [[1, 512]]), True, True, ["B_tok", ("xw", i2)], [PB(7)])
                            TT("dve", V(t1[i2][:], [[64, 8], [1, 64]]), V(ps[:, 3, :], [[64, 8], [1, 64]]),
                               V(ecum[i2][:, 0:8], [[1, 8], [0, 64]]), ALU.mult, [PB(3), ("ecum", i2)], [("t1", i2)])
                            TT("dve", t2[i2][:], ps[:, 2, :], t1[i2][:], ALU.add, [PB(2), ("t1", i2)], [("t2", i2)])
                            TT("pool", V(t3[i2][:], [[64, 8], [1, 64]]), V(xs_tok[:, c, :], [[64, 8], [1, 64]]),
                               V(dsk_bc[:, g * 8:(g + 1) * 8], [[1, 8], [0, 64]]), ALU.mult, ["xs_tok", "dsk"], [("t3", i2)])
                            TT("pool", t3[i2][:], t3[i2][:], t2[i2][:], ALU.add, [("t3", i2), ("t2", i2)], [("t3", i2)])
                            ACT(zs[i2][:], ps[:, 6, :], AF.Silu, [PB(6)], [("zs", i2)])
                            TT("dve", t1[i2][:], t3[i2][:], zs[i2][:], ALU.mult, [("t3", i2), ("zs", i2)], [("t1", i2)])
                            ACT(junk3[:], t1[i2][:], AF.Square, [("t1", i2)], ["junk3", ("ssy", c)], accum=ssy[:, c:c + 1])
                            ACT(rsy[:, c:c + 1], ssy[:, c:c + 1], AF.Sqrt, [("ssy", c)], [("rsy", c)], bias=1e-5, scale=1.0 / 512)
                            RECIP(rsy[:, c:c + 1], rsy[:, c:c + 1], [("rsy", c)], [("rsy", c)])
                            STT(ynb[i2][:], t1[i2][:], rsy[:, c:c + 1], ssmg_bc[:, g * 512:(g + 1) * 512], ALU.mult, ALU.mult,
                                [("t1", i2), ("rsy", c), "ssmg"], [("ynb", i2)])
                            pv = psbf(4)
                            for q in range(4):
                                TR(pv[:, 512 + q * 128:512 + (q + 1) * 128], ynb[i2][:, q * 128:(q + 1) * 128], ident[:],
                                   [("ynb", i2), "ident"], ["ps4b"])
                            CP("act", ysT[i2][:], V(pv[:, 512:1024], [[128, 4], [1, 128]]), ["ps4b"], [("ysT", i2)])
                            dst = bass.AP(tensor=yss_d.tensor,
                                          offset=yss_d.offset + ((b * NT + c) * 128 * 16 + g * 4) * 128,
                                          ap=[[16 * 128, 128], [128, 4], [1, 128]])
                            o_ = S.dma("sp", dst, ysT[i2][:], reads=[("ysT", i2)], writes=[("yss_d", b, c, g)])
                            if dbg:
                                finals.append(o_)
                            if c < NT - 1:
                                TT("dve", V(s1[:], [[64, 8], [1, 64]]), V(state[:], [[64, 8], [1, 64]]),
                                   V(ecum[i2][:, 8:16], [[1, 8], [0, 64]]), ALU.mult, ["state", ("ecum", i2)], ["s1"])
                                TT("dve", state[:], ps[:, 7, :], s1[:], ALU.add, [PB(7), "s1"], ["state"])
                                CP("act", state_bf[:], state[:], ["state"], ["state_bf"])
                    S.barrier()
                if stage <= 2:
                    continue

                with contextlib.ExitStack() as P:
                    def sbp(name, shape, dt):
                        return P.enter_context(nc.sbuf_tensor(f"{name}_{b}", list(shape), dt))
                    wpa = sbp("p3_wpa", [128, 8, 1024], BF16)
                    wps = sbp("p3_wps", [128, 16, 1024], BF16)
                    wg = sbp("p3_wg", [128, 8, 2048], BF16)
                    wo = sbp("p3_wo", [128, 8, 1024], BF16)
                    yat = [sbp(f"p3_yat{i}", [128, 8, 128], BF16) for i in range(2)]
                    ysst = [sbp(f"p3_ysst{i}", [128, 16, 128], BF16) for i in range(2)]
                    xt3 = [sbp(f"p3_xt{i}", [128, D], F32) for i in range(2)]
                    sa = [sbp(f"p3_sa{i}", [128, 512], F32) for i in range(2)]
                    sg_ = [sbp(f"p3_sg{i}", [128, 512], F32) for i in range(2)]
                    mixed = [sbp(f"p3_mixed{i}", [128, D], BF16) for i in range(2)]
                    mixT = [sbp(f"p3_mixT{i}", [128, 8, 128], BF16) for i in range(2)]
                    x1 = [sbp(f"p3_x1{i}", [128, D], F32) for i in range(2)]
                    hn3 = [sbp(f"p3_hn{i}", [128, D], BF16) for i in range(2)]
                    junk4 = sbp("p3_junk", [128, D], BF16)
                    ss3 = sbp("p3_ss", [128, NT], F32)
                    rs3 = sbp("p3_rs", [128, NT], F32)
                    for q in range(2):
                        S.dma("pool", wpa[:, :, q * 512:(q + 1) * 512], w_pa_v[:, :, q * 512:(q + 1) * 512], writes=["wpa"])
                        S.dma("pool", wps[:, :, q * 512:(q + 1) * 512], w_ps_v[:, :, q * 512:(q + 1) * 512], writes=["wps"])
                    for q in range(4):
                        S.dma("pool", wg[:, :, q * 512:(q + 1) * 512], w_in_v[:, :, C_G + q * 512:C_G + (q + 1) * 512], writes=["wg"])
                    for q in range(2):
                        S.dma("pool", wo[:, :, q * 512:(q + 1) * 512], w_out_v[:, :, q * 512:(q + 1) * 512], writes=["wo"])
                    hc = 0
                    for t in range(NT):
                        i2 = t % 2
                        S.dma("sp", yat[i2][:], ya_d[b, t], reads=[("ya_d", b, t // 4)], writes=[("yat", i2)])
                        S.dma("sp", ysst[i2][:], yss_d[b, t], reads=[("yss_d", b, t, g) for g in range(4)], writes=[("ysst", i2)])
                        S.dma("sp", xt3[i2][:], x[b, t * 128:(t + 1) * 128, :], writes=[("xt3", i2)])
                        for j in range(2):
                            h2 = hc % 2
                            hc += 1
                            cj = slice(j * 512, (j + 1) * 512)
                            for c in range(8):
                                MM(ps[:, 0, :], yat[i2][:, c, :], wpa[:, c, cj], c == 0, c == 7, [("yat", i2), "wpa"], [PB(0)])
                            for c in range(16):
                                MM(ps[:, 1, :], ysst[i2][:, c, :], wps[:, c, cj], c == 0, c == 15, [("ysst", i2), "wps"], [PB(1)])
                            for kc in range(8):
                                MM(ps[:, 2, :], hT[:, kc, t * 128:(t + 1) * 128], wg[:, kc, cj], kc == 0, kc == 7,
                                   [("hT", t), "wg"], [PB(2)])
                            for kc in range(8):
                                MM(ps[:, 3, :], hT[:, kc, t * 128:(t + 1) * 128], wg[:, kc, 1024 + j * 512:1024 + (j + 1) * 512],
                                   kc == 0, kc == 7, [("hT", t), "wg"], [PB(3)])
                            ACT(sa[h2][:], ps[:, 2, :], AF.Sigmoid, [PB(2)], [("sa", h2)])
                            ACT(sg_[h2][:], ps[:, 3, :], AF.Sigmoid, [PB(3)], [("sg", h2)])
                            TT("dve", sa[h2][:], ps[:, 0, :], sa[h2][:], ALU.mult, [PB(0), ("sa", h2)], [("sa", h2)])
                            TT("dve", sg_[h2][:], ps[:, 1, :], sg_[h2][:], ALU.mult, [PB(1), ("sg", h2)], [("sg", h2)])
                            TT("pool", mixed[i2][:, cj], sa[h2][:], sg_[h2][:], ALU.add, [("sa", h2), ("sg", h2)], [("mixed", i2)])
                        pv = psbf(4)
                        for c in range(8):
                            TR(pv[:, c * 128:(c + 1) * 128], mixed[i2][:, c * 128:(c + 1) * 128], ident[:], [("mixed", i2), "ident"], [PB(4)])
                        CP("act", mixT[i2][:], V(pv, [[128, 8], [1, 128]]), [PB(4)], [("mixT", i2)])
                        for j in range(2):
                            for c in range(8):
                                MM(ps[:, 5 + j, :], mixT[i2][:, c, :], wo[:, c, j * 512:(j + 1) * 512], c == 0, c == 7,
                                   [("mixT", i2), "wo"], [PB(5 + j)])
                        TT("dve", x1[i2][:], V(ps[:, 5, :], [[1, 1024]]), xt3[i2][:], ALU.add, [PB(5), PB(6), ("xt3", i2)], [("x1", i2)])
                        o_ = S.dma("sp", out[b, t * 128:(t + 1) * 128, :], x1[i2][:], reads=[("x1", i2)], writes=[("x1d", b, t)])
                        if stage <= 3:
                            finals.append(o_)
                        ACT(junk4[:], x1[i2][:], AF.Square, [("x1", i2)], ["junk4", ("ss3", t)], accum=ss3[:, t:t + 1])
                        ACT(rs3[:, t:t + 1], ss3[:, t:t + 1], AF.Sqrt, [("ss3", t)], [("rs3", t)], bias=1e-6, scale=1.0 / D)
                        RECIP(rs3[:, t:t + 1], rs3[:, t:t + 1], [("rs3", t)], [("rs3", t)])
                        STT(hn3[i2][:], x1[i2][:], rs3[:, t:t + 1], gffn_bc[:], ALU.mult, ALU.mult,
                            [("x1", i2), ("rs3", t), "gffn"], [("hn3", i2)])
                        pv = psbf(7)
                        for c in range(8):
                            TR(pv[:, c * 128:(c + 1) * 128], hn3[i2][:, c * 128:(c + 1) * 128], ident[:], [("hn3", i2), "ident"], [PB(7)])
                        CP("act", hT[:, :, t * 128:(t + 1) * 128], V(pv, [[128, 8], [1, 128]]), [PB(7)], [("hT", t)])
                    S.barrier()
                if stage <= 3:
                    continue

                with contextlib.ExitStack() as P:
                    def sbp(name, shape, dt):
                        return P.enter_context(nc.sbuf_tensor(f"{name}_{b}", list(shape), dt))
                    wdn = sbp("p4_wdn", [128, NFC, 1024], BF16)
                    aT = sbp("p4_aT", [128, NFC, 1024], BF16)
                    wup = [sbp(f"p4_wup{i}", [128, 8, 256], BF16) for i in range(3)]
                    accg = [sbp(f"p4_accg{i}", [128, 1024], F32) for i in range(2)]
                    accv = [sbp(f"p4_accv{i}", [128, 1024], F32) for i in range(2)]
                    fhalo = sbp("p4_halo", [128, 2 * NFC, 2], F32)
                    x1t = [sbp(f"p4_x1t{i}", [128, D], F32) for i in range(2)]
                    ot = [sbp(f"p4_ot{i}", [128, D], F32) for i in range(2)]
                    for q in range(2):
                        S.dma("pool", wdn[:, 0:11, q * 512:(q + 1) * 512], w_dn_v[:, 0:11, q * 512:(q + 1) * 512], writes=["wdn"])
                        S.dma("pool", wdn[:, 11:22, q * 512:(q + 1) * 512], w_dn_v[:, 11:22, q * 512:(q + 1) * 512], writes=["wdn"])
                    wctr = 0
                    for blk in range(2):
                        for fc in range(NFC):
                            wi = wctr % 3
                            i2 = wctr % 2
                            wctr += 1
                            S.dma("pool", wup[wi][:, :, 0:128], w_up_v[:, :, fc * 128:(fc + 1) * 128], writes=[("wup", wi)])
                            S.dma("pool", wup[wi][:, :, 128:256], w_up_v[:, :, D_FF + fc * 128:D_FF + (fc + 1) * 128], writes=[("wup", wi)])
                            for part in range(2):
                                base = 4 * i2 + 2 * part
                                cch = part * NFC + fc
                                A = (accg if part == 0 else accv)[i2]
                                ak = ("accg" if part == 0 else "accv", i2)
                                for tt in range(2):
                                    tok0 = blk * 1024 + tt * 512
                                    for kc in range(8):
                                        MM(ps[:, base + tt, :], wup[wi][:, kc, part * 128:(part + 1) * 128], hT[:, kc, tok0:tok0 + 512],
                                           kc == 0, kc == 7, [("wup", wi)] + [("hT", tok0 // 128 + q) for q in range(4)], [PB(base + tt)])
                                rb = [PB(base), PB(base + 1)]
                                ACT(A[:], V(ps[:, base, :], [[1, 1024]]), AF.Identity, rb + ["fconvw", "fconvb"], [ak],
                                    bias=fconvb[:, cch:cch + 1], scale=fconvw[:, cch, 2:3])
                                for k in (1, 0):
                                    d_ = 2 - k
                                    STT(A[:, d_:1024], V(ps[:, base, :], [[1, 1024 - d_]]), fconvw[:, cch, k:k + 1], A[:, d_:1024],
                                        ALU.mult, ALU.add, rb + ["fconvw", ak], [ak])
                                    if blk == 1:
                                        STT(A[:, 0:d_], fhalo[:, cch, 2 - d_:2], fconvw[:, cch, k:k + 1], A[:, 0:d_],
                                            ALU.mult, ALU.add, [("fhalo", cch), "fconvw", ak], [ak])
                                if blk == 0:
                                    CP("act", fhalo[:, cch, :], ps[:, base + 1, 510:512], [PB(base + 1)], [("fhalo", cch)])
                            ACT(accg[i2][:], accg[i2][:], AF.Silu, [("accg", i2)], [("accg", i2)])
                            TT("pool", aT[:, fc, :], accg[i2][:], accv[i2][:], ALU.mult, [("accg", i2), ("accv", i2)], [("aT", fc)])
                        for tt in range(8):
                            t = blk * 8 + tt
                            i2 = t % 2
                            S.dma("sp", x1t[i2][:], out[b, t * 128:(t + 1) * 128, :], reads=[("x1d", b, t)], writes=[("x1t", i2)])
                            for j in range(2):
                                pb = 2 * i2 + j
                                for fc in range(NFC):
                                    MM(ps[:, pb, :], aT[:, fc, tt * 128:(tt + 1) * 128], wdn[:, fc, j * 512:(j + 1) * 512],
                                       fc == 0, fc == NFC - 1, [("aT", fc), "wdn"], [PB(pb)])
                            TT("dve", ot[i2][:], V(ps[:, 2 * i2, :], [[1, 1024]]), x1t[i2][:], ALU.add,
                               [PB(2 * i2), PB(2 * i2 + 1), ("x1t", i2)], [("ot", i2)])
                            finals.append(S.dma("sp", out[b, t * 128:(t + 1) * 128, :], ot[i2][:], reads=[("ot", i2)],
                                                writes=[("x1d", b, t)]))
                    S.barrier()
        S.emit(final_wait_ops=finals)
    return nc


_NC_CACHE = {}
PARAM_NAMES = ["rel_bias", "norm_mix_g", "w_in", "q_norm_g", "k_norm_g", "lambda_q1", "lambda_k1", "lambda_q2",
               "lambda_k2", "attn_subln_g", "conv_ssm_w", "conv_ssm_b", "dt_bias", "a_log", "d_skip", "ssm_norm_g",
               "w_proj_attn", "w_proj_ssm", "w_out", "norm_ffn_g", "w_up", "conv_ffn_w", "conv_ffn_b", "w_down"]


def make_in_maps(inputs, n_cores=8, nseq=2):
    x = np.ascontiguousarray(np.asarray(inputs["x"], dtype=np.float32))
    shared = {}
    for k in PARAM_NAMES:
        a = np.asarray(inputs[k], dtype=np.float32)
        if k == "rel_bias":
            shared[k] = np.ascontiguousarray(a)
        elif a.ndim == 2:
            shared[k] = np.ascontiguousarray(a[0:1])
        else:
            shared[k] = np.ascontiguousarray(a[0])
    in_maps = []
    for i in range(n_cores):
        m = dict(shared)
        m["x"] = np.ascontiguousarray(x[i * nseq:(i + 1) * nseq])
        in_maps.append(m)
    return in_maps


def kernel(**inputs):
    n_cores, nseq = 8, 2
    if "nc" not in _NC_CACHE:
        _NC_CACHE["nc"] = build(nseq=nseq)
    nc = _NC_CACHE["nc"]
    in_maps = make_in_maps(inputs, n_cores, nseq)
    res = run_bass_kernel_spmd(nc, in_maps, core_ids=list(range(n_cores)))
    return np.concatenate([np.asarray(r["out"], dtype=np.float32) for r in res.results], axis=0)
```

```python
import contextlib
import os
import numpy as np
import ml_dtypes
import concourse.bass as bass
import concourse.mybir as mybir
from concourse.bass_utils import run_bass_kernel_spmd

F32 = mybir.dt.float32
BF16 = mybir.dt.bfloat16
AF = mybir.ActivationFunctionType
ALU = mybir.AluOpType
AX = mybir.AxisListType

ENGS = ("pe", "act", "dve", "pool", "sp")

S_LEN = 2048
D = 1024
NT = 16
IN_COLS = 10272
C_Q, C_K, C_V, C_Z, C_XS, C_B, C_C, C_DT, C_G = 0, 1024, 2048, 3072, 5120, 7168, 7680, 8192, 8224
D_FF = 2816
NFC = 22


class Op:
    __slots__ = ("eng", "fn", "deps", "is_dma", "sig", "need_sig")

    def __init__(self, eng, fn, is_dma):
        self.eng = eng
        self.fn = fn
        self.is_dma = is_dma
        self.deps = []
        self.sig = None
        self.need_sig = False


class Sched:
    SEM_LIMIT = 30000

    def __init__(self, nc, n_dma_sems=12):
        self.nc = nc
        self.streams = {e: [] for e in ENGS}
        self.last_w = {}
        self.readers = {}
        self.n_dma_sems = n_dma_sems
        self.live_dmas = []

    def op(self, eng, fn, reads=(), writes=(), dma=False):
        o = Op(eng, fn, dma)
        deps = []
        for k in reads:
            w = self.last_w.get(k)
            if w is not None:
                deps.append(w)
        for k in writes:
            w = self.last_w.get(k)
            if w is not None:
                deps.append(w)
            deps.extend(self.readers.get(k, ()))
        seen = set()
        for d in deps:
            if d is o or id(d) in seen:
                continue
            seen.add(id(d))
            if (not d.is_dma) and d.eng == eng and eng == "pe":
                continue
            o.deps.append(d)
            d.need_sig = True
        for k in reads:
            self.readers.setdefault(k, []).append(o)
        for k in writes:
            self.last_w[k] = o
            self.readers[k] = []
        self.streams[eng].append(o)
        if dma:
            self.live_dmas.append(o)
        return o

    def dma(self, q, out, in_, reads=(), writes=(), **kw):
        return self.op(q, lambda e: e.dma_start(out=out, in_=in_, **kw), reads, writes, dma=True)

    def barrier(self):
        lasts = []
        for e in ENGS:
            for o in reversed(self.streams[e]):
                if o.fn is not None and not o.is_dma:
                    lasts.append(o)
                    break
        dmas = list(self.live_dmas)
        for e in ENGS:
            b = Op(e, None, False)
            for d in lasts:
                if d.eng != e:
                    b.deps.append(d)
                    d.need_sig = True
            b.deps.extend(dmas)
            self.streams[e].append(b)
        self.live_dmas = []
        self.last_w = {}
        self.readers = {}

    def emit(self, final_wait_ops=()):
        nc = self.nc
        with contextlib.ExitStack() as es:
            for e in ENGS:
                sigs = [o for o in self.streams[e] if o.need_sig and not o.is_dma]
                n_sems = max(1, (len(sigs) + self.SEM_LIMIT - 1) // self.SEM_LIMIT)
                sems = [es.enter_context(nc.semaphore(f"s_{e}_{i}")) for i in range(n_sems)]
                for cnt, o in enumerate(sigs):
                    o.sig = (sems[cnt // self.SEM_LIMIT], cnt % self.SEM_LIMIT + 1)
            dma_prev = {}
            for e in ENGS:
                dmas = [o for o in self.streams[e] if o.is_dma]
                if not dmas:
                    continue
                k = min(self.n_dma_sems, len(dmas))
                sems = [es.enter_context(nc.semaphore(f"d_{e}_{i}")) for i in range(k)]
                uses = [0] * k
                for cnt, o in enumerate(dmas):
                    s = cnt % k
                    uses[s] += 1
                    o.sig = (sems[s], 16 * uses[s])
                    dma_prev[id(o)] = (sems[s], 16 * (uses[s] - 1)) if uses[s] > 1 else None
            blk = es.enter_context(nc.Block())
            handles = {"pe": blk.tensor, "act": blk.scalar, "dve": blk.vector,
                       "pool": blk.gpsimd, "sp": blk.sync}
            for e in ENGS:
                ops = self.streams[e]

                def body(h, ops=ops, e=e):
                    waited = {}

                    def wait(sem, val):
                        if waited.get(id(sem), 0) >= val:
                            return
                        waited[id(sem)] = val
                        h.wait_ge(sem, val)

                    for o in ops:
                        for d in o.deps:
                            wait(*d.sig)
                        if o.fn is None:
                            continue
                        if o.is_dma:
                            p = dma_prev.get(id(o))
                            if p is not None:
                                wait(*p)
                        inst = o.fn(h)
                        if o.is_dma:
                            inst.then_inc(o.sig[0], 16)
                        elif o.need_sig:
                            inst.then_inc(o.sig[0], 1)
                    if e == "sp":
                        for o in final_wait_ops:
                            wait(*o.sig)

                handles[e](body)


def V(ap, dims, off=0):
    return bass.AP(tensor=ap.tensor, offset=ap.offset + off,
                   ap=[list(ap.ap[0])] + [list(d) for d in dims])


def t5_bucket_table(nmax=256):
    n = np.arange(nmax)
    nf = np.maximum(n, 1).astype(np.float32)
    large = 16 + (np.log(nf / np.float32(16)) / np.float32(np.log(128 / 16)) * np.float32(16)).astype(np.int32)
    large = np.minimum(large, 31)
    return np.where(n < 16, n, large)


YDEF = 4
CONV_SPLIT = False


def build(nseq=2, stage=99, dbg=False):
    nc = bass.Bass("TRN2", target_bir_lowering=False)

    def din(name, shape):
        return nc.dram_tensor(name, list(shape), F32, kind="ExternalInput").ap()

    x = din("x", [nseq, S_LEN, D])
    rel_bias = din("rel_bias", [32, 8])
    norm_mix_g = din("norm_mix_g", [1, D])
    w_in = din("w_in", [D, IN_COLS])
    q_norm_g = din("q_norm_g", [1, 64])
    k_norm_g = din("k_norm_g", [1, 64])
    lam_q1 = din("lambda_q1", [1, 64])
    lam_k1 = din("lambda_k1", [1, 64])
    lam_q2 = din("lambda_q2", [1, 64])
    lam_k2 = din("lambda_k2", [1, 64])
    subln_g = din("attn_subln_g", [1, 128])
    conv_ssm_w = din("conv_ssm_w", [4, 3072])
    conv_ssm_b = din("conv_ssm_b", [1, 3072])
    dt_bias = din("dt_bias", [1, 32])
    a_log = din("a_log", [1, 32])
    d_skip = din("d_skip", [1, 32])
    ssm_norm_g = din("ssm_norm_g", [1, 2048])
    w_pa = din("w_proj_attn", [1024, 1024])
    w_ps = din("w_proj_ssm", [2048, 1024])
    w_out = din("w_out", [1024, 1024])
    norm_ffn_g = din("norm_ffn_g", [1, D])
    w_up = din("w_up", [D, 2 * D_FF])
    conv_ffn_w = din("conv_ffn_w", [3, 2 * D_FF])
    conv_ffn_b = din("conv_ffn_b", [1, 2 * D_FF])
    w_down = din("w_down", [D_FF, D])
    out = nc.dram_tensor("out", [nseq, S_LEN, D], F32, kind="ExternalOutput").ap()

    skind = "ExternalOutput" if dbg else "Internal"
    ext_d = nc.dram_tensor("ext_d", [8, 384], F32, kind="Internal").ap()
    Zd = nc.dram_tensor("Zd", [8, 128, 384], F32, kind="Internal").ap()
    ya_d = nc.dram_tensor("ya_d", [nseq, NT, 128, 8, 128], BF16, kind=skind).ap()
    yss_d = nc.dram_tensor("yss_d", [nseq, NT, 128, 16, 128], BF16, kind=skind).ap()
    if dbg:
        hT_d = nc.dram_tensor("hT_d", [128, 8, S_LEN], BF16, kind="ExternalOutput").ap()
        dt_d = nc.dram_tensor("dt_d", [128, NT, 32], F32, kind="ExternalOutput").ap()

    w_in_v = w_in.rearrange("(kc p) c -> p kc c", p=128)
    w_up_v = w_up.rearrange("(kc p) c -> p kc c", p=128)
    w_pa_v = w_pa.rearrange("(kc p) c -> p kc c", p=128)
    w_ps_v = w_ps.rearrange("(kc p) c -> p kc c", p=128)
    w_out_v = w_out.rearrange("(kc p) c -> p kc c", p=128)
    w_dn_v = w_down.rearrange("(kc p) c -> p kc c", p=128)

    S = Sched(nc)
    finals = []

    def MM(o, lhsT, rhs, start, stop, r, w):
        S.op("pe", lambda e: e.matmul(out=o, lhsT=lhsT, rhs=rhs, start=start, stop=stop), r, w)

    def TR(o, in_, ident, r, w):
        S.op("pe", lambda e: e.transpose(out=o, in_=in_, identity=ident), r, w)

    def ACT(o, in_, func, r, w, bias=None, scale=None, accum=None):
        kw = {}
        if bias is not None:
            kw["bias"] = bias
        if scale is not None:
            kw["scale"] = scale
        if accum is not None:
            kw["accum_out"] = accum
        S.op("act", lambda e: e.activation(out=o, in_=in_, func=func, **kw), r, w)

    def TT(eng, o, in0, in1, op, r, w):
        S.op(eng, lambda e: e.tensor_tensor(out=o, in0=in0, in1=in1, op=op), r, w)

    def TS(eng, o, in0, s1, op0, r, w, s2=None, op1=None):
        if op1 is None:
            S.op(eng, lambda e: e.tensor_scalar(out=o, in0=in0, scalar1=s1, scalar2=None, op0=op0), r, w)
        else:
            S.op(eng, lambda e: e.tensor_scalar(out=o, in0=in0, scalar1=s1, scalar2=s2, op0=op0, op1=op1), r, w)

    def STT(o, in0, scalar, in1, op0, op1, r, w):
        S.op("dve", lambda e: e.scalar_tensor_tensor(out=o, in0=in0, scalar=scalar, in1=in1, op0=op0, op1=op1), r, w)

    def CP(eng, o, in_, r, w):
        if eng == "act":
            S.op("act", lambda e: e.copy(out=o, in_=in_), r, w)
        else:
            S.op(eng, lambda e: e.tensor_copy(out=o, in_=in_), r, w)

    def RECIP(o, in_, r, w):
        S.op("dve", lambda e: e.reciprocal(out=o, in_=in_), r, w)

    def MEMSET(eng, ap, val, w):
        S.op(eng, lambda e: e.memset(ap, val), (), w)

    def bc(ap, n=128):
        return ap.broadcast_to([n, ap.shape[-1]])

    with contextlib.ExitStack() as G:
        def sbg(name, shape, dt):
            return G.enter_context(nc.sbuf_tensor(name, list(shape), dt))

        ps = G.enter_context(nc.psum_tensor("ps", [128, 8, 512], F32))

        def PB(b):
            return ("ps", b)

        def psbf(b):
            return ps[:, b, :].bitcast(BF16)

        ident = sbg("ident", [128, 128], BF16)
        identf = sbg("identf", [128, 128], F32)
        U = sbg("U", [128, 128], F32)
        Lst = sbg("Lst", [128, 128], F32)
        ones_f = sbg("ones_f", [128, 128], F32)
        gmix_bc = sbg("gmix_bc", [128, D], F32)
        gffn_bc = sbg("gffn_bc", [128, D], F32)
        gqk_bc = sbg("gqk_bc", [128, 256], F32)
        subln_bc = sbg("subln_bc", [128, 128], F32)
        neglam = sbg("neglam", [128, 1], F32)
        lamt = sbg("lamt", [128, 4, 64], F32)
        lamp = sbg("lamp", [128, 2, 64], F32)
        lame = sbg("lame", [128, 2], F32)
        b31 = sbg("b31", [128, 8], F32)
        convw = sbg("convw", [128, 24, 4], F32)
        convb = sbg("convb", [128, 24], F32)
        fconvw = sbg("fconvw", [128, 44, 3], F32)
        fconvb = sbg("fconvb", [128, 44], F32)
        dtb_bc = sbg("dtb_bc", [128, 32], F32)
        a_bc = sbg("a_bc", [128, 32], F32)
        dsk_bc = sbg("dsk_bc", [128, 32], F32)
        RB = sbg("RB", [8, 32], F32)
        ext_sb = sbg("ext_sb", [8, 384], F32)

        MEMSET("pool", ones_f[:], 1.0, ["ones_f"])
        MEMSET("pool", identf[:], 0.0, ["identf"])
        S.op("pool", lambda e: e.affine_select(out=identf[:], in_=identf[:], pattern=[[-1, 128]],
                                               compare_op=ALU.not_equal, fill=1.0, base=0, channel_multiplier=1),
             ["identf"], ["identf"])
        CP("dve", ident[:], identf[:], ["identf"], ["ident"])
        S.op("pool", lambda e: e.affine_select(out=U[:], in_=ones_f[:], pattern=[[1, 128]],
                                               compare_op=ALU.is_ge, fill=0.0, base=0, channel_multiplier=-1),
             ["ones_f"], ["U"])
        S.op("pool", lambda e: e.affine_select(out=Lst[:], in_=ones_f[:], pattern=[[-1, 128]],
                                               compare_op=ALU.is_ge, fill=0.0, base=-1, channel_multiplier=1),
             ["ones_f"], ["Lst"])
        S.dma("sp", gmix_bc[:], bc(norm_mix_g), writes=["gmix"])
        S.dma("sp", gffn_bc[:], bc(norm_ffn_g), writes=["gffn"])
        S.dma("sp", gqk_bc[:, 0:64], bc(q_norm_g), writes=["gqk"])
        S.dma("sp", gqk_bc[:, 64:128], bc(q_norm_g), writes=["gqk"])
        S.dma("sp", gqk_bc[:, 128:192], bc(k_norm_g), writes=["gqk"])
        S.dma("sp", gqk_bc[:, 192:256], bc(k_norm_g), writes=["gqk"])
        ACT(gqk_bc[:, 0:128], gqk_bc[:, 0:128], AF.Identity, ["gqk"], ["gqk"], scale=0.125)
        S.dma("sp", subln_bc[:], bc(subln_g), writes=["subln"])
        ACT(subln_bc[:], subln_bc[:], AF.Identity, ["subln"], ["subln"], scale=0.8)
        for i, a in enumerate((lam_q1, lam_q2, lam_k1, lam_k2)):
            S.dma("sp", lamt[:, i, :], bc(a), writes=["lamt"])
        TT("dve", lamp[:], lamt[:, 0:2, :], lamt[:, 2:4, :], ALU.mult, ["lamt"], ["lamp"])
        S.op("dve", lambda e: e.tensor_reduce(out=lame[:], in_=lamp[:], axis=AX.X, op=ALU.add), ["lamp"], ["lame"])
        ACT(lame[:], lame[:], AF.Exp, ["lame"], ["lame"])
        TT("dve", neglam[:], lame[:, 1:2], lame[:, 0:1], ALU.subtract, ["lame"], ["neglam"])
        TS("dve", neglam[:], neglam[:], -0.2, ALU.add, ["neglam"], ["neglam"])
        S.dma("sp", b31[:], bc(rel_bias[31:32, :]), writes=["b31"])
        S.dma("sp", dtb_bc[:], bc(dt_bias), writes=["dtb"])
        S.dma("sp", dsk_bc[:], bc(d_skip), writes=["dsk"])
        S.dma("sp", a_bc[:], bc(a_log), writes=["a_bc"])
        ACT(a_bc[:], a_bc[:], AF.Exp, ["a_bc"], ["a_bc"])
        ACT(a_bc[:], a_bc[:], AF.Identity, ["a_bc"], ["a_bc"], scale=-1.0)
        stg = sbg("stg", [64, 9, 128], F32)
        rbs = sbg("rbs", [32, 8], F32)
        MEMSET("pool", stg[:], 0.0, ["stg"])
        S.dma("sp", stg[0:24, 0:4, :], conv_ssm_w.rearrange("k (cc p) -> cc k p", p=128), writes=["stg"])
        S.dma("sp", stg[0:24, 4, :], conv_ssm_b[0, :].rearrange("(cc p) -> cc p", p=128), writes=["stg"])
        S.dma("sp", stg[0:44, 5:8, :], conv_ffn_w.rearrange("k (cc p) -> cc k p", p=128), writes=["stg"])
        S.dma("sp", stg[0:44, 8, :], conv_ffn_b[0, :].rearrange("(cc p) -> cc p", p=128), writes=["stg"])
        S.dma("sp", rbs[:], rel_bias, writes=["rbs"])
        def pcol(k):
            return k * 32 if k < 5 else 160 + (k - 5) * 64
        for k in range(9):
            n = 32 if k < 5 else 64
            TR(ps[:, 0, pcol(k):pcol(k) + n], stg[0:n, k, :], identf[0:n, 0:n], ["stg", "identf"], [PB(0)])
        for k in range(4):
            CP("dve", convw[:, :, k], ps[:, 0, pcol(k):pcol(k) + 24], [PB(0)], ["convw"])
        CP("dve", convb[:], ps[:, 0, pcol(4):pcol(4) + 24], [PB(0)], ["convb"])
        for k in range(3):
            CP("dve", fconvw[:, :, k], ps[:, 0, pcol(5 + k):pcol(5 + k) + 44], [PB(0)], ["fconvw"])
        CP("dve", fconvb[:], ps[:, 0, pcol(8):pcol(8) + 44], [PB(0)], ["fconvb"])
        TR(ps[0:8, 1, 0:32], rbs[:], identf[0:32, 0:32], ["rbs", "identf"], [PB(1)])
        CP("dve", RB[:], ps[0:8, 1, 0:32], [PB(1)], ["RB"])
        MEMSET("pool", ext_sb[:], -30000.0, ["ext_sb"])
        CP("dve", ext_sb[:, 127:143], RB[:, 0:16], ["RB", "ext_sb"], ["ext_sb"])
        bt = t5_bucket_table(256)
        assert (bt[113:] == 31).all()
        for bk in range(16, 32):
            idx = np.nonzero(bt == bk)[0]
            if len(idx) == 0:
                continue
            n0, n1 = int(idx[0]), int(idx[-1]) + 1
            assert n1 - n0 == len(idx)
            CP("dve", ext_sb[:, 127 + n0:127 + n1], V(RB[:, bk:bk + 1], [[0, n1 - n0]]), ["RB", "ext_sb"], ["ext_sb"])
        S.dma("sp", ext_d, ext_sb[:, :], reads=["ext_sb"], writes=["ext_d"])
        S.dma("sp", Zd, bass.AP(tensor=ext_d.tensor, offset=ext_d.offset, ap=[[384, 8], [0, 128], [1, 384]]),
              reads=["ext_d"], writes=["Zd"])
        S.barrier()

        for b in range(nseq):
            with contextlib.ExitStack() as Q:
                def sbq(name, shape, dt):
                    return Q.enter_context(nc.sbuf_tensor(f"{name}_{b}", list(shape), dt))

                hT = sbq("hT", [128, 8, S_LEN], BF16)
                HT_ALL = [("hT", t) for t in range(NT)]

                with contextlib.ExitStack() as P:
                    def sbp(name, shape, dt):
                        return P.enter_context(nc.sbuf_tensor(f"{name}_{b}", list(shape), dt))
                    xts = [sbp(f"p0_xt{i}", [128, D], F32) for i in range(2)]
                    hns = [sbp(f"p0_hn{i}", [128, D], BF16) for i in range(2)]
                    junk = sbp("p0_junk", [128, D], BF16)
                    ssq = sbp("p0_ssq", [128, NT], F32)
                    rs = sbp("p0_rs", [128, NT], F32)
                    for t in range(NT):
                        i2 = t % 2
                        xt, hn = xts[i2], hns[i2]
                        S.dma("sp", xt[:], x[b, t * 128:(t + 1) * 128, :], writes=[("xt", i2)])
                        ACT(junk[:], xt[:], AF.Square, [("xt", i2)], ["junk", ("ssq", t)], accum=ssq[:, t:t + 1])
                        ACT(rs[:, t:t + 1], ssq[:, t:t + 1], AF.Sqrt, [("ssq", t)], [("rs", t)], bias=1e-6, scale=1.0 / D)
                        RECIP(rs[:, t:t + 1], rs[:, t:t + 1], [("rs", t)], [("rs", t)])
                        STT(hn[:], xt[:], rs[:, t:t + 1], gmix_bc[:], ALU.mult, ALU.mult,
                            [("xt", i2), ("rs", t), "gmix"], [("hn", i2)])
                        pb = 2 * i2
                        pv = psbf(pb)
                        for c in range(8):
                            TR(pv[:, c * 128:(c + 1) * 128], hn[:, c * 128:(c + 1) * 128], ident[:],
                               [("hn", i2), "ident"], [PB(pb)])
                        CP("act" if i2 else "dve", hT[:, :, t * 128:(t + 1) * 128],
                           V(pv, [[128, 8], [1, 128]]), [PB(pb)], [("hT", t)])
                    if dbg and b == 0:
                        finals.append(S.dma("sp", hT_d, hT[:], reads=HT_ALL))
                    S.barrier()
                if stage <= 0:
                    continue

                with contextlib.ExitStack() as P:
                    def sbp(name, shape, dt):
                        return P.enter_context(nc.sbuf_tensor(f"{name}_{b}", list(shape), dt))
                    NP = 28
                    BT = sbp("p1_BT", [128, 8, 256], F32)
                    wqkv = [sbp(f"p1_w{i}", [128, 8, 384], BF16) for i in range(2)]
                    qkT = [sbp(f"p1_qkT{i}", [128, 2, S_LEN], BF16) for i in range(2)]
                    vaug = [sbp(f"p1_v{i}", [128, NT, 132], BF16) for i in range(2)]
                    PT = [sbp(f"p1_PT{i}", [128, 2, 512], BF16) for i in range(NP)]
                    sq = [sbp(f"p1_sq{i}", [128, 256], F32) for i in range(2)]
                    tmpn = [sbp(f"p1_tmpn{i}", [128, 256], F32) for i in range(2)]
                    qkn = [sbp(f"p1_qkn{i}", [128, 256], BF16) for i in range(4)]
                    ssq4 = sbp("p1_ssq4", [128, NT, 4], F32)
                    rs4 = sbp("p1_rs4", [128, NT, 4], F32)
                    etmp = [sbp(f"p1_et{i}", [128, 2, 256], F32) for i in range(2)]
                    rl = sbp("p1_rl", [128, NT, 2], F32)
                    nrl = sbp("p1_nrl", [128, NT], F32)
                    o1 = [sbp(f"p1_o1{i}", [128, 128], F32) for i in range(2)]
                    oo = [sbp(f"p1_oo{i}", [128, 128], F32) for i in range(2)]
                    junk2 = sbp("p1_junk2", [128, 128], BF16)
                    sso = sbp("p1_sso", [128, NT], F32)
                    rso = sbp("p1_rso", [128, NT], F32)
                    yn = [sbp(f"p1_yn{i}", [128, 128], BF16) for i in range(4)]
                    yst = [sbp(f"p1_yst{i}", [128, 4, 128], BF16) for i in range(2)]

                    for h in range(8):
                        S.dma("sp", BT[:, h, :],
                              bass.AP(tensor=Zd.tensor, offset=Zd.offset + h * 128 * 384 + 127, ap=[[383, 128], [1, 256]]),
                              writes=[("BT", h)])
                    for i in range(2):
                        MEMSET("pool", vaug[i][:, :, 128:129], 1.0, [("vaug", i)])

                    pends = {"q": [], "y": []}

                    def tick():
                        for pend in pends.values():
                            for it in pend:
                                it[0] -= 1
                            while pend and pend[0][0] <= 0:
                                pend.pop(0)[1]()

                    def defer(n, fn, q="q"):
                        pends[q].append([n, fn])

                    def flush():
                        for pend in pends.values():
                            while pend:
                                pend.pop(0)[1]()

                    st_ = {"pt": 0, "sb": 0, "yst": 0, "qkn": 0, "yn": 0}

                    def load_w(h):
                        sl = h % 2
                        for j3, c0 in enumerate((C_Q, C_K, C_V)):
                            S.dma("pool", wqkv[sl][:, :, j3 * 128:(j3 + 1) * 128],
                                  w_in_v[:, :, c0 + h * 128:c0 + (h + 1) * 128], writes=[("wqkv", sl, j3)])

                    def proj_item(h, t):
                        def f():
                            sl = h % 2
                            W = wqkv[sl]
                            i2 = t % 2
                            pb = 4 + 2 * i2
                            for kc in range(8):
                                MM(ps[:, pb, 0:384], hT[:, kc, t * 128:(t + 1) * 128], W[:, kc, :], kc == 0, kc == 7,
                                   [("hT", t)] + [("wqkv", sl, q) for q in range(3)], [PB(pb)])
                            ACT(sq[i2][:], ps[:, pb, 0:256], AF.Square, [PB(pb)], [("sq", i2)])
                            S.op("dve", lambda e: e.tensor_reduce(
                                out=ssq4[:, t, :], in_=V(sq[i2][:], [[64, 4], [1, 64]]), axis=AX.X, op=ALU.add),
                                [("sq", i2)], [("ssq4", t)])
                            ACT(rs4[:, t, :], ssq4[:, t, :], AF.Ln, [("ssq4", t)], [("rs4", t)], bias=1e-6, scale=1.0 / 64)
                            ACT(rs4[:, t, :], rs4[:, t, :], AF.Exp, [("rs4", t)], [("rs4", t)], scale=-0.5)
                            TT("dve", V(tmpn[i2][:], [[64, 4], [1, 64]]), V(ps[:, pb, 0:256], [[64, 4], [1, 64]]),
                               V(rs4[:, t, :], [[1, 4], [0, 64]]), ALU.mult, [PB(pb), ("rs4", t)], [("tmpn", i2)])
                            qi = st_["qkn"] % 4
                            st_["qkn"] += 1
                            TT("pool", qkn[qi][:], tmpn[i2][:], gqk_bc[:], ALU.mult, [("tmpn", i2), "gqk"], [("qkn", qi)])
                            CP("act", vaug[sl][:, t, 0:128], ps[:, pb, 256:384], [PB(pb)], [("vaug", sl)])

                            def g():
                                pv = psbf(2)
                                hf = i2 * 512
                                for m2 in range(2):
                                    TR(pv[:, hf + m2 * 128:hf + (m2 + 1) * 128], qkn[qi][:, m2 * 128:(m2 + 1) * 128], ident[:],
                                       [("qkn", qi), "ident"], [PB(2)])
                                CP("act" if i2 else "dve", qkT[sl][:, :, t * 128:(t + 1) * 128],
                                   V(pv[:, hf:hf + 256], [[128, 2], [1, 128]]), [PB(2)], [("qkT", sl, t)])
                            defer(3, g)
                            tick()
                        return f

                    PTc = {}

                    def qk_item(h, c, j):
                        def f():
                            sl = h % 2
                            r = j - 4 * c
                            st = max(0, r) * 128
                            b0 = 4 + 2 * (st_["sb"] % 2)
                            st_["sb"] += 1
                            qkeys = [("qkT", sl, j)] + [("qkT", sl, q) for q in range(4 * c + st // 128, 4 * c + 4)]
                            for m in range(2):
                                MM(ps[:, b0 + m, st:512], qkT[sl][64 * m:64 * m + 64, 1, j * 128:(j + 1) * 128],
                                   qkT[sl][64 * m:64 * m + 64, 0, 512 * c + st:512 * c + 512], True, True,
                                   qkeys, [PB(b0 + m)])
                            pi = st_["pt"] % NP
                            st_["pt"] += 1
                            PTc[(h, c, j)] = pi
                            Pt = PT[pi]
                            rb = [PB(b0), PB(b0 + 1)]
                            if r >= -1:
                                if r >= 0:
                                    nb = min(2, 4 - r)
                                    btsl = BT[:, h, 0:128 * nb]
                                else:
                                    nb = 1
                                    btsl = BT[:, h, 128:256]
                                wdt = 128 * nb
                                ei = st_["sb"] % 2
                                TT("dve", etmp[ei][:, :, 0:wdt], ps[:, b0:b0 + 2, st:st + wdt],
                                   V(btsl, [[0, 2], [1, wdt]]), ALU.add, rb + [("BT", h)], [("etmp", ei)])
                                ACT(Pt[:, :, st:st + wdt], etmp[ei][:, :, 0:wdt], AF.Exp, [("etmp", ei)], [("PT", pi)])
                                if st + wdt < 512:
                                    ACT(Pt[:, :, st + wdt:512], ps[:, b0:b0 + 2, st + wdt:512], AF.Exp,
                                        rb + ["b31"], [("PT", pi)], bias=b31[:, h:h + 1])
                            else:
                                ACT(Pt[:, :, :], ps[:, b0:b0 + 2, :], AF.Exp, rb + ["b31"], [("PT", pi)],
                                    bias=b31[:, h:h + 1])
                            tick()
                        return f

                    def av_item(h, c, i, m):
                        def f():
                            sl = h % 2
                            ob = i % 2
                            i2 = i % 2
                            for j in range(i + 1):
                                pi = PTc[(h, c, j)]
                                MM(ps[:, ob, 256 * m:256 * m + 129],
                                   PT[pi][:, m, (i - 4 * c) * 128:(i - 4 * c + 1) * 128],
                                   vaug[sl][:, j, 0:129], j == 0, j == i,
                                   [("PT", pi), ("vaug", sl)], [PB(ob)])
                            if m == 1:
                                RECIP(rl[:, i, :], V(ps[:, ob, 128:129], [[256, 2]]), [PB(ob)], [("rl", i)])
                                TS("dve", nrl[:, i:i + 1], rl[:, i, 1:2], neglam[:, 0:1], ALU.mult,
                                   [("rl", i), "neglam"], [("nrl", i)])
                                ACT(o1[i2][:], ps[:, ob, 0:128], AF.Identity, [PB(ob), ("rl", i)], [("o1", i2)],
                                    scale=rl[:, i, 0:1])
                                STT(oo[i2][:], ps[:, ob, 256:384], nrl[:, i:i + 1], o1[i2][:], ALU.mult, ALU.add,
                                    [PB(ob), ("nrl", i), ("o1", i2)], [("oo", i2)])
                                ACT(junk2[:], oo[i2][:], AF.Square, [("oo", i2)], ["junk2", ("sso", i)], accum=sso[:, i:i + 1])
                                ACT(rso[:, i:i + 1], sso[:, i:i + 1], AF.Ln, [("sso", i)], [("rso", i)],
                                    bias=1e-5, scale=1.0 / 128)
                                ACT(rso[:, i:i + 1], rso[:, i:i + 1], AF.Exp, [("rso", i)], [("rso", i)], scale=-0.5)
                                yi = st_["yn"] % 4
                                st_["yn"] += 1
                                STT(yn[yi][:], oo[i2][:], rso[:, i:i + 1], subln_bc[:], ALU.mult, ALU.mult,
                                    [("oo", i2), ("rso", i), "subln"], [("yn", yi)])

                                def g():
                                    hf = c % 2
                                    pv = psbf(3)
                                    col = hf * 512 + (i % 4) * 128
                                    TR(pv[:, col:col + 128], yn[yi][:], ident[:], [("yn", yi), "ident"], [PB(3)])
                                    if i % 4 == 3:
                                        yc = st_["yst"] % 2
                                        st_["yst"] += 1
                                        ys = yst[yc]
                                        CP("act", ys[:], V(pv[:, hf * 512:hf * 512 + 512], [[128, 4], [1, 128]]),
                                           [PB(3)], [("yst", yc)])
                                        dst = bass.AP(tensor=ya_d.tensor,
                                                      offset=ya_d.offset + ((b * NT + 4 * c) * 128 * 8 + h) * 128,
                                                      ap=[[8 * 128, 128], [128 * 8 * 128, 4], [1, 128]])
                                        o_ = S.dma("sp", dst, ys[:], reads=[("yst", yc)], writes=[("ya_d", b, c, h)])
                                        if dbg:
                                            finals.append(o_)
                                defer(YDEF, g, "y")
                            tick()
                        return f

                    def run(items):
                        for it in items:
                            it()

                    def merge(A, B):
                        nb_done = 0
                        for idx, a in enumerate(A):
                            a()
                            tgt = (idx + 1) * len(B) // len(A)
                            while nb_done < tgt:
                                B[nb_done]()
                                nb_done += 1
                        while nb_done < len(B):
                            B[nb_done]()
                            nb_done += 1

                    def qk_items(h, c):
                        return [qk_item(h, c, j) for j in range(4 * c + 4)]

                    def av_items(h, c):
                        return [av_item(h, c, i, m) for i in range(4 * c, 4 * c + 4) for m in range(2)]

                    def proj_items(h):
                        return [proj_item(h, t) for t in range(NT)]

                    load_w(0)
                    run(proj_items(0))
                    for h in range(8):
                        if h < 7:
                            load_w(h + 1)
                        run(qk_items(h, 0))
                        for c in range(3):
                            merge(av_items(h, c), qk_items(h, c + 1))
                        merge(av_items(h, 3), proj_items(h + 1) if h < 7 else [])
                    flush()
                    S.barrier()
                if stage <= 1:
                    continue

                with contextlib.ExitStack() as P:
                    def sbp(name, shape, dt):
                        return P.enter_context(nc.sbuf_tensor(f"{name}_{b}", list(shape), dt))
                    ssmg_bc = sbp("p2_ssmg", [128, 2048], F32)
                    dt_all = sbp("p2_dt", [128, NT, 32], F32)
                    adt_all = sbp("p2_adt", [128, NT, 32], F32)
                    wdt_ = sbp("p2_wdt", [128, 8, 32], BF16)
                    dtt = [sbp(f"p2_dtt{i}", [128, 32], F32) for i in range(2)]
                    wx = [sbp(f"p2_wx{i}", [128, 8, 128], BF16) for i in range(3)]
                    wzs = [sbp(f"p2_wz{i}", [128, 8, 512], BF16) for i in range(2)]
                    acc = [sbp(f"p2_acc{i}", [128, 1024], F32) for i in range(2)]
                    accb = [sbp(f"p2_accb{i}", [128, 1024], F32) for i in range(2)]
                    zs_all = sbp("p2_zsall", [128, NT, 512], BF16)
                    halo = sbp("p2_halo", [128, 4], F32)
                    fmx = [sbp(f"p2_fmx{i}", [128, S_LEN], BF16) for i in range(2)]
                    BTg = sbp("p2_BTg", [128, S_LEN], BF16)
                    CTg = sbp("p2_CTg", [128, S_LEN], BF16)
                    xs_tok = sbp("p2_xs", [128, NT, 512], BF16)
                    B_tok = sbp("p2_Btok", [128, NT, 128], BF16)
                    state = sbp("p2_state", [128, 512], F32)
                    state_bf = sbp("p2_statebf", [128, 512], BF16)
                    s1 = sbp("p2_s1", [128, 512], F32)
                    cumtot = [sbp(f"p2_ct{i}", [128, 16], F32) for i in range(2)]
                    ecum = [sbp(f"p2_ec{i}", [128, 16], F32) for i in range(2)]
                    w8 = [sbp(f"p2_w8{i}", [128, 8], F32) for i in range(2)]
                    adtU = [sbp(f"p2_adtU{i}", [128, 8, 128], F32) for i in range(2)]
                    eseg = [sbp(f"p2_eseg{i}", [128, 8, 128], BF16) for i in range(2)]
                    cbTm = [sbp(f"p2_cbT{i}", [128, 128], BF16) for i in range(2)]
                    MT = [sbp(f"p2_MT{i}", [128, 8, 128], BF16) for i in range(2)]
                    xdt = [sbp(f"p2_xdt{i}", [128, 8, 64], BF16) for i in range(2)]
                    xw = [sbp(f"p2_xw{i}", [128, 8, 64], BF16) for i in range(2)]
                    t1 = [sbp(f"p2_t1{i}", [128, 512], F32) for i in range(2)]
                    t3 = [sbp(f"p2_t3{i}", [128, 512], F32) for i in range(2)]
                    junk3 = sbp("p2_junk3", [128, 512], BF16)
                    ssy = sbp("p2_ssy", [128, NT], F32)
                    rsy = sbp("p2_rsy", [128, NT], F32)
                    ynb = [sbp(f"p2_ynb{i}", [128, 512], BF16) for i in range(2)]
                    ysT = [sbp(f"p2_ysT{i}", [128, 4, 128], BF16) for i in range(2)]

                    S.dma("sp", ssmg_bc[:], bc(ssm_norm_g), writes=["ssmg"])
                    S.dma("pool", wdt_[:], w_in_v[:, :, C_DT:C_DT + 32], writes=["wdt"])
                    for t in range(NT):
                        i2 = t % 2
                        for kc in range(8):
                            MM(ps[:, 5, 0:32], hT[:, kc, t * 128:(t + 1) * 128], wdt_[:, kc, :], kc == 0, kc == 7,
                               [("hT", t), "wdt"], [PB(5)])
                        TT("dve", dtt[i2][:], ps[:, 5, 0:32], dtb_bc[:], ALU.add, [PB(5), "dtb"], [("dtt", i2)])
                        ACT(dtt[i2][:], dtt[i2][:], AF.Exp, [("dtt", i2)], [("dtt", i2)])
                        ACT(dt_all[:, t, :], dtt[i2][:], AF.Ln, [("dtt", i2)], [("dt", t)], bias=1.0, scale=1.0)
                        TT("pool", adt_all[:, t, :], dt_all[:, t, :], a_bc[:], ALU.mult, [("dt", t), "a_bc"], [("adt", t)])
                    if dbg and b == 0:
                        finals.append(S.dma("sp", dt_d, dt_all[:], reads=[("dt", t) for t in range(NT)]))

                    st2 = {"pp": 0, "ab": 0, "zb": 0}
                    pend2 = []

                    def chunk_desc(n):
                        g_, ci_ = n // 6, n % 6
                        if ci_ < 4:
                            return g_, ci_, C_XS + g_ * 512 + ci_ * 128, g_ * 4 + ci_, fmx[ci_ % 2], ("fmx", ci_ % 2)
                        if ci_ == 4:
                            return g_, ci_, C_B + g_ * 128, 16 + g_, BTg, "BTg"
                        return g_, ci_, C_C + g_ * 128, 20 + g_, CTg, "CTg"

                    def load_wx(n):
                        if n >= 24:
                            return
                        col0 = chunk_desc(n)[2]
                        S.dma("pool", wx[n % 3][:], w_in_v[:, :, col0:col0 + 128], writes=[("wx", n % 3)])

                    def load_wz(g_):
                        S.dma("pool", wzs[g_ % 2][:], w_in_v[:, :, C_Z + g_ * 512:C_Z + (g_ + 1) * 512], writes=[("wz", g_ % 2)])

                    def emit_pend2():
                        while pend2:
                            pend2.pop(0)()

                    load_wx(0)
                    load_wx(1)
                    load_wz(0)
                    for g in range(4):
                        for ci in range(6):
                            n = g * 6 + ci
                            _, _, col0, cch, dstT, dkey = chunk_desc(n)
                            wi = n % 3
                            load_wx(n + 2)
                            for half in range(2):
                                bp = 2 * (st2["pp"] % 2)
                                st2["pp"] += 1
                                ai = st2["ab"] % 2
                                st2["ab"] += 1
                                A, Bv = acc[ai], accb[ai]
                                ak, bk = ("acc", ai), ("accb", ai)
                                for tt in range(2):
                                    tok0 = half * 1024 + tt * 512
                                    for kc in range(8):
                                        MM(ps[:, bp + tt, :], wx[wi][:, kc, :], hT[:, kc, tok0:tok0 + 512], kc == 0, kc == 7,
                                           [("wx", wi)] + [("hT", tok0 // 128 + q) for q in range(4)], [PB(bp + tt)])
                                emit_pend2()
                                pin = V(ps[:, bp, :], [[1, 1024]])
                                pin1 = V(ps[:, bp, :], [[1, 1023]])
                                rb = [PB(bp), PB(bp + 1)]
                                if not CONV_SPLIT:
                                    ACT(A[:], pin, AF.Identity, rb + ["convw", "convb"], [ak],
                                        bias=convb[:, cch:cch + 1], scale=convw[:, cch, 3:4])
                                    for k in (2, 1, 0):
                                        d_ = 3 - k
                                        STT(A[:, d_:1024], V(ps[:, bp, :], [[1, 1024 - d_]]), convw[:, cch, k:k + 1], A[:, d_:1024],
                                            ALU.mult, ALU.add, rb + ["convw", ak], [ak])
                                        if half == 1:
                                            STT(A[:, 0:d_], halo[:, 3 - d_:3], convw[:, cch, k:k + 1], A[:, 0:d_],
                                                ALU.mult, ALU.add, ["halo", "convw", ak], [ak])
                                    if half == 0:
                                        CP("act", halo[:, 0:3], ps[:, bp + 1, 509:512], [PB(bp + 1)], ["halo"])
                                else:
                                    ACT(A[:], pin, AF.Identity, rb + ["convw", "convb"], [ak],
                                        bias=convb[:, cch:cch + 1], scale=convw[:, cch, 3:4])
                                    if "8" in os.environ.get("KF", ""):
                                        TS("dve", Bv[:], pin, convw[:, cch, 1:2], ALU.mult, rb + ["convw"], [bk])
                                    else:
                                        ACT(Bv[:], pin, AF.Identity, rb + ["convw"], [bk], scale=convw[:, cch, 1:2])
                                    STT(A[:, 1:1024], pin1, convw[:, cch, 2:3], A[:, 1:1024], ALU.mult, ALU.add, rb + ["convw", ak], [ak])
                                    STT(Bv[:, 1:1024], pin1, convw[:, cch, 0:1], Bv[:, 1:1024], ALU.mult, ALU.add, rb + ["convw", bk], [bk])
                                    if half == 1 and "7" not in os.environ.get("KF", ""):
                                        STT(A[:, 0:1], halo[:, 2:3], convw[:, cch, 2:3], A[:, 0:1], ALU.mult, ALU.add, ["halo", "convw", ak], [ak])
                                        STT(A[:, 0:2], halo[:, 1:3], convw[:, cch, 1:2], A[:, 0:2], ALU.mult, ALU.add, ["halo", "convw", ak], [ak])
                                        STT(A[:, 0:2], halo[:, 0:2], convw[:, cch, 0:1], A[:, 0:2], ALU.mult, ALU.add, ["halo", "convw", ak], [ak])
                                        STT(Bv[:, 0:1], halo[:, 2:3], convw[:, cch, 0:1], Bv[:, 0:1], ALU.mult, ALU.add, ["halo", "convw", bk], [bk])
                                    if half == 0:
                                        CP("act", halo[:, 0:3], ps[:, bp + 1, 509:512], [PB(bp + 1)], ["halo"])
                                    TT("dve" if "5" in os.environ.get("KF", "") else "pool", A[:, 2:1024], A[:, 2:1024], Bv[:, 0:1022], ALU.add, [ak, bk], [ak])
                                ACT(dstT[:, half * 1024:(half + 1) * 1024], A[:], AF.Silu, [ak], [dkey])
                            if ci < 5:
                                def trans(ci=ci, dstT=dstT, dkey=dkey):
                                    for tb in range(2):
                                        pv = psbf(4)
                                        for q in range(8):
                                            t = tb * 8 + q
                                            TR(pv[:, q * 128:(q + 1) * 128], dstT[:, t * 128:(t + 1) * 128], ident[:],
                                               [dkey, "ident"], [PB(4)])
                                        if ci < 4:
                                            CP("act", xs_tok[:, tb * 8:(tb + 1) * 8, ci * 128:(ci + 1) * 128],
                                               V(pv, [[128, 8], [1, 128]]), [PB(4)], ["xs_tok"])
                                        else:
                                            CP("act", B_tok[:, tb * 8:(tb + 1) * 8, :], V(pv, [[128, 8], [1, 128]]), [PB(4)], ["B_tok"])
                                pend2.append(trans)
                                if "1" in os.environ.get("KF", ""):
                                    emit_pend2()
                        if "a" in os.environ.get("KSTOP", ""):
                            break
                        wz = wzs[g % 2]
                        for t in range(NT):
                            zb = 6 + (st2["zb"] % 2)
                            st2["zb"] += 1
                            for kc in range(8):
                                MM(ps[:, zb, :], hT[:, kc, t * 128:(t + 1) * 128], wz[:, kc, :], kc == 0, kc == 7,
                                   [("hT", t), ("wz", g % 2)], [PB(zb)])
                            if t == 1:
                                emit_pend2()
                            ACT(zs_all[:, t, :], ps[:, zb, :], AF.Silu, [PB(zb)], [("zs", t)])
                        if "z" in os.environ.get("KSTOP", ""):
                            break
                        if g < 3:
                            load_wz(g + 1)
                        MEMSET("pool", state[:], 0.0, ["state"])
                        MEMSET("pool", state_bf[:], 0.0, ["state_bf"])

                        def stageA(c):
                            i2 = c % 2
                            cs = slice(c * 128, (c + 1) * 128)
                            adt_c = adt_all[:, c, g * 8:(g + 1) * 8]
                            dt_c = dt_all[:, c, g * 8:(g + 1) * 8]
                            MM(ps[:, 5, 0:8], U[:], adt_c, True, True, ["U", ("adt", c)], [PB(5)])
                            MM(ps[:, 5, 8:16], ones_f[:], adt_c, True, True, ["ones_f", ("adt", c)], [PB(5)])
                            CP("act", cumtot[i2][:], ps[:, 5, 0:16], [PB(5)], [("cumtot", i2)])
                            ACT(ecum[i2][:], cumtot[i2][:], AF.Exp, [("cumtot", i2)], [("ecum", i2)])
                            TT("dve", w8[i2][:], cumtot[i2][:, 8:16], cumtot[i2][:, 0:8], ALU.subtract, [("cumtot", i2)], [("w8", i2)])
                            ACT(w8[i2][:], w8[i2][:], AF.Exp, [("w8", i2)], [("w8", i2)])
                            TT("pool", adtU[i2][:], V(adt_c, [[1, 8], [0, 128]]), V(U[:], [[0, 8], [1, 128]]), ALU.mult,
                               [("adt", c), "U"], [("adtU", i2)])
                            for q in range(2):
                                MM(ps[:, q, :], Lst[:], adtU[i2][:, 4 * q:4 * q + 4, :], True, True, ["Lst", ("adtU", i2)], [PB(q)])
                            ACT(V(eseg[i2][:], [[512, 2], [1, 512]]), ps[:, 0:2, :], AF.Exp, [PB(0), PB(1)], [("eseg", i2)])
                            MM(ps[:, 4, 0:128], BTg[:, cs], CTg[:, cs], True, True, ["BTg", "CTg"], [PB(4)])
                            TT("dve", cbTm[i2][:], ps[:, 4, 0:128], U[:], ALU.mult, [PB(4), "U"], [("cbTm", i2)])
                            TT("dve", MT[i2][:], eseg[i2][:], V(cbTm[i2][:], [[0, 8], [1, 128]]), ALU.mult,
                               [("eseg", i2), ("cbTm", i2)], [("MT", i2)])
                            TT("pool", xdt[i2][:], V(xs_tok[:, c, :], [[64, 8], [1, 64]]), V(dt_c, [[1, 8], [0, 64]]), ALU.mult,
                               ["xs_tok", ("dt", c)], [("xdt", i2)])
                            TT("pool", xw[i2][:], xdt[i2][:], V(w8[i2][:], [[1, 8], [0, 64]]), ALU.mult,
                               [("xdt", i2), ("w8", i2)], [("xw", i2)])

                        def stageB(c):
                            i2 = c % 2
                            cs = slice(c * 128, (c + 1) * 128)
                            for hh in range(8):
                                MM(ps[:, 2, hh * 64:(hh + 1) * 64], MT[i2][:, hh, :], xdt[i2][:, hh, :], True, True,
                                   [("MT", i2), ("xdt", i2)], [PB(2)])
                            MM(ps[:, 3, :], CTg[:, cs], state_bf[:], True, True, ["CTg", "state_bf"], [PB(3)])
                            if c < NT - 1:
                                MM(ps[:, 7, :], B_tok[:, c, :], V(xw[i2][:], [[1, 512]]), True, True, ["B_tok", ("xw", i2)], [PB(7)])
                            emit_pend2()
                            TT("dve", V(t1[i2][:], [[64, 8], [1, 64]]), V(ps[:, 3, :], [[64, 8], [1, 64]]),
                               V(ecum[i2][:, 0:8], [[1, 8], [0, 64]]), ALU.mult, [PB(3), ("ecum", i2)], [("t1", i2)])
                            TT("dve", t1[i2][:], ps[:, 2, :], t1[i2][:], ALU.add, [PB(2), ("t1", i2)], [("t1", i2)])
                            TT("pool", V(t3[i2][:], [[64, 8], [1, 64]]), V(xs_tok[:, c, :], [[64, 8], [1, 64]]),
                               V(dsk_bc[:, g * 8:(g + 1) * 8], [[1, 8], [0, 64]]), ALU.mult, ["xs_tok", "dsk"], [("t3", i2)])
                            TT("pool", t3[i2][:], t3[i2][:], t1[i2][:], ALU.add, [("t3", i2), ("t1", i2)], [("t3", i2)])
                            TT("pool" if "4" in os.environ.get("KF", "") else "dve", t1[i2][:], t3[i2][:], zs_all[:, c, :], ALU.mult, [("t3", i2), ("zs", c)], [("t1", i2)])
                            ACT(junk3[:], t1[i2][:], AF.Square, [("t1", i2)], ["junk3", ("ssy", c)], accum=ssy[:, c:c + 1])
                            ACT(rsy[:, c:c + 1], ssy[:, c:c + 1], AF.Ln, [("ssy", c)], [("rsy", c)], bias=1e-5, scale=1.0 / 512)
                            ACT(rsy[:, c:c + 1], rsy[:, c:c + 1], AF.Exp, [("rsy", c)], [("rsy", c)], scale=-0.5)
                            STT(ynb[i2][:], t1[i2][:], rsy[:, c:c + 1], ssmg_bc[:, g * 512:(g + 1) * 512], ALU.mult, ALU.mult,
                                [("t1", i2), ("rsy", c), "ssmg"], [("ynb", i2)])

                            def trans_y(c=c, i2=i2, g=g):
                                pv = psbf(6)
                                for q in range(4):
                                    TR(pv[:, q * 128:(q + 1) * 128], ynb[i2][:, q * 128:(q + 1) * 128], ident[:],
                                       [("ynb", i2), "ident"], [PB(6)])
                                CP("act", ysT[i2][:], V(pv[:, 0:512], [[128, 4], [1, 128]]), [PB(6)], [("ysT", i2)])
                                dst = bass.AP(tensor=yss_d.tensor,
                                              offset=yss_d.offset + ((b * NT + c) * 128 * 16 + g * 4) * 128,
                                              ap=[[16 * 128, 128], [128, 4], [1, 128]])
                                o_ = S.dma("sp", dst, ysT[i2][:], reads=[("ysT", i2)], writes=[("yss_d", b, c, g)])
                                if dbg:
                                    finals.append(o_)
                            pend2.append(trans_y)
                            if "3" in os.environ.get("KF", ""):
                                emit_pend2()
                            if c < NT - 1:
                                TT("dve", V(s1[:], [[64, 8], [1, 64]]), V(state[:], [[64, 8], [1, 64]]),
                                   V(ecum[i2][:, 8:16], [[1, 8], [0, 64]]), ALU.mult, ["state", ("ecum", i2)], ["s1"])
                                TT("dve", state[:], ps[:, 7, :], s1[:], ALU.add, [PB(7), "s1"], ["state"])
                                CP("act", state_bf[:], state[:], ["state"], ["state_bf"])

                        if "2" in os.environ.get("KF", ""):
                            for c in range(NT):
                                stageA(c)
                                stageB(c)
                        else:
                            stageA(0)
                            for c in range(NT):
                                if c + 1 < NT:
                                    stageA(c + 1)
                                stageB(c)
                    emit_pend2()
                    S.barrier()
                if stage <= 2:
                    continue

                with contextlib.ExitStack() as P:
                    def sbp(name, shape, dt):
                        return P.enter_context(nc.sbuf_tensor(f"{name}_{b}", list(shape), dt))
                    wpa = sbp("p3_wpa", [128, 8, 1024], BF16)
                    wps = sbp("p3_wps", [128, 16, 1024], BF16)
                    wg = sbp("p3_wg", [128, 8, 2048], BF16)
                    wo = sbp("p3_wo", [128, 8, 1024], BF16)
                    yat = [sbp(f"p3_yat{i}", [128, 8, 128], BF16) for i in range(2)]
                    ysst = [sbp(f"p3_ysst{i}", [128, 16, 128], BF16) for i in range(2)]
                    xt3 = [sbp(f"p3_xt{i}", [128, D], F32) for i in range(2)]
                    sa = [sbp(f"p3_sa{i}", [128, 512], F32) for i in range(2)]
                    sg_ = [sbp(f"p3_sg{i}", [128, 512], F32) for i in range(2)]
                    mixed = [sbp(f"p3_mixed{i}", [128, D], BF16) for i in range(2)]
                    mixT = [sbp(f"p3_mixT{i}", [128, 8, 128], BF16) for i in range(2)]
                    x1 = [sbp(f"p3_x1{i}", [128, D], F32) for i in range(2)]
                    hn3 = [sbp(f"p3_hn{i}", [128, D], BF16) for i in range(2)]
                    junk4 = sbp("p3_junk", [128, D], BF16)
                    ss3 = sbp("p3_ss", [128, NT], F32)
                    rs3 = sbp("p3_rs", [128, NT], F32)
                    for q in range(2):
                        S.dma("pool", wpa[:, :, q * 512:(q + 1) * 512], w_pa_v[:, :, q * 512:(q + 1) * 512], writes=["wpa"])
                        S.dma("pool", wps[:, :, q * 512:(q + 1) * 512], w_ps_v[:, :, q * 512:(q + 1) * 512], writes=["wps"])
                    for q in range(4):
                        S.dma("pool", wg[:, :, q * 512:(q + 1) * 512], w_in_v[:, :, C_G + q * 512:C_G + (q + 1) * 512], writes=["wg"])
                    for q in range(2):
                        S.dma("pool", wo[:, :, q * 512:(q + 1) * 512], w_out_v[:, :, q * 512:(q + 1) * 512], writes=["wo"])
                    hc = 0
                    for t in range(NT):
                        i2 = t % 2
                        S.dma("sp", yat[i2][:], ya_d[b, t], reads=[("ya_d", b, t // 4, hh) for hh in range(8)], writes=[("yat", i2)])
                        S.dma("sp", ysst[i2][:], yss_d[b, t], reads=[("yss_d", b, t, g) for g in range(4)], writes=[("ysst", i2)])
                        S.dma("sp", xt3[i2][:], x[b, t * 128:(t + 1) * 128, :], writes=[("xt3", i2)])
                        for j in range(2):
                            h2 = hc % 2
                            hc += 1
                            cj = slice(j * 512, (j + 1) * 512)
                            for c in range(8):
                                MM(ps[:, 0, :], yat[i2][:, c, :], wpa[:, c, cj], c == 0, c == 7, [("yat", i2), "wpa"], [PB(0)])
                            for c in range(16):
                                MM(ps[:, 1, :], ysst[i2][:, c, :], wps[:, c, cj], c == 0, c == 15, [("ysst", i2), "wps"], [PB(1)])
                            for kc in range(8):
                                MM(ps[:, 2, :], hT[:, kc, t * 128:(t + 1) * 128], wg[:, kc, cj], kc == 0, kc == 7,
                                   [("hT", t), "wg"], [PB(2)])
                            for kc in range(8):
                                MM(ps[:, 3, :], hT[:, kc, t * 128:(t + 1) * 128], wg[:, kc, 1024 + j * 512:1024 + (j + 1) * 512],
                                   kc == 0, kc == 7, [("hT", t), "wg"], [PB(3)])
                            ACT(sa[h2][:], ps[:, 2, :], AF.Sigmoid, [PB(2)], [("sa", h2)])
                            ACT(sg_[h2][:], ps[:, 3, :], AF.Sigmoid, [PB(3)], [("sg", h2)])
                            TT("dve", sa[h2][:], ps[:, 0, :], sa[h2][:], ALU.mult, [PB(0), ("sa", h2)], [("sa", h2)])
                            TT("dve", sg_[h2][:], ps[:, 1, :], sg_[h2][:], ALU.mult, [PB(1), ("sg", h2)], [("sg", h2)])
                            TT("pool", mixed[i2][:, cj], sa[h2][:], sg_[h2][:], ALU.add, [("sa", h2), ("sg", h2)], [("mixed", i2)])
                        pv = psbf(4)
                        for c in range(8):
                            TR(pv[:, c * 128:(c + 1) * 128], mixed[i2][:, c * 128:(c + 1) * 128], ident[:], [("mixed", i2), "ident"], [PB(4)])
                        CP("act", mixT[i2][:], V(pv, [[128, 8], [1, 128]]), [PB(4)], [("mixT", i2)])
                        for j in range(2):
                            for c in range(8):
                                MM(ps[:, 5 + j, :], mixT[i2][:, c, :], wo[:, c, j * 512:(j + 1) * 512], c == 0, c == 7,
                                   [("mixT", i2), "wo"], [PB(5 + j)])
                        TT("dve", x1[i2][:], V(ps[:, 5, :], [[1, 1024]]), xt3[i2][:], ALU.add, [PB(5), PB(6), ("xt3", i2)], [("x1", i2)])
                        o_ = S.dma("sp", out[b, t * 128:(t + 1) * 128, :], x1[i2][:], reads=[("x1", i2)], writes=[("x1d", b, t)])
                        if stage <= 3:
                            finals.append(o_)
                        ACT(junk4[:], x1[i2][:], AF.Square, [("x1", i2)], ["junk4", ("ss3", t)], accum=ss3[:, t:t + 1])
                        ACT(rs3[:, t:t + 1], ss3[:, t:t + 1], AF.Sqrt, [("ss3", t)], [("rs3", t)], bias=1e-6, scale=1.0 / D)
                        RECIP(rs3[:, t:t + 1], rs3[:, t:t + 1], [("rs3", t)], [("rs3", t)])
                        STT(hn3[i2][:], x1[i2][:], rs3[:, t:t + 1], gffn_bc[:], ALU.mult, ALU.mult,
                            [("x1", i2), ("rs3", t), "gffn"], [("hn3", i2)])
                        pv = psbf(7)
                        for c in range(8):
                            TR(pv[:, c * 128:(c + 1) * 128], hn3[i2][:, c * 128:(c + 1) * 128], ident[:], [("hn3", i2), "ident"], [PB(7)])
                        CP("act", hT[:, :, t * 128:(t + 1) * 128], V(pv, [[128, 8], [1, 128]]), [PB(7)], [("hT", t)])
                    S.barrier()
                if stage <= 3:
                    continue

                with contextlib.ExitStack() as P:
                    def sbp(name, shape, dt):
                        return P.enter_context(nc.sbuf_tensor(f"{name}_{b}", list(shape), dt))
                    wdn = sbp("p4_wdn", [128, NFC, 1024], BF16)
                    aT = sbp("p4_aT", [128, NFC, 1024], BF16)
                    wup = [sbp(f"p4_wup{i}", [128, 8, 256], BF16) for i in range(3)]
                    accg = [sbp(f"p4_accg{i}", [128, 1024], F32) for i in range(2)]
                    accb = [sbp(f"p4_accb{i}", [128, 1024], F32) for i in range(2)]
                    accv = [sbp(f"p4_accv{i}", [128, 1024], F32) for i in range(2)]
                    fhalo = sbp("p4_halo", [128, 2 * NFC, 2], F32)
                    x1t = [sbp(f"p4_x1t{i}", [128, D], F32) for i in range(2)]
                    ot = [sbp(f"p4_ot{i}", [128, D], F32) for i in range(2)]

                    def load_wup(n):
                        fc_ = n % NFC
                        wi_ = n % 3
                        S.dma("pool", wup[wi_][:, :, 0:128], w_up_v[:, :, fc_ * 128:(fc_ + 1) * 128], writes=[("wup", wi_, 0)])
                        S.dma("pool", wup[wi_][:, :, 128:256], w_up_v[:, :, D_FF + fc_ * 128:D_FF + (fc_ + 1) * 128],
                              writes=[("wup", wi_, 1)])

                    load_wup(0)
                    load_wup(1)
                    for q in range(2):
                        S.dma("pool", wdn[:, 0:11, q * 512:(q + 1) * 512], w_dn_v[:, 0:11, q * 512:(q + 1) * 512], writes=[("wdn", q, 0)])
                        S.dma("pool", wdn[:, 11:22, q * 512:(q + 1) * 512], w_dn_v[:, 11:22, q * 512:(q + 1) * 512], writes=[("wdn", q, 1)])
                    WDN = [("wdn", q, r_) for q in range(2) for r_ in range(2)]
                    wctr = 0
                    for blk in range(2):
                        for fc in range(NFC):
                            wi = wctr % 3
                            i2 = wctr % 2
                            if wctr + 2 < 2 * NFC:
                                load_wup(wctr + 2)
                            wctr += 1
                            for part in range(2):
                                base = 4 * i2 + 2 * part
                                cch = part * NFC + fc
                                A = (accg if part == 0 else accv)[i2]
                                ak = ("accg" if part == 0 else "accv", i2)
                                for tt in range(2):
                                    tok0 = blk * 1024 + tt * 512
                                    for kc in range(8):
                                        MM(ps[:, base + tt, :], wup[wi][:, kc, part * 128:(part + 1) * 128], hT[:, kc, tok0:tok0 + 512],
                                           kc == 0, kc == 7, [("wup", wi, part)] + [("hT", tok0 // 128 + q) for q in range(4)], [PB(base + tt)])
                                rb = [PB(base), PB(base + 1)]
                                pin = V(ps[:, base, :], [[1, 1024]])
                                if not CONV_SPLIT:
                                    ACT(A[:], pin, AF.Identity, rb + ["fconvw", "fconvb"], [ak],
                                        bias=fconvb[:, cch:cch + 1], scale=fconvw[:, cch, 2:3])
                                    for k in (1, 0):
                                        d_ = 2 - k
                                        STT(A[:, d_:1024], V(ps[:, base, :], [[1, 1024 - d_]]), fconvw[:, cch, k:k + 1], A[:, d_:1024],
                                            ALU.mult, ALU.add, rb + ["fconvw", ak], [ak])
                                        if blk == 1:
                                            STT(A[:, 0:d_], fhalo[:, cch, 2 - d_:2], fconvw[:, cch, k:k + 1], A[:, 0:d_],
                                                ALU.mult, ALU.add, [("fhalo", cch), "fconvw", ak], [ak])
                                    if blk == 0:
                                        CP("act", fhalo[:, cch, :], ps[:, base + 1, 510:512], [PB(base + 1)], [("fhalo", cch)])
                                else:
                                    ACT(A[:], pin, AF.Identity, rb + ["fconvw", "fconvb"], [ak],
                                        bias=fconvb[:, cch:cch + 1], scale=fconvw[:, cch, 2:3])
                                    if part == 0:
                                        Bv = accb[i2]
                                        bk = ("accb", i2)
                                        ACT(Bv[:], pin, AF.Identity, rb + ["fconvw"], [bk], scale=fconvw[:, cch, 0:1])
                                    STT(A[:, 1:1024], V(ps[:, base, :], [[1, 1023]]), fconvw[:, cch, 1:2], A[:, 1:1024],
                                        ALU.mult, ALU.add, rb + ["fconvw", ak], [ak])
                                    if part == 1:
                                        STT(A[:, 2:1024], V(ps[:, base, :], [[1, 1022]]), fconvw[:, cch, 0:1], A[:, 2:1024],
                                            ALU.mult, ALU.add, rb + ["fconvw", ak], [ak])
                                    if blk == 1:
                                        STT(A[:, 0:1], fhalo[:, cch, 1:2], fconvw[:, cch, 1:2], A[:, 0:1],
                                            ALU.mult, ALU.add, [("fhalo", cch), "fconvw", ak], [ak])
                                        STT(A[:, 0:2], fhalo[:, cch, 0:2], fconvw[:, cch, 0:1], A[:, 0:2],
                                            ALU.mult, ALU.add, [("fhalo", cch), "fconvw", ak], [ak])
                                    if blk == 0:
                                        CP("act", fhalo[:, cch, :], ps[:, base + 1, 510:512], [PB(base + 1)], [("fhalo", cch)])
                                    if part == 0:
                                        TT("pool", A[:, 2:1024], A[:, 2:1024], Bv[:, 0:1022], ALU.add, [ak, bk], [ak])
                            ACT(accg[i2][:], accg[i2][:], AF.Silu, [("accg", i2)], [("accg", i2)])
                            TT("pool", aT[:, fc, :], accg[i2][:], accv[i2][:], ALU.mult, [("accg", i2), ("accv", i2)], [("aT", fc)])
                        for tt in range(8):
                            t = blk * 8 + tt
                            i2 = t % 2
                            S.dma("sp", x1t[i2][:], out[b, t * 128:(t + 1) * 128, :], reads=[("x1d", b, t)], writes=[("x1t", i2)])
                            for j in range(2):
                                pb = 2 * i2 + j
                                for fc in range(NFC):
                                    MM(ps[:, pb, :], aT[:, fc, tt * 128:(tt + 1) * 128], wdn[:, fc, j * 512:(j + 1) * 512],
                                       fc == 0, fc == NFC - 1, [("aT", fc)] + WDN, [PB(pb)])
                            TT("dve", ot[i2][:], V(ps[:, 2 * i2, :], [[1, 1024]]), x1t[i2][:], ALU.add,
                               [PB(2 * i2), PB(2 * i2 + 1), ("x1t", i2)], [("ot", i2)])
                            finals.append(S.dma("sp", out[b, t * 128:(t + 1) * 128, :], ot[i2][:], reads=[("ot", i2)],
                                                writes=[("x1d", b, t)]))
                    S.barrier()
        S.emit(final_wait_ops=finals)
    return nc


_NC_CACHE = {}
PARAM_NAMES = ["rel_bias", "norm_mix_g", "w_in", "q_norm_g", "k_norm_g", "lambda_q1", "lambda_k1", "lambda_q2",
               "lambda_k2", "attn_subln_g", "conv_ssm_w", "conv_ssm_b", "dt_bias", "a_log", "d_skip", "ssm_norm_g",
               "w_proj_attn", "w_proj_ssm", "w_out", "norm_ffn_g", "w_up", "conv_ffn_w", "conv_ffn_b", "w_down"]


def make_in_maps(inputs, n_cores=8, nseq=2):
    x = np.ascontiguousarray(np.asarray(inputs["x"], dtype=np.float32))
    shared = {}
    for k in PARAM_NAMES:
        a = np.asarray(inputs[k], dtype=np.float32)
        if k == "rel_bias":
            shared[k] = np.ascontiguousarray(a)
        elif a.ndim == 2:
            shared[k] = np.ascontiguousarray(a[0:1])
        else:
            shared[k] = np.ascontiguousarray(a[0])
    in_maps = []
    for i in range(n_cores):
        m = dict(shared)
        m["x"] = np.ascontiguousarray(x[i * nseq:(i + 1) * nseq])
        in_maps.append(m)
    return in_maps


def kernel(**inputs):
    n_cores, nseq = 8, 2
    if "nc" not in _NC_CACHE:
        _NC_CACHE["nc"] = build(nseq=nseq)
    nc = _NC_CACHE["nc"]
    in_maps = make_in_maps(inputs, n_cores, nseq)
    res = run_bass_kernel_spmd(nc, in_maps, core_ids=list(range(n_cores)))
    return np.concatenate([np.asarray(r["out"], dtype=np.float32) for r in res.results], axis=0)
```

```python
import contextlib
import os
import numpy as np
import ml_dtypes
import concourse.bass as bass
import concourse.mybir as mybir
from concourse.bass_utils import run_bass_kernel_spmd

F32 = mybir.dt.float32
BF16 = mybir.dt.bfloat16
AF = mybir.ActivationFunctionType
ALU = mybir.AluOpType
AX = mybir.AxisListType

ENGS = ("pe", "act", "dve", "pool", "sp")

S_LEN = 2048
D = 1024
NT = 16
IN_COLS = 10272
C_Q, C_K, C_V, C_Z, C_XS, C_B, C_C, C_DT, C_G = 0, 1024, 2048, 3072, 5120, 7168, 7680, 8192, 8224
D_FF = 2816
NFC = 22


class Op:
    __slots__ = ("eng", "fn", "deps", "is_dma", "sig", "need_sig")

    def __init__(self, eng, fn, is_dma):
        self.eng = eng
        self.fn = fn
        self.is_dma = is_dma
        self.deps = []
        self.sig = None
        self.need_sig = False


class Sched:
    SEM_LIMIT = 30000

    def __init__(self, nc, n_dma_sems=12):
        self.nc = nc
        self.streams = {e: [] for e in ENGS}
        self.last_w = {}
        self.readers = {}
        self.n_dma_sems = n_dma_sems
        self.live_dmas = []

    def op(self, eng, fn, reads=(), writes=(), dma=False):
        o = Op(eng, fn, dma)
        deps = []
        for k in reads:
            w = self.last_w.get(k)
            if w is not None:
                deps.append(w)
            if isinstance(k, tuple) and k[0] == "ps":
                for r in self.readers.get(k, ()):
                    if r.eng != eng:
                        deps.append(r)
        for k in writes:
            w = self.last_w.get(k)
            if w is not None:
                deps.append(w)
            deps.extend(self.readers.get(k, ()))
        seen = set()
        for d in deps:
            if d is o or id(d) in seen:
                continue
            seen.add(id(d))
            if (not d.is_dma) and d.eng == eng and eng == "pe":
                continue
            o.deps.append(d)
            d.need_sig = True
        for k in reads:
            self.readers.setdefault(k, []).append(o)
        for k in writes:
            self.last_w[k] = o
            self.readers[k] = []
        self.streams[eng].append(o)
        if dma:
            self.live_dmas.append(o)
        return o

    def dma(self, q, out, in_, reads=(), writes=(), **kw):
        return self.op(q, lambda e: e.dma_start(out=out, in_=in_, **kw), reads, writes, dma=True)

    def barrier(self):
        lasts = []
        for e in ENGS:
            for o in reversed(self.streams[e]):
                if o.fn is not None and not o.is_dma:
                    lasts.append(o)
                    break
        dmas = list(self.live_dmas)
        for e in ENGS:
            b = Op(e, None, False)
            for d in lasts:
                if d.eng != e:
                    b.deps.append(d)
                    d.need_sig = True
            b.deps.extend(dmas)
            self.streams[e].append(b)
        self.live_dmas = []
        self.last_w = {}
        self.readers = {}

    def emit(self, final_wait_ops=()):
        nc = self.nc
        with contextlib.ExitStack() as es:
            for e in ENGS:
                sigs = [o for o in self.streams[e] if o.need_sig and not o.is_dma]
                n_sems = max(1, (len(sigs) + self.SEM_LIMIT - 1) // self.SEM_LIMIT)
                sems = [es.enter_context(nc.semaphore(f"s_{e}_{i}")) for i in range(n_sems)]
                for cnt, o in enumerate(sigs):
                    o.sig = (sems[cnt // self.SEM_LIMIT], cnt % self.SEM_LIMIT + 1)
            dma_prev = {}
            for e in ENGS:
                dmas = [o for o in self.streams[e] if o.is_dma]
                if not dmas:
                    continue
                k = min(self.n_dma_sems, len(dmas))
                sems = [es.enter_context(nc.semaphore(f"d_{e}_{i}")) for i in range(k)]
                uses = [0] * k
                for cnt, o in enumerate(dmas):
                    s = cnt % k
                    uses[s] += 1
                    o.sig = (sems[s], 16 * uses[s])
                    dma_prev[id(o)] = (sems[s], 16 * (uses[s] - 1)) if uses[s] > 1 else None
            blk = es.enter_context(nc.Block())
            handles = {"pe": blk.tensor, "act": blk.scalar, "dve": blk.vector,
                       "pool": blk.gpsimd, "sp": blk.sync}
            for e in ENGS:
                ops = self.streams[e]

                def body(h, ops=ops, e=e):
                    waited = {}

                    def wait(sem, val):
                        if waited.get(id(sem), 0) >= val:
                            return
                        waited[id(sem)] = val
                        h.wait_ge(sem, val)

                    for o in ops:
                        for d in o.deps:
                            wait(*d.sig)
                        if o.fn is None:
                            continue
                        if o.is_dma:
                            p = dma_prev.get(id(o))
                            if p is not None:
                                wait(*p)
                        inst = o.fn(h)
                        if o.is_dma:
                            inst.then_inc(o.sig[0], 16)
                        elif o.need_sig:
                            inst.then_inc(o.sig[0], 1)
                    if e == "sp":
                        for o in final_wait_ops:
                            wait(*o.sig)

                handles[e](body)


def V(ap, dims, off=0):
    return bass.AP(tensor=ap.tensor, offset=ap.offset + off,
                   ap=[list(ap.ap[0])] + [list(d) for d in dims])


def t5_bucket_table(nmax=256):
    n = np.arange(nmax)
    nf = np.maximum(n, 1).astype(np.float32)
    large = 16 + (np.log(nf / np.float32(16)) / np.float32(np.log(128 / 16)) * np.float32(16)).astype(np.int32)
    large = np.minimum(large, 31)
    return np.where(n < 16, n, large)


YDEF = 4
CONV_SPLIT = True


def build(nseq=2, stage=99, dbg=False):
    nc = bass.Bass("TRN2", target_bir_lowering=False)

    def din(name, shape):
        return nc.dram_tensor(name, list(shape), F32, kind="ExternalInput").ap()

    x = din("x", [nseq, S_LEN, D])
    rel_bias = din("rel_bias", [32, 8])
    norm_mix_g = din("norm_mix_g", [1, D])
    w_in = din("w_in", [D, IN_COLS])
    q_norm_g = din("q_norm_g", [1, 64])
    k_norm_g = din("k_norm_g", [1, 64])
    lam_q1 = din("lambda_q1", [1, 64])
    lam_k1 = din("lambda_k1", [1, 64])
    lam_q2 = din("lambda_q2", [1, 64])
    lam_k2 = din("lambda_k2", [1, 64])
    subln_g = din("attn_subln_g", [1, 128])
    conv_ssm_w = din("conv_ssm_w", [4, 3072])
    conv_ssm_b = din("conv_ssm_b", [1, 3072])
    dt_bias = din("dt_bias", [1, 32])
    a_log = din("a_log", [1, 32])
    d_skip = din("d_skip", [1, 32])
    ssm_norm_g = din("ssm_norm_g", [1, 2048])
    w_pa = din("w_proj_attn", [1024, 1024])
    w_ps = din("w_proj_ssm", [2048, 1024])
    w_out = din("w_out", [1024, 1024])
    norm_ffn_g = din("norm_ffn_g", [1, D])
    w_up = din("w_up", [D, 2 * D_FF])
    conv_ffn_w = din("conv_ffn_w", [3, 2 * D_FF])
    conv_ffn_b = din("conv_ffn_b", [1, 2 * D_FF])
    w_down = din("w_down", [D_FF, D])
    out = nc.dram_tensor("out", [nseq, S_LEN, D], F32, kind="ExternalOutput").ap()

    skind = "ExternalOutput" if dbg else "Internal"
    ext_d = nc.dram_tensor("ext_d", [8, 384], F32, kind="Internal").ap()
    Zd = nc.dram_tensor("Zd", [8, 128, 384], F32, kind="Internal").ap()
    ya_d = nc.dram_tensor("ya_d", [nseq, NT, 128, 8, 128], BF16, kind=skind).ap()
    yss_d = nc.dram_tensor("yss_d", [nseq, NT, 128, 16, 128], BF16, kind=skind).ap()
    if dbg:
        hT_d = nc.dram_tensor("hT_d", [128, 8, S_LEN], BF16, kind="ExternalOutput").ap()
        dt_d = nc.dram_tensor("dt_d", [128, NT, 32], F32, kind="ExternalOutput").ap()

    w_in_v = w_in.rearrange("(kc p) c -> p kc c", p=128)
    w_up_v = w_up.rearrange("(kc p) c -> p kc c", p=128)
    w_pa_v = w_pa.rearrange("(kc p) c -> p kc c", p=128)
    w_ps_v = w_ps.rearrange("(kc p) c -> p kc c", p=128)
    w_out_v = w_out.rearrange("(kc p) c -> p kc c", p=128)
    w_dn_v = w_down.rearrange("(kc p) c -> p kc c", p=128)

    S = Sched(nc)
    finals = []

    def MM(o, lhsT, rhs, start, stop, r, w):
        S.op("pe", lambda e: e.matmul(out=o, lhsT=lhsT, rhs=rhs, start=start, stop=stop), r, w)

    def TR(o, in_, ident, r, w):
        S.op("pe", lambda e: e.transpose(out=o, in_=in_, identity=ident), r, w)

    def ACT(o, in_, func, r, w, bias=None, scale=None, accum=None):
        kw = {}
        if bias is not None:
            kw["bias"] = bias
        if scale is not None:
            kw["scale"] = scale
        if accum is not None:
            kw["accum_out"] = accum
        S.op("act", lambda e: e.activation(out=o, in_=in_, func=func, **kw), r, w)

    def TT(eng, o, in0, in1, op, r, w):
        S.op(eng, lambda e: e.tensor_tensor(out=o, in0=in0, in1=in1, op=op), r, w)

    def TS(eng, o, in0, s1, op0, r, w, s2=None, op1=None):
        if op1 is None:
            S.op(eng, lambda e: e.tensor_scalar(out=o, in0=in0, scalar1=s1, scalar2=None, op0=op0), r, w)
        else:
            S.op(eng, lambda e: e.tensor_scalar(out=o, in0=in0, scalar1=s1, scalar2=s2, op0=op0, op1=op1), r, w)

    def STT(o, in0, scalar, in1, op0, op1, r, w):
        S.op("dve", lambda e: e.scalar_tensor_tensor(out=o, in0=in0, scalar=scalar, in1=in1, op0=op0, op1=op1), r, w)

    def CP(eng, o, in_, r, w):
        if eng == "act":
            S.op("act", lambda e: e.copy(out=o, in_=in_), r, w)
        else:
            S.op(eng, lambda e: e.tensor_copy(out=o, in_=in_), r, w)

    def RECIP(o, in_, r, w):
        S.op("dve", lambda e: e.reciprocal(out=o, in_=in_), r, w)

    def MEMSET(eng, ap, val, w):
        S.op(eng, lambda e: e.memset(ap, val), (), w)

    def bc(ap, n=128):
        return ap.broadcast_to([n, ap.shape[-1]])

    with contextlib.ExitStack() as G:
        def sbg(name, shape, dt):
            return G.enter_context(nc.sbuf_tensor(name, list(shape), dt))

        ps = G.enter_context(nc.psum_tensor("ps", [128, 8, 512], F32))

        def PB(b):
            return ("ps", b)

        def psbf(b):
            return ps[:, b, :].bitcast(BF16)

        ident = sbg("ident", [128, 128], BF16)
        identf = sbg("identf", [128, 128], F32)
        U = sbg("U", [128, 128], F32)
        Lst = sbg("Lst", [128, 128], F32)
        ones_f = sbg("ones_f", [128, 128], F32)
        gmix_bc = sbg("gmix_bc", [128, D], F32)
        gffn_bc = sbg("gffn_bc", [128, D], F32)
        gqk_bc = sbg("gqk_bc", [128, 256], F32)
        subln_bc = sbg("subln_bc", [128, 128], F32)
        neglam = sbg("neglam", [128, 1], F32)
        lamt = sbg("lamt", [128, 4, 64], F32)
        lamp = sbg("lamp", [128, 2, 64], F32)
        lame = sbg("lame", [128, 2], F32)
        b31 = sbg("b31", [128, 8], F32)
        convw = sbg("convw", [128, 24, 4], F32)
        convb = sbg("convb", [128, 24], F32)
        fconvw = sbg("fconvw", [128, 44, 3], F32)
        fconvb = sbg("fconvb", [128, 44], F32)
        dtb_bc = sbg("dtb_bc", [128, 32], F32)
        a_bc = sbg("a_bc", [128, 32], F32)
        dsk_bc = sbg("dsk_bc", [128, 32], F32)
        RB = sbg("RB", [8, 32], F32)
        ext_sb = sbg("ext_sb", [8, 384], F32)

        MEMSET("pool", ones_f[:], 1.0, ["ones_f"])
        MEMSET("pool", identf[:], 0.0, ["identf"])
        S.op("pool", lambda e: e.affine_select(out=identf[:], in_=identf[:], pattern=[[-1, 128]],
                                               compare_op=ALU.not_equal, fill=1.0, base=0, channel_multiplier=1),
             ["identf"], ["identf"])
        CP("dve", ident[:], identf[:], ["identf"], ["ident"])
        S.op("pool", lambda e: e.affine_select(out=U[:], in_=ones_f[:], pattern=[[1, 128]],
                                               compare_op=ALU.is_ge, fill=0.0, base=0, channel_multiplier=-1),
             ["ones_f"], ["U"])
        S.op("pool", lambda e: e.affine_select(out=Lst[:], in_=ones_f[:], pattern=[[-1, 128]],
                                               compare_op=ALU.is_ge, fill=0.0, base=-1, channel_multiplier=1),
             ["ones_f"], ["Lst"])
        S.dma("sp", gmix_bc[:], bc(norm_mix_g), writes=["gmix"])
        S.dma("sp", gffn_bc[:], bc(norm_ffn_g), writes=["gffn"])
        S.dma("sp", gqk_bc[:, 0:64], bc(q_norm_g), writes=["gqk"])
        S.dma("sp", gqk_bc[:, 64:128], bc(q_norm_g), writes=["gqk"])
        S.dma("sp", gqk_bc[:, 128:192], bc(k_norm_g), writes=["gqk"])
        S.dma("sp", gqk_bc[:, 192:256], bc(k_norm_g), writes=["gqk"])
        ACT(gqk_bc[:, 0:128], gqk_bc[:, 0:128], AF.Identity, ["gqk"], ["gqk"], scale=0.125)
        S.dma("sp", subln_bc[:], bc(subln_g), writes=["subln"])
        ACT(subln_bc[:], subln_bc[:], AF.Identity, ["subln"], ["subln"], scale=0.8)
        for i, a in enumerate((lam_q1, lam_q2, lam_k1, lam_k2)):
            S.dma("sp", lamt[:, i, :], bc(a), writes=["lamt"])
        TT("dve", lamp[:], lamt[:, 0:2, :], lamt[:, 2:4, :], ALU.mult, ["lamt"], ["lamp"])
        S.op("dve", lambda e: e.tensor_reduce(out=lame[:], in_=lamp[:], axis=AX.X, op=ALU.add), ["lamp"], ["lame"])
        ACT(lame[:], lame[:], AF.Exp, ["lame"], ["lame"])
        TT("dve", neglam[:], lame[:, 1:2], lame[:, 0:1], ALU.subtract, ["lame"], ["neglam"])
        TS("dve", neglam[:], neglam[:], -0.2, ALU.add, ["neglam"], ["neglam"])
        S.dma("sp", b31[:], bc(rel_bias[31:32, :]), writes=["b31"])
        S.dma("sp", dtb_bc[:], bc(dt_bias), writes=["dtb"])
        S.dma("sp", dsk_bc[:], bc(d_skip), writes=["dsk"])
        S.dma("sp", a_bc[:], bc(a_log), writes=["a_bc"])
        ACT(a_bc[:], a_bc[:], AF.Exp, ["a_bc"], ["a_bc"])
        ACT(a_bc[:], a_bc[:], AF.Identity, ["a_bc"], ["a_bc"], scale=-1.0)
        stg = sbg("stg", [64, 9, 128], F32)
        rbs = sbg("rbs", [32, 8], F32)
        MEMSET("pool", stg[:], 0.0, ["stg"])
        S.dma("sp", stg[0:24, 0:4, :], conv_ssm_w.rearrange("k (cc p) -> cc k p", p=128), writes=["stg"])
        S.dma("sp", stg[0:24, 4, :], conv_ssm_b[0, :].rearrange("(cc p) -> cc p", p=128), writes=["stg"])
        S.dma("sp", stg[0:44, 5:8, :], conv_ffn_w.rearrange("k (cc p) -> cc k p", p=128), writes=["stg"])
        S.dma("sp", stg[0:44, 8, :], conv_ffn_b[0, :].rearrange("(cc p) -> cc p", p=128), writes=["stg"])
        S.dma("sp", rbs[:], rel_bias, writes=["rbs"])
        def pcol(k):
            return k * 32 if k < 5 else 160 + (k - 5) * 64
        for k in range(9):
            n = 32 if k < 5 else 64
            TR(ps[:, 0, pcol(k):pcol(k) + n], stg[0:n, k, :], identf[0:n, 0:n], ["stg", "identf"], [PB(0)])
        for k in range(4):
            CP("dve", convw[:, :, k], ps[:, 0, pcol(k):pcol(k) + 24], [PB(0)], ["convw"])
        CP("dve", convb[:], ps[:, 0, pcol(4):pcol(4) + 24], [PB(0)], ["convb"])
        for k in range(3):
            CP("dve", fconvw[:, :, k], ps[:, 0, pcol(5 + k):pcol(5 + k) + 44], [PB(0)], ["fconvw"])
        CP("dve", fconvb[:], ps[:, 0, pcol(8):pcol(8) + 44], [PB(0)], ["fconvb"])
        TR(ps[0:8, 1, 0:32], rbs[:], identf[0:32, 0:32], ["rbs", "identf"], [PB(1)])
        CP("dve", RB[:], ps[0:8, 1, 0:32], [PB(1)], ["RB"])
        MEMSET("pool", ext_sb[:], -30000.0, ["ext_sb"])
        CP("dve", ext_sb[:, 127:143], RB[:, 0:16], ["RB", "ext_sb"], ["ext_sb"])
        bt = t5_bucket_table(256)
        assert (bt[113:] == 31).all()
        for bk in range(16, 32):
            idx = np.nonzero(bt == bk)[0]
            if len(idx) == 0:
                continue
            n0, n1 = int(idx[0]), int(idx[-1]) + 1
            assert n1 - n0 == len(idx)
            CP("dve", ext_sb[:, 127 + n0:127 + n1], V(RB[:, bk:bk + 1], [[0, n1 - n0]]), ["RB", "ext_sb"], ["ext_sb"])
        S.dma("sp", ext_d, ext_sb[:, :], reads=["ext_sb"], writes=["ext_d"])
        S.dma("sp", Zd, bass.AP(tensor=ext_d.tensor, offset=ext_d.offset, ap=[[384, 8], [0, 128], [1, 384]]),
              reads=["ext_d"], writes=["Zd"])
        S.barrier()

        for b in range(nseq):
            with contextlib.ExitStack() as Q:
                def sbq(name, shape, dt):
                    return Q.enter_context(nc.sbuf_tensor(f"{name}_{b}", list(shape), dt))

                hT = sbq("hT", [128, 8, S_LEN], BF16)
                HT_ALL = [("hT", t) for t in range(NT)]

                with contextlib.ExitStack() as P:
                    def sbp(name, shape, dt):
                        return P.enter_context(nc.sbuf_tensor(f"{name}_{b}", list(shape), dt))
                    xts = [sbp(f"p0_xt{i}", [128, D], F32) for i in range(2)]
                    hns = [sbp(f"p0_hn{i}", [128, D], BF16) for i in range(2)]
                    junk = sbp("p0_junk", [128, D], BF16)
                    ssq = sbp("p0_ssq", [128, NT], F32)
                    rs = sbp("p0_rs", [128, NT], F32)
                    for t in range(NT):
                        i2 = t % 2
                        xt, hn = xts[i2], hns[i2]
                        S.dma("sp", xt[:], x[b, t * 128:(t + 1) * 128, :], writes=[("xt", i2)])
                        ACT(junk[:], xt[:], AF.Square, [("xt", i2)], ["junk", ("ssq", t)], accum=ssq[:, t:t + 1])
                        ACT(rs[:, t:t + 1], ssq[:, t:t + 1], AF.Sqrt, [("ssq", t)], [("rs", t)], bias=1e-6, scale=1.0 / D)
                        RECIP(rs[:, t:t + 1], rs[:, t:t + 1], [("rs", t)], [("rs", t)])
                        STT(hn[:], xt[:], rs[:, t:t + 1], gmix_bc[:], ALU.mult, ALU.mult,
                            [("xt", i2), ("rs", t), "gmix"], [("hn", i2)])
                        pb = 2 * i2
                        pv = psbf(pb)
                        for c in range(8):
                            TR(pv[:, c * 128:(c + 1) * 128], hn[:, c * 128:(c + 1) * 128], ident[:],
                               [("hn", i2), "ident"], [PB(pb)])
                        CP("act" if i2 else "dve", hT[:, :, t * 128:(t + 1) * 128],
                           V(pv, [[128, 8], [1, 128]]), [PB(pb)], [("hT", t)])
                    if dbg and b == 0:
                        finals.append(S.dma("sp", hT_d, hT[:], reads=HT_ALL))
                    S.barrier()
                if stage <= 0:
                    continue

                with contextlib.ExitStack() as P:
                    def sbp(name, shape, dt):
                        return P.enter_context(nc.sbuf_tensor(f"{name}_{b}", list(shape), dt))
                    NP = 28
                    BT = sbp("p1_BT", [128, 8, 256], F32)
                    wqkv = [sbp(f"p1_w{i}", [128, 8, 384], BF16) for i in range(2)]
                    qkT = [sbp(f"p1_qkT{i}", [128, 2, S_LEN], BF16) for i in range(2)]
                    vaug = [sbp(f"p1_v{i}", [128, NT, 132], BF16) for i in range(2)]
                    PT = [sbp(f"p1_PT{i}", [128, 2, 512], BF16) for i in range(NP)]
                    sq = [sbp(f"p1_sq{i}", [128, 256], F32) for i in range(2)]
                    tmpn = [sbp(f"p1_tmpn{i}", [128, 256], F32) for i in range(2)]
                    qkn = [sbp(f"p1_qkn{i}", [128, 256], BF16) for i in range(4)]
                    ssq4 = sbp("p1_ssq4", [128, NT, 4], F32)
                    rs4 = sbp("p1_rs4", [128, NT, 4], F32)
                    etmp = [sbp(f"p1_et{i}", [128, 2, 256], F32) for i in range(2)]
                    rl = sbp("p1_rl", [128, NT, 2], F32)
                    nrl = sbp("p1_nrl", [128, NT], F32)
                    o1 = [sbp(f"p1_o1{i}", [128, 128], F32) for i in range(2)]
                    oo = [sbp(f"p1_oo{i}", [128, 128], F32) for i in range(2)]
                    junk2 = sbp("p1_junk2", [128, 128], BF16)
                    sso = sbp("p1_sso", [128, NT], F32)
                    rso = sbp("p1_rso", [128, NT], F32)
                    yn = [sbp(f"p1_yn{i}", [128, 128], BF16) for i in range(4)]
                    yst = [sbp(f"p1_yst{i}", [128, 4, 128], BF16) for i in range(2)]

                    for h in range(8):
                        S.dma("sp", BT[:, h, :],
                              bass.AP(tensor=Zd.tensor, offset=Zd.offset + h * 128 * 384 + 127, ap=[[383, 128], [1, 256]]),
                              writes=[("BT", h)])
                    for i in range(2):
                        MEMSET("pool", vaug[i][:, :, 128:129], 1.0, [("vaug", i)])

                    pends = {"q": [], "y": []}

                    def tick():
                        for pend in pends.values():
                            for it in pend:
                                it[0] -= 1
                            while pend and pend[0][0] <= 0:
                                pend.pop(0)[1]()

                    def defer(n, fn, q="q"):
                        pends[q].append([n, fn])

                    def flush():
                        for pend in pends.values():
                            while pend:
                                pend.pop(0)[1]()

                    st_ = {"pt": 0, "sb": 0, "yst": 0, "qkn": 0, "yn": 0}

                    def load_w(h):
                        sl = h % 2
                        for j3, c0 in enumerate((C_Q, C_K, C_V)):
                            S.dma("pool", wqkv[sl][:, :, j3 * 128:(j3 + 1) * 128],
                                  w_in_v[:, :, c0 + h * 128:c0 + (h + 1) * 128], writes=[("wqkv", sl, j3)])

                    def proj_item(h, t):
                        def f():
                            sl = h % 2
                            W = wqkv[sl]
                            i2 = t % 2
                            pb = 4 + 2 * i2
                            for kc in range(8):
                                MM(ps[:, pb, 0:384], hT[:, kc, t * 128:(t + 1) * 128], W[:, kc, :], kc == 0, kc == 7,
                                   [("hT", t)] + [("wqkv", sl, q) for q in range(3)], [PB(pb)])
                            ACT(sq[i2][:], ps[:, pb, 0:256], AF.Square, [PB(pb)], [("sq", i2)])
                            S.op("dve", lambda e: e.tensor_reduce(
                                out=ssq4[:, t, :], in_=V(sq[i2][:], [[64, 4], [1, 64]]), axis=AX.X, op=ALU.add),
                                [("sq", i2)], [("ssq4", t)])
                            ACT(rs4[:, t, :], ssq4[:, t, :], AF.Ln, [("ssq4", t)], [("rs4", t)], bias=1e-6, scale=1.0 / 64)
                            ACT(rs4[:, t, :], rs4[:, t, :], AF.Exp, [("rs4", t)], [("rs4", t)], scale=-0.5)
                            TT("dve", V(tmpn[i2][:], [[64, 4], [1, 64]]), V(ps[:, pb, 0:256], [[64, 4], [1, 64]]),
                               V(rs4[:, t, :], [[1, 4], [0, 64]]), ALU.mult, [PB(pb), ("rs4", t)], [("tmpn", i2)])
                            qi = st_["qkn"] % 4
                            st_["qkn"] += 1
                            TT("pool", qkn[qi][:], tmpn[i2][:], gqk_bc[:], ALU.mult, [("tmpn", i2), "gqk"], [("qkn", qi)])
                            CP("act", vaug[sl][:, t, 0:128], ps[:, pb, 256:384], [PB(pb)], [("vaug", sl)])

                            def g():
                                pv = psbf(2)
                                hf = i2 * 512
                                for m2 in range(2):
                                    TR(pv[:, hf + m2 * 128:hf + (m2 + 1) * 128], qkn[qi][:, m2 * 128:(m2 + 1) * 128], ident[:],
                                       [("qkn", qi), "ident"], [PB(2)])
                                CP("act" if i2 else "dve", qkT[sl][:, :, t * 128:(t + 1) * 128],
                                   V(pv[:, hf:hf + 256], [[128, 2], [1, 128]]), [PB(2)], [("qkT", sl, t)])
                            defer(3, g)
                            tick()
                        return f

                    PTc = {}

                    def qk_item(h, c, j):
                        def f():
                            sl = h % 2
                            r = j - 4 * c
                            st = max(0, r) * 128
                            b0 = 4 + 2 * (st_["sb"] % 2)
                            st_["sb"] += 1
                            qkeys = [("qkT", sl, j)] + [("qkT", sl, q) for q in range(4 * c + st // 128, 4 * c + 4)]
                            for m in range(2):
                                MM(ps[:, b0 + m, st:512], qkT[sl][64 * m:64 * m + 64, 1, j * 128:(j + 1) * 128],
                                   qkT[sl][64 * m:64 * m + 64, 0, 512 * c + st:512 * c + 512], True, True,
                                   qkeys, [PB(b0 + m)])
                            pi = st_["pt"] % NP
                            st_["pt"] += 1
                            PTc[(h, c, j)] = pi
                            Pt = PT[pi]
                            rb = [PB(b0), PB(b0 + 1)]
                            if r >= -1:
                                if r >= 0:
                                    nb = min(2, 4 - r)
                                    btsl = BT[:, h, 0:128 * nb]
                                else:
                                    nb = 1
                                    btsl = BT[:, h, 128:256]
                                wdt = 128 * nb
                                ei = st_["sb"] % 2
                                TT("dve", etmp[ei][:, :, 0:wdt], ps[:, b0:b0 + 2, st:st + wdt],
                                   V(btsl, [[0, 2], [1, wdt]]), ALU.add, rb + [("BT", h)], [("etmp", ei)])
                                ACT(Pt[:, :, st:st + wdt], etmp[ei][:, :, 0:wdt], AF.Exp, [("etmp", ei)], [("PT", pi)])
                                if st + wdt < 512:
                                    ACT(Pt[:, :, st + wdt:512], ps[:, b0:b0 + 2, st + wdt:512], AF.Exp,
                                        rb + ["b31"], [("PT", pi)], bias=b31[:, h:h + 1])
                            else:
                                ACT(Pt[:, :, :], ps[:, b0:b0 + 2, :], AF.Exp, rb + ["b31"], [("PT", pi)],
                                    bias=b31[:, h:h + 1])
                            tick()
                        return f

                    def av_item(h, c, i, m):
                        def f():
                            sl = h % 2
                            ob = i % 2
                            i2 = i % 2
                            for j in range(i + 1):
                                pi = PTc[(h, c, j)]
                                MM(ps[:, ob, 256 * m:256 * m + 129],
                                   PT[pi][:, m, (i - 4 * c) * 128:(i - 4 * c + 1) * 128],
                                   vaug[sl][:, j, 0:129], j == 0, j == i,
                                   [("PT", pi), ("vaug", sl)], [PB(ob)])
                            if m == 1:
                                RECIP(rl[:, i, :], V(ps[:, ob, 128:129], [[256, 2]]), [PB(ob)], [("rl", i)])
                                TS("dve", nrl[:, i:i + 1], rl[:, i, 1:2], neglam[:, 0:1], ALU.mult,
                                   [("rl", i), "neglam"], [("nrl", i)])
                                ACT(o1[i2][:], ps[:, ob, 0:128], AF.Identity, [PB(ob), ("rl", i)], [("o1", i2)],
                                    scale=rl[:, i, 0:1])
                                STT(oo[i2][:], ps[:, ob, 256:384], nrl[:, i:i + 1], o1[i2][:], ALU.mult, ALU.add,
                                    [PB(ob), ("nrl", i), ("o1", i2)], [("oo", i2)])
                                ACT(junk2[:], oo[i2][:], AF.Square, [("oo", i2)], ["junk2", ("sso", i)], accum=sso[:, i:i + 1])
                                ACT(rso[:, i:i + 1], sso[:, i:i + 1], AF.Ln, [("sso", i)], [("rso", i)],
                                    bias=1e-5, scale=1.0 / 128)
                                ACT(rso[:, i:i + 1], rso[:, i:i + 1], AF.Exp, [("rso", i)], [("rso", i)], scale=-0.5)
                                yi = st_["yn"] % 4
                                st_["yn"] += 1
                                STT(yn[yi][:], oo[i2][:], rso[:, i:i + 1], subln_bc[:], ALU.mult, ALU.mult,
                                    [("oo", i2), ("rso", i), "subln"], [("yn", yi)])

                                def g():
                                    hf = c % 2
                                    pv = psbf(3)
                                    col = hf * 512 + (i % 4) * 128
                                    TR(pv[:, col:col + 128], yn[yi][:], ident[:], [("yn", yi), "ident"], [PB(3)])
                                    if i % 4 == 3:
                                        yc = st_["yst"] % 2
                                        st_["yst"] += 1
                                        ys = yst[yc]
                                        CP("act", ys[:], V(pv[:, hf * 512:hf * 512 + 512], [[128, 4], [1, 128]]),
                                           [PB(3)], [("yst", yc)])
                                        dst = bass.AP(tensor=ya_d.tensor,
                                                      offset=ya_d.offset + ((b * NT + 4 * c) * 128 * 8 + h) * 128,
                                                      ap=[[8 * 128, 128], [128 * 8 * 128, 4], [1, 128]])
                                        o_ = S.dma("sp", dst, ys[:], reads=[("yst", yc)], writes=[("ya_d", b, c, h)])
                                        if dbg:
                                            finals.append(o_)
                                defer(YDEF, g, "y")
                            tick()
                        return f

                    def run(items):
                        for it in items:
                            it()

                    def merge(A, B):
                        nb_done = 0
                        for idx, a in enumerate(A):
                            a()
                            tgt = (idx + 1) * len(B) // len(A)
                            while nb_done < tgt:
                                B[nb_done]()
                                nb_done += 1
                        while nb_done < len(B):
                            B[nb_done]()
                            nb_done += 1

                    def qk_items(h, c):
                        return [qk_item(h, c, j) for j in range(4 * c + 4)]

                    def av_items(h, c):
                        return [av_item(h, c, i, m) for i in range(4 * c, 4 * c + 4) for m in range(2)]

                    def proj_items(h):
                        return [proj_item(h, t) for t in range(NT)]

                    load_w(0)
                    run(proj_items(0))
                    for h in range(8):
                        if h < 7:
                            load_w(h + 1)
                        run(qk_items(h, 0))
                        for c in range(3):
                            merge(av_items(h, c), qk_items(h, c + 1))
                        merge(av_items(h, 3), proj_items(h + 1) if h < 7 else [])
                    flush()
                    S.barrier()
                if stage <= 1:
                    continue

                with contextlib.ExitStack() as P:
                    def sbp(name, shape, dt):
                        return P.enter_context(nc.sbuf_tensor(f"{name}_{b}", list(shape), dt))
                    ssmg_bc = sbp("p2_ssmg", [128, 2048], F32)
                    dt_all = sbp("p2_dt", [128, NT, 32], F32)
                    adt_all = sbp("p2_adt", [128, NT, 32], F32)
                    wdt_ = sbp("p2_wdt", [128, 8, 32], BF16)
                    dtt = [sbp(f"p2_dtt{i}", [128, 32], F32) for i in range(2)]
                    wx = [sbp(f"p2_wx{i}", [128, 8, 128], BF16) for i in range(3)]
                    wzs = [sbp(f"p2_wz{i}", [128, 8, 512], BF16) for i in range(2)]
                    acc = [sbp(f"p2_acc{i}", [128, 1024], F32) for i in range(2)]
                    accb = [sbp(f"p2_accb{i}", [128, 1024], F32) for i in range(2)]
                    zs_all = sbp("p2_zsall", [128, NT, 512], BF16)
                    halo = sbp("p2_halo", [128, 4], F32)
                    fmx = [sbp(f"p2_fmx{i}", [128, S_LEN], BF16) for i in range(2)]
                    BTg = sbp("p2_BTg", [128, S_LEN], BF16)
                    CTg = sbp("p2_CTg", [128, S_LEN], BF16)
                    xs_tok = sbp("p2_xs", [128, NT, 512], BF16)
                    B_tok = sbp("p2_Btok", [128, NT, 128], BF16)
                    state = sbp("p2_state", [128, 512], F32)
                    state_bf = sbp("p2_statebf", [128, 512], BF16)
                    s1 = sbp("p2_s1", [128, 512], F32)
                    cumtot = [sbp(f"p2_ct{i}", [128, 16], F32) for i in range(2)]
                    ecum = [sbp(f"p2_ec{i}", [128, 16], F32) for i in range(2)]
                    w8 = [sbp(f"p2_w8{i}", [128, 8], F32) for i in range(2)]
                    adtU = [sbp(f"p2_adtU{i}", [128, 8, 128], F32) for i in range(2)]
                    eseg = [sbp(f"p2_eseg{i}", [128, 8, 128], BF16) for i in range(2)]
                    cbTm = [sbp(f"p2_cbT{i}", [128, 128], BF16) for i in range(2)]
                    MT = [sbp(f"p2_MT{i}", [128, 8, 128], BF16) for i in range(2)]
                    xdt = [sbp(f"p2_xdt{i}", [128, 8, 64], BF16) for i in range(2)]
                    xw = [sbp(f"p2_xw{i}", [128, 8, 64], BF16) for i in range(2)]
                    t1 = [sbp(f"p2_t1{i}", [128, 512], F32) for i in range(2)]
                    t3 = [sbp(f"p2_t3{i}", [128, 512], F32) for i in range(2)]
                    junk3 = sbp("p2_junk3", [128, 512], BF16)
                    ssy = sbp("p2_ssy", [128, NT], F32)
                    rsy = sbp("p2_rsy", [128, NT], F32)
                    ynb = [sbp(f"p2_ynb{i}", [128, 512], BF16) for i in range(2)]
                    ysT = [sbp(f"p2_ysT{i}", [128, 4, 128], BF16) for i in range(2)]

                    S.dma("sp", ssmg_bc[:], bc(ssm_norm_g), writes=["ssmg"])
                    S.dma("pool", wdt_[:], w_in_v[:, :, C_DT:C_DT + 32], writes=["wdt"])
                    for t in range(NT):
                        i2 = t % 2
                        for kc in range(8):
                            MM(ps[:, 5, 0:32], hT[:, kc, t * 128:(t + 1) * 128], wdt_[:, kc, :], kc == 0, kc == 7,
                               [("hT", t), "wdt"], [PB(5)])
                        TT("dve", dtt[i2][:], ps[:, 5, 0:32], dtb_bc[:], ALU.add, [PB(5), "dtb"], [("dtt", i2)])
                        ACT(dtt[i2][:], dtt[i2][:], AF.Exp, [("dtt", i2)], [("dtt", i2)])
                        ACT(dt_all[:, t, :], dtt[i2][:], AF.Ln, [("dtt", i2)], [("dt", t)], bias=1.0, scale=1.0)
                        TT("pool", adt_all[:, t, :], dt_all[:, t, :], a_bc[:], ALU.mult, [("dt", t), "a_bc"], [("adt", t)])
                    if dbg and b == 0:
                        finals.append(S.dma("sp", dt_d, dt_all[:], reads=[("dt", t) for t in range(NT)]))

                    st2 = {"pp": 0, "ab": 0, "zb": 0}
                    pend2 = []

                    def chunk_desc(n):
                        g_, ci_ = n // 6, n % 6
                        if ci_ < 4:
                            return g_, ci_, C_XS + g_ * 512 + ci_ * 128, g_ * 4 + ci_, fmx[ci_ % 2], ("fmx", ci_ % 2)
                        if ci_ == 4:
                            return g_, ci_, C_B + g_ * 128, 16 + g_, BTg, "BTg"
                        return g_, ci_, C_C + g_ * 128, 20 + g_, CTg, "CTg"

                    def load_wx(n):
                        if n >= 24:
                            return
                        col0 = chunk_desc(n)[2]
                        S.dma("pool", wx[n % 3][:], w_in_v[:, :, col0:col0 + 128], writes=[("wx", n % 3)])

                    def load_wz(g_):
                        S.dma("pool", wzs[g_ % 2][:], w_in_v[:, :, C_Z + g_ * 512:C_Z + (g_ + 1) * 512], writes=[("wz", g_ % 2)])

                    def emit_pend2():
                        while pend2:
                            pend2.pop(0)()

                    load_wx(0)
                    load_wx(1)
                    load_wz(0)
                    for g in range(4):
                        for ci in range(6):
                            n = g * 6 + ci
                            _, _, col0, cch, dstT, dkey = chunk_desc(n)
                            wi = n % 3
                            load_wx(n + 2)
                            for half in range(2):
                                bp = 2 * (st2["pp"] % 2)
                                st2["pp"] += 1
                                ai = st2["ab"] % 2
                                st2["ab"] += 1
                                A, Bv = acc[ai], accb[ai]
                                ak, bk = ("acc", ai), ("accb", ai)
                                for tt in range(2):
                                    tok0 = half * 1024 + tt * 512
                                    for kc in range(8):
                                        MM(ps[:, bp + tt, :], wx[wi][:, kc, :], hT[:, kc, tok0:tok0 + 512], kc == 0, kc == 7,
                                           [("wx", wi)] + [("hT", tok0 // 128 + q) for q in range(4)], [PB(bp + tt)])
                                emit_pend2()
                                pin = V(ps[:, bp, :], [[1, 1024]])
                                pin1 = V(ps[:, bp, :], [[1, 1023]])
                                rb = [PB(bp), PB(bp + 1)]
                                if not CONV_SPLIT:
                                    ACT(A[:], pin, AF.Identity, rb + ["convw", "convb"], [ak],
                                        bias=convb[:, cch:cch + 1], scale=convw[:, cch, 3:4])
                                    for k in (2, 1, 0):
                                        d_ = 3 - k
                                        STT(A[:, d_:1024], V(ps[:, bp, :], [[1, 1024 - d_]]), convw[:, cch, k:k + 1], A[:, d_:1024],
                                            ALU.mult, ALU.add, rb + ["convw", ak], [ak])
                                        if half == 1:
                                            STT(A[:, 0:d_], halo[:, 3 - d_:3], convw[:, cch, k:k + 1], A[:, 0:d_],
                                                ALU.mult, ALU.add, ["halo", "convw", ak], [ak])
                                    if half == 0:
                                        CP("act", halo[:, 0:3], ps[:, bp + 1, 509:512], [PB(bp + 1)], ["halo"])
                                else:
                                    ACT(A[:], pin, AF.Identity, rb + ["convw", "convb"], [ak],
                                        bias=convb[:, cch:cch + 1], scale=convw[:, cch, 3:4])
                                    if "8" in os.environ.get("KF", ""):
                                        TS("dve", Bv[:], pin, convw[:, cch, 1:2], ALU.mult, rb + ["convw"], [bk])
                                    else:
                                        ACT(Bv[:], pin, AF.Identity, rb + ["convw"], [bk], scale=convw[:, cch, 1:2])
                                    STT(A[:, 1:1024], pin1, convw[:, cch, 2:3], A[:, 1:1024], ALU.mult, ALU.add, rb + ["convw", ak], [ak])
                                    STT(Bv[:, 1:1024], pin1, convw[:, cch, 0:1], Bv[:, 1:1024], ALU.mult, ALU.add, rb + ["convw", bk], [bk])
                                    if half == 1 and "7" not in os.environ.get("KF", ""):
                                        STT(A[:, 0:1], halo[:, 2:3], convw[:, cch, 2:3], A[:, 0:1], ALU.mult, ALU.add, ["halo", "convw", ak], [ak])
                                        STT(A[:, 0:2], halo[:, 1:3], convw[:, cch, 1:2], A[:, 0:2], ALU.mult, ALU.add, ["halo", "convw", ak], [ak])
                                        STT(A[:, 0:2], halo[:, 0:2], convw[:, cch, 0:1], A[:, 0:2], ALU.mult, ALU.add, ["halo", "convw", ak], [ak])
                                        STT(Bv[:, 0:1], halo[:, 2:3], convw[:, cch, 0:1], Bv[:, 0:1], ALU.mult, ALU.add, ["halo", "convw", bk], [bk])
                                    if half == 0:
                                        CP("act", halo[:, 0:3], ps[:, bp + 1, 509:512], [PB(bp + 1)], ["halo"])
                                    TT("dve" if "5" in os.environ.get("KF", "") else "pool", A[:, 2:1024], A[:, 2:1024], Bv[:, 0:1022], ALU.add, [ak, bk], [ak])
                                ACT(dstT[:, half * 1024:(half + 1) * 1024], A[:], AF.Silu, [ak], [dkey])
                            if ci < 5:
                                def trans(ci=ci, dstT=dstT, dkey=dkey):
                                    for tb in range(2):
                                        pv = psbf(4)
                                        for q in range(8):
                                            t = tb * 8 + q
                                            TR(pv[:, q * 128:(q + 1) * 128], dstT[:, t * 128:(t + 1) * 128], ident[:],
                                               [dkey, "ident"], [PB(4)])
                                        if ci < 4:
                                            CP("act", xs_tok[:, tb * 8:(tb + 1) * 8, ci * 128:(ci + 1) * 128],
                                               V(pv, [[128, 8], [1, 128]]), [PB(4)], ["xs_tok"])
                                        else:
                                            CP("act", B_tok[:, tb * 8:(tb + 1) * 8, :], V(pv, [[128, 8], [1, 128]]), [PB(4)], ["B_tok"])
                                pend2.append(trans)
                                if "1" in os.environ.get("KF", ""):
                                    emit_pend2()
                        if "a" in os.environ.get("KSTOP", ""):
                            break
                        wz = wzs[g % 2]
                        for t in range(NT):
                            zb = 6 + (st2["zb"] % 2)
                            st2["zb"] += 1
                            for kc in range(8):
                                MM(ps[:, zb, :], hT[:, kc, t * 128:(t + 1) * 128], wz[:, kc, :], kc == 0, kc == 7,
                                   [("hT", t), ("wz", g % 2)], [PB(zb)])
                            if t == 1:
                                emit_pend2()
                            ACT(zs_all[:, t, :], ps[:, zb, :], AF.Silu, [PB(zb)], [("zs", t)])
                        if "z" in os.environ.get("KSTOP", ""):
                            break
                        if g < 3:
                            load_wz(g + 1)
                        MEMSET("pool", state[:], 0.0, ["state"])
                        MEMSET("pool", state_bf[:], 0.0, ["state_bf"])

                        def stageA(c):
                            i2 = c % 2
                            cs = slice(c * 128, (c + 1) * 128)
                            adt_c = adt_all[:, c, g * 8:(g + 1) * 8]
                            dt_c = dt_all[:, c, g * 8:(g + 1) * 8]
                            MM(ps[:, 5, 0:8], U[:], adt_c, True, True, ["U", ("adt", c)], [PB(5)])
                            MM(ps[:, 5, 8:16], ones_f[:], adt_c, True, True, ["ones_f", ("adt", c)], [PB(5)])
                            CP("act", cumtot[i2][:], ps[:, 5, 0:16], [PB(5)], [("cumtot", i2)])
                            ACT(ecum[i2][:], cumtot[i2][:], AF.Exp, [("cumtot", i2)], [("ecum", i2)])
                            TT("dve", w8[i2][:], cumtot[i2][:, 8:16], cumtot[i2][:, 0:8], ALU.subtract, [("cumtot", i2)], [("w8", i2)])
                            ACT(w8[i2][:], w8[i2][:], AF.Exp, [("w8", i2)], [("w8", i2)])
                            TT("pool", adtU[i2][:], V(adt_c, [[1, 8], [0, 128]]), V(U[:], [[0, 8], [1, 128]]), ALU.mult,
                               [("adt", c), "U"], [("adtU", i2)])
                            for q in range(2):
                                MM(ps[:, q, :], Lst[:], adtU[i2][:, 4 * q:4 * q + 4, :], True, True, ["Lst", ("adtU", i2)], [PB(q)])
                            ACT(V(eseg[i2][:], [[512, 2], [1, 512]]), ps[:, 0:2, :], AF.Exp, [PB(0), PB(1)], [("eseg", i2)])
                            MM(ps[:, 4, 0:128], BTg[:, cs], CTg[:, cs], True, True, ["BTg", "CTg"], [PB(4)])
                            TT("dve", cbTm[i2][:], ps[:, 4, 0:128], U[:], ALU.mult, [PB(4), "U"], [("cbTm", i2)])
                            TT("dve", MT[i2][:], eseg[i2][:], V(cbTm[i2][:], [[0, 8], [1, 128]]), ALU.mult,
                               [("eseg", i2), ("cbTm", i2)], [("MT", i2)])
                            TT("pool", xdt[i2][:], V(xs_tok[:, c, :], [[64, 8], [1, 64]]), V(dt_c, [[1, 8], [0, 64]]), ALU.mult,
                               ["xs_tok", ("dt", c)], [("xdt", i2)])
                            TT("pool", xw[i2][:], xdt[i2][:], V(w8[i2][:], [[1, 8], [0, 64]]), ALU.mult,
                               [("xdt", i2), ("w8", i2)], [("xw", i2)])

                        def stageB(c):
                            i2 = c % 2
                            cs = slice(c * 128, (c + 1) * 128)
                            for hh in range(8):
                                MM(ps[:, 2, hh * 64:(hh + 1) * 64], MT[i2][:, hh, :], xdt[i2][:, hh, :], True, True,
                                   [("MT", i2), ("xdt", i2)], [PB(2)])
                            MM(ps[:, 3, :], CTg[:, cs], state_bf[:], True, True, ["CTg", "state_bf"], [PB(3)])
                            if c < NT - 1:
                                MM(ps[:, 7, :], B_tok[:, c, :], V(xw[i2][:], [[1, 512]]), True, True, ["B_tok", ("xw", i2)], [PB(7)])
                            emit_pend2()
                            TT("dve", V(t1[i2][:], [[64, 8], [1, 64]]), V(ps[:, 3, :], [[64, 8], [1, 64]]),
                               V(ecum[i2][:, 0:8], [[1, 8], [0, 64]]), ALU.mult, [PB(3), ("ecum", i2)], [("t1", i2)])
                            TT("dve", t1[i2][:], ps[:, 2, :], t1[i2][:], ALU.add, [PB(2), ("t1", i2)], [("t1", i2)])
                            TT("pool", V(t3[i2][:], [[64, 8], [1, 64]]), V(xs_tok[:, c, :], [[64, 8], [1, 64]]),
                               V(dsk_bc[:, g * 8:(g + 1) * 8], [[1, 8], [0, 64]]), ALU.mult, ["xs_tok", "dsk"], [("t3", i2)])
                            TT("pool", t3[i2][:], t3[i2][:], t1[i2][:], ALU.add, [("t3", i2), ("t1", i2)], [("t3", i2)])
                            TT("pool" if "4" in os.environ.get("KF", "") else "dve", t1[i2][:], t3[i2][:], zs_all[:, c, :], ALU.mult, [("t3", i2), ("zs", c)], [("t1", i2)])
                            ACT(junk3[:], t1[i2][:], AF.Square, [("t1", i2)], ["junk3", ("ssy", c)], accum=ssy[:, c:c + 1])
                            ACT(rsy[:, c:c + 1], ssy[:, c:c + 1], AF.Ln, [("ssy", c)], [("rsy", c)], bias=1e-5, scale=1.0 / 512)
                            ACT(rsy[:, c:c + 1], rsy[:, c:c + 1], AF.Exp, [("rsy", c)], [("rsy", c)], scale=-0.5)
                            STT(ynb[i2][:], t1[i2][:], rsy[:, c:c + 1], ssmg_bc[:, g * 512:(g + 1) * 512], ALU.mult, ALU.mult,
                                [("t1", i2), ("rsy", c), "ssmg"], [("ynb", i2)])

                            def trans_y(c=c, i2=i2, g=g):
                                pv = psbf(6)
                                for q in range(4):
                                    TR(pv[:, q * 128:(q + 1) * 128], ynb[i2][:, q * 128:(q + 1) * 128], ident[:],
                                       [("ynb", i2), "ident"], [PB(6)])
                                CP("act", ysT[i2][:], V(pv[:, 0:512], [[128, 4], [1, 128]]), [PB(6)], [("ysT", i2)])
                                dst = bass.AP(tensor=yss_d.tensor,
                                              offset=yss_d.offset + ((b * NT + c) * 128 * 16 + g * 4) * 128,
                                              ap=[[16 * 128, 128], [128, 4], [1, 128]])
                                o_ = S.dma("sp", dst, ysT[i2][:], reads=[("ysT", i2)], writes=[("yss_d", b, c, g)])
                                if dbg:
                                    finals.append(o_)
                            pend2.append(trans_y)
                            if "3" in os.environ.get("KF", ""):
                                emit_pend2()
                            if c < NT - 1:
                                TT("dve", V(s1[:], [[64, 8], [1, 64]]), V(state[:], [[64, 8], [1, 64]]),
                                   V(ecum[i2][:, 8:16], [[1, 8], [0, 64]]), ALU.mult, ["state", ("ecum", i2)], ["s1"])
                                TT("dve", state[:], ps[:, 7, :], s1[:], ALU.add, [PB(7), "s1"], ["state"])
                                CP("act", state_bf[:], state[:], ["state"], ["state_bf"])

                        if "2" in os.environ.get("KF", ""):
                            for c in range(NT):
                                stageA(c)
                                stageB(c)
                        else:
                            stageA(0)
                            for c in range(NT):
                                if c + 1 < NT:
                                    stageA(c + 1)
                                stageB(c)
                    emit_pend2()
                    S.barrier()
                if stage <= 2:
                    continue

                with contextlib.ExitStack() as P:
                    def sbp(name, shape, dt):
                        return P.enter_context(nc.sbuf_tensor(f"{name}_{b}", list(shape), dt))
                    wpa = sbp("p3_wpa", [128, 8, 1024], BF16)
                    wps = sbp("p3_wps", [128, 16, 1024], BF16)
                    wg = sbp("p3_wg", [128, 8, 2048], BF16)
                    wo = sbp("p3_wo", [128, 8, 1024], BF16)
                    yat = [sbp(f"p3_yat{i}", [128, 8, 128], BF16) for i in range(2)]
                    ysst = [sbp(f"p3_ysst{i}", [128, 16, 128], BF16) for i in range(2)]
                    xt3 = [sbp(f"p3_xt{i}", [128, D], F32) for i in range(2)]
                    sa = [sbp(f"p3_sa{i}", [128, 512], F32) for i in range(2)]
                    sg_ = [sbp(f"p3_sg{i}", [128, 512], F32) for i in range(2)]
                    mixed = [sbp(f"p3_mixed{i}", [128, D], BF16) for i in range(2)]
                    mixT = [sbp(f"p3_mixT{i}", [128, 8, 128], BF16) for i in range(2)]
                    x1 = [sbp(f"p3_x1{i}", [128, D], F32) for i in range(2)]
                    hn3 = [sbp(f"p3_hn{i}", [128, D], BF16) for i in range(2)]
                    junk4 = sbp("p3_junk", [128, D], BF16)
                    ss3 = sbp("p3_ss", [128, NT], F32)
                    rs3 = sbp("p3_rs", [128, NT], F32)
                    def lw(dst, src, q, key):
                        S.dma("pool", dst[:, :, q * 512:(q + 1) * 512], src[:, :, q * 512:(q + 1) * 512], writes=[(key, q)])
                    w_g_v = w_in_v[:, :, C_G:C_G + 2048]
                    for j in range(2):
                        lw(wg, w_g_v, j, "wg")
                        lw(wg, w_g_v, 2 + j, "wg")
                        lw(wpa, w_pa_v, j, "wpa")
                        lw(wps, w_ps_v, j, "wps")
                    for j in range(2):
                        lw(wo, w_out_v, j, "wo")
                    hcs = {"hc": 0}

                    def M3(t):
                        i2 = t % 2
                        S.dma("sp", yat[i2][:], ya_d[b, t], reads=[("ya_d", b, t // 4, hh) for hh in range(8)], writes=[("yat", i2)])
                        S.dma("sp", ysst[i2][:], yss_d[b, t], reads=[("yss_d", b, t, g) for g in range(4)], writes=[("ysst", i2)])
                        S.dma("sp", xt3[i2][:], x[b, t * 128:(t + 1) * 128, :], writes=[("xt3", i2)])
                        for j in range(2):
                            h2 = hcs["hc"] % 2
                            hcs["hc"] += 1
                            cj = slice(j * 512, (j + 1) * 512)
                            for kc in range(8):
                                MM(ps[:, 2, :], hT[:, kc, t * 128:(t + 1) * 128], wg[:, kc, cj], kc == 0, kc == 7,
                                   [("hT", t), ("wg", j)], [PB(2)])
                            ACT(sa[h2][:], ps[:, 2, :], AF.Sigmoid, [PB(2)], [("sa", h2)])
                            for kc in range(8):
                                MM(ps[:, 3, :], hT[:, kc, t * 128:(t + 1) * 128], wg[:, kc, 1024 + j * 512:1024 + (j + 1) * 512],
                                   kc == 0, kc == 7, [("hT", t), ("wg", 2 + j)], [PB(3)])
                            ACT(sg_[h2][:], ps[:, 3, :], AF.Sigmoid, [PB(3)], [("sg", h2)])
                            for c in range(8):
                                MM(ps[:, 0, :], yat[i2][:, c, :], wpa[:, c, cj], c == 0, c == 7, [("yat", i2), ("wpa", j)], [PB(0)])
                            TT("dve", sa[h2][:], ps[:, 0, :], sa[h2][:], ALU.mult, [PB(0), ("sa", h2)], [("sa", h2)])
                            for c in range(16):
                                MM(ps[:, 1, :], ysst[i2][:, c, :], wps[:, c, cj], c == 0, c == 15, [("ysst", i2), ("wps", j)], [PB(1)])
                            TT("dve", sg_[h2][:], ps[:, 1, :], sg_[h2][:], ALU.mult, [PB(1), ("sg", h2)], [("sg", h2)])
                            TT("pool", mixed[i2][:, cj], sa[h2][:], sg_[h2][:], ALU.add, [("sa", h2), ("sg", h2)], [("mixed", i2)])

                    def T31(t):
                        i2 = t % 2
                        pv = psbf(4)
                        for c in range(8):
                            TR(pv[:, c * 128:(c + 1) * 128], mixed[i2][:, c * 128:(c + 1) * 128], ident[:], [("mixed", i2), "ident"], [PB(4)])
                        CP("act", mixT[i2][:], V(pv, [[128, 8], [1, 128]]), [PB(4)], [("mixT", i2)])
                        for j in range(2):
                            for c in range(8):
                                MM(ps[:, 5 + j, :], mixT[i2][:, c, :], wo[:, c, j * 512:(j + 1) * 512], c == 0, c == 7,
                                   [("mixT", i2), ("wo", j)], [PB(5 + j)])
                        TT("dve", x1[i2][:], V(ps[:, 5, :], [[1, 1024]]), xt3[i2][:], ALU.add, [PB(5), PB(6), ("xt3", i2)], [("x1", i2)])
                        o_ = S.dma("sp", out[b, t * 128:(t + 1) * 128, :], x1[i2][:], reads=[("x1", i2)], writes=[("x1d", b, t)])
                        if stage <= 3:
                            finals.append(o_)
                        ACT(junk4[:], x1[i2][:], AF.Square, [("x1", i2)], ["junk4", ("ss3", t)], accum=ss3[:, t:t + 1])
                        ACT(rs3[:, t:t + 1], ss3[:, t:t + 1], AF.Sqrt, [("ss3", t)], [("rs3", t)], bias=1e-6, scale=1.0 / D)
                        RECIP(rs3[:, t:t + 1], rs3[:, t:t + 1], [("rs3", t)], [("rs3", t)])
                        STT(hn3[i2][:], x1[i2][:], rs3[:, t:t + 1], gffn_bc[:], ALU.mult, ALU.mult,
                            [("x1", i2), ("rs3", t), "gffn"], [("hn3", i2)])

                    def T32(t):
                        i2 = t % 2
                        pv = psbf(7)
                        for c in range(8):
                            TR(pv[:, c * 128:(c + 1) * 128], hn3[i2][:, c * 128:(c + 1) * 128], ident[:], [("hn3", i2), "ident"], [PB(7)])
                        CP("act", hT[:, :, t * 128:(t + 1) * 128], V(pv, [[128, 8], [1, 128]]), [PB(7)], [("hT", t)])

                    for t in range(NT + 2):
                        if t < NT:
                            M3(t)
                        if 1 <= t <= NT:
                            T31(t - 1)
                        if t >= 2:
                            T32(t - 2)
                    S.barrier()
                if stage <= 3:
                    continue

                with contextlib.ExitStack() as P:
                    def sbp(name, shape, dt):
                        return P.enter_context(nc.sbuf_tensor(f"{name}_{b}", list(shape), dt))
                    wdn = sbp("p4_wdn", [128, NFC, 1024], BF16)
                    aT = sbp("p4_aT", [128, NFC, 1024], BF16)
                    wup = [sbp(f"p4_wup{i}", [128, 8, 256], BF16) for i in range(3)]
                    accg = [sbp(f"p4_accg{i}", [128, 1024], F32) for i in range(2)]
                    accb = [sbp(f"p4_accb{i}", [128, 1024], F32) for i in range(2)]
                    accv = [sbp(f"p4_accv{i}", [128, 1024], F32) for i in range(2)]
                    fhalo = sbp("p4_halo", [128, 2 * NFC, 2], F32)
                    x1t = [sbp(f"p4_x1t{i}", [128, D], F32) for i in range(2)]
                    ot = [sbp(f"p4_ot{i}", [128, D], F32) for i in range(2)]

                    def load_wup(n):
                        fc_ = n % NFC
                        wi_ = n % 3
                        S.dma("pool", wup[wi_][:, :, 0:128], w_up_v[:, :, fc_ * 128:(fc_ + 1) * 128], writes=[("wup", wi_, 0)])
                        S.dma("pool", wup[wi_][:, :, 128:256], w_up_v[:, :, D_FF + fc_ * 128:D_FF + (fc_ + 1) * 128],
                              writes=[("wup", wi_, 1)])

                    load_wup(0)
                    load_wup(1)
                    for q in range(2):
                        S.dma("pool", wdn[:, 0:11, q * 512:(q + 1) * 512], w_dn_v[:, 0:11, q * 512:(q + 1) * 512], writes=[("wdn", q, 0)])
                        S.dma("pool", wdn[:, 11:22, q * 512:(q + 1) * 512], w_dn_v[:, 11:22, q * 512:(q + 1) * 512], writes=[("wdn", q, 1)])
                    WDN = [("wdn", q, r_) for q in range(2) for r_ in range(2)]
                    wctr = 0
                    for blk in range(2):
                        for fc in range(NFC):
                            wi = wctr % 3
                            i2 = wctr % 2
                            if wctr + 2 < 2 * NFC:
                                load_wup(wctr + 2)
                            wctr += 1
                            for part in range(2):
                                base = 4 * i2 + 2 * part
                                cch = part * NFC + fc
                                A = (accg if part == 0 else accv)[i2]
                                ak = ("accg" if part == 0 else "accv", i2)
                                for tt in range(2):
                                    tok0 = blk * 1024 + tt * 512
                                    for kc in range(8):
                                        MM(ps[:, base + tt, :], wup[wi][:, kc, part * 128:(part + 1) * 128], hT[:, kc, tok0:tok0 + 512],
                                           kc == 0, kc == 7, [("wup", wi, part)] + [("hT", tok0 // 128 + q) for q in range(4)], [PB(base + tt)])
                                rb = [PB(base), PB(base + 1)]
                                pin = V(ps[:, base, :], [[1, 1024]])
                                if not CONV_SPLIT:
                                    ACT(A[:], pin, AF.Identity, rb + ["fconvw", "fconvb"], [ak],
                                        bias=fconvb[:, cch:cch + 1], scale=fconvw[:, cch, 2:3])
                                    for k in (1, 0):
                                        d_ = 2 - k
                                        STT(A[:, d_:1024], V(ps[:, base, :], [[1, 1024 - d_]]), fconvw[:, cch, k:k + 1], A[:, d_:1024],
                                            ALU.mult, ALU.add, rb + ["fconvw", ak], [ak])
                                        if blk == 1:
                                            STT(A[:, 0:d_], fhalo[:, cch, 2 - d_:2], fconvw[:, cch, k:k + 1], A[:, 0:d_],
                                                ALU.mult, ALU.add, [("fhalo", cch), "fconvw", ak], [ak])
                                    if blk == 0:
                                        CP("act", fhalo[:, cch, :], ps[:, base + 1, 510:512], [PB(base + 1)], [("fhalo", cch)])
                                else:
                                    ACT(A[:], pin, AF.Identity, rb + ["fconvw", "fconvb"], [ak],
                                        bias=fconvb[:, cch:cch + 1], scale=fconvw[:, cch, 2:3])
                                    if part == 0:
                                        Bv = accb[i2]
                                        bk = ("accb", i2)
                                        ACT(Bv[:], pin, AF.Identity, rb + ["fconvw"], [bk], scale=fconvw[:, cch, 0:1])
                                    STT(A[:, 1:1024], V(ps[:, base, :], [[1, 1023]]), fconvw[:, cch, 1:2], A[:, 1:1024],
                                        ALU.mult, ALU.add, rb + ["fconvw", ak], [ak])
                                    if part == 1:
                                        STT(A[:, 2:1024], V(ps[:, base, :], [[1, 1022]]), fconvw[:, cch, 0:1], A[:, 2:1024],
                                            ALU.mult, ALU.add, rb + ["fconvw", ak], [ak])
                                    if blk == 1:
                                        STT(A[:, 0:1], fhalo[:, cch, 1:2], fconvw[:, cch, 1:2], A[:, 0:1],
                                            ALU.mult, ALU.add, [("fhalo", cch), "fconvw", ak], [ak])
                                        STT(A[:, 0:2], fhalo[:, cch, 0:2], fconvw[:, cch, 0:1], A[:, 0:2],
                                            ALU.mult, ALU.add, [("fhalo", cch), "fconvw", ak], [ak])
                                    if blk == 0:
                                        CP("act", fhalo[:, cch, :], ps[:, base + 1, 510:512], [PB(base + 1)], [("fhalo", cch)])
                                    if part == 0:
                                        TT("pool", A[:, 2:1024], A[:, 2:1024], Bv[:, 0:1022], ALU.add, [ak, bk], [ak])
                            ACT(accg[i2][:], accg[i2][:], AF.Silu, [("accg", i2)], [("accg", i2)])
                            TT("pool", aT[:, fc, :], accg[i2][:], accv[i2][:], ALU.mult, [("accg", i2), ("accv", i2)], [("aT", fc)])
                        for tt in range(8):
                            t = blk * 8 + tt
                            i2 = t % 2
                            S.dma("sp", x1t[i2][:], out[b, t * 128:(t + 1) * 128, :], reads=[("x1d", b, t)], writes=[("x1t", i2)])
                            for j in range(2):
                                pb = 2 * i2 + j
                                for fc in range(NFC):
                                    MM(ps[:, pb, :], aT[:, fc, tt * 128:(tt + 1) * 128], wdn[:, fc, j * 512:(j + 1) * 512],
                                       fc == 0, fc == NFC - 1, [("aT", fc)] + WDN, [PB(pb)])
                            TT("dve", ot[i2][:], V(ps[:, 2 * i2, :], [[1, 1024]]), x1t[i2][:], ALU.add,
                               [PB(2 * i2), PB(2 * i2 + 1), ("x1t", i2)], [("ot", i2)])
                            finals.append(S.dma("sp", out[b, t * 128:(t + 1) * 128, :], ot[i2][:], reads=[("ot", i2)],
                                                writes=[("x1d", b, t)]))
                    S.barrier()
        S.emit(final_wait_ops=finals)
    return nc


_NC_CACHE = {}
PARAM_NAMES = ["rel_bias", "norm_mix_g", "w_in", "q_norm_g", "k_norm_g", "lambda_q1", "lambda_k1", "lambda_q2",
               "lambda_k2", "attn_subln_g", "conv_ssm_w", "conv_ssm_b", "dt_bias", "a_log", "d_skip", "ssm_norm_g",
               "w_proj_attn", "w_proj_ssm", "w_out", "norm_ffn_g", "w_up", "conv_ffn_w", "conv_ffn_b", "w_down"]


def make_in_maps(inputs, n_cores=8, nseq=2):
    x = np.ascontiguousarray(np.asarray(inputs["x"], dtype=np.float32))
    shared = {}
    for k in PARAM_NAMES:
        a = np.asarray(inputs[k], dtype=np.float32)
        if k == "rel_bias":
            shared[k] = np.ascontiguousarray(a)
        elif a.ndim == 2:
            shared[k] = np.ascontiguousarray(a[0:1])
        else:
            shared[k] = np.ascontiguousarray(a[0])
    in_maps = []
    for i in range(n_cores):
        m = dict(shared)
        m["x"] = np.ascontiguousarray(x[i * nseq:(i + 1) * nseq])
        in_maps.append(m)
    return in_maps


def kernel(**inputs):
    n_cores, nseq = 8, 2
    if "nc" not in _NC_CACHE:
        _NC_CACHE["nc"] = build(nseq=nseq)
    nc = _NC_CACHE["nc"]
    in_maps = make_in_maps(inputs, n_cores, nseq)
    res = run_bass_kernel_spmd(nc, in_maps, core_ids=list(range(n_cores)))
    return np.concatenate([np.asarray(r["out"], dtype=np.float32) for r in res.results], axis=0)
```

```python
import contextlib
import os
import numpy as np
import ml_dtypes
import concourse.bass as bass
import concourse.mybir as mybir
from concourse.bass_utils import run_bass_kernel_spmd

F32 = mybir.dt.float32
BF16 = mybir.dt.bfloat16
AF = mybir.ActivationFunctionType
ALU = mybir.AluOpType
AX = mybir.AxisListType

ENGS = ("pe", "act", "dve", "pool", "sp")

S_LEN = 2048
D = 1024
NT = 16
IN_COLS = 10272
C_Q, C_K, C_V, C_Z, C_XS, C_B, C_C, C_DT, C_G = 0, 1024, 2048, 3072, 5120, 7168, 7680, 8192, 8224
D_FF = 2816
NFC = 22


class Op:
    __slots__ = ("eng", "fn", "deps", "is_dma", "sig", "need_sig")

    def __init__(self, eng, fn, is_dma):
        self.eng = eng
        self.fn = fn
        self.is_dma = is_dma
        self.deps = []
        self.sig = None
        self.need_sig = False


class Sched:
    SEM_LIMIT = 30000

    def __init__(self, nc, n_dma_sems=12):
        self.nc = nc
        self.streams = {e: [] for e in ENGS}
        self.last_w = {}
        self.readers = {}
        self.n_dma_sems = n_dma_sems
        self.live_dmas = []

    def op(self, eng, fn, reads=(), writes=(), dma=False):
        o = Op(eng, fn, dma)
        deps = []
        for k in reads:
            w = self.last_w.get(k)
            if w is not None:
                deps.append(w)
            if isinstance(k, tuple) and k[0] == "ps":
                for r in self.readers.get(k, ()):
                    if r.eng != eng:
                        deps.append(r)
        for k in writes:
            w = self.last_w.get(k)
            if w is not None:
                deps.append(w)
            deps.extend(self.readers.get(k, ()))
        seen = set()
        for d in deps:
            if d is o or id(d) in seen:
                continue
            seen.add(id(d))
            if (not d.is_dma) and d.eng == eng and eng == "pe":
                continue
            o.deps.append(d)
            d.need_sig = True
        for k in reads:
            self.readers.setdefault(k, []).append(o)
        for k in writes:
            self.last_w[k] = o
            self.readers[k] = []
        self.streams[eng].append(o)
        if dma:
            self.live_dmas.append(o)
        return o

    def dma(self, q, out, in_, reads=(), writes=(), **kw):
        return self.op(q, lambda e: e.dma_start(out=out, in_=in_, **kw), reads, writes, dma=True)

    def barrier(self):
        lasts = []
        for e in ENGS:
            for o in reversed(self.streams[e]):
                if o.fn is not None and not o.is_dma:
                    lasts.append(o)
                    break
        dmas = list(self.live_dmas)
        for e in ENGS:
            b = Op(e, None, False)
            for d in lasts:
                if d.eng != e:
                    b.deps.append(d)
                    d.need_sig = True
            b.deps.extend(dmas)
            self.streams[e].append(b)
        self.live_dmas = []
        self.last_w = {}
        self.readers = {}

    def emit(self, final_wait_ops=()):
        nc = self.nc
        with contextlib.ExitStack() as es:
            for e in ENGS:
                sigs = [o for o in self.streams[e] if o.need_sig and not o.is_dma]
                n_sems = max(1, (len(sigs) + self.SEM_LIMIT - 1) // self.SEM_LIMIT)
                sems = [es.enter_context(nc.semaphore(f"s_{e}_{i}")) for i in range(n_sems)]
                for cnt, o in enumerate(sigs):
                    o.sig = (sems[cnt // self.SEM_LIMIT], cnt % self.SEM_LIMIT + 1)
            dma_prev = {}
            for e in ENGS:
                dmas = [o for o in self.streams[e] if o.is_dma]
                if not dmas:
                    continue
                k = min(self.n_dma_sems, len(dmas))
                sems = [es.enter_context(nc.semaphore(f"d_{e}_{i}")) for i in range(k)]
                uses = [0] * k
                for cnt, o in enumerate(dmas):
                    s = cnt % k
                    uses[s] += 1
                    o.sig = (sems[s], 16 * uses[s])
                    dma_prev[id(o)] = (sems[s], 16 * (uses[s] - 1)) if uses[s] > 1 else None
            blk = es.enter_context(nc.Block())
            handles = {"pe": blk.tensor, "act": blk.scalar, "dve": blk.vector,
                       "pool": blk.gpsimd, "sp": blk.sync}
            for e in ENGS:
                ops = self.streams[e]

                def body(h, ops=ops, e=e):
                    waited = {}

                    def wait(sem, val):
                        if waited.get(id(sem), 0) >= val:
                            return
                        waited[id(sem)] = val
                        h.wait_ge(sem, val)

                    for o in ops:
                        for d in o.deps:
                            wait(*d.sig)
                        if o.fn is None:
                            continue
                        if o.is_dma:
                            p = dma_prev.get(id(o))
                            if p is not None:
                                wait(*p)
                        inst = o.fn(h)
                        if o.is_dma:
                            inst.then_inc(o.sig[0], 16)
                        elif o.need_sig:
                            inst.then_inc(o.sig[0], 1)
                    if e == "sp":
                        for o in final_wait_ops:
                            wait(*o.sig)

                handles[e](body)


def V(ap, dims, off=0):
    return bass.AP(tensor=ap.tensor, offset=ap.offset + off,
                   ap=[list(ap.ap[0])] + [list(d) for d in dims])


def t5_bucket_table(nmax=256):
    n = np.arange(nmax)
    nf = np.maximum(n, 1).astype(np.float32)
    large = 16 + (np.log(nf / np.float32(16)) / np.float32(np.log(128 / 16)) * np.float32(16)).astype(np.int32)
    large = np.minimum(large, 31)
    return np.where(n < 16, n, large)


YDEF = 4
CONV_SPLIT = True


def build(nseq=2, stage=99, dbg=False):
    nc = bass.Bass("TRN2", target_bir_lowering=False)

    def din(name, shape):
        return nc.dram_tensor(name, list(shape), F32, kind="ExternalInput").ap()

    x = din("x", [nseq, S_LEN, D])
    rel_bias = din("rel_bias", [32, 8])
    norm_mix_g = din("norm_mix_g", [1, D])
    w_in = din("w_in", [D, IN_COLS])
    q_norm_g = din("q_norm_g", [1, 64])
    k_norm_g = din("k_norm_g", [1, 64])
    lam_q1 = din("lambda_q1", [1, 64])
    lam_k1 = din("lambda_k1", [1, 64])
    lam_q2 = din("lambda_q2", [1, 64])
    lam_k2 = din("lambda_k2", [1, 64])
    subln_g = din("attn_subln_g", [1, 128])
    conv_ssm_w = din("conv_ssm_w", [4, 3072])
    conv_ssm_b = din("conv_ssm_b", [1, 3072])
    dt_bias = din("dt_bias", [1, 32])
    a_log = din("a_log", [1, 32])
    d_skip = din("d_skip", [1, 32])
    ssm_norm_g = din("ssm_norm_g", [1, 2048])
    w_pa = din("w_proj_attn", [1024, 1024])
    w_ps = din("w_proj_ssm", [2048, 1024])
    w_out = din("w_out", [1024, 1024])
    norm_ffn_g = din("norm_ffn_g", [1, D])
    w_up = din("w_up", [D, 2 * D_FF])
    conv_ffn_w = din("conv_ffn_w", [3, 2 * D_FF])
    conv_ffn_b = din("conv_ffn_b", [1, 2 * D_FF])
    w_down = din("w_down", [D_FF, D])
    out = nc.dram_tensor("out", [nseq, S_LEN, D], F32, kind="ExternalOutput").ap()

    skind = "ExternalOutput" if dbg else "Internal"
    ext_d = nc.dram_tensor("ext_d", [8, 384], F32, kind="Internal").ap()
    Zd = nc.dram_tensor("Zd", [8, 128, 384], F32, kind="Internal").ap()
    ya_d = nc.dram_tensor("ya_d", [nseq, NT, 128, 8, 128], BF16, kind=skind).ap()
    yss_d = nc.dram_tensor("yss_d", [nseq, NT, 128, 16, 128], BF16, kind=skind).ap()
    if dbg:
        hT_d = nc.dram_tensor("hT_d", [128, 8, S_LEN], BF16, kind="ExternalOutput").ap()
        dt_d = nc.dram_tensor("dt_d", [128, NT, 32], F32, kind="ExternalOutput").ap()

    w_in_v = w_in.rearrange("(kc p) c -> p kc c", p=128)
    w_up_v = w_up.rearrange("(kc p) c -> p kc c", p=128)
    w_pa_v = w_pa.rearrange("(kc p) c -> p kc c", p=128)
    w_ps_v = w_ps.rearrange("(kc p) c -> p kc c", p=128)
    w_out_v = w_out.rearrange("(kc p) c -> p kc c", p=128)
    w_dn_v = w_down.rearrange("(kc p) c -> p kc c", p=128)

    S = Sched(nc)
    finals = []

    def MM(o, lhsT, rhs, start, stop, r, w):
        S.op("pe", lambda e: e.matmul(out=o, lhsT=lhsT, rhs=rhs, start=start, stop=stop), r, w)

    def TR(o, in_, ident, r, w):
        S.op("pe", lambda e: e.transpose(out=o, in_=in_, identity=ident), r, w)

    def ACT(o, in_, func, r, w, bias=None, scale=None, accum=None):
        kw = {}
        if bias is not None:
            kw["bias"] = bias
        if scale is not None:
            kw["scale"] = scale
        if accum is not None:
            kw["accum_out"] = accum
        S.op("act", lambda e: e.activation(out=o, in_=in_, func=func, **kw), r, w)

    def TT(eng, o, in0, in1, op, r, w):
        S.op(eng, lambda e: e.tensor_tensor(out=o, in0=in0, in1=in1, op=op), r, w)

    def TS(eng, o, in0, s1, op0, r, w, s2=None, op1=None):
        if op1 is None:
            S.op(eng, lambda e: e.tensor_scalar(out=o, in0=in0, scalar1=s1, scalar2=None, op0=op0), r, w)
        else:
            S.op(eng, lambda e: e.tensor_scalar(out=o, in0=in0, scalar1=s1, scalar2=s2, op0=op0, op1=op1), r, w)

    def STT(o, in0, scalar, in1, op0, op1, r, w):
        S.op("dve", lambda e: e.scalar_tensor_tensor(out=o, in0=in0, scalar=scalar, in1=in1, op0=op0, op1=op1), r, w)

    def CP(eng, o, in_, r, w):
        if eng == "act":
            S.op("act", lambda e: e.copy(out=o, in_=in_), r, w)
        else:
            S.op(eng, lambda e: e.tensor_copy(out=o, in_=in_), r, w)

    def RECIP(o, in_, r, w):
        S.op("dve", lambda e: e.reciprocal(out=o, in_=in_), r, w)

    def MEMSET(eng, ap, val, w):
        S.op(eng, lambda e: e.memset(ap, val), (), w)

    def bc(ap, n=128):
        return ap.broadcast_to([n, ap.shape[-1]])

    with contextlib.ExitStack() as G:
        def sbg(name, shape, dt):
            return G.enter_context(nc.sbuf_tensor(name, list(shape), dt))

        ps = G.enter_context(nc.psum_tensor("ps", [128, 8, 512], F32))

        def PB(b):
            return ("ps", b)

        def psbf(b):
            return ps[:, b, :].bitcast(BF16)

        ident = sbg("ident", [128, 128], BF16)
        identf = sbg("identf", [128, 128], F32)
        U = sbg("U", [128, 128], F32)
        Lst = sbg("Lst", [128, 128], F32)
        ones_f = sbg("ones_f", [128, 128], F32)
        gmix_bc = sbg("gmix_bc", [128, D], F32)
        gffn_bc = sbg("gffn_bc", [128, D], F32)
        gqk_bc = sbg("gqk_bc", [128, 256], F32)
        subln_bc = sbg("subln_bc", [128, 128], F32)
        neglam = sbg("neglam", [128, 1], F32)
        lamt = sbg("lamt", [128, 4, 64], F32)
        lamp = sbg("lamp", [128, 2, 64], F32)
        lame = sbg("lame", [128, 2], F32)
        b31 = sbg("b31", [128, 8], F32)
        convw = sbg("convw", [128, 24, 4], F32)
        convb = sbg("convb", [128, 24], F32)
        fconvw = sbg("fconvw", [128, 44, 3], F32)
        fconvb = sbg("fconvb", [128, 44], F32)
        dtb_bc = sbg("dtb_bc", [128, 32], F32)
        a_bc = sbg("a_bc", [128, 32], F32)
        dsk_bc = sbg("dsk_bc", [128, 32], F32)
        RB = sbg("RB", [8, 32], F32)
        ext_sb = sbg("ext_sb", [8, 384], F32)

        MEMSET("pool", ones_f[:], 1.0, ["ones_f"])
        MEMSET("pool", identf[:], 0.0, ["identf"])
        S.op("pool", lambda e: e.affine_select(out=identf[:], in_=identf[:], pattern=[[-1, 128]],
                                               compare_op=ALU.not_equal, fill=1.0, base=0, channel_multiplier=1),
             ["identf"], ["identf"])
        CP("dve", ident[:], identf[:], ["identf"], ["ident"])
        S.op("pool", lambda e: e.affine_select(out=U[:], in_=ones_f[:], pattern=[[1, 128]],
                                               compare_op=ALU.is_ge, fill=0.0, base=0, channel_multiplier=-1),
             ["ones_f"], ["U"])
        S.op("pool", lambda e: e.affine_select(out=Lst[:], in_=ones_f[:], pattern=[[-1, 128]],
                                               compare_op=ALU.is_ge, fill=0.0, base=-1, channel_multiplier=1),
             ["ones_f"], ["Lst"])
        S.dma("sp", gmix_bc[:], bc(norm_mix_g), writes=["gmix"])
        S.dma("sp", gffn_bc[:], bc(norm_ffn_g), writes=["gffn"])
        S.dma("sp", gqk_bc[:, 0:64], bc(q_norm_g), writes=["gqk"])
        S.dma("sp", gqk_bc[:, 64:128], bc(q_norm_g), writes=["gqk"])
        S.dma("sp", gqk_bc[:, 128:192], bc(k_norm_g), writes=["gqk"])
        S.dma("sp", gqk_bc[:, 192:256], bc(k_norm_g), writes=["gqk"])
        ACT(gqk_bc[:, 0:128], gqk_bc[:, 0:128], AF.Identity, ["gqk"], ["gqk"], scale=0.125)
        S.dma("sp", subln_bc[:], bc(subln_g), writes=["subln"])
        ACT(subln_bc[:], subln_bc[:], AF.Identity, ["subln"], ["subln"], scale=0.8)
        for i, a in enumerate((lam_q1, lam_q2, lam_k1, lam_k2)):
            S.dma("sp", lamt[:, i, :], bc(a), writes=["lamt"])
        TT("dve", lamp[:], lamt[:, 0:2, :], lamt[:, 2:4, :], ALU.mult, ["lamt"], ["lamp"])
        S.op("dve", lambda e: e.tensor_reduce(out=lame[:], in_=lamp[:], axis=AX.X, op=ALU.add), ["lamp"], ["lame"])
        ACT(lame[:], lame[:], AF.Exp, ["lame"], ["lame"])
        TT("dve", neglam[:], lame[:, 1:2], lame[:, 0:1], ALU.subtract, ["lame"], ["neglam"])
        TS("dve", neglam[:], neglam[:], -0.2, ALU.add, ["neglam"], ["neglam"])
        S.dma("sp", b31[:], bc(rel_bias[31:32, :]), writes=["b31"])
        S.dma("sp", dtb_bc[:], bc(dt_bias), writes=["dtb"])
        S.dma("sp", dsk_bc[:], bc(d_skip), writes=["dsk"])
        S.dma("sp", a_bc[:], bc(a_log), writes=["a_bc"])
        ACT(a_bc[:], a_bc[:], AF.Exp, ["a_bc"], ["a_bc"])
        ACT(a_bc[:], a_bc[:], AF.Identity, ["a_bc"], ["a_bc"], scale=-1.0)
        stg = sbg("stg", [64, 9, 128], F32)
        rbs = sbg("rbs", [32, 8], F32)
        MEMSET("pool", stg[:], 0.0, ["stg"])
        S.dma("sp", stg[0:24, 0:4, :], conv_ssm_w.rearrange("k (cc p) -> cc k p", p=128), writes=["stg"])
        S.dma("sp", stg[0:24, 4, :], conv_ssm_b[0, :].rearrange("(cc p) -> cc p", p=128), writes=["stg"])
        S.dma("sp", stg[0:44, 5:8, :], conv_ffn_w.rearrange("k (cc p) -> cc k p", p=128), writes=["stg"])
        S.dma("sp", stg[0:44, 8, :], conv_ffn_b[0, :].rearrange("(cc p) -> cc p", p=128), writes=["stg"])
        S.dma("sp", rbs[:], rel_bias, writes=["rbs"])
        def pcol(k):
            return k * 32 if k < 5 else 160 + (k - 5) * 64
        for k in range(9):
            n = 32 if k < 5 else 64
            TR(ps[:, 0, pcol(k):pcol(k) + n], stg[0:n, k, :], identf[0:n, 0:n], ["stg", "identf"], [PB(0)])
        for k in range(4):
            CP("dve", convw[:, :, k], ps[:, 0, pcol(k):pcol(k) + 24], [PB(0)], ["convw"])
        CP("dve", convb[:], ps[:, 0, pcol(4):pcol(4) + 24], [PB(0)], ["convb"])
        for k in range(3):
            CP("dve", fconvw[:, :, k], ps[:, 0, pcol(5 + k):pcol(5 + k) + 44], [PB(0)], ["fconvw"])
        CP("dve", fconvb[:], ps[:, 0, pcol(8):pcol(8) + 44], [PB(0)], ["fconvb"])
        TR(ps[0:8, 1, 0:32], rbs[:], identf[0:32, 0:32], ["rbs", "identf"], [PB(1)])
        CP("dve", RB[:], ps[0:8, 1, 0:32], [PB(1)], ["RB"])
        MEMSET("pool", ext_sb[:], -30000.0, ["ext_sb"])
        CP("dve", ext_sb[:, 127:143], RB[:, 0:16], ["RB", "ext_sb"], ["ext_sb"])
        bt = t5_bucket_table(256)
        assert (bt[113:] == 31).all()
        for bk in range(16, 32):
            idx = np.nonzero(bt == bk)[0]
            if len(idx) == 0:
                continue
            n0, n1 = int(idx[0]), int(idx[-1]) + 1
            assert n1 - n0 == len(idx)
            CP("dve", ext_sb[:, 127 + n0:127 + n1], V(RB[:, bk:bk + 1], [[0, n1 - n0]]), ["RB", "ext_sb"], ["ext_sb"])
        S.dma("sp", ext_d, ext_sb[:, :], reads=["ext_sb"], writes=["ext_d"])
        S.dma("sp", Zd, bass.AP(tensor=ext_d.tensor, offset=ext_d.offset, ap=[[384, 8], [0, 128], [1, 384]]),
              reads=["ext_d"], writes=["Zd"])
        S.barrier()

        for b in range(nseq):
            with contextlib.ExitStack() as Q:
                def sbq(name, shape, dt):
                    return Q.enter_context(nc.sbuf_tensor(f"{name}_{b}", list(shape), dt))

                hT = sbq("hT", [128, 8, S_LEN], BF16)
                HT_ALL = [("hT", t) for t in range(NT)]

                with contextlib.ExitStack() as P:
                    def sbp(name, shape, dt):
                        return P.enter_context(nc.sbuf_tensor(f"{name}_{b}", list(shape), dt))
                    xts = [sbp(f"p0_xt{i}", [128, D], F32) for i in range(2)]
                    hns = [sbp(f"p0_hn{i}", [128, D], BF16) for i in range(2)]
                    junk = sbp("p0_junk", [128, D], BF16)
                    ssq = sbp("p0_ssq", [128, NT], F32)
                    rs = sbp("p0_rs", [128, NT], F32)
                    for t in range(NT):
                        i2 = t % 2
                        xt, hn = xts[i2], hns[i2]
                        S.dma("sp", xt[:], x[b, t * 128:(t + 1) * 128, :], writes=[("xt", i2)])
                        ACT(junk[:], xt[:], AF.Square, [("xt", i2)], ["junk", ("ssq", t)], accum=ssq[:, t:t + 1])
                        ACT(rs[:, t:t + 1], ssq[:, t:t + 1], AF.Sqrt, [("ssq", t)], [("rs", t)], bias=1e-6, scale=1.0 / D)
                        RECIP(rs[:, t:t + 1], rs[:, t:t + 1], [("rs", t)], [("rs", t)])
                        STT(hn[:], xt[:], rs[:, t:t + 1], gmix_bc[:], ALU.mult, ALU.mult,
                            [("xt", i2), ("rs", t), "gmix"], [("hn", i2)])
                        pb = 2 * i2
                        pv = psbf(pb)
                        for c in range(8):
                            TR(pv[:, c * 128:(c + 1) * 128], hn[:, c * 128:(c + 1) * 128], ident[:],
                               [("hn", i2), "ident"], [PB(pb)])
                        CP("act" if i2 else "dve", hT[:, :, t * 128:(t + 1) * 128],
                           V(pv, [[128, 8], [1, 128]]), [PB(pb)], [("hT", t)])
                    if dbg and b == 0:
                        finals.append(S.dma("sp", hT_d, hT[:], reads=HT_ALL))
                    S.barrier()
                if stage <= 0:
                    continue

                with contextlib.ExitStack() as P:
                    def sbp(name, shape, dt):
                        return P.enter_context(nc.sbuf_tensor(f"{name}_{b}", list(shape), dt))
                    NP = 28
                    BT = sbp("p1_BT", [128, 8, 256], F32)
                    wqkv = [sbp(f"p1_w{i}", [128, 8, 384], BF16) for i in range(2)]
                    qkT = [sbp(f"p1_qkT{i}", [128, 2, S_LEN], BF16) for i in range(2)]
                    vaug = [sbp(f"p1_v{i}", [128, NT, 132], BF16) for i in range(2)]
                    PT = [sbp(f"p1_PT{i}", [128, 2, 512], BF16) for i in range(NP)]
                    sq = [sbp(f"p1_sq{i}", [128, 256], F32) for i in range(2)]
                    tmpn = [sbp(f"p1_tmpn{i}", [128, 256], F32) for i in range(2)]
                    qkn = [sbp(f"p1_qkn{i}", [128, 256], BF16) for i in range(4)]
                    ssq4 = sbp("p1_ssq4", [128, NT, 4], F32)
                    rs4 = sbp("p1_rs4", [128, NT, 4], F32)
                    etmp = [sbp(f"p1_et{i}", [128, 2, 256], F32) for i in range(2)]
                    rl = sbp("p1_rl", [128, NT, 2], F32)
                    nrl = sbp("p1_nrl", [128, NT], F32)
                    o1 = [sbp(f"p1_o1{i}", [128, 128], F32) for i in range(2)]
                    oo = [sbp(f"p1_oo{i}", [128, 128], F32) for i in range(2)]
                    junk2 = sbp("p1_junk2", [128, 128], BF16)
                    sso = sbp("p1_sso", [128, NT], F32)
                    rso = sbp("p1_rso", [128, NT], F32)
                    yn = [sbp(f"p1_yn{i}", [128, 128], BF16) for i in range(4)]
                    yst = [sbp(f"p1_yst{i}", [128, 4, 128], BF16) for i in range(2)]

                    for h in range(8):
                        S.dma("sp", BT[:, h, :],
                              bass.AP(tensor=Zd.tensor, offset=Zd.offset + h * 128 * 384 + 127, ap=[[383, 128], [1, 256]]),
                              writes=[("BT", h)])
                    for i in range(2):
                        MEMSET("pool", vaug[i][:, :, 128:129], 1.0, [("vaug", i)])

                    pends = {"q": [], "y": []}

                    def tick():
                        for pend in pends.values():
                            for it in pend:
                                it[0] -= 1
                            while pend and pend[0][0] <= 0:
                                pend.pop(0)[1]()

                    def defer(n, fn, q="q"):
                        pends[q].append([n, fn])

                    def flush():
                        for pend in pends.values():
                            while pend:
                                pend.pop(0)[1]()

                    st_ = {"pt": 0, "sb": 0, "yst": 0, "qkn": 0, "yn": 0}

                    def load_w(h):
                        sl = h % 2
                        for j3, c0 in enumerate((C_Q, C_K, C_V)):
                            S.dma("pool", wqkv[sl][:, :, j3 * 128:(j3 + 1) * 128],
                                  w_in_v[:, :, c0 + h * 128:c0 + (h + 1) * 128], writes=[("wqkv", sl, j3)])

                    def proj_item(h, t):
                        def f():
                            sl = h % 2
                            W = wqkv[sl]
                            i2 = t % 2
                            pb = 4 + 2 * i2
                            for kc in range(8):
                                MM(ps[:, pb, 0:384], hT[:, kc, t * 128:(t + 1) * 128], W[:, kc, :], kc == 0, kc == 7,
                                   [("hT", t)] + [("wqkv", sl, q) for q in range(3)], [PB(pb)])
                            ACT(sq[i2][:], ps[:, pb, 0:256], AF.Square, [PB(pb)], [("sq", i2)])
                            S.op("dve", lambda e: e.tensor_reduce(
                                out=ssq4[:, t, :], in_=V(sq[i2][:], [[64, 4], [1, 64]]), axis=AX.X, op=ALU.add),
                                [("sq", i2)], [("ssq4", t)])
                            ACT(rs4[:, t, :], ssq4[:, t, :], AF.Ln, [("ssq4", t)], [("rs4", t)], bias=1e-6, scale=1.0 / 64)
                            ACT(rs4[:, t, :], rs4[:, t, :], AF.Exp, [("rs4", t)], [("rs4", t)], scale=-0.5)
                            TT("dve", V(tmpn[i2][:], [[64, 4], [1, 64]]), V(ps[:, pb, 0:256], [[64, 4], [1, 64]]),
                               V(rs4[:, t, :], [[1, 4], [0, 64]]), ALU.mult, [PB(pb), ("rs4", t)], [("tmpn", i2)])
                            qi = st_["qkn"] % 4
                            st_["qkn"] += 1
                            TT("pool", qkn[qi][:], tmpn[i2][:], gqk_bc[:], ALU.mult, [("tmpn", i2), "gqk"], [("qkn", qi)])
                            CP("act", vaug[sl][:, t, 0:128], ps[:, pb, 256:384], [PB(pb)], [("vaug", sl)])

                            def g():
                                pv = psbf(2)
                                hf = i2 * 512
                                for m2 in range(2):
                                    TR(pv[:, hf + m2 * 128:hf + (m2 + 1) * 128], qkn[qi][:, m2 * 128:(m2 + 1) * 128], ident[:],
                                       [("qkn", qi), "ident"], [PB(2)])
                                CP("act" if i2 else "dve", qkT[sl][:, :, t * 128:(t + 1) * 128],
                                   V(pv[:, hf:hf + 256], [[128, 2], [1, 128]]), [PB(2)], [("qkT", sl, t)])
                            defer(3, g)
                            tick()
                        return f

                    PTc = {}

                    def qk_item(h, c, j):
                        def f():
                            sl = h % 2
                            r = j - 4 * c
                            st = max(0, r) * 128
                            b0 = 4 + 2 * (st_["sb"] % 2)
                            st_["sb"] += 1
                            qkeys = [("qkT", sl, j)] + [("qkT", sl, q) for q in range(4 * c + st // 128, 4 * c + 4)]
                            for m in range(2):
                                MM(ps[:, b0 + m, st:512], qkT[sl][64 * m:64 * m + 64, 1, j * 128:(j + 1) * 128],
                                   qkT[sl][64 * m:64 * m + 64, 0, 512 * c + st:512 * c + 512], True, True,
                                   qkeys, [PB(b0 + m)])
                            pi = st_["pt"] % NP
                            st_["pt"] += 1
                            PTc[(h, c, j)] = pi
                            Pt = PT[pi]
                            rb = [PB(b0), PB(b0 + 1)]
                            if r >= -1:
                                if r >= 0:
                                    nb = min(2, 4 - r)
                                    btsl = BT[:, h, 0:128 * nb]
                                else:
                                    nb = 1
                                    btsl = BT[:, h, 128:256]
                                wdt = 128 * nb
                                ei = st_["sb"] % 2
                                TT("dve", etmp[ei][:, :, 0:wdt], ps[:, b0:b0 + 2, st:st + wdt],
                                   V(btsl, [[0, 2], [1, wdt]]), ALU.add, rb + [("BT", h)], [("etmp", ei)])
                                ACT(Pt[:, :, st:st + wdt], etmp[ei][:, :, 0:wdt], AF.Exp, [("etmp", ei)], [("PT", pi)])
                                if st + wdt < 512:
                                    ACT(Pt[:, :, st + wdt:512], ps[:, b0:b0 + 2, st + wdt:512], AF.Exp,
                                        rb + ["b31"], [("PT", pi)], bias=b31[:, h:h + 1])
                            else:
                                ACT(Pt[:, :, :], ps[:, b0:b0 + 2, :], AF.Exp, rb + ["b31"], [("PT", pi)],
                                    bias=b31[:, h:h + 1])
                            tick()
                        return f

                    def av_item(h, c, i, m):
                        def f():
                            sl = h % 2
                            ob = i % 2
                            i2 = i % 2
                            for j in range(i + 1):
                                pi = PTc[(h, c, j)]
                                MM(ps[:, ob, 256 * m:256 * m + 129],
                                   PT[pi][:, m, (i - 4 * c) * 128:(i - 4 * c + 1) * 128],
                                   vaug[sl][:, j, 0:129], j == 0, j == i,
                                   [("PT", pi), ("vaug", sl)], [PB(ob)])
                            if m == 1:
                                RECIP(rl[:, i, :], V(ps[:, ob, 128:129], [[256, 2]]), [PB(ob)], [("rl", i)])
                                TS("dve", nrl[:, i:i + 1], rl[:, i, 1:2], neglam[:, 0:1], ALU.mult,
                                   [("rl", i), "neglam"], [("nrl", i)])
                                ACT(o1[i2][:], ps[:, ob, 0:128], AF.Identity, [PB(ob), ("rl", i)], [("o1", i2)],
                                    scale=rl[:, i, 0:1])
                                STT(oo[i2][:], ps[:, ob, 256:384], nrl[:, i:i + 1], o1[i2][:], ALU.mult, ALU.add,
                                    [PB(ob), ("nrl", i), ("o1", i2)], [("oo", i2)])
                                ACT(junk2[:], oo[i2][:], AF.Square, [("oo", i2)], ["junk2", ("sso", i)], accum=sso[:, i:i + 1])
                                ACT(rso[:, i:i + 1], sso[:, i:i + 1], AF.Ln, [("sso", i)], [("rso", i)],
                                    bias=1e-5, scale=1.0 / 128)
                                ACT(rso[:, i:i + 1], rso[:, i:i + 1], AF.Exp, [("rso", i)], [("rso", i)], scale=-0.5)
                                yi = st_["yn"] % 4
                                st_["yn"] += 1
                                STT(yn[yi][:], oo[i2][:], rso[:, i:i + 1], subln_bc[:], ALU.mult, ALU.mult,
                                    [("oo", i2), ("rso", i), "subln"], [("yn", yi)])

                                def g():
                                    hf = c % 2
                                    pv = psbf(3)
                                    col = hf * 512 + (i % 4) * 128
                                    TR(pv[:, col:col + 128], yn[yi][:], ident[:], [("yn", yi), "ident"], [PB(3)])
                                    if i % 4 == 3:
                                        yc = st_["yst"] % 2
                                        st_["yst"] += 1
                                        ys = yst[yc]
                                        CP("act", ys[:], V(pv[:, hf * 512:hf * 512 + 512], [[128, 4], [1, 128]]),
                                           [PB(3)], [("yst", yc)])
                                        dst = bass.AP(tensor=ya_d.tensor,
                                                      offset=ya_d.offset + ((b * NT + 4 * c) * 128 * 8 + h) * 128,
                                                      ap=[[8 * 128, 128], [128 * 8 * 128, 4], [1, 128]])
                                        o_ = S.dma("sp", dst, ys[:], reads=[("yst", yc)], writes=[("ya_d", b, c, h)])
                                        if dbg:
                                            finals.append(o_)
                                defer(YDEF, g, "y")
                            tick()
                        return f

                    def run(items):
                        for it in items:
                            it()

                    def merge(A, B):
                        nb_done = 0
                        for idx, a in enumerate(A):
                            a()
                            tgt = (idx + 1) * len(B) // len(A)
                            while nb_done < tgt:
                                B[nb_done]()
                                nb_done += 1
                        while nb_done < len(B):
                            B[nb_done]()
                            nb_done += 1

                    def qk_items(h, c):
                        return [qk_item(h, c, j) for j in range(4 * c + 4)]

                    def av_items(h, c):
                        return [av_item(h, c, i, m) for i in range(4 * c, 4 * c + 4) for m in range(2)]

                    def proj_items(h):
                        return [proj_item(h, t) for t in range(NT)]

                    load_w(0)
                    run(proj_items(0))
                    for h in range(8):
                        if h < 7:
                            load_w(h + 1)
                        run(qk_items(h, 0))
                        for c in range(3):
                            merge(av_items(h, c), qk_items(h, c + 1))
                        merge(av_items(h, 3), proj_items(h + 1) if h < 7 else [])
                    flush()
                    S.barrier()
                if stage <= 1:
                    continue

                with contextlib.ExitStack() as P:
                    def sbp(name, shape, dt):
                        return P.enter_context(nc.sbuf_tensor(f"{name}_{b}", list(shape), dt))
                    ssmg_bc = sbp("p2_ssmg", [128, 2048], F32)
                    dt_all = sbp("p2_dt", [128, NT, 32], F32)
                    adt_all = sbp("p2_adt", [128, NT, 32], F32)
                    wdt_ = sbp("p2_wdt", [128, 8, 32], BF16)
                    dtt = [sbp(f"p2_dtt{i}", [128, 32], F32) for i in range(2)]
                    wx = [sbp(f"p2_wx{i}", [128, 8, 128], BF16) for i in range(3)]
                    wzs = [sbp(f"p2_wz{i}", [128, 8, 512], BF16) for i in range(2)]
                    acc = [sbp(f"p2_acc{i}", [128, 1024], F32) for i in range(2)]
                    accb = [sbp(f"p2_accb{i}", [128, 1024], F32) for i in range(2)]
                    zs_all = sbp("p2_zsall", [128, NT, 512], BF16)
                    halo = sbp("p2_halo", [128, 4], F32)
                    fmx = [sbp(f"p2_fmx{i}", [128, S_LEN], BF16) for i in range(2)]
                    BTg = sbp("p2_BTg", [128, S_LEN], BF16)
                    CTg = sbp("p2_CTg", [128, S_LEN], BF16)
                    xs_tok = sbp("p2_xs", [128, NT, 512], BF16)
                    B_tok = sbp("p2_Btok", [128, NT, 128], BF16)
                    state = sbp("p2_state", [128, 512], F32)
                    state_bf = sbp("p2_statebf", [128, 512], BF16)
                    s1 = sbp("p2_s1", [128, 512], F32)
                    cumtot = [sbp(f"p2_ct{i}", [128, 16], F32) for i in range(2)]
                    ecum = [sbp(f"p2_ec{i}", [128, 16], F32) for i in range(2)]
                    w8 = [sbp(f"p2_w8{i}", [128, 8], F32) for i in range(2)]
                    adtU = [sbp(f"p2_adtU{i}", [128, 8, 128], F32) for i in range(2)]
                    eseg = [sbp(f"p2_eseg{i}", [128, 8, 128], BF16) for i in range(2)]
                    cbTm = [sbp(f"p2_cbT{i}", [128, 128], BF16) for i in range(2)]
                    MT = [sbp(f"p2_MT{i}", [128, 8, 128], BF16) for i in range(2)]
                    xdt = [sbp(f"p2_xdt{i}", [128, 8, 64], BF16) for i in range(2)]
                    xw = [sbp(f"p2_xw{i}", [128, 8, 64], BF16) for i in range(2)]
                    t1 = [sbp(f"p2_t1{i}", [128, 512], F32) for i in range(2)]
                    t3 = [sbp(f"p2_t3{i}", [128, 512], F32) for i in range(2)]
                    junk3 = sbp("p2_junk3", [128, 512], BF16)
                    ssy = sbp("p2_ssy", [128, NT], F32)
                    rsy = sbp("p2_rsy", [128, NT], F32)
                    ynb = [sbp(f"p2_ynb{i}", [128, 512], BF16) for i in range(2)]
                    ysT = [sbp(f"p2_ysT{i}", [128, 4, 128], BF16) for i in range(2)]

                    S.dma("sp", ssmg_bc[:], bc(ssm_norm_g), writes=["ssmg"])
                    S.dma("pool", wdt_[:], w_in_v[:, :, C_DT:C_DT + 32], writes=["wdt"])
                    for t in range(NT):
                        i2 = t % 2
                        for kc in range(8):
                            MM(ps[:, 5, 0:32], hT[:, kc, t * 128:(t + 1) * 128], wdt_[:, kc, :], kc == 0, kc == 7,
                               [("hT", t), "wdt"], [PB(5)])
                        TT("dve", dtt[i2][:], ps[:, 5, 0:32], dtb_bc[:], ALU.add, [PB(5), "dtb"], [("dtt", i2)])
                        ACT(dtt[i2][:], dtt[i2][:], AF.Exp, [("dtt", i2)], [("dtt", i2)])
                        ACT(dt_all[:, t, :], dtt[i2][:], AF.Ln, [("dtt", i2)], [("dt", t)], bias=1.0, scale=1.0)
                        TT("pool", adt_all[:, t, :], dt_all[:, t, :], a_bc[:], ALU.mult, [("dt", t), "a_bc"], [("adt", t)])
                    if dbg and b == 0:
                        finals.append(S.dma("sp", dt_d, dt_all[:], reads=[("dt", t) for t in range(NT)]))

                    st2 = {"pp": 0, "ab": 0, "zb": 0}
                    pend2 = []

                    def chunk_desc(n):
                        g_, ci_ = n // 6, n % 6
                        if ci_ < 4:
                            return g_, ci_, C_XS + g_ * 512 + ci_ * 128, g_ * 4 + ci_, fmx[ci_ % 2], ("fmx", ci_ % 2)
                        if ci_ == 4:
                            return g_, ci_, C_B + g_ * 128, 16 + g_, BTg, "BTg"
                        return g_, ci_, C_C + g_ * 128, 20 + g_, CTg, "CTg"

                    def load_wx(n):
                        if n >= 24:
                            return
                        col0 = chunk_desc(n)[2]
                        S.dma("pool", wx[n % 3][:], w_in_v[:, :, col0:col0 + 128], writes=[("wx", n % 3)])

                    def load_wz(g_):
                        S.dma("pool", wzs[g_ % 2][:], w_in_v[:, :, C_Z + g_ * 512:C_Z + (g_ + 1) * 512], writes=[("wz", g_ % 2)])

                    def emit_pend2():
                        while pend2:
                            pend2.pop(0)()

                    load_wx(0)
                    load_wx(1)
                    load_wz(0)
                    for g in range(4):
                        for ci in range(6):
                            n = g * 6 + ci
                            _, _, col0, cch, dstT, dkey = chunk_desc(n)
                            wi = n % 3
                            load_wx(n + 2)
                            for half in range(2):
                                bp = 2 * (st2["pp"] % 2)
                                st2["pp"] += 1
                                ai = st2["ab"] % 2
                                st2["ab"] += 1
                                A, Bv = acc[ai], accb[ai]
                                ak, bk = ("acc", ai), ("accb", ai)
                                for tt in range(2):
                                    tok0 = half * 1024 + tt * 512
                                    for kc in range(8):
                                        MM(ps[:, bp + tt, :], wx[wi][:, kc, :], hT[:, kc, tok0:tok0 + 512], kc == 0, kc == 7,
                                           [("wx", wi)] + [("hT", tok0 // 128 + q) for q in range(4)], [PB(bp + tt)])
                                emit_pend2()
                                pin = V(ps[:, bp, :], [[1, 1024]])
                                pin1 = V(ps[:, bp, :], [[1, 1023]])
                                rb = [PB(bp), PB(bp + 1)]
                                if not CONV_SPLIT:
                                    ACT(A[:], pin, AF.Identity, rb + ["convw", "convb"], [ak],
                                        bias=convb[:, cch:cch + 1], scale=convw[:, cch, 3:4])
                                    for k in (2, 1, 0):
                                        d_ = 3 - k
                                        STT(A[:, d_:1024], V(ps[:, bp, :], [[1, 1024 - d_]]), convw[:, cch, k:k + 1], A[:, d_:1024],
                                            ALU.mult, ALU.add, rb + ["convw", ak], [ak])
                                        if half == 1:
                                            STT(A[:, 0:d_], halo[:, 3 - d_:3], convw[:, cch, k:k + 1], A[:, 0:d_],
                                                ALU.mult, ALU.add, ["halo", "convw", ak], [ak])
                                    if half == 0:
                                        CP("act", halo[:, 0:3], ps[:, bp + 1, 509:512], [PB(bp + 1)], ["halo"])
                                else:
                                    ACT(A[:], pin, AF.Identity, rb + ["convw", "convb"], [ak],
                                        bias=convb[:, cch:cch + 1], scale=convw[:, cch, 3:4])
                                    if "8" in os.environ.get("KF", ""):
                                        TS("dve", Bv[:], pin, convw[:, cch, 1:2], ALU.mult, rb + ["convw"], [bk])
                                    else:
                                        ACT(Bv[:], pin, AF.Identity, rb + ["convw"], [bk], scale=convw[:, cch, 1:2])
                                    STT(A[:, 1:1024], pin1, convw[:, cch, 2:3], A[:, 1:1024], ALU.mult, ALU.add, rb + ["convw", ak], [ak])
                                    STT(Bv[:, 1:1024], pin1, convw[:, cch, 0:1], Bv[:, 1:1024], ALU.mult, ALU.add, rb + ["convw", bk], [bk])
                                    if half == 1 and "7" not in os.environ.get("KF", ""):
                                        STT(A[:, 0:1], halo[:, 2:3], convw[:, cch, 2:3], A[:, 0:1], ALU.mult, ALU.add, ["halo", "convw", ak], [ak])
                                        STT(A[:, 0:2], halo[:, 1:3], convw[:, cch, 1:2], A[:, 0:2], ALU.mult, ALU.add, ["halo", "convw", ak], [ak])
                                        STT(A[:, 0:2], halo[:, 0:2], convw[:, cch, 0:1], A[:, 0:2], ALU.mult, ALU.add, ["halo", "convw", ak], [ak])
                                        STT(Bv[:, 0:1], halo[:, 2:3], convw[:, cch, 0:1], Bv[:, 0:1], ALU.mult, ALU.add, ["halo", "convw", bk], [bk])
                                    if half == 0:
                                        CP("act", halo[:, 0:3], ps[:, bp + 1, 509:512], [PB(bp + 1)], ["halo"])
                                    TT("dve" if "5" in os.environ.get("KF", "") else "pool", A[:, 2:1024], A[:, 2:1024], Bv[:, 0:1022], ALU.add, [ak, bk], [ak])
                                ACT(dstT[:, half * 1024:(half + 1) * 1024], A[:], AF.Silu, [ak], [dkey])
                            if ci < 5:
                                def trans(ci=ci, dstT=dstT, dkey=dkey):
                                    for tb in range(2):
                                        pv = psbf(4)
                                        for q in range(8):
                                            t = tb * 8 + q
                                            TR(pv[:, q * 128:(q + 1) * 128], dstT[:, t * 128:(t + 1) * 128], ident[:],
                                               [dkey, "ident"], [PB(4)])
                                        if ci < 4:
                                            CP("act", xs_tok[:, tb * 8:(tb + 1) * 8, ci * 128:(ci + 1) * 128],
                                               V(pv, [[128, 8], [1, 128]]), [PB(4)], ["xs_tok"])
                                        else:
                                            CP("act", B_tok[:, tb * 8:(tb + 1) * 8, :], V(pv, [[128, 8], [1, 128]]), [PB(4)], ["B_tok"])
                                pend2.append(trans)
                                if "1" in os.environ.get("KF", ""):
                                    emit_pend2()
                        if "a" in os.environ.get("KSTOP", ""):
                            break
                        wz = wzs[g % 2]
                        for t in range(NT):
                            zb = 6 + (st2["zb"] % 2)
                            st2["zb"] += 1
                            for kc in range(8):
                                MM(ps[:, zb, :], hT[:, kc, t * 128:(t + 1) * 128], wz[:, kc, :], kc == 0, kc == 7,
                                   [("hT", t), ("wz", g % 2)], [PB(zb)])
                            if t == 1:
                                emit_pend2()
                            ACT(zs_all[:, t, :], ps[:, zb, :], AF.Silu, [PB(zb)], [("zs", t)])
                        if "z" in os.environ.get("KSTOP", ""):
                            break
                        if g < 3:
                            load_wz(g + 1)
                        MEMSET("pool", state[:], 0.0, ["state"])
                        MEMSET("pool", state_bf[:], 0.0, ["state_bf"])

                        def stageA(c):
                            i2 = c % 2
                            cs = slice(c * 128, (c + 1) * 128)
                            adt_c = adt_all[:, c, g * 8:(g + 1) * 8]
                            dt_c = dt_all[:, c, g * 8:(g + 1) * 8]
                            MM(ps[:, 5, 0:8], U[:], adt_c, True, True, ["U", ("adt", c)], [PB(5)])
                            MM(ps[:, 5, 8:16], ones_f[:], adt_c, True, True, ["ones_f", ("adt", c)], [PB(5)])
                            CP("act", cumtot[i2][:], ps[:, 5, 0:16], [PB(5)], [("cumtot", i2)])
                            ACT(ecum[i2][:], cumtot[i2][:], AF.Exp, [("cumtot", i2)], [("ecum", i2)])
                            TT("dve", w8[i2][:], cumtot[i2][:, 8:16], cumtot[i2][:, 0:8], ALU.subtract, [("cumtot", i2)], [("w8", i2)])
                            ACT(w8[i2][:], w8[i2][:], AF.Exp, [("w8", i2)], [("w8", i2)])
                            TT("pool", adtU[i2][:], V(adt_c, [[1, 8], [0, 128]]), V(U[:], [[0, 8], [1, 128]]), ALU.mult,
                               [("adt", c), "U"], [("adtU", i2)])
                            for q in range(2):
                                MM(ps[:, q, :], Lst[:], adtU[i2][:, 4 * q:4 * q + 4, :], True, True, ["Lst", ("adtU", i2)], [PB(q)])
                            ACT(V(eseg[i2][:], [[512, 2], [1, 512]]), ps[:, 0:2, :], AF.Exp, [PB(0), PB(1)], [("eseg", i2)])
                            MM(ps[:, 4, 0:128], BTg[:, cs], CTg[:, cs], True, True, ["BTg", "CTg"], [PB(4)])
                            TT("dve", cbTm[i2][:], ps[:, 4, 0:128], U[:], ALU.mult, [PB(4), "U"], [("cbTm", i2)])
                            TT("dve", MT[i2][:], eseg[i2][:], V(cbTm[i2][:], [[0, 8], [1, 128]]), ALU.mult,
                               [("eseg", i2), ("cbTm", i2)], [("MT", i2)])
                            TT("pool", xdt[i2][:], V(xs_tok[:, c, :], [[64, 8], [1, 64]]), V(dt_c, [[1, 8], [0, 64]]), ALU.mult,
                               ["xs_tok", ("dt", c)], [("xdt", i2)])
                            TT("pool", xw[i2][:], xdt[i2][:], V(w8[i2][:], [[1, 8], [0, 64]]), ALU.mult,
                               [("xdt", i2), ("w8", i2)], [("xw", i2)])

                        def stageB(c):
                            i2 = c % 2
                            cs = slice(c * 128, (c + 1) * 128)
                            for hh in range(8):
                                MM(ps[:, 2, hh * 64:(hh + 1) * 64], MT[i2][:, hh, :], xdt[i2][:, hh, :], True, True,
                                   [("MT", i2), ("xdt", i2)], [PB(2)])
                            MM(ps[:, 3, :], CTg[:, cs], state_bf[:], True, True, ["CTg", "state_bf"], [PB(3)])
                            if c < NT - 1:
                                MM(ps[:, 7, :], B_tok[:, c, :], V(xw[i2][:], [[1, 512]]), True, True, ["B_tok", ("xw", i2)], [PB(7)])
                            emit_pend2()
                            TT("pool", V(t3[i2][:], [[64, 8], [1, 64]]), V(xs_tok[:, c, :], [[64, 8], [1, 64]]),
                               V(dsk_bc[:, g * 8:(g + 1) * 8], [[1, 8], [0, 64]]), ALU.mult, ["xs_tok", "dsk"], [("t3", i2)])
                            TT("dve", V(t1[i2][:], [[64, 8], [1, 64]]), V(ps[:, 3, :], [[64, 8], [1, 64]]),
                               V(ecum[i2][:, 0:8], [[1, 8], [0, 64]]), ALU.mult, [PB(3), ("ecum", i2)], [("t1", i2)])
                            TT("dve", t1[i2][:], ps[:, 2, :], t1[i2][:], ALU.add, [PB(2), ("t1", i2)], [("t1", i2)])
                            TT("dve", t1[i2][:], t1[i2][:], t3[i2][:], ALU.add, [("t1", i2), ("t3", i2)], [("t1", i2)])
                            TT("pool", t3[i2][:], t1[i2][:], zs_all[:, c, :], ALU.mult, [("t1", i2), ("zs", c)], [("t3", i2)])
                            ACT(junk3[:], t3[i2][:], AF.Square, [("t3", i2)], ["junk3", ("ssy", c)], accum=ssy[:, c:c + 1])
                            ACT(rsy[:, c:c + 1], ssy[:, c:c + 1], AF.Ln, [("ssy", c)], [("rsy", c)], bias=1e-5, scale=1.0 / 512)
                            ACT(rsy[:, c:c + 1], rsy[:, c:c + 1], AF.Exp, [("rsy", c)], [("rsy", c)], scale=-0.5)

                            def trans_y(c=c, i2=i2, g=g):
                                STT(ynb[i2][:], t3[i2][:], rsy[:, c:c + 1], ssmg_bc[:, g * 512:(g + 1) * 512], ALU.mult, ALU.mult,
                                    [("t3", i2), ("rsy", c), "ssmg"], [("ynb", i2)])
                                pv = psbf(6)
                                for q in range(4):
                                    TR(pv[:, q * 128:(q + 1) * 128], ynb[i2][:, q * 128:(q + 1) * 128], ident[:],
                                       [("ynb", i2), "ident"], [PB(6)])
                                CP("act", ysT[i2][:], V(pv[:, 0:512], [[128, 4], [1, 128]]), [PB(6)], [("ysT", i2)])
                                dst = bass.AP(tensor=yss_d.tensor,
                                              offset=yss_d.offset + ((b * NT + c) * 128 * 16 + g * 4) * 128,
                                              ap=[[16 * 128, 128], [128, 4], [1, 128]])
                                o_ = S.dma("sp", dst, ysT[i2][:], reads=[("ysT", i2)], writes=[("yss_d", b, c, g)])
                                if dbg:
                                    finals.append(o_)
                            pend2.append(trans_y)
                            if "3" in os.environ.get("KF", ""):
                                emit_pend2()
                            if c < NT - 1:
                                TT("dve", V(s1[:], [[64, 8], [1, 64]]), V(state[:], [[64, 8], [1, 64]]),
                                   V(ecum[i2][:, 8:16], [[1, 8], [0, 64]]), ALU.mult, ["state", ("ecum", i2)], ["s1"])
                                TT("dve", state[:], ps[:, 7, :], s1[:], ALU.add, [PB(7), "s1"], ["state"])
                                CP("act", state_bf[:], state[:], ["state"], ["state_bf"])

                        if "2" in os.environ.get("KF", ""):
                            for c in range(NT):
                                stageA(c)
                                stageB(c)
                        else:
                            stageA(0)
                            for c in range(NT):
                                if c + 1 < NT:
                                    stageA(c + 1)
                                stageB(c)
                    emit_pend2()
                    S.barrier()
                if stage <= 2:
                    continue

                with contextlib.ExitStack() as P:
                    def sbp(name, shape, dt):
                        return P.enter_context(nc.sbuf_tensor(f"{name}_{b}", list(shape), dt))
                    wpa = sbp("p3_wpa", [128, 8, 1024], BF16)
                    wps = sbp("p3_wps", [128, 16, 1024], BF16)
                    wg = sbp("p3_wg", [128, 8, 2048], BF16)
                    wo = sbp("p3_wo", [128, 8, 1024], BF16)
                    yat = [sbp(f"p3_yat{i}", [128, 8, 128], BF16) for i in range(2)]
                    ysst = [sbp(f"p3_ysst{i}", [128, 16, 128], BF16) for i in range(2)]
                    xt3 = [sbp(f"p3_xt{i}", [128, D], F32) for i in range(2)]
                    sa = [sbp(f"p3_sa{i}", [128, 512], F32) for i in range(2)]
                    sg_ = [sbp(f"p3_sg{i}", [128, 512], F32) for i in range(2)]
                    mixed = [sbp(f"p3_mixed{i}", [128, D], BF16) for i in range(2)]
                    mixT = [sbp(f"p3_mixT{i}", [128, 8, 128], BF16) for i in range(2)]
                    x1 = [sbp(f"p3_x1{i}", [128, D], F32) for i in range(2)]
                    hn3 = [sbp(f"p3_hn{i}", [128, D], BF16) for i in range(2)]
                    junk4 = sbp("p3_junk", [128, D], BF16)
                    ss3 = sbp("p3_ss", [128, NT], F32)
                    rs3 = sbp("p3_rs", [128, NT], F32)
                    def lw(dst, src, q, key):
                        S.dma("pool", dst[:, :, q * 512:(q + 1) * 512], src[:, :, q * 512:(q + 1) * 512], writes=[(key, q)])
                    w_g_v = w_in_v[:, :, C_G:C_G + 2048]
                    for j in range(2):
                        lw(wg, w_g_v, j, "wg")
                        lw(wg, w_g_v, 2 + j, "wg")
                        lw(wpa, w_pa_v, j, "wpa")
                        lw(wps, w_ps_v, j, "wps")
                    for j in range(2):
                        lw(wo, w_out_v, j, "wo")
                    hcs = {"hc": 0}

                    def M3(t):
                        i2 = t % 2
                        S.dma("sp", yat[i2][:], ya_d[b, t], reads=[("ya_d", b, t // 4, hh) for hh in range(8)], writes=[("yat", i2)])
                        S.dma("sp", ysst[i2][:], yss_d[b, t], reads=[("yss_d", b, t, g) for g in range(4)], writes=[("ysst", i2)])
                        S.dma("sp", xt3[i2][:], x[b, t * 128:(t + 1) * 128, :], writes=[("xt3", i2)])
                        for j in range(2):
                            h2 = hcs["hc"] % 2
                            hcs["hc"] += 1
                            cj = slice(j * 512, (j + 1) * 512)
                            for kc in range(8):
                                MM(ps[:, 2, :], hT[:, kc, t * 128:(t + 1) * 128], wg[:, kc, cj], kc == 0, kc == 7,
                                   [("hT", t), ("wg", j)], [PB(2)])
                            ACT(sa[h2][:], ps[:, 2, :], AF.Sigmoid, [PB(2)], [("sa", h2)])
                            for kc in range(8):
                                MM(ps[:, 3, :], hT[:, kc, t * 128:(t + 1) * 128], wg[:, kc, 1024 + j * 512:1024 + (j + 1) * 512],
                                   kc == 0, kc == 7, [("hT", t), ("wg", 2 + j)], [PB(3)])
                            ACT(sg_[h2][:], ps[:, 3, :], AF.Sigmoid, [PB(3)], [("sg", h2)])
                            for c in range(8):
                                MM(ps[:, 0, :], yat[i2][:, c, :], wpa[:, c, cj], c == 0, c == 7, [("yat", i2), ("wpa", j)], [PB(0)])
                            TT("dve", sa[h2][:], ps[:, 0, :], sa[h2][:], ALU.mult, [PB(0), ("sa", h2)], [("sa", h2)])
                            for c in range(16):
                                MM(ps[:, 1, :], ysst[i2][:, c, :], wps[:, c, cj], c == 0, c == 15, [("ysst", i2), ("wps", j)], [PB(1)])
                            TT("dve", sg_[h2][:], ps[:, 1, :], sg_[h2][:], ALU.mult, [PB(1), ("sg", h2)], [("sg", h2)])
                            TT("pool", mixed[i2][:, cj], sa[h2][:], sg_[h2][:], ALU.add, [("sa", h2), ("sg", h2)], [("mixed", i2)])

                    def T31(t):
                        i2 = t % 2
                        pv = psbf(4)
                        for c in range(8):
                            TR(pv[:, c * 128:(c + 1) * 128], mixed[i2][:, c * 128:(c + 1) * 128], ident[:], [("mixed", i2), "ident"], [PB(4)])
                        CP("act", mixT[i2][:], V(pv, [[128, 8], [1, 128]]), [PB(4)], [("mixT", i2)])
                        for j in range(2):
                            for c in range(8):
                                MM(ps[:, 5 + j, :], mixT[i2][:, c, :], wo[:, c, j * 512:(j + 1) * 512], c == 0, c == 7,
                                   [("mixT", i2), ("wo", j)], [PB(5 + j)])
                        TT("dve", x1[i2][:], V(ps[:, 5, :], [[1, 1024]]), xt3[i2][:], ALU.add, [PB(5), PB(6), ("xt3", i2)], [("x1", i2)])
                        o_ = S.dma("sp", out[b, t * 128:(t + 1) * 128, :], x1[i2][:], reads=[("x1", i2)], writes=[("x1d", b, t)])
                        if stage <= 3:
                            finals.append(o_)
                        ACT(junk4[:], x1[i2][:], AF.Square, [("x1", i2)], ["junk4", ("ss3", t)], accum=ss3[:, t:t + 1])
                        ACT(rs3[:, t:t + 1], ss3[:, t:t + 1], AF.Sqrt, [("ss3", t)], [("rs3", t)], bias=1e-6, scale=1.0 / D)
                        RECIP(rs3[:, t:t + 1], rs3[:, t:t + 1], [("rs3", t)], [("rs3", t)])
                        STT(hn3[i2][:], x1[i2][:], rs3[:, t:t + 1], gffn_bc[:], ALU.mult, ALU.mult,
                            [("x1", i2), ("rs3", t), "gffn"], [("hn3", i2)])

                    def T32(t):
                        i2 = t % 2
                        pv = psbf(7)
                        for c in range(8):
                            TR(pv[:, c * 128:(c + 1) * 128], hn3[i2][:, c * 128:(c + 1) * 128], ident[:], [("hn3", i2), "ident"], [PB(7)])
                        CP("act", hT[:, :, t * 128:(t + 1) * 128], V(pv, [[128, 8], [1, 128]]), [PB(7)], [("hT", t)])

                    for t in range(NT + 2):
                        if t < NT:
                            M3(t)
                        if 1 <= t <= NT:
                            T31(t - 1)
                        if t >= 2:
                            T32(t - 2)
                    S.barrier()
                if stage <= 3:
                    continue

                with contextlib.ExitStack() as P:
                    def sbp(name, shape, dt):
                        return P.enter_context(nc.sbuf_tensor(f"{name}_{b}", list(shape), dt))
                    wdn = sbp("p4_wdn", [128, NFC, 1024], BF16)
                    aT = sbp("p4_aT", [128, NFC, 1024], BF16)
                    wup = [sbp(f"p4_wup{i}", [128, 8, 256], BF16) for i in range(3)]
                    accg = [sbp(f"p4_accg{i}", [128, 1024], F32) for i in range(2)]
                    accb = [sbp(f"p4_accb{i}", [128, 1024], F32) for i in range(2)]
                    accv = [sbp(f"p4_accv{i}", [128, 1024], F32) for i in range(2)]
                    fhalo = sbp("p4_halo", [128, 2 * NFC, 2], F32)
                    x1t = [sbp(f"p4_x1t{i}", [128, D], F32) for i in range(2)]
                    ot = [sbp(f"p4_ot{i}", [128, D], F32) for i in range(2)]

                    def load_wup(n):
                        fc_ = n % NFC
                        wi_ = n % 3
                        S.dma("pool", wup[wi_][:, :, 0:128], w_up_v[:, :, fc_ * 128:(fc_ + 1) * 128], writes=[("wup", wi_, 0)])
                        S.dma("pool", wup[wi_][:, :, 128:256], w_up_v[:, :, D_FF + fc_ * 128:D_FF + (fc_ + 1) * 128],
                              writes=[("wup", wi_, 1)])

                    load_wup(0)
                    load_wup(1)
                    for q in range(2):
                        S.dma("pool", wdn[:, 0:11, q * 512:(q + 1) * 512], w_dn_v[:, 0:11, q * 512:(q + 1) * 512], writes=[("wdn", q, 0)])
                        S.dma("pool", wdn[:, 11:22, q * 512:(q + 1) * 512], w_dn_v[:, 11:22, q * 512:(q + 1) * 512], writes=[("wdn", q, 1)])
                    WDN = [("wdn", q, r_) for q in range(2) for r_ in range(2)]
                    wctr = 0
                    for blk in range(2):
                        for fc in range(NFC):
                            wi = wctr % 3
                            i2 = wctr % 2
                            if wctr + 2 < 2 * NFC:
                                load_wup(wctr + 2)
                            wctr += 1
                            for part in range(2):
                                base = 4 * i2 + 2 * part
                                cch = part * NFC + fc
                                A = (accg if part == 0 else accv)[i2]
                                ak = ("accg" if part == 0 else "accv", i2)
                                for tt in range(2):
                                    tok0 = blk * 1024 + tt * 512
                                    for kc in range(8):
                                        MM(ps[:, base + tt, :], wup[wi][:, kc, part * 128:(part + 1) * 128], hT[:, kc, tok0:tok0 + 512],
                                           kc == 0, kc == 7, [("wup", wi, part)] + [("hT", tok0 // 128 + q) for q in range(4)], [PB(base + tt)])
                                rb = [PB(base), PB(base + 1)]
                                pin = V(ps[:, base, :], [[1, 1024]])
                                if not CONV_SPLIT:
                                    ACT(A[:], pin, AF.Identity, rb + ["fconvw", "fconvb"], [ak],
                                        bias=fconvb[:, cch:cch + 1], scale=fconvw[:, cch, 2:3])
                                    for k in (1, 0):
                                        d_ = 2 - k
                                        STT(A[:, d_:1024], V(ps[:, base, :], [[1, 1024 - d_]]), fconvw[:, cch, k:k + 1], A[:, d_:1024],
                                            ALU.mult, ALU.add, rb + ["fconvw", ak], [ak])
                                        if blk == 1:
                                            STT(A[:, 0:d_], fhalo[:, cch, 2 - d_:2], fconvw[:, cch, k:k + 1], A[:, 0:d_],
                                                ALU.mult, ALU.add, [("fhalo", cch), "fconvw", ak], [ak])
                                    if blk == 0:
                                        CP("act", fhalo[:, cch, :], ps[:, base + 1, 510:512], [PB(base + 1)], [("fhalo", cch)])
                                else:
                                    ACT(A[:], pin, AF.Identity, rb + ["fconvw", "fconvb"], [ak],
                                        bias=fconvb[:, cch:cch + 1], scale=fconvw[:, cch, 2:3])
                                    if part == 0:
                                        Bv = accb[i2]
                                        bk = ("accb", i2)
                                        ACT(Bv[:], pin, AF.Identity, rb + ["fconvw"], [bk], scale=fconvw[:, cch, 0:1])
                                    STT(A[:, 1:1024], V(ps[:, base, :], [[1, 1023]]), fconvw[:, cch, 1:2], A[:, 1:1024],
                                        ALU.mult, ALU.add, rb + ["fconvw", ak], [ak])
                                    if part == 1:
                                        STT(A[:, 2:1024], V(ps[:, base, :], [[1, 1022]]), fconvw[:, cch, 0:1], A[:, 2:1024],
                                            ALU.mult, ALU.add, rb + ["fconvw", ak], [ak])
                                    if blk == 1:
                                        STT(A[:, 0:1], fhalo[:, cch, 1:2], fconvw[:, cch, 1:2], A[:, 0:1],
                                            ALU.mult, ALU.add, [("fhalo", cch), "fconvw", ak], [ak])
                                        STT(A[:, 0:2], fhalo[:, cch, 0:2], fconvw[:, cch, 0:1], A[:, 0:2],
                                            ALU.mult, ALU.add, [("fhalo", cch), "fconvw", ak], [ak])
                                    if blk == 0:
                                        CP("act", fhalo[:, cch, :], ps[:, base + 1, 510:512], [PB(base + 1)], [("fhalo", cch)])
                                    if part == 0:
                                        TT("pool", A[:, 2:1024], A[:, 2:1024], Bv[:, 0:1022], ALU.add, [ak, bk], [ak])
                            ACT(accg[i2][:], accg[i2][:], AF.Silu, [("accg", i2)], [("accg", i2)])
                            TT("pool", aT[:, fc, :], accg[i2][:], accv[i2][:], ALU.mult, [("accg", i2), ("accv", i2)], [("aT", fc)])
                        for tt in range(8):
                            t = blk * 8 + tt
                            i2 = t % 2
                            S.dma("sp", x1t[i2][:], out[b, t * 128:(t + 1) * 128, :], reads=[("x1d", b, t)], writes=[("x1t", i2)])
                            for j in range(2):
                                pb = 2 * i2 + j
                                for fc in range(NFC):
                                    MM(ps[:, pb, :], aT[:, fc, tt * 128:(tt + 1) * 128], wdn[:, fc, j * 512:(j + 1) * 512],
                                       fc == 0, fc == NFC - 1, [("aT", fc)] + WDN, [PB(pb)])
                            TT("dve", ot[i2][:], V(ps[:, 2 * i2, :], [[1, 1024]]), x1t[i2][:], ALU.add,
                               [PB(2 * i2), PB(2 * i2 + 1), ("x1t", i2)], [("ot", i2)])
                            finals.append(S.dma("sp", out[b, t * 128:(t + 1) * 128, :], ot[i2][:], reads=[("ot", i2)],
                                                writes=[("x1d", b, t)]))
                    S.barrier()
        S.emit(final_wait_ops=finals)
    return nc


_NC_CACHE = {}
PARAM_NAMES = ["rel_bias", "norm_mix_g", "w_in", "q_norm_g", "k_norm_g", "lambda_q1", "lambda_k1", "lambda_q2",
               "lambda_k2", "attn_subln_g", "conv_ssm_w", "conv_ssm_b", "dt_bias", "a_log", "d_skip", "ssm_norm_g",
               "w_proj_attn", "w_proj_ssm", "w_out", "norm_ffn_g", "w_up", "conv_ffn_w", "conv_ffn_b", "w_down"]


def make_in_maps(inputs, n_cores=8, nseq=2):
    x = np.ascontiguousarray(np.asarray(inputs["x"], dtype=np.float32))
    shared = {}
    for k in PARAM_NAMES:
        a = np.asarray(inputs[k], dtype=np.float32)
        if k == "rel_bias":
            shared[k] = np.ascontiguousarray(a)
        elif a.ndim == 2:
            shared[k] = np.ascontiguousarray(a[0:1])
        else:
            shared[k] = np.ascontiguousarray(a[0])
    in_maps = []
    for i in range(n_cores):
        m = dict(shared)
        m["x"] = np.ascontiguousarray(x[i * nseq:(i + 1) * nseq])
        in_maps.append(m)
    return in_maps


def kernel(**inputs):
    n_cores, nseq = 8, 2
    if "nc" not in _NC_CACHE:
        _NC_CACHE["nc"] = build(nseq=nseq)
    nc = _NC_CACHE["nc"]
    in_maps = make_in_maps(inputs, n_cores, nseq)
    res = run_bass_kernel_spmd(nc, in_maps, core_ids=list(range(n_cores)))
    return np.concatenate([np.asarray(r["out"], dtype=np.float32) for r in res.results], axis=0)
```

```python
import contextlib
import os
import numpy as np
import ml_dtypes
import concourse.bass as bass
import concourse.mybir as mybir
from concourse.bass_utils import run_bass_kernel_spmd

F32 = mybir.dt.float32
BF16 = mybir.dt.bfloat16
AF = mybir.ActivationFunctionType
ALU = mybir.AluOpType
AX = mybir.AxisListType

ENGS = ("pe", "act", "dve", "pool", "sp")

S_LEN = 2048
D = 1024
NT = 16
IN_COLS = 10272
C_Q, C_K, C_V, C_Z, C_XS, C_B, C_C, C_DT, C_G = 0, 1024, 2048, 3072, 5120, 7168, 7680, 8192, 8224
D_FF = 2816
NFC = 22


class Op:
    __slots__ = ("eng", "fn", "deps", "is_dma", "sig", "need_sig")

    def __init__(self, eng, fn, is_dma):
        self.eng = eng
        self.fn = fn
        self.is_dma = is_dma
        self.deps = []
        self.sig = None
        self.need_sig = False


class Sched:
    SEM_LIMIT = 30000

    def __init__(self, nc, n_dma_sems=12):
        self.nc = nc
        self.streams = {e: [] for e in ENGS}
        self.last_w = {}
        self.readers = {}
        self.n_dma_sems = n_dma_sems
        self.live_dmas = []

    def op(self, eng, fn, reads=(), writes=(), dma=False):
        o = Op(eng, fn, dma)
        deps = []
        for k in reads:
            w = self.last_w.get(k)
            if w is not None:
                deps.append(w)
            if isinstance(k, tuple) and k[0] == "ps":
                for r in self.readers.get(k, ()):
                    if r.eng != eng:
                        deps.append(r)
        for k in writes:
            w = self.last_w.get(k)
            if w is not None:
                deps.append(w)
            deps.extend(self.readers.get(k, ()))
        seen = set()
        for d in deps:
            if d is o or id(d) in seen:
                continue
            seen.add(id(d))
            if (not d.is_dma) and d.eng == eng and eng == "pe":
                continue
            o.deps.append(d)
            d.need_sig = True
        for k in reads:
            self.readers.setdefault(k, []).append(o)
        for k in writes:
            self.last_w[k] = o
            self.readers[k] = []
        self.streams[eng].append(o)
        if dma:
            self.live_dmas.append(o)
        return o

    def dma(self, q, out, in_, reads=(), writes=(), **kw):
        return self.op(q, lambda e: e.dma_start(out=out, in_=in_, **kw), reads, writes, dma=True)

    def barrier(self):
        lasts = []
        for e in ENGS:
            for o in reversed(self.streams[e]):
                if o.fn is not None and not o.is_dma:
                    lasts.append(o)
                    break
        dmas = list(self.live_dmas)
        for e in ENGS:
            b = Op(e, None, False)
            for d in lasts:
                if d.eng != e:
                    b.deps.append(d)
                    d.need_sig = True
            b.deps.extend(dmas)
            self.streams[e].append(b)
        self.live_dmas = []
        self.last_w = {}
        self.readers = {}

    def emit(self, final_wait_ops=()):
        nc = self.nc
        with contextlib.ExitStack() as es:
            for e in ENGS:
                sigs = [o for o in self.streams[e] if o.need_sig and not o.is_dma]
                n_sems = max(1, (len(sigs) + self.SEM_LIMIT - 1) // self.SEM_LIMIT)
                sems = [es.enter_context(nc.semaphore(f"s_{e}_{i}")) for i in range(n_sems)]
                for cnt, o in enumerate(sigs):
                    o.sig = (sems[cnt // self.SEM_LIMIT], cnt % self.SEM_LIMIT + 1)
            dma_prev = {}
            for e in ENGS:
                dmas = [o for o in self.streams[e] if o.is_dma]
                if not dmas:
                    continue
                k = min(self.n_dma_sems, len(dmas))
                sems = [es.enter_context(nc.semaphore(f"d_{e}_{i}")) for i in range(k)]
                uses = [0] * k
                for cnt, o in enumerate(dmas):
                    s = cnt % k
                    uses[s] += 1
                    o.sig = (sems[s], 16 * uses[s])
                    dma_prev[id(o)] = (sems[s], 16 * (uses[s] - 1)) if uses[s] > 1 else None
            blk = es.enter_context(nc.Block())
            handles = {"pe": blk.tensor, "act": blk.scalar, "dve": blk.vector,
                       "pool": blk.gpsimd, "sp": blk.sync}
            for e in ENGS:
                ops = self.streams[e]

                def body(h, ops=ops, e=e):
                    waited = {}

                    def wait(sem, val):
                        if waited.get(id(sem), 0) >= val:
                            return
                        waited[id(sem)] = val
                        h.wait_ge(sem, val)

                    for o in ops:
                        for d in o.deps:
                            wait(*d.sig)
                        if o.fn is None:
                            continue
                        if o.is_dma:
                            p = dma_prev.get(id(o))
                            if p is not None:
                                wait(*p)
                        inst = o.fn(h)
                        if o.is_dma:
                            inst.then_inc(o.sig[0], 16)
                        elif o.need_sig:
                            inst.then_inc(o.sig[0], 1)
                    if e == "sp":
                        for o in final_wait_ops:
                            wait(*o.sig)

                handles[e](body)


def V(ap, dims, off=0):
    return bass.AP(tensor=ap.tensor, offset=ap.offset + off,
                   ap=[list(ap.ap[0])] + [list(d) for d in dims])


def t5_bucket_table(nmax=256):
    n = np.arange(nmax)
    nf = np.maximum(n, 1).astype(np.float32)
    large = 16 + (np.log(nf / np.float32(16)) / np.float32(np.log(128 / 16)) * np.float32(16)).astype(np.int32)
    large = np.minimum(large, 31)
    return np.where(n < 16, n, large)


YDEF = 6
CONV_SPLIT = True


def build(nseq=2, stage=99, dbg=False):
    nc = bass.Bass("TRN2", target_bir_lowering=False)

    def din(name, shape):
        return nc.dram_tensor(name, list(shape), F32, kind="ExternalInput").ap()

    x = din("x", [nseq, S_LEN, D])
    rel_bias = din("rel_bias", [32, 8])
    norm_mix_g = din("norm_mix_g", [1, D])
    w_in = din("w_in", [D, IN_COLS])
    q_norm_g = din("q_norm_g", [1, 64])
    k_norm_g = din("k_norm_g", [1, 64])
    lam_q1 = din("lambda_q1", [1, 64])
    lam_k1 = din("lambda_k1", [1, 64])
    lam_q2 = din("lambda_q2", [1, 64])
    lam_k2 = din("lambda_k2", [1, 64])
    subln_g = din("attn_subln_g", [1, 128])
    conv_ssm_w = din("conv_ssm_w", [4, 3072])
    conv_ssm_b = din("conv_ssm_b", [1, 3072])
    dt_bias = din("dt_bias", [1, 32])
    a_log = din("a_log", [1, 32])
    d_skip = din("d_skip", [1, 32])
    ssm_norm_g = din("ssm_norm_g", [1, 2048])
    w_pa = din("w_proj_attn", [1024, 1024])
    w_ps = din("w_proj_ssm", [2048, 1024])
    w_out = din("w_out", [1024, 1024])
    norm_ffn_g = din("norm_ffn_g", [1, D])
    w_up = din("w_up", [D, 2 * D_FF])
    conv_ffn_w = din("conv_ffn_w", [3, 2 * D_FF])
    conv_ffn_b = din("conv_ffn_b", [1, 2 * D_FF])
    w_down = din("w_down", [D_FF, D])
    out = nc.dram_tensor("out", [nseq, S_LEN, D], F32, kind="ExternalOutput").ap()

    skind = "ExternalOutput" if dbg else "Internal"
    ext_d = nc.dram_tensor("ext_d", [8, 384], F32, kind="Internal").ap()
    Zd = nc.dram_tensor("Zd", [8, 128, 384], F32, kind="Internal").ap()
    ya_d = nc.dram_tensor("ya_d", [nseq, NT, 128, 8, 128], BF16, kind=skind).ap()
    yss_d = nc.dram_tensor("yss_d", [nseq, NT, 128, 16, 128], BF16, kind=skind).ap()
    if dbg:
        hT_d = nc.dram_tensor("hT_d", [128, 8, S_LEN], BF16, kind="ExternalOutput").ap()
        dt_d = nc.dram_tensor("dt_d", [128, NT, 32], F32, kind="ExternalOutput").ap()

    w_in_v = w_in.rearrange("(kc p) c -> p kc c", p=128)
    w_up_v = w_up.rearrange("(kc p) c -> p kc c", p=128)
    w_pa_v = w_pa.rearrange("(kc p) c -> p kc c", p=128)
    w_ps_v = w_ps.rearrange("(kc p) c -> p kc c", p=128)
    w_out_v = w_out.rearrange("(kc p) c -> p kc c", p=128)
    w_dn_v = w_down.rearrange("(kc p) c -> p kc c", p=128)

    S = Sched(nc)
    finals = []

    def MM(o, lhsT, rhs, start, stop, r, w):
        S.op("pe", lambda e: e.matmul(out=o, lhsT=lhsT, rhs=rhs, start=start, stop=stop), r, w)

    def TR(o, in_, ident, r, w):
        S.op("pe", lambda e: e.transpose(out=o, in_=in_, identity=ident), r, w)

    def ACT(o, in_, func, r, w, bias=None, scale=None, accum=None):
        kw = {}
        if bias is not None:
            kw["bias"] = bias
        if scale is not None:
            kw["scale"] = scale
        if accum is not None:
            kw["accum_out"] = accum
        S.op("act", lambda e: e.activation(out=o, in_=in_, func=func, **kw), r, w)

    def TT(eng, o, in0, in1, op, r, w):
        S.op(eng, lambda e: e.tensor_tensor(out=o, in0=in0, in1=in1, op=op), r, w)

    def TS(eng, o, in0, s1, op0, r, w, s2=None, op1=None):
        if op1 is None:
            S.op(eng, lambda e: e.tensor_scalar(out=o, in0=in0, scalar1=s1, scalar2=None, op0=op0), r, w)
        else:
            S.op(eng, lambda e: e.tensor_scalar(out=o, in0=in0, scalar1=s1, scalar2=s2, op0=op0, op1=op1), r, w)

    def STT(o, in0, scalar, in1, op0, op1, r, w):
        S.op("dve", lambda e: e.scalar_tensor_tensor(out=o, in0=in0, scalar=scalar, in1=in1, op0=op0, op1=op1), r, w)

    def CP(eng, o, in_, r, w):
        if eng == "act":
            S.op("act", lambda e: e.copy(out=o, in_=in_), r, w)
        else:
            S.op(eng, lambda e: e.tensor_copy(out=o, in_=in_), r, w)

    def RECIP(o, in_, r, w):
        S.op("dve", lambda e: e.reciprocal(out=o, in_=in_), r, w)

    def MEMSET(eng, ap, val, w):
        S.op(eng, lambda e: e.memset(ap, val), (), w)

    def bc(ap, n=128):
        return ap.broadcast_to([n, ap.shape[-1]])

    with contextlib.ExitStack() as G:
        def sbg(name, shape, dt):
            return G.enter_context(nc.sbuf_tensor(name, list(shape), dt))

        ps = G.enter_context(nc.psum_tensor("ps", [128, 8, 512], F32))

        def PB(b):
            return ("ps", b)

        def psbf(b):
            return ps[:, b, :].bitcast(BF16)

        ident = sbg("ident", [128, 128], BF16)
        identf = sbg("identf", [128, 128], F32)
        U = sbg("U", [128, 128], F32)
        Lst = sbg("Lst", [128, 128], F32)
        ones_f = sbg("ones_f", [128, 128], F32)
        gmix_bc = sbg("gmix_bc", [128, D], F32)
        gffn_bc = sbg("gffn_bc", [128, D], F32)
        gqk_bc = sbg("gqk_bc", [128, 256], F32)
        subln_bc = sbg("subln_bc", [128, 128], F32)
        neglam = sbg("neglam", [128, 1], F32)
        lamt = sbg("lamt", [128, 4, 64], F32)
        lamp = sbg("lamp", [128, 2, 64], F32)
        lame = sbg("lame", [128, 2], F32)
        b31 = sbg("b31", [128, 8], F32)
        convw = sbg("convw", [128, 24, 4], F32)
        convb = sbg("convb", [128, 24], F32)
        fconvw = sbg("fconvw", [128, 44, 3], F32)
        fconvb = sbg("fconvb", [128, 44], F32)
        dtb_bc = sbg("dtb_bc", [128, 32], F32)
        a_bc = sbg("a_bc", [128, 32], F32)
        dsk_bc = sbg("dsk_bc", [128, 32], F32)
        RB = sbg("RB", [8, 32], F32)
        ext_sb = sbg("ext_sb", [8, 384], F32)

        MEMSET("pool", ones_f[:], 1.0, ["ones_f"])
        MEMSET("pool", identf[:], 0.0, ["identf"])
        S.op("pool", lambda e: e.affine_select(out=identf[:], in_=identf[:], pattern=[[-1, 128]],
                                               compare_op=ALU.not_equal, fill=1.0, base=0, channel_multiplier=1),
             ["identf"], ["identf"])
        CP("dve", ident[:], identf[:], ["identf"], ["ident"])
        S.op("pool", lambda e: e.affine_select(out=U[:], in_=ones_f[:], pattern=[[1, 128]],
                                               compare_op=ALU.is_ge, fill=0.0, base=0, channel_multiplier=-1),
             ["ones_f"], ["U"])
        S.op("pool", lambda e: e.affine_select(out=Lst[:], in_=ones_f[:], pattern=[[-1, 128]],
                                               compare_op=ALU.is_ge, fill=0.0, base=-1, channel_multiplier=1),
             ["ones_f"], ["Lst"])
        S.dma("sp", gmix_bc[:], bc(norm_mix_g), writes=["gmix"])
        S.dma("sp", gffn_bc[:], bc(norm_ffn_g), writes=["gffn"])
        S.dma("sp", gqk_bc[:, 0:64], bc(q_norm_g), writes=["gqk"])
        S.dma("sp", gqk_bc[:, 64:128], bc(q_norm_g), writes=["gqk"])
        S.dma("sp", gqk_bc[:, 128:192], bc(k_norm_g), writes=["gqk"])
        S.dma("sp", gqk_bc[:, 192:256], bc(k_norm_g), writes=["gqk"])
        ACT(gqk_bc[:, 0:128], gqk_bc[:, 0:128], AF.Identity, ["gqk"], ["gqk"], scale=0.125)
        S.dma("sp", subln_bc[:], bc(subln_g), writes=["subln"])
        ACT(subln_bc[:], subln_bc[:], AF.Identity, ["subln"], ["subln"], scale=0.8)
        for i, a in enumerate((lam_q1, lam_q2, lam_k1, lam_k2)):
            S.dma("sp", lamt[:, i, :], bc(a), writes=["lamt"])
        TT("dve", lamp[:], lamt[:, 0:2, :], lamt[:, 2:4, :], ALU.mult, ["lamt"], ["lamp"])
        S.op("dve", lambda e: e.tensor_reduce(out=lame[:], in_=lamp[:], axis=AX.X, op=ALU.add), ["lamp"], ["lame"])
        ACT(lame[:], lame[:], AF.Exp, ["lame"], ["lame"])
        TT("dve", neglam[:], lame[:, 1:2], lame[:, 0:1], ALU.subtract, ["lame"], ["neglam"])
        TS("dve", neglam[:], neglam[:], -0.2, ALU.add, ["neglam"], ["neglam"])
        S.dma("sp", b31[:], bc(rel_bias[31:32, :]), writes=["b31"])
        S.dma("sp", dtb_bc[:], bc(dt_bias), writes=["dtb"])
        S.dma("sp", dsk_bc[:], bc(d_skip), writes=["dsk"])
        S.dma("sp", a_bc[:], bc(a_log), writes=["a_bc"])
        ACT(a_bc[:], a_bc[:], AF.Exp, ["a_bc"], ["a_bc"])
        ACT(a_bc[:], a_bc[:], AF.Identity, ["a_bc"], ["a_bc"], scale=-1.0)
        stg = sbg("stg", [64, 9, 128], F32)
        rbs = sbg("rbs", [32, 8], F32)
        MEMSET("pool", stg[:], 0.0, ["stg"])
        S.dma("sp", stg[0:24, 0:4, :], conv_ssm_w.rearrange("k (cc p) -> cc k p", p=128), writes=["stg"])
        S.dma("sp", stg[0:24, 4, :], conv_ssm_b[0, :].rearrange("(cc p) -> cc p", p=128), writes=["stg"])
        S.dma("sp", stg[0:44, 5:8, :], conv_ffn_w.rearrange("k (cc p) -> cc k p", p=128), writes=["stg"])
        S.dma("sp", stg[0:44, 8, :], conv_ffn_b[0, :].rearrange("(cc p) -> cc p", p=128), writes=["stg"])
        S.dma("sp", rbs[:], rel_bias, writes=["rbs"])
        def pcol(k):
            return k * 32 if k < 5 else 160 + (k - 5) * 64
        for k in range(9):
            n = 32 if k < 5 else 64
            TR(ps[:, 0, pcol(k):pcol(k) + n], stg[0:n, k, :], identf[0:n, 0:n], ["stg", "identf"], [PB(0)])
        for k in range(4):
            CP("dve", convw[:, :, k], ps[:, 0, pcol(k):pcol(k) + 24], [PB(0)], ["convw"])
        CP("dve", convb[:], ps[:, 0, pcol(4):pcol(4) + 24], [PB(0)], ["convb"])
        for k in range(3):
            CP("dve", fconvw[:, :, k], ps[:, 0, pcol(5 + k):pcol(5 + k) + 44], [PB(0)], ["fconvw"])
        CP("dve", fconvb[:], ps[:, 0, pcol(8):pcol(8) + 44], [PB(0)], ["fconvb"])
        TR(ps[0:8, 1, 0:32], rbs[:], identf[0:32, 0:32], ["rbs", "identf"], [PB(1)])
        CP("dve", RB[:], ps[0:8, 1, 0:32], [PB(1)], ["RB"])
        MEMSET("pool", ext_sb[:], -30000.0, ["ext_sb"])
        CP("dve", ext_sb[:, 127:143], RB[:, 0:16], ["RB", "ext_sb"], ["ext_sb"])
        bt = t5_bucket_table(256)
        assert (bt[113:] == 31).all()
        for bk in range(16, 32):
            idx = np.nonzero(bt == bk)[0]
            if len(idx) == 0:
                continue
            n0, n1 = int(idx[0]), int(idx[-1]) + 1
            assert n1 - n0 == len(idx)
            CP("dve", ext_sb[:, 127 + n0:127 + n1], V(RB[:, bk:bk + 1], [[0, n1 - n0]]), ["RB", "ext_sb"], ["ext_sb"])
        S.dma("sp", ext_d, ext_sb[:, :], reads=["ext_sb"], writes=["ext_d"])
        S.dma("sp", Zd, bass.AP(tensor=ext_d.tensor, offset=ext_d.offset, ap=[[384, 8], [0, 128], [1, 384]]),
              reads=["ext_d"], writes=["Zd"])
        S.barrier()

        for b in range(nseq):
            with contextlib.ExitStack() as Q:
                def sbq(name, shape, dt):
                    return Q.enter_context(nc.sbuf_tensor(f"{name}_{b}", list(shape), dt))

                hT = sbq("hT", [128, 8, S_LEN], BF16)
                HT_ALL = [("hT", t) for t in range(NT)]

                with contextlib.ExitStack() as P:
                    def sbp(name, shape, dt):
                        return P.enter_context(nc.sbuf_tensor(f"{name}_{b}", list(shape), dt))
                    xts = [sbp(f"p0_xt{i}", [128, D], F32) for i in range(2)]
                    hns = [sbp(f"p0_hn{i}", [128, D], BF16) for i in range(2)]
                    junk = sbp("p0_junk", [128, D], BF16)
                    ssq = sbp("p0_ssq", [128, NT], F32)
                    rs = sbp("p0_rs", [128, NT], F32)
                    for t in range(NT):
                        i2 = t % 2
                        xt, hn = xts[i2], hns[i2]
                        S.dma("sp", xt[:], x[b, t * 128:(t + 1) * 128, :], writes=[("xt", i2)])
                        ACT(junk[:], xt[:], AF.Square, [("xt", i2)], ["junk", ("ssq", t)], accum=ssq[:, t:t + 1])
                        ACT(rs[:, t:t + 1], ssq[:, t:t + 1], AF.Sqrt, [("ssq", t)], [("rs", t)], bias=1e-6, scale=1.0 / D)
                        RECIP(rs[:, t:t + 1], rs[:, t:t + 1], [("rs", t)], [("rs", t)])
                        STT(hn[:], xt[:], rs[:, t:t + 1], gmix_bc[:], ALU.mult, ALU.mult,
                            [("xt", i2), ("rs", t), "gmix"], [("hn", i2)])
                        pb = 2 * i2
                        pv = psbf(pb)
                        for c in range(8):
                            TR(pv[:, c * 128:(c + 1) * 128], hn[:, c * 128:(c + 1) * 128], ident[:],
                               [("hn", i2), "ident"], [PB(pb)])
                        CP("act" if i2 else "dve", hT[:, :, t * 128:(t + 1) * 128],
                           V(pv, [[128, 8], [1, 128]]), [PB(pb)], [("hT", t)])
                    if dbg and b == 0:
                        finals.append(S.dma("sp", hT_d, hT[:], reads=HT_ALL))
                    S.barrier()
                if stage <= 0:
                    continue

                with contextlib.ExitStack() as P:
                    def sbp(name, shape, dt):
                        return P.enter_context(nc.sbuf_tensor(f"{name}_{b}", list(shape), dt))
                    NP = 28
                    BT = sbp("p1_BT", [128, 8, 256], F32)
                    wqkv = [sbp(f"p1_w{i}", [128, 8, 384], BF16) for i in range(2)]
                    qkT = [sbp(f"p1_qkT{i}", [128, 2, S_LEN], BF16) for i in range(2)]
                    vaug = [sbp(f"p1_v{i}", [128, NT, 132], BF16) for i in range(2)]
                    PT = [sbp(f"p1_PT{i}", [128, 2, 512], BF16) for i in range(NP)]
                    sq = [sbp(f"p1_sq{i}", [128, 256], F32) for i in range(2)]
                    tmpn = [sbp(f"p1_tmpn{i}", [128, 256], F32) for i in range(2)]
                    qkn = [sbp(f"p1_qkn{i}", [128, 256], BF16) for i in range(6)]
                    ssq4 = sbp("p1_ssq4", [128, NT, 4], F32)
                    rs4 = sbp("p1_rs4", [128, NT, 4], F32)
                    etmp = [sbp(f"p1_et{i}", [128, 2, 256], F32) for i in range(2)]
                    rl = sbp("p1_rl", [128, NT, 2], F32)
                    nrl = sbp("p1_nrl", [128, NT], F32)
                    o1 = [sbp(f"p1_o1{i}", [128, 128], F32) for i in range(2)]
                    oo = [sbp(f"p1_oo{i}", [128, 128], F32) for i in range(2)]
                    junk2 = sbp("p1_junk2", [128, 128], BF16)
                    sso = sbp("p1_sso", [128, NT], F32)
                    rso = sbp("p1_rso", [128, NT], F32)
                    yn = [sbp(f"p1_yn{i}", [128, 128], BF16) for i in range(6)]
                    yst = [sbp(f"p1_yst{i}", [128, 4, 128], BF16) for i in range(2)]

                    for h in range(8):
                        S.dma("sp", BT[:, h, :],
                              bass.AP(tensor=Zd.tensor, offset=Zd.offset + h * 128 * 384 + 127, ap=[[383, 128], [1, 256]]),
                              writes=[("BT", h)])
                    for i in range(2):
                        MEMSET("pool", vaug[i][:, :, 128:129], 1.0, [("vaug", i)])

                    pends = {"q": [], "y": []}

                    def tick():
                        for pend in pends.values():
                            for it in pend:
                                it[0] -= 1
                            while pend and pend[0][0] <= 0:
                                pend.pop(0)[1]()

                    def defer(n, fn, q="q"):
                        pends[q].append([n, fn])

                    def flush():
                        for pend in pends.values():
                            while pend:
                                pend.pop(0)[1]()

                    st_ = {"pt": 0, "sb": 0, "yst": 0, "qkn": 0, "yn": 0}

                    def load_w(h):
                        sl = h % 2
                        for j3, c0 in enumerate((C_Q, C_K, C_V)):
                            S.dma("pool", wqkv[sl][:, :, j3 * 128:(j3 + 1) * 128],
                                  w_in_v[:, :, c0 + h * 128:c0 + (h + 1) * 128], writes=[("wqkv", sl, j3)])

                    def proj_item(h, t):
                        def f():
                            sl = h % 2
                            W = wqkv[sl]
                            i2 = t % 2
                            pb = 4 + 2 * i2
                            for kc in range(8):
                                MM(ps[:, pb, 0:384], hT[:, kc, t * 128:(t + 1) * 128], W[:, kc, :], kc == 0, kc == 7,
                                   [("hT", t)] + [("wqkv", sl, q) for q in range(3)], [PB(pb)])
                            ACT(sq[i2][:], ps[:, pb, 0:256], AF.Square, [PB(pb)], [("sq", i2)])
                            S.op("dve", lambda e: e.tensor_reduce(
                                out=ssq4[:, t, :], in_=V(sq[i2][:], [[64, 4], [1, 64]]), axis=AX.X, op=ALU.add),
                                [("sq", i2)], [("ssq4", t)])
                            ACT(rs4[:, t, :], ssq4[:, t, :], AF.Ln, [("ssq4", t)], [("rs4", t)], bias=1e-6, scale=1.0 / 64)
                            ACT(rs4[:, t, :], rs4[:, t, :], AF.Exp, [("rs4", t)], [("rs4", t)], scale=-0.5)
                            TT("dve", V(tmpn[i2][:], [[64, 4], [1, 64]]), V(ps[:, pb, 0:256], [[64, 4], [1, 64]]),
                               V(rs4[:, t, :], [[1, 4], [0, 64]]), ALU.mult, [PB(pb), ("rs4", t)], [("tmpn", i2)])
                            qi = st_["qkn"] % 6
                            st_["qkn"] += 1
                            TT("pool", qkn[qi][:], tmpn[i2][:], gqk_bc[:], ALU.mult, [("tmpn", i2), "gqk"], [("qkn", qi)])
                            CP("act", vaug[sl][:, t, 0:128], ps[:, pb, 256:384], [PB(pb)], [("vaug", sl)])

                            def g():
                                pv = psbf(2)
                                hf = i2 * 512
                                for m2 in range(2):
                                    TR(pv[:, hf + m2 * 128:hf + (m2 + 1) * 128], qkn[qi][:, m2 * 128:(m2 + 1) * 128], ident[:],
                                       [("qkn", qi), "ident"], [PB(2)])
                                CP("act" if i2 else "dve", qkT[sl][:, :, t * 128:(t + 1) * 128],
                                   V(pv[:, hf:hf + 256], [[128, 2], [1, 128]]), [PB(2)], [("qkT", sl, t)])
                            defer(4, g)
                            tick()
                        return f

                    PTc = {}

                    def qk_item(h, c, j):
                        def f():
                            sl = h % 2
                            r = j - 4 * c
                            st = max(0, r) * 128
                            b0 = 4 + 2 * (st_["sb"] % 2)
                            st_["sb"] += 1
                            qkeys = [("qkT", sl, j)] + [("qkT", sl, q) for q in range(4 * c + st // 128, 4 * c + 4)]
                            for m in range(2):
                                MM(ps[:, b0 + m, st:512], qkT[sl][64 * m:64 * m + 64, 1, j * 128:(j + 1) * 128],
                                   qkT[sl][64 * m:64 * m + 64, 0, 512 * c + st:512 * c + 512], True, True,
                                   qkeys, [PB(b0 + m)])
                            pi = st_["pt"] % NP
                            st_["pt"] += 1
                            PTc[(h, c, j)] = pi
                            Pt = PT[pi]
                            rb = [PB(b0), PB(b0 + 1)]
                            if r >= -1:
                                if r >= 0:
                                    nb = min(2, 4 - r)
                                    btsl = BT[:, h, 0:128 * nb]
                                else:
                                    nb = 1
                                    btsl = BT[:, h, 128:256]
                                wdt = 128 * nb
                                ei = st_["sb"] % 2
                                TT("dve", etmp[ei][:, :, 0:wdt], ps[:, b0:b0 + 2, st:st + wdt],
                                   V(btsl, [[0, 2], [1, wdt]]), ALU.add, rb + [("BT", h)], [("etmp", ei)])
                                ACT(Pt[:, :, st:st + wdt], etmp[ei][:, :, 0:wdt], AF.Exp, [("etmp", ei)], [("PT", pi)])
                                if st + wdt < 512:
                                    ACT(Pt[:, :, st + wdt:512], ps[:, b0:b0 + 2, st + wdt:512], AF.Exp,
                                        rb + ["b31"], [("PT", pi)], bias=b31[:, h:h + 1])
                            else:
                                ACT(Pt[:, :, :], ps[:, b0:b0 + 2, :], AF.Exp, rb + ["b31"], [("PT", pi)],
                                    bias=b31[:, h:h + 1])
                            tick()
                        return f

                    def av_item(h, c, i, m):
                        def f():
                            sl = h % 2
                            ob = i % 2
                            i2 = i % 2
                            for j in range(i + 1):
                                pi = PTc[(h, c, j)]
                                MM(ps[:, ob, 256 * m:256 * m + 129],
                                   PT[pi][:, m, (i - 4 * c) * 128:(i - 4 * c + 1) * 128],
                                   vaug[sl][:, j, 0:129], j == 0, j == i,
                                   [("PT", pi), ("vaug", sl)], [PB(ob)])
                            if m == 1:
                                RECIP(rl[:, i, :], V(ps[:, ob, 128:129], [[256, 2]]), [PB(ob)], [("rl", i)])
                                TS("dve", nrl[:, i:i + 1], rl[:, i, 1:2], neglam[:, 0:1], ALU.mult,
                                   [("rl", i), "neglam"], [("nrl", i)])
                                ACT(o1[i2][:], ps[:, ob, 0:128], AF.Identity, [PB(ob), ("rl", i)], [("o1", i2)],
                                    scale=rl[:, i, 0:1])
                                STT(oo[i2][:], ps[:, ob, 256:384], nrl[:, i:i + 1], o1[i2][:], ALU.mult, ALU.add,
                                    [PB(ob), ("nrl", i), ("o1", i2)], [("oo", i2)])
                                ACT(junk2[:], oo[i2][:], AF.Square, [("oo", i2)], ["junk2", ("sso", i)], accum=sso[:, i:i + 1])
                                ACT(rso[:, i:i + 1], sso[:, i:i + 1], AF.Ln, [("sso", i)], [("rso", i)],
                                    bias=1e-5, scale=1.0 / 128)
                                ACT(rso[:, i:i + 1], rso[:, i:i + 1], AF.Exp, [("rso", i)], [("rso", i)], scale=-0.5)
                                yi = st_["yn"] % 6
                                st_["yn"] += 1
                                STT(yn[yi][:], oo[i2][:], rso[:, i:i + 1], subln_bc[:], ALU.mult, ALU.mult,
                                    [("oo", i2), ("rso", i), "subln"], [("yn", yi)])

                                def g():
                                    hf = c % 2
                                    pv = psbf(3)
                                    col = hf * 512 + (i % 4) * 128
                                    TR(pv[:, col:col + 128], yn[yi][:], ident[:], [("yn", yi), "ident"], [PB(3)])
                                    if i % 4 == 3:
                                        yc = st_["yst"] % 2
                                        st_["yst"] += 1
                                        ys = yst[yc]
                                        CP("act", ys[:], V(pv[:, hf * 512:hf * 512 + 512], [[128, 4], [1, 128]]),
                                           [PB(3)], [("yst", yc)])
                                        dst = bass.AP(tensor=ya_d.tensor,
                                                      offset=ya_d.offset + ((b * NT + 4 * c) * 128 * 8 + h) * 128,
                                                      ap=[[8 * 128, 128], [128 * 8 * 128, 4], [1, 128]])
                                        o_ = S.dma("sp", dst, ys[:], reads=[("yst", yc)], writes=[("ya_d", b, c, h)])
                                        if dbg:
                                            finals.append(o_)
                                defer(YDEF, g, "y")
                            tick()
                        return f

                    def run(items):
                        for it in items:
                            it()

                    def merge(A, B):
                        nb_done = 0
                        for idx, a in enumerate(A):
                            a()
                            tgt = (idx + 1) * len(B) // len(A)
                            while nb_done < tgt:
                                B[nb_done]()
                                nb_done += 1
                        while nb_done < len(B):
                            B[nb_done]()
                            nb_done += 1

                    def qk_items(h, c):
                        return [qk_item(h, c, j) for j in range(4 * c + 4)]

                    def av_items(h, c):
                        return [av_item(h, c, i, m) for i in range(4 * c, 4 * c + 4) for m in range(2)]

                    def proj_items(h):
                        return [proj_item(h, t) for t in range(NT)]

                    load_w(0)
                    run(proj_items(0))
                    for h in range(8):
                        if h < 7:
                            load_w(h + 1)
                        run(qk_items(h, 0))
                        for c in range(3):
                            merge(av_items(h, c), qk_items(h, c + 1))
                        merge(av_items(h, 3), proj_items(h + 1) if h < 7 else [])
                    flush()
                    S.barrier()
                if stage <= 1:
                    continue

                with contextlib.ExitStack() as P:
                    def sbp(name, shape, dt):
                        return P.enter_context(nc.sbuf_tensor(f"{name}_{b}", list(shape), dt))
                    ssmg_bc = sbp("p2_ssmg", [128, 2048], F32)
                    dt_all = sbp("p2_dt", [128, NT, 32], F32)
                    adt_all = sbp("p2_adt", [128, NT, 32], F32)
                    wdt_ = sbp("p2_wdt", [128, 8, 32], BF16)
                    dtt = [sbp(f"p2_dtt{i}", [128, 32], F32) for i in range(2)]
                    wx = [sbp(f"p2_wx{i}", [128, 8, 128], BF16) for i in range(3)]
                    wzs = [sbp(f"p2_wz{i}", [128, 8, 512], BF16) for i in range(2)]
                    acc = [sbp(f"p2_acc{i}", [128, 1024], F32) for i in range(2)]
                    accb = [sbp(f"p2_accb{i}", [128, 1024], F32) for i in range(2)]
                    zs_all = sbp("p2_zsall", [128, NT, 512], BF16)
                    halo = sbp("p2_halo", [128, 4], F32)
                    fmx = [sbp(f"p2_fmx{i}", [128, S_LEN], BF16) for i in range(2)]
                    BTg = sbp("p2_BTg", [128, S_LEN], BF16)
                    CTg = sbp("p2_CTg", [128, S_LEN], BF16)
                    xs_tok = sbp("p2_xs", [128, NT, 512], BF16)
                    B_tok = sbp("p2_Btok", [128, NT, 128], BF16)
                    state = sbp("p2_state", [128, 512], F32)
                    state_bf = sbp("p2_statebf", [128, 512], BF16)
                    s1 = sbp("p2_s1", [128, 512], F32)
                    cumtot = [sbp(f"p2_ct{i}", [128, 16], F32) for i in range(2)]
                    ecum = [sbp(f"p2_ec{i}", [128, 16], F32) for i in range(2)]
                    w8 = [sbp(f"p2_w8{i}", [128, 8], F32) for i in range(2)]
                    adtU = [sbp(f"p2_adtU{i}", [128, 8, 128], F32) for i in range(2)]
                    eseg = [sbp(f"p2_eseg{i}", [128, 8, 128], BF16) for i in range(2)]
                    cbTm = [sbp(f"p2_cbT{i}", [128, 128], BF16) for i in range(2)]
                    MT = [sbp(f"p2_MT{i}", [128, 8, 128], BF16) for i in range(2)]
                    xdt = [sbp(f"p2_xdt{i}", [128, 8, 64], BF16) for i in range(2)]
                    xw = [sbp(f"p2_xw{i}", [128, 8, 64], BF16) for i in range(2)]
                    t1 = [sbp(f"p2_t1{i}", [128, 512], F32) for i in range(2)]
                    t3 = [sbp(f"p2_t3{i}", [128, 512], F32) for i in range(2)]
                    junk3 = sbp("p2_junk3", [128, 512], BF16)
                    ssy = sbp("p2_ssy", [128, NT], F32)
                    rsy = sbp("p2_rsy", [128, NT], F32)
                    ynb = [sbp(f"p2_ynb{i}", [128, 512], BF16) for i in range(2)]
                    ysT = [sbp(f"p2_ysT{i}", [128, 4, 128], BF16) for i in range(2)]

                    S.dma("sp", ssmg_bc[:], bc(ssm_norm_g), writes=["ssmg"])
                    S.dma("pool", wdt_[:], w_in_v[:, :, C_DT:C_DT + 32], writes=["wdt"])
                    for t in range(NT):
                        i2 = t % 2
                        for kc in range(8):
                            MM(ps[:, 5, 0:32], hT[:, kc, t * 128:(t + 1) * 128], wdt_[:, kc, :], kc == 0, kc == 7,
                               [("hT", t), "wdt"], [PB(5)])
                        TT("dve", dtt[i2][:], ps[:, 5, 0:32], dtb_bc[:], ALU.add, [PB(5), "dtb"], [("dtt", i2)])
                        ACT(dtt[i2][:], dtt[i2][:], AF.Exp, [("dtt", i2)], [("dtt", i2)])
                        ACT(dt_all[:, t, :], dtt[i2][:], AF.Ln, [("dtt", i2)], [("dt", t)], bias=1.0, scale=1.0)
                        TT("pool", adt_all[:, t, :], dt_all[:, t, :], a_bc[:], ALU.mult, [("dt", t), "a_bc"], [("adt", t)])
                    if dbg and b == 0:
                        finals.append(S.dma("sp", dt_d, dt_all[:], reads=[("dt", t) for t in range(NT)]))

                    st2 = {"pp": 0, "ab": 0, "zb": 0}
                    pend2 = []

                    def chunk_desc(n):
                        g_, ci_ = n // 6, n % 6
                        if ci_ < 4:
                            return g_, ci_, C_XS + g_ * 512 + ci_ * 128, g_ * 4 + ci_, fmx[ci_ % 2], ("fmx", ci_ % 2)
                        if ci_ == 4:
                            return g_, ci_, C_B + g_ * 128, 16 + g_, BTg, "BTg"
                        return g_, ci_, C_C + g_ * 128, 20 + g_, CTg, "CTg"

                    def load_wx(n):
                        if n >= 24:
                            return
                        col0 = chunk_desc(n)[2]
                        S.dma("pool", wx[n % 3][:], w_in_v[:, :, col0:col0 + 128], writes=[("wx", n % 3)])

                    def load_wz(g_):
                        S.dma("pool", wzs[g_ % 2][:], w_in_v[:, :, C_Z + g_ * 512:C_Z + (g_ + 1) * 512], writes=[("wz", g_ % 2)])

                    def emit_pend2():
                        while pend2:
                            pend2.pop(0)()

                    load_wx(0)
                    load_wx(1)
                    load_wz(0)
                    for g in range(4):
                        for ci in range(6):
                            n = g * 6 + ci
                            _, _, col0, cch, dstT, dkey = chunk_desc(n)
                            wi = n % 3
                            load_wx(n + 2)
                            for half in range(2):
                                bp = 2 * (st2["pp"] % 2)
                                st2["pp"] += 1
                                ai = st2["ab"] % 2
                                st2["ab"] += 1
                                A, Bv = acc[ai], accb[ai]
                                ak, bk = ("acc", ai), ("accb", ai)
                                for tt in range(2):
                                    tok0 = half * 1024 + tt * 512
                                    for kc in range(8):
                                        MM(ps[:, bp + tt, :], wx[wi][:, kc, :], hT[:, kc, tok0:tok0 + 512], kc == 0, kc == 7,
                                           [("wx", wi)] + [("hT", tok0 // 128 + q) for q in range(4)], [PB(bp + tt)])
                                emit_pend2()
                                pin = V(ps[:, bp, :], [[1, 1024]])
                                pin1 = V(ps[:, bp, :], [[1, 1023]])
                                rb = [PB(bp), PB(bp + 1)]
                                if not CONV_SPLIT:
                                    ACT(A[:], pin, AF.Identity, rb + ["convw", "convb"], [ak],
                                        bias=convb[:, cch:cch + 1], scale=convw[:, cch, 3:4])
                                    for k in (2, 1, 0):
                                        d_ = 3 - k
                                        STT(A[:, d_:1024], V(ps[:, bp, :], [[1, 1024 - d_]]), convw[:, cch, k:k + 1], A[:, d_:1024],
                                            ALU.mult, ALU.add, rb + ["convw", ak], [ak])
                                        if half == 1:
                                            STT(A[:, 0:d_], halo[:, 3 - d_:3], convw[:, cch, k:k + 1], A[:, 0:d_],
                                                ALU.mult, ALU.add, ["halo", "convw", ak], [ak])
                                    if half == 0:
                                        CP("act", halo[:, 0:3], ps[:, bp + 1, 509:512], [PB(bp + 1)], ["halo"])
                                else:
                                    ACT(A[:], pin, AF.Identity, rb + ["convw", "convb"], [ak],
                                        bias=convb[:, cch:cch + 1], scale=convw[:, cch, 3:4])
                                    if "8" in os.environ.get("KF", ""):
                                        TS("dve", Bv[:], pin, convw[:, cch, 1:2], ALU.mult, rb + ["convw"], [bk])
                                    else:
                                        ACT(Bv[:], pin, AF.Identity, rb + ["convw"], [bk], scale=convw[:, cch, 1:2])
                                    STT(A[:, 1:1024], pin1, convw[:, cch, 2:3], A[:, 1:1024], ALU.mult, ALU.add, rb + ["convw", ak], [ak])
                                    STT(Bv[:, 1:1024], pin1, convw[:, cch, 0:1], Bv[:, 1:1024], ALU.mult, ALU.add, rb + ["convw", bk], [bk])
                                    if half == 1 and "7" not in os.environ.get("KF", ""):
                                        STT(A[:, 0:1], halo[:, 2:3], convw[:, cch, 2:3], A[:, 0:1], ALU.mult, ALU.add, ["halo", "convw", ak], [ak])
                                        STT(A[:, 0:2], halo[:, 1:3], convw[:, cch, 1:2], A[:, 0:2], ALU.mult, ALU.add, ["halo", "convw", ak], [ak])
                                        STT(A[:, 0:2], halo[:, 0:2], convw[:, cch, 0:1], A[:, 0:2], ALU.mult, ALU.add, ["halo", "convw", ak], [ak])
                                        STT(Bv[:, 0:1], halo[:, 2:3], convw[:, cch, 0:1], Bv[:, 0:1], ALU.mult, ALU.add, ["halo", "convw", bk], [bk])
                                    if half == 0:
                                        CP("act", halo[:, 0:3], ps[:, bp + 1, 509:512], [PB(bp + 1)], ["halo"])
                                    TT("dve" if "5" in os.environ.get("KF", "") else "pool", A[:, 2:1024], A[:, 2:1024], Bv[:, 0:1022], ALU.add, [ak, bk], [ak])
                                ACT(dstT[:, half * 1024:(half + 1) * 1024], A[:], AF.Silu, [ak], [dkey])
                            if ci < 5:
                                def trans(ci=ci, dstT=dstT, dkey=dkey):
                                    for tb in range(2):
                                        pv = psbf(4)
                                        for q in range(8):
                                            t = tb * 8 + q
                                            TR(pv[:, q * 128:(q + 1) * 128], dstT[:, t * 128:(t + 1) * 128], ident[:],
                                               [dkey, "ident"], [PB(4)])
                                        if ci < 4:
                                            CP("act", xs_tok[:, tb * 8:(tb + 1) * 8, ci * 128:(ci + 1) * 128],
                                               V(pv, [[128, 8], [1, 128]]), [PB(4)], ["xs_tok"])
                                        else:
                                            CP("act", B_tok[:, tb * 8:(tb + 1) * 8, :], V(pv, [[128, 8], [1, 128]]), [PB(4)], ["B_tok"])
                                pend2.append(trans)
                                if "1" in os.environ.get("KF", ""):
                                    emit_pend2()
                        if "a" in os.environ.get("KSTOP", ""):
                            break
                        wz = wzs[g % 2]
                        for t in range(NT):
                            zb = 6 + (st2["zb"] % 2)
                            st2["zb"] += 1
                            for kc in range(8):
                                MM(ps[:, zb, :], hT[:, kc, t * 128:(t + 1) * 128], wz[:, kc, :], kc == 0, kc == 7,
                                   [("hT", t), ("wz", g % 2)], [PB(zb)])
                            if t == 1:
                                emit_pend2()
                            ACT(zs_all[:, t, :], ps[:, zb, :], AF.Silu, [PB(zb)], [("zs", t)])
                        if "z" in os.environ.get("KSTOP", ""):
                            break
                        if g < 3:
                            load_wz(g + 1)
                        MEMSET("pool", state[:], 0.0, ["state"])
                        MEMSET("pool", state_bf[:], 0.0, ["state_bf"])

                        def stageA(c):
                            i2 = c % 2
                            cs = slice(c * 128, (c + 1) * 128)
                            adt_c = adt_all[:, c, g * 8:(g + 1) * 8]
                            dt_c = dt_all[:, c, g * 8:(g + 1) * 8]
                            MM(ps[:, 5, 0:8], U[:], adt_c, True, True, ["U", ("adt", c)], [PB(5)])
                            MM(ps[:, 5, 8:16], ones_f[:], adt_c, True, True, ["ones_f", ("adt", c)], [PB(5)])
                            CP("act", cumtot[i2][:], ps[:, 5, 0:16], [PB(5)], [("cumtot", i2)])
                            ACT(ecum[i2][:], cumtot[i2][:], AF.Exp, [("cumtot", i2)], [("ecum", i2)])
                            TT("dve", w8[i2][:], cumtot[i2][:, 8:16], cumtot[i2][:, 0:8], ALU.subtract, [("cumtot", i2)], [("w8", i2)])
                            ACT(w8[i2][:], w8[i2][:], AF.Exp, [("w8", i2)], [("w8", i2)])
                            TT("pool", adtU[i2][:], V(adt_c, [[1, 8], [0, 128]]), V(U[:], [[0, 8], [1, 128]]), ALU.mult,
                               [("adt", c), "U"], [("adtU", i2)])
                            for q in range(2):
                                MM(ps[:, q, :], Lst[:], adtU[i2][:, 4 * q:4 * q + 4, :], True, True, ["Lst", ("adtU", i2)], [PB(q)])
                            ACT(V(eseg[i2][:], [[512, 2], [1, 512]]), ps[:, 0:2, :], AF.Exp, [PB(0), PB(1)], [("eseg", i2)])
                            MM(ps[:, 4, 0:128], BTg[:, cs], CTg[:, cs], True, True, ["BTg", "CTg"], [PB(4)])
                            TT("dve", cbTm[i2][:], ps[:, 4, 0:128], U[:], ALU.mult, [PB(4), "U"], [("cbTm", i2)])
                            TT("dve", MT[i2][:], eseg[i2][:], V(cbTm[i2][:], [[0, 8], [1, 128]]), ALU.mult,
                               [("eseg", i2), ("cbTm", i2)], [("MT", i2)])
                            TT("pool", xdt[i2][:], V(xs_tok[:, c, :], [[64, 8], [1, 64]]), V(dt_c, [[1, 8], [0, 64]]), ALU.mult,
                               ["xs_tok", ("dt", c)], [("xdt", i2)])
                            TT("pool", xw[i2][:], xdt[i2][:], V(w8[i2][:], [[1, 8], [0, 64]]), ALU.mult,
                               [("xdt", i2), ("w8", i2)], [("xw", i2)])

                        def stageB(c):
                            i2 = c % 2
                            cs = slice(c * 128, (c + 1) * 128)
                            for hh in range(8):
                                MM(ps[:, 2, hh * 64:(hh + 1) * 64], MT[i2][:, hh, :], xdt[i2][:, hh, :], True, True,
                                   [("MT", i2), ("xdt", i2)], [PB(2)])
                            MM(ps[:, 3, :], CTg[:, cs], state_bf[:], True, True, ["CTg", "state_bf"], [PB(3)])
                            if c < NT - 1:
                                MM(ps[:, 7, :], B_tok[:, c, :], V(xw[i2][:], [[1, 512]]), True, True, ["B_tok", ("xw", i2)], [PB(7)])
                            emit_pend2()
                            TT("pool", V(t3[i2][:], [[64, 8], [1, 64]]), V(xs_tok[:, c, :], [[64, 8], [1, 64]]),
                               V(dsk_bc[:, g * 8:(g + 1) * 8], [[1, 8], [0, 64]]), ALU.mult, ["xs_tok", "dsk"], [("t3", i2)])
                            TT("dve", V(t1[i2][:], [[64, 8], [1, 64]]), V(ps[:, 3, :], [[64, 8], [1, 64]]),
                               V(ecum[i2][:, 0:8], [[1, 8], [0, 64]]), ALU.mult, [PB(3), ("ecum", i2)], [("t1", i2)])
                            TT("dve", t1[i2][:], ps[:, 2, :], t1[i2][:], ALU.add, [PB(2), ("t1", i2)], [("t1", i2)])
                            TT("dve", t1[i2][:], t1[i2][:], t3[i2][:], ALU.add, [("t1", i2), ("t3", i2)], [("t1", i2)])
                            TT("pool", t3[i2][:], t1[i2][:], zs_all[:, c, :], ALU.mult, [("t1", i2), ("zs", c)], [("t3", i2)])
                            ACT(junk3[:], t3[i2][:], AF.Square, [("t3", i2)], ["junk3", ("ssy", c)], accum=ssy[:, c:c + 1])
                            ACT(rsy[:, c:c + 1], ssy[:, c:c + 1], AF.Ln, [("ssy", c)], [("rsy", c)], bias=1e-5, scale=1.0 / 512)
                            ACT(rsy[:, c:c + 1], rsy[:, c:c + 1], AF.Exp, [("rsy", c)], [("rsy", c)], scale=-0.5)

                            def trans_y(c=c, i2=i2, g=g):
                                STT(ynb[i2][:], t3[i2][:], rsy[:, c:c + 1], ssmg_bc[:, g * 512:(g + 1) * 512], ALU.mult, ALU.mult,
                                    [("t3", i2), ("rsy", c), "ssmg"], [("ynb", i2)])
                                pv = psbf(6)
                                for q in range(4):
                                    TR(pv[:, q * 128:(q + 1) * 128], ynb[i2][:, q * 128:(q + 1) * 128], ident[:],
                                       [("ynb", i2), "ident"], [PB(6)])
                                CP("act", ysT[i2][:], V(pv[:, 0:512], [[128, 4], [1, 128]]), [PB(6)], [("ysT", i2)])
                                dst = bass.AP(tensor=yss_d.tensor,
                                              offset=yss_d.offset + ((b * NT + c) * 128 * 16 + g * 4) * 128,
                                              ap=[[16 * 128, 128], [128, 4], [1, 128]])
                                o_ = S.dma("sp", dst, ysT[i2][:], reads=[("ysT", i2)], writes=[("yss_d", b, c, g)])
                                if dbg:
                                    finals.append(o_)
                            pend2.append(trans_y)
                            if "3" in os.environ.get("KF", ""):
                                emit_pend2()
                            if c < NT - 1:
                                TT("dve", V(s1[:], [[64, 8], [1, 64]]), V(state[:], [[64, 8], [1, 64]]),
                                   V(ecum[i2][:, 8:16], [[1, 8], [0, 64]]), ALU.mult, ["state", ("ecum", i2)], ["s1"])
                                TT("dve", state[:], ps[:, 7, :], s1[:], ALU.add, [PB(7), "s1"], ["state"])
                                CP("act", state_bf[:], state[:], ["state"], ["state_bf"])

                        if "2" in os.environ.get("KF", ""):
                            for c in range(NT):
                                stageA(c)
                                stageB(c)
                        else:
                            stageA(0)
                            for c in range(NT):
                                if c + 1 < NT:
                                    stageA(c + 1)
                                stageB(c)
                    emit_pend2()
                    S.barrier()
                if stage <= 2:
                    continue

                with contextlib.ExitStack() as P:
                    def sbp(name, shape, dt):
                        return P.enter_context(nc.sbuf_tensor(f"{name}_{b}", list(shape), dt))
                    wpa = sbp("p3_wpa", [128, 8, 1024], BF16)
                    wps = sbp("p3_wps", [128, 16, 1024], BF16)
                    wg = sbp("p3_wg", [128, 8, 2048], BF16)
                    wo = sbp("p3_wo", [128, 8, 1024], BF16)
                    yat = [sbp(f"p3_yat{i}", [128, 8, 128], BF16) for i in range(2)]
                    ysst = [sbp(f"p3_ysst{i}", [128, 16, 128], BF16) for i in range(2)]
                    xt3 = [sbp(f"p3_xt{i}", [128, D], F32) for i in range(2)]
                    sa = [sbp(f"p3_sa{i}", [128, 512], F32) for i in range(2)]
                    sg_ = [sbp(f"p3_sg{i}", [128, 512], F32) for i in range(2)]
                    mixed = [sbp(f"p3_mixed{i}", [128, D], BF16) for i in range(2)]
                    mixT = [sbp(f"p3_mixT{i}", [128, 8, 128], BF16) for i in range(2)]
                    x1 = [sbp(f"p3_x1{i}", [128, D], F32) for i in range(2)]
                    hn3 = [sbp(f"p3_hn{i}", [128, D], BF16) for i in range(2)]
                    junk4 = sbp("p3_junk", [128, D], BF16)
                    ss3 = sbp("p3_ss", [128, NT], F32)
                    rs3 = sbp("p3_rs", [128, NT], F32)
                    def lw(dst, src, q, key):
                        S.dma("pool", dst[:, :, q * 512:(q + 1) * 512], src[:, :, q * 512:(q + 1) * 512], writes=[(key, q)])
                    w_g_v = w_in_v[:, :, C_G:C_G + 2048]
                    for j in range(2):
                        lw(wg, w_g_v, j, "wg")
                        lw(wg, w_g_v, 2 + j, "wg")
                        lw(wpa, w_pa_v, j, "wpa")
                        lw(wps, w_ps_v, j, "wps")
                    for j in range(2):
                        lw(wo, w_out_v, j, "wo")
                    hcs = {"hc": 0}

                    def M3(t):
                        i2 = t % 2
                        S.dma("sp", yat[i2][:], ya_d[b, t], reads=[("ya_d", b, t // 4, hh) for hh in range(8)], writes=[("yat", i2)])
                        S.dma("sp", ysst[i2][:], yss_d[b, t], reads=[("yss_d", b, t, g) for g in range(4)], writes=[("ysst", i2)])
                        S.dma("sp", xt3[i2][:], x[b, t * 128:(t + 1) * 128, :], writes=[("xt3", i2)])
                        for j in range(2):
                            h2 = hcs["hc"] % 2
                            hcs["hc"] += 1
                            cj = slice(j * 512, (j + 1) * 512)
                            for kc in range(8):
                                MM(ps[:, 2, :], hT[:, kc, t * 128:(t + 1) * 128], wg[:, kc, cj], kc == 0, kc == 7,
                                   [("hT", t), ("wg", j)], [PB(2)])
                            ACT(sa[h2][:], ps[:, 2, :], AF.Sigmoid, [PB(2)], [("sa", h2)])
                            for kc in range(8):
                                MM(ps[:, 3, :], hT[:, kc, t * 128:(t + 1) * 128], wg[:, kc, 1024 + j * 512:1024 + (j + 1) * 512],
                                   kc == 0, kc == 7, [("hT", t), ("wg", 2 + j)], [PB(3)])
                            ACT(sg_[h2][:], ps[:, 3, :], AF.Sigmoid, [PB(3)], [("sg", h2)])
                            for c in range(8):
                                MM(ps[:, 0, :], yat[i2][:, c, :], wpa[:, c, cj], c == 0, c == 7, [("yat", i2), ("wpa", j)], [PB(0)])
                            TT("dve", sa[h2][:], ps[:, 0, :], sa[h2][:], ALU.mult, [PB(0), ("sa", h2)], [("sa", h2)])
                            for c in range(16):
                                MM(ps[:, 1, :], ysst[i2][:, c, :], wps[:, c, cj], c == 0, c == 15, [("ysst", i2), ("wps", j)], [PB(1)])
                            TT("dve", sg_[h2][:], ps[:, 1, :], sg_[h2][:], ALU.mult, [PB(1), ("sg", h2)], [("sg", h2)])
                            TT("pool", mixed[i2][:, cj], sa[h2][:], sg_[h2][:], ALU.add, [("sa", h2), ("sg", h2)], [("mixed", i2)])

                    def T31(t):
                        i2 = t % 2
                        pv = psbf(4)
                        for c in range(8):
                            TR(pv[:, c * 128:(c + 1) * 128], mixed[i2][:, c * 128:(c + 1) * 128], ident[:], [("mixed", i2), "ident"], [PB(4)])
                        CP("act", mixT[i2][:], V(pv, [[128, 8], [1, 128]]), [PB(4)], [("mixT", i2)])
                        for j in range(2):
                            for c in range(8):
                                MM(ps[:, 5 + j, :], mixT[i2][:, c, :], wo[:, c, j * 512:(j + 1) * 512], c == 0, c == 7,
                                   [("mixT", i2), ("wo", j)], [PB(5 + j)])
                        TT("dve", x1[i2][:], V(ps[:, 5, :], [[1, 1024]]), xt3[i2][:], ALU.add, [PB(5), PB(6), ("xt3", i2)], [("x1", i2)])
                        o_ = S.dma("sp", out[b, t * 128:(t + 1) * 128, :], x1[i2][:], reads=[("x1", i2)], writes=[("x1d", b, t)])
                        if stage <= 3:
                            finals.append(o_)
                        ACT(junk4[:], x1[i2][:], AF.Square, [("x1", i2)], ["junk4", ("ss3", t)], accum=ss3[:, t:t + 1])
                        ACT(rs3[:, t:t + 1], ss3[:, t:t + 1], AF.Sqrt, [("ss3", t)], [("rs3", t)], bias=1e-6, scale=1.0 / D)
                        RECIP(rs3[:, t:t + 1], rs3[:, t:t + 1], [("rs3", t)], [("rs3", t)])
                        STT(hn3[i2][:], x1[i2][:], rs3[:, t:t + 1], gffn_bc[:], ALU.mult, ALU.mult,
                            [("x1", i2), ("rs3", t), "gffn"], [("hn3", i2)])

                    def T32(t):
                        i2 = t % 2
                        pv = psbf(7)
                        for c in range(8):
                            TR(pv[:, c * 128:(c + 1) * 128], hn3[i2][:, c * 128:(c + 1) * 128], ident[:], [("hn3", i2), "ident"], [PB(7)])
                        CP("act", hT[:, :, t * 128:(t + 1) * 128], V(pv, [[128, 8], [1, 128]]), [PB(7)], [("hT", t)])

                    for t in range(NT + 2):
                        if t < NT:
                            M3(t)
                        if 1 <= t <= NT:
                            T31(t - 1)
                        if t >= 2:
                            T32(t - 2)
                    S.barrier()
                if stage <= 3:
                    continue

                with contextlib.ExitStack() as P:
                    def sbp(name, shape, dt):
                        return P.enter_context(nc.sbuf_tensor(f"{name}_{b}", list(shape), dt))
                    wdn = sbp("p4_wdn", [128, NFC, 1024], BF16)
                    aT = sbp("p4_aT", [128, NFC, 1024], BF16)
                    wup = [sbp(f"p4_wup{i}", [128, 8, 256], BF16) for i in range(3)]
                    accg = [sbp(f"p4_accg{i}", [128, 1024], F32) for i in range(2)]
                    accb = [sbp(f"p4_accb{i}", [128, 1024], F32) for i in range(2)]
                    accv = [sbp(f"p4_accv{i}", [128, 1024], F32) for i in range(2)]
                    fhalo = sbp("p4_halo", [128, 2 * NFC, 2], F32)
                    x1t = [sbp(f"p4_x1t{i}", [128, D], F32) for i in range(2)]
                    ot = [sbp(f"p4_ot{i}", [128, D], F32) for i in range(2)]

                    def load_wup(n):
                        fc_ = n % NFC
                        wi_ = n % 3
                        S.dma("pool", wup[wi_][:, :, 0:128], w_up_v[:, :, fc_ * 128:(fc_ + 1) * 128], writes=[("wup", wi_, 0)])
                        S.dma("pool", wup[wi_][:, :, 128:256], w_up_v[:, :, D_FF + fc_ * 128:D_FF + (fc_ + 1) * 128],
                              writes=[("wup", wi_, 1)])

                    load_wup(0)
                    load_wup(1)
                    for q in range(2):
                        S.dma("pool", wdn[:, 0:11, q * 512:(q + 1) * 512], w_dn_v[:, 0:11, q * 512:(q + 1) * 512], writes=[("wdn", q, 0)])
                        S.dma("pool", wdn[:, 11:22, q * 512:(q + 1) * 512], w_dn_v[:, 11:22, q * 512:(q + 1) * 512], writes=[("wdn", q, 1)])
                    WDN = [("wdn", q, r_) for q in range(2) for r_ in range(2)]
                    wctr = 0
                    for blk in range(2):
                        for fc in range(NFC):
                            wi = wctr % 3
                            i2 = wctr % 2
                            if wctr + 2 < 2 * NFC:
                                load_wup(wctr + 2)
                            wctr += 1
                            for part in range(2):
                                base = 4 * i2 + 2 * part
                                cch = part * NFC + fc
                                A = (accg if part == 0 else accv)[i2]
                                ak = ("accg" if part == 0 else "accv", i2)
                                for tt in range(2):
                                    tok0 = blk * 1024 + tt * 512
                                    for kc in range(8):
                                        MM(ps[:, base + tt, :], wup[wi][:, kc, part * 128:(part + 1) * 128], hT[:, kc, tok0:tok0 + 512],
                                           kc == 0, kc == 7, [("wup", wi, part)] + [("hT", tok0 // 128 + q) for q in range(4)], [PB(base + tt)])
                                rb = [PB(base), PB(base + 1)]
                                pin = V(ps[:, base, :], [[1, 1024]])
                                if not CONV_SPLIT:
                                    ACT(A[:], pin, AF.Identity, rb + ["fconvw", "fconvb"], [ak],
                                        bias=fconvb[:, cch:cch + 1], scale=fconvw[:, cch, 2:3])
                                    for k in (1, 0):
                                        d_ = 2 - k
                                        STT(A[:, d_:1024], V(ps[:, base, :], [[1, 1024 - d_]]), fconvw[:, cch, k:k + 1], A[:, d_:1024],
                                            ALU.mult, ALU.add, rb + ["fconvw", ak], [ak])
                                        if blk == 1:
                                            STT(A[:, 0:d_], fhalo[:, cch, 2 - d_:2], fconvw[:, cch, k:k + 1], A[:, 0:d_],
                                                ALU.mult, ALU.add, [("fhalo", cch), "fconvw", ak], [ak])
                                    if blk == 0:
                                        CP("act", fhalo[:, cch, :], ps[:, base + 1, 510:512], [PB(base + 1)], [("fhalo", cch)])
                                else:
                                    ACT(A[:], pin, AF.Identity, rb + ["fconvw", "fconvb"], [ak],
                                        bias=fconvb[:, cch:cch + 1], scale=fconvw[:, cch, 2:3])
                                    if part == 0:
                                        Bv = accb[i2]
                                        bk = ("accb", i2)
                                        ACT(Bv[:], pin, AF.Identity, rb + ["fconvw"], [bk], scale=fconvw[:, cch, 0:1])
                                    STT(A[:, 1:1024], V(ps[:, base, :], [[1, 1023]]), fconvw[:, cch, 1:2], A[:, 1:1024],
                                        ALU.mult, ALU.add, rb + ["fconvw", ak], [ak])
                                    if part == 1:
                                        STT(A[:, 2:1024], V(ps[:, base, :], [[1, 1022]]), fconvw[:, cch, 0:1], A[:, 2:1024],
                                            ALU.mult, ALU.add, rb + ["fconvw", ak], [ak])
                                    if blk == 1:
                                        STT(A[:, 0:1], fhalo[:, cch, 1:2], fconvw[:, cch, 1:2], A[:, 0:1],
                                            ALU.mult, ALU.add, [("fhalo", cch), "fconvw", ak], [ak])
                                        STT(A[:, 0:2], fhalo[:, cch, 0:2], fconvw[:, cch, 0:1], A[:, 0:2],
                                            ALU.mult, ALU.add, [("fhalo", cch), "fconvw", ak], [ak])
                                    if blk == 0:
                                        CP("act", fhalo[:, cch, :], ps[:, base + 1, 510:512], [PB(base + 1)], [("fhalo", cch)])
                                    if part == 0:
                                        TT("pool", A[:, 2:1024], A[:, 2:1024], Bv[:, 0:1022], ALU.add, [ak, bk], [ak])
                            ACT(accg[i2][:], accg[i2][:], AF.Silu, [("accg", i2)], [("accg", i2)])
                            TT("pool", aT[:, fc, :], accg[i2][:], accv[i2][:], ALU.mult, [("accg", i2), ("accv", i2)], [("aT", fc)])
                        for tt in range(8):
                            t = blk * 8 + tt
                            i2 = t % 2
                            S.dma("sp", x1t[i2][:], out[b, t * 128:(t + 1) * 128, :], reads=[("x1d", b, t)], writes=[("x1t", i2)])
                            for j in range(2):
                                pb = 2 * i2 + j
                                for fc in range(NFC):
                                    MM(ps[:, pb, :], aT[:, fc, tt * 128:(tt + 1) * 128], wdn[:, fc, j * 512:(j + 1) * 512],
                                       fc == 0, fc == NFC - 1, [("aT", fc)] + WDN, [PB(pb)])
                            TT("dve", ot[i2][:], V(ps[:, 2 * i2, :], [[1, 1024]]), x1t[i2][:], ALU.add,
                               [PB(2 * i2), PB(2 * i2 + 1), ("x1t", i2)], [("ot", i2)])
                            finals.append(S.dma("sp", out[b, t * 128:(t + 1) * 128, :], ot[i2][:], reads=[("ot", i2)],
                                                writes=[("x1d", b, t)]))
                    S.barrier()
        S.emit(final_wait_ops=finals)
    return nc


_NC_CACHE = {}
PARAM_NAMES = ["rel_bias", "norm_mix_g", "w_in", "q_norm_g", "k_norm_g", "lambda_q1", "lambda_k1", "lambda_q2",
               "lambda_k2", "attn_subln_g", "conv_ssm_w", "conv_ssm_b", "dt_bias", "a_log", "d_skip", "ssm_norm_g",
               "w_proj_attn", "w_proj_ssm", "w_out", "norm_ffn_g", "w_up", "conv_ffn_w", "conv_ffn_b", "w_down"]


def make_in_maps(inputs, n_cores=8, nseq=2):
    x = np.ascontiguousarray(np.asarray(inputs["x"], dtype=np.float32))
    shared = {}
    for k in PARAM_NAMES:
        a = np.asarray(inputs[k], dtype=np.float32)
        if k == "rel_bias":
            shared[k] = np.ascontiguousarray(a)
        elif a.ndim == 2:
            shared[k] = np.ascontiguousarray(a[0:1])
        else:
            shared[k] = np.ascontiguousarray(a[0])
    in_maps = []
    for i in range(n_cores):
        m = dict(shared)
        m["x"] = np.ascontiguousarray(x[i * nseq:(i + 1) * nseq])
        in_maps.append(m)
    return in_maps


def kernel(**inputs):
    n_cores, nseq = 8, 2
    if "nc" not in _NC_CACHE:
        _NC_CACHE["nc"] = build(nseq=nseq)
    nc = _NC_CACHE["nc"]
    in_maps = make_in_maps(inputs, n_cores, nseq)
    res = run_bass_kernel_spmd(nc, in_maps, core_ids=list(range(n_cores)))
    return np.concatenate([np.asarray(r["out"], dtype=np.float32) for r in res.results], axis=0)
```

```python
import contextlib
import os
import numpy as np
import ml_dtypes
import concourse.bass as bass
import concourse.mybir as mybir
from concourse.bass_utils import run_bass_kernel_spmd

F32 = mybir.dt.float32
BF16 = mybir.dt.bfloat16
AF = mybir.ActivationFunctionType
ALU = mybir.AluOpType
AX = mybir.AxisListType

ENGS = ("pe", "act", "dve", "pool", "sp")

S_LEN = 2048
D = 1024
NT = 16
IN_COLS = 10272
C_Q, C_K, C_V, C_Z, C_XS, C_B, C_C, C_DT, C_G = 0, 1024, 2048, 3072, 5120, 7168, 7680, 8192, 8224
D_FF = 2816
NFC = 22


class Op:
    __slots__ = ("eng", "fn", "deps", "is_dma", "sig", "need_sig")

    def __init__(self, eng, fn, is_dma):
        self.eng = eng
        self.fn = fn
        self.is_dma = is_dma
        self.deps = []
        self.sig = None
        self.need_sig = False


class Sched:
    SEM_LIMIT = 30000

    def __init__(self, nc, n_dma_sems=12):
        self.nc = nc
        self.streams = {e: [] for e in ENGS}
        self.last_w = {}
        self.readers = {}
        self.n_dma_sems = n_dma_sems
        self.live_dmas = []

    def op(self, eng, fn, reads=(), writes=(), dma=False):
        o = Op(eng, fn, dma)
        deps = []
        for k in reads:
            w = self.last_w.get(k)
            if w is not None:
                deps.append(w)
            if isinstance(k, tuple) and k[0] == "ps":
                for r in self.readers.get(k, ()):
                    if r.eng != eng:
                        deps.append(r)
        for k in writes:
            w = self.last_w.get(k)
            if w is not None:
                deps.append(w)
            deps.extend(self.readers.get(k, ()))
        seen = set()
        for d in deps:
            if d is o or id(d) in seen:
                continue
            seen.add(id(d))
            if (not d.is_dma) and d.eng == eng and eng == "pe":
                continue
            o.deps.append(d)
            d.need_sig = True
        for k in reads:
            self.readers.setdefault(k, []).append(o)
        for k in writes:
            self.last_w[k] = o
            self.readers[k] = []
        self.streams[eng].append(o)
        if dma:
            self.live_dmas.append(o)
        return o

    def dma(self, q, out, in_, reads=(), writes=(), **kw):
        return self.op(q, lambda e: e.dma_start(out=out, in_=in_, **kw), reads, writes, dma=True)

    def barrier(self):
        lasts = []
        for e in ENGS:
            for o in reversed(self.streams[e]):
                if o.fn is not None and not o.is_dma:
                    lasts.append(o)
                    break
        dmas = list(self.live_dmas)
        for e in ENGS:
            b = Op(e, None, False)
            for d in lasts:
                if d.eng != e:
                    b.deps.append(d)
                    d.need_sig = True
            b.deps.extend(dmas)
            self.streams[e].append(b)
        self.live_dmas = []
        self.last_w = {}
        self.readers = {}

    def emit(self, final_wait_ops=()):
        nc = self.nc
        with contextlib.ExitStack() as es:
            for e in ENGS:
                sigs = [o for o in self.streams[e] if o.need_sig and not o.is_dma]
                n_sems = max(1, (len(sigs) + self.SEM_LIMIT - 1) // self.SEM_LIMIT)
                sems = [es.enter_context(nc.semaphore(f"s_{e}_{i}")) for i in range(n_sems)]
                for cnt, o in enumerate(sigs):
                    o.sig = (sems[cnt // self.SEM_LIMIT], cnt % self.SEM_LIMIT + 1)
            dma_prev = {}
            for e in ENGS:
                dmas = [o for o in self.streams[e] if o.is_dma]
                if not dmas:
                    continue
                k = min(self.n_dma_sems, len(dmas))
                sems = [es.enter_context(nc.semaphore(f"d_{e}_{i}")) for i in range(k)]
                uses = [0] * k
                for cnt, o in enumerate(dmas):
                    s = cnt % k
                    uses[s] += 1
                    o.sig = (sems[s], 16 * uses[s])
                    dma_prev[id(o)] = (sems[s], 16 * (uses[s] - 1)) if uses[s] > 1 else None
            blk = es.enter_context(nc.Block())
            handles = {"pe": blk.tensor, "act": blk.scalar, "dve": blk.vector,
                       "pool": blk.gpsimd, "sp": blk.sync}
            for e in ENGS:
                ops = self.streams[e]

                def body(h, ops=ops, e=e):
                    waited = {}

                    def wait(sem, val):
                        if waited.get(id(sem), 0) >= val:
                            return
                        waited[id(sem)] = val
                        h.wait_ge(sem, val)

                    for o in ops:
                        for d in o.deps:
                            wait(*d.sig)
                        if o.fn is None:
                            continue
                        if o.is_dma:
                            p = dma_prev.get(id(o))
                            if p is not None:
                                wait(*p)
                        inst = o.fn(h)
                        if o.is_dma:
                            inst.then_inc(o.sig[0], 16)
                        elif o.need_sig:
                            inst.then_inc(o.sig[0], 1)
                    if e == "sp":
                        for o in final_wait_ops:
                            wait(*o.sig)

                handles[e](body)


def V(ap, dims, off=0):
    return bass.AP(tensor=ap.tensor, offset=ap.offset + off,
                   ap=[list(ap.ap[0])] + [list(d) for d in dims])


def t5_bucket_table(nmax=256):
    n = np.arange(nmax)
    nf = np.maximum(n, 1).astype(np.float32)
    large = 16 + (np.log(nf / np.float32(16)) / np.float32(np.log(128 / 16)) * np.float32(16)).astype(np.int32)
    large = np.minimum(large, 31)
    return np.where(n < 16, n, large)


YDEF = 6
CONV_SPLIT = True


def build(nseq=2, stage=99, dbg=False):
    nc = bass.Bass("TRN2", target_bir_lowering=False)

    def din(name, shape):
        return nc.dram_tensor(name, list(shape), F32, kind="ExternalInput").ap()

    x = din("x", [nseq, S_LEN, D])
    rel_bias = din("rel_bias", [32, 8])
    norm_mix_g = din("norm_mix_g", [1, D])
    w_in = din("w_in", [D, IN_COLS])
    q_norm_g = din("q_norm_g", [1, 64])
    k_norm_g = din("k_norm_g", [1, 64])
    lam_q1 = din("lambda_q1", [1, 64])
    lam_k1 = din("lambda_k1", [1, 64])
    lam_q2 = din("lambda_q2", [1, 64])
    lam_k2 = din("lambda_k2", [1, 64])
    subln_g = din("attn_subln_g", [1, 128])
    conv_ssm_w = din("conv_ssm_w", [4, 3072])
    conv_ssm_b = din("conv_ssm_b", [1, 3072])
    dt_bias = din("dt_bias", [1, 32])
    a_log = din("a_log", [1, 32])
    d_skip = din("d_skip", [1, 32])
    ssm_norm_g = din("ssm_norm_g", [1, 2048])
    w_pa = din("w_proj_attn", [1024, 1024])
    w_ps = din("w_proj_ssm", [2048, 1024])
    w_out = din("w_out", [1024, 1024])
    norm_ffn_g = din("norm_ffn_g", [1, D])
    w_up = din("w_up", [D, 2 * D_FF])
    conv_ffn_w = din("conv_ffn_w", [3, 2 * D_FF])
    conv_ffn_b = din("conv_ffn_b", [1, 2 * D_FF])
    w_down = din("w_down", [D_FF, D])
    out = nc.dram_tensor("out", [nseq, S_LEN, D], F32, kind="ExternalOutput").ap()

    skind = "ExternalOutput" if dbg else "Internal"
    ext_d = nc.dram_tensor("ext_d", [8, 384], F32, kind="Internal").ap()
    Zd = nc.dram_tensor("Zd", [8, 128, 384], F32, kind="Internal").ap()
    ya_d = nc.dram_tensor("ya_d", [nseq, NT, 128, 8, 128], BF16, kind=skind).ap()
    yss_d = nc.dram_tensor("yss_d", [nseq, NT, 128, 16, 128], BF16, kind=skind).ap()
    if dbg:
        hT_d = nc.dram_tensor("hT_d", [128, 8, S_LEN], BF16, kind="ExternalOutput").ap()
        dt_d = nc.dram_tensor("dt_d", [128, NT, 32], F32, kind="ExternalOutput").ap()

    w_in_v = w_in.rearrange("(kc p) c -> p kc c", p=128)
    w_up_v = w_up.rearrange("(kc p) c -> p kc c", p=128)
    w_pa_v = w_pa.rearrange("(kc p) c -> p kc c", p=128)
    w_ps_v = w_ps.rearrange("(kc p) c -> p kc c", p=128)
    w_out_v = w_out.rearrange("(kc p) c -> p kc c", p=128)
    w_dn_v = w_down.rearrange("(kc p) c -> p kc c", p=128)

    S = Sched(nc)
    finals = []

    def MM(o, lhsT, rhs, start, stop, r, w):
        S.op("pe", lambda e: e.matmul(out=o, lhsT=lhsT, rhs=rhs, start=start, stop=stop), r, w)

    def TR(o, in_, ident, r, w):
        S.op("pe", lambda e: e.transpose(out=o, in_=in_, identity=ident), r, w)

    def ACT(o, in_, func, r, w, bias=None, scale=None, accum=None):
        kw = {}
        if bias is not None:
            kw["bias"] = bias
        if scale is not None:
            kw["scale"] = scale
        if accum is not None:
            kw["accum_out"] = accum
        S.op("act", lambda e: e.activation(out=o, in_=in_, func=func, **kw), r, w)

    def TT(eng, o, in0, in1, op, r, w):
        S.op(eng, lambda e: e.tensor_tensor(out=o, in0=in0, in1=in1, op=op), r, w)

    def TS(eng, o, in0, s1, op0, r, w, s2=None, op1=None):
        if op1 is None:
            S.op(eng, lambda e: e.tensor_scalar(out=o, in0=in0, scalar1=s1, scalar2=None, op0=op0), r, w)
        else:
            S.op(eng, lambda e: e.tensor_scalar(out=o, in0=in0, scalar1=s1, scalar2=s2, op0=op0, op1=op1), r, w)

    def STT(o, in0, scalar, in1, op0, op1, r, w):
        S.op("dve", lambda e: e.scalar_tensor_tensor(out=o, in0=in0, scalar=scalar, in1=in1, op0=op0, op1=op1), r, w)

    def CP(eng, o, in_, r, w):
        if eng == "act":
            S.op("act", lambda e: e.copy(out=o, in_=in_), r, w)
        else:
            S.op(eng, lambda e: e.tensor_copy(out=o, in_=in_), r, w)

    def RECIP(o, in_, r, w):
        S.op("dve", lambda e: e.reciprocal(out=o, in_=in_), r, w)

    def MEMSET(eng, ap, val, w):
        S.op(eng, lambda e: e.memset(ap, val), (), w)

    def bc(ap, n=128):
        return ap.broadcast_to([n, ap.shape[-1]])

    with contextlib.ExitStack() as G:
        def sbg(name, shape, dt):
            return G.enter_context(nc.sbuf_tensor(name, list(shape), dt))

        ps = G.enter_context(nc.psum_tensor("ps", [128, 8, 512], F32))

        def PB(b):
            return ("ps", b)

        def psbf(b):
            return ps[:, b, :].bitcast(BF16)

        ident = sbg("ident", [128, 128], BF16)
        identf = sbg("identf", [128, 128], F32)
        U = sbg("U", [128, 128], F32)
        Lst = sbg("Lst", [128, 128], F32)
        ones_f = sbg("ones_f", [128, 128], F32)
        gmix_bc = sbg("gmix_bc", [128, D], F32)
        gffn_bc = sbg("gffn_bc", [128, D], F32)
        gqk_bc = sbg("gqk_bc", [128, 256], F32)
        subln_bc = sbg("subln_bc", [128, 128], F32)
        neglam = sbg("neglam", [128, 1], F32)
        lamt = sbg("lamt", [128, 4, 64], F32)
        lamp = sbg("lamp", [128, 2, 64], F32)
        lame = sbg("lame", [128, 2], F32)
        b31 = sbg("b31", [128, 8], F32)
        convw = sbg("convw", [128, 24, 4], F32)
        convb = sbg("convb", [128, 24], F32)
        fconvw = sbg("fconvw", [128, 44, 3], F32)
        fconvb = sbg("fconvb", [128, 44], F32)
        dtb_bc = sbg("dtb_bc", [128, 32], F32)
        a_bc = sbg("a_bc", [128, 32], F32)
        dsk_bc = sbg("dsk_bc", [128, 32], F32)
        RB = sbg("RB", [8, 32], F32)
        ext_sb = sbg("ext_sb", [8, 384], F32)

        MEMSET("pool", ones_f[:], 1.0, ["ones_f"])
        MEMSET("pool", identf[:], 0.0, ["identf"])
        S.op("pool", lambda e: e.affine_select(out=identf[:], in_=identf[:], pattern=[[-1, 128]],
                                               compare_op=ALU.not_equal, fill=1.0, base=0, channel_multiplier=1),
             ["identf"], ["identf"])
        CP("dve", ident[:], identf[:], ["identf"], ["ident"])
        S.op("pool", lambda e: e.affine_select(out=U[:], in_=ones_f[:], pattern=[[1, 128]],
                                               compare_op=ALU.is_ge, fill=0.0, base=0, channel_multiplier=-1),
             ["ones_f"], ["U"])
        S.op("pool", lambda e: e.affine_select(out=Lst[:], in_=ones_f[:], pattern=[[-1, 128]],
                                               compare_op=ALU.is_ge, fill=0.0, base=-1, channel_multiplier=1),
             ["ones_f"], ["Lst"])
        S.dma("sp", gmix_bc[:], bc(norm_mix_g), writes=["gmix"])
        S.dma("sp", gffn_bc[:], bc(norm_ffn_g), writes=["gffn"])
        S.dma("sp", gqk_bc[:, 0:64], bc(q_norm_g), writes=["gqk"])
        S.dma("sp", gqk_bc[:, 64:128], bc(q_norm_g), writes=["gqk"])
        S.dma("sp", gqk_bc[:, 128:192], bc(k_norm_g), writes=["gqk"])
        S.dma("sp", gqk_bc[:, 192:256], bc(k_norm_g), writes=["gqk"])
        ACT(gqk_bc[:, 0:128], gqk_bc[:, 0:128], AF.Identity, ["gqk"], ["gqk"], scale=0.125)
        S.dma("sp", subln_bc[:], bc(subln_g), writes=["subln"])
        ACT(subln_bc[:], subln_bc[:], AF.Identity, ["subln"], ["subln"], scale=0.8)
        for i, a in enumerate((lam_q1, lam_q2, lam_k1, lam_k2)):
            S.dma("sp", lamt[:, i, :], bc(a), writes=["lamt"])
        TT("dve", lamp[:], lamt[:, 0:2, :], lamt[:, 2:4, :], ALU.mult, ["lamt"], ["lamp"])
        S.op("dve", lambda e: e.tensor_reduce(out=lame[:], in_=lamp[:], axis=AX.X, op=ALU.add), ["lamp"], ["lame"])
        ACT(lame[:], lame[:], AF.Exp, ["lame"], ["lame"])
        TT("dve", neglam[:], lame[:, 1:2], lame[:, 0:1], ALU.subtract, ["lame"], ["neglam"])
        TS("dve", neglam[:], neglam[:], -0.2, ALU.add, ["neglam"], ["neglam"])
        S.dma("sp", b31[:], bc(rel_bias[31:32, :]), writes=["b31"])
        S.dma("sp", dtb_bc[:], bc(dt_bias), writes=["dtb"])
        S.dma("sp", dsk_bc[:], bc(d_skip), writes=["dsk"])
        S.dma("sp", a_bc[:], bc(a_log), writes=["a_bc"])
        ACT(a_bc[:], a_bc[:], AF.Exp, ["a_bc"], ["a_bc"])
        ACT(a_bc[:], a_bc[:], AF.Identity, ["a_bc"], ["a_bc"], scale=-1.0)
        stg = sbg("stg", [64, 9, 128], F32)
        rbs = sbg("rbs", [32, 8], F32)
        MEMSET("pool", stg[:], 0.0, ["stg"])
        S.dma("sp", stg[0:24, 0:4, :], conv_ssm_w.rearrange("k (cc p) -> cc k p", p=128), writes=["stg"])
        S.dma("sp", stg[0:24, 4, :], conv_ssm_b[0, :].rearrange("(cc p) -> cc p", p=128), writes=["stg"])
        S.dma("sp", stg[0:44, 5:8, :], conv_ffn_w.rearrange("k (cc p) -> cc k p", p=128), writes=["stg"])
        S.dma("sp", stg[0:44, 8, :], conv_ffn_b[0, :].rearrange("(cc p) -> cc p", p=128), writes=["stg"])
        S.dma("sp", rbs[:], rel_bias, writes=["rbs"])
        def pcol(k):
            return k * 32 if k < 5 else 160 + (k - 5) * 64
        for k in range(9):
            n = 32 if k < 5 else 64
            TR(ps[:, 0, pcol(k):pcol(k) + n], stg[0:n, k, :], identf[0:n, 0:n], ["stg", "identf"], [PB(0)])
        for k in range(4):
            CP("dve", convw[:, :, k], ps[:, 0, pcol(k):pcol(k) + 24], [PB(0)], ["convw"])
        CP("dve", convb[:], ps[:, 0, pcol(4):pcol(4) + 24], [PB(0)], ["convb"])
        for k in range(3):
            CP("dve", fconvw[:, :, k], ps[:, 0, pcol(5 + k):pcol(5 + k) + 44], [PB(0)], ["fconvw"])
        CP("dve", fconvb[:], ps[:, 0, pcol(8):pcol(8) + 44], [PB(0)], ["fconvb"])
        TR(ps[0:8, 1, 0:32], rbs[:], identf[0:32, 0:32], ["rbs", "identf"], [PB(1)])
        CP("dve", RB[:], ps[0:8, 1, 0:32], [PB(1)], ["RB"])
        MEMSET("pool", ext_sb[:], -30000.0, ["ext_sb"])
        CP("dve", ext_sb[:, 127:143], RB[:, 0:16], ["RB", "ext_sb"], ["ext_sb"])
        bt = t5_bucket_table(256)
        assert (bt[113:] == 31).all()
        for bk in range(16, 32):
            idx = np.nonzero(bt == bk)[0]
            if len(idx) == 0:
                continue
            n0, n1 = int(idx[0]), int(idx[-1]) + 1
            assert n1 - n0 == len(idx)
            CP("dve", ext_sb[:, 127 + n0:127 + n1], V(RB[:, bk:bk + 1], [[0, n1 - n0]]), ["RB", "ext_sb"], ["ext_sb"])
        S.dma("sp", ext_d, ext_sb[:, :], reads=["ext_sb"], writes=["ext_d"])
        S.dma("sp", Zd, bass.AP(tensor=ext_d.tensor, offset=ext_d.offset, ap=[[384, 8], [0, 128], [1, 384]]),
              reads=["ext_d"], writes=["Zd"])
        S.barrier()

        for b in range(nseq):
            with contextlib.ExitStack() as Q:
                def sbq(name, shape, dt):
                    return Q.enter_context(nc.sbuf_tensor(f"{name}_{b}", list(shape), dt))

                hT = sbq("hT", [128, 8, S_LEN], BF16)
                HT_ALL = [("hT", t) for t in range(NT)]

                with contextlib.ExitStack() as P:
                    def sbp(name, shape, dt):
                        return P.enter_context(nc.sbuf_tensor(f"{name}_{b}", list(shape), dt))
                    xts = [sbp(f"p0_xt{i}", [128, D], F32) for i in range(2)]
                    hns = [sbp(f"p0_hn{i}", [128, D], BF16) for i in range(2)]
                    junk = sbp("p0_junk", [128, D], BF16)
                    ssq = sbp("p0_ssq", [128, NT], F32)
                    rs = sbp("p0_rs", [128, NT], F32)
                    for t in range(NT):
                        i2 = t % 2
                        xt, hn = xts[i2], hns[i2]
                        S.dma("sp", xt[:], x[b, t * 128:(t + 1) * 128, :], writes=[("xt", i2)])
                        ACT(junk[:], xt[:], AF.Square, [("xt", i2)], ["junk", ("ssq", t)], accum=ssq[:, t:t + 1])
                        ACT(rs[:, t:t + 1], ssq[:, t:t + 1], AF.Sqrt, [("ssq", t)], [("rs", t)], bias=1e-6, scale=1.0 / D)
                        RECIP(rs[:, t:t + 1], rs[:, t:t + 1], [("rs", t)], [("rs", t)])
                        STT(hn[:], xt[:], rs[:, t:t + 1], gmix_bc[:], ALU.mult, ALU.mult,
                            [("xt", i2), ("rs", t), "gmix"], [("hn", i2)])
                        pb = 2 * i2
                        pv = psbf(pb)
                        for c in range(8):
                            TR(pv[:, c * 128:(c + 1) * 128], hn[:, c * 128:(c + 1) * 128], ident[:],
                               [("hn", i2), "ident"], [PB(pb)])
                        CP("act" if i2 else "dve", hT[:, :, t * 128:(t + 1) * 128],
                           V(pv, [[128, 8], [1, 128]]), [PB(pb)], [("hT", t)])
                    if dbg and b == 0:
                        finals.append(S.dma("sp", hT_d, hT[:], reads=HT_ALL))
                    S.barrier()
                if stage <= 0:
                    continue

                with contextlib.ExitStack() as P:
                    def sbp(name, shape, dt):
                        return P.enter_context(nc.sbuf_tensor(f"{name}_{b}", list(shape), dt))
                    NP = 28
                    BT = sbp("p1_BT", [128, 8, 256], F32)
                    wqkv = [sbp(f"p1_w{i}", [128, 8, 384], BF16) for i in range(2)]
                    qkT = [sbp(f"p1_qkT{i}", [128, 2, S_LEN], BF16) for i in range(2)]
                    vaug = [sbp(f"p1_v{i}", [128, NT, 132], BF16) for i in range(2)]
                    PT = [sbp(f"p1_PT{i}", [128, 2, 512], BF16) for i in range(NP)]
                    sq = [sbp(f"p1_sq{i}", [128, 256], F32) for i in range(2)]
                    tmpn = [sbp(f"p1_tmpn{i}", [128, 256], F32) for i in range(2)]
                    qkn = [sbp(f"p1_qkn{i}", [128, 256], BF16) for i in range(6)]
                    ssq4 = sbp("p1_ssq4", [128, NT, 4], F32)
                    rs4 = sbp("p1_rs4", [128, NT, 4], F32)
                    etmp = [sbp(f"p1_et{i}", [128, 2, 256], F32) for i in range(2)]
                    rl = sbp("p1_rl", [128, NT, 2], F32)
                    nrl = sbp("p1_nrl", [128, NT], F32)
                    o1 = [sbp(f"p1_o1{i}", [128, 128], F32) for i in range(2)]
                    oo = [sbp(f"p1_oo{i}", [128, 128], F32) for i in range(2)]
                    junk2 = sbp("p1_junk2", [128, 128], BF16)
                    sso = sbp("p1_sso", [128, NT], F32)
                    rso = sbp("p1_rso", [128, NT], F32)
                    yn = [sbp(f"p1_yn{i}", [128, 128], BF16) for i in range(6)]
                    yst = [sbp(f"p1_yst{i}", [128, 4, 128], BF16) for i in range(2)]

                    for h in range(8):
                        S.dma("sp", BT[:, h, :],
                              bass.AP(tensor=Zd.tensor, offset=Zd.offset + h * 128 * 384 + 127, ap=[[383, 128], [1, 256]]),
                              writes=[("BT", h)])
                    for i in range(2):
                        MEMSET("pool", vaug[i][:, :, 128:129], 1.0, [("vaug", i)])

                    pends = {"q": [], "y": []}

                    def tick():
                        for pend in pends.values():
                            for it in pend:
                                it[0] -= 1
                            while pend and pend[0][0] <= 0:
                                pend.pop(0)[1]()

                    def defer(n, fn, q="q"):
                        pends[q].append([n, fn])

                    def flush():
                        for pend in pends.values():
                            while pend:
                                pend.pop(0)[1]()

                    st_ = {"pt": 0, "sb": 0, "yst": 0, "qkn": 0, "yn": 0}

                    def load_w(h):
                        sl = h % 2
                        for j3, c0 in enumerate((C_Q, C_K, C_V)):
                            S.dma("pool", wqkv[sl][:, :, j3 * 128:(j3 + 1) * 128],
                                  w_in_v[:, :, c0 + h * 128:c0 + (h + 1) * 128], writes=[("wqkv", sl, j3)])

                    def proj_item(h, t):
                        def f():
                            sl = h % 2
                            W = wqkv[sl]
                            i2 = t % 2
                            pb = 4 + 2 * i2
                            for kc in range(8):
                                MM(ps[:, pb, 0:384], hT[:, kc, t * 128:(t + 1) * 128], W[:, kc, :], kc == 0, kc == 7,
                                   [("hT", t)] + [("wqkv", sl, q) for q in range(3)], [PB(pb)])
                            ACT(sq[i2][:], ps[:, pb, 0:256], AF.Square, [PB(pb)], [("sq", i2)])
                            S.op("dve", lambda e: e.tensor_reduce(
                                out=ssq4[:, t, :], in_=V(sq[i2][:], [[64, 4], [1, 64]]), axis=AX.X, op=ALU.add),
                                [("sq", i2)], [("ssq4", t)])
                            ACT(rs4[:, t, :], ssq4[:, t, :], AF.Ln, [("ssq4", t)], [("rs4", t)], bias=1e-6, scale=1.0 / 64)
                            ACT(rs4[:, t, :], rs4[:, t, :], AF.Exp, [("rs4", t)], [("rs4", t)], scale=-0.5)
                            TT("dve", V(tmpn[i2][:], [[64, 4], [1, 64]]), V(ps[:, pb, 0:256], [[64, 4], [1, 64]]),
                               V(rs4[:, t, :], [[1, 4], [0, 64]]), ALU.mult, [PB(pb), ("rs4", t)], [("tmpn", i2)])
                            qi = st_["qkn"] % 6
                            st_["qkn"] += 1
                            TT("pool", qkn[qi][:], tmpn[i2][:], gqk_bc[:], ALU.mult, [("tmpn", i2), "gqk"], [("qkn", qi)])
                            CP("act", vaug[sl][:, t, 0:128], ps[:, pb, 256:384], [PB(pb)], [("vaug", sl)])

                            def g():
                                pv = psbf(2)
                                hf = i2 * 512
                                for m2 in range(2):
                                    TR(pv[:, hf + m2 * 128:hf + (m2 + 1) * 128], qkn[qi][:, m2 * 128:(m2 + 1) * 128], ident[:],
                                       [("qkn", qi), "ident"], [PB(2)])
                                CP("act" if i2 else "dve", qkT[sl][:, :, t * 128:(t + 1) * 128],
                                   V(pv[:, hf:hf + 256], [[128, 2], [1, 128]]), [PB(2)], [("qkT", sl, t)])
                            defer(4, g)
                            tick()
                        return f

                    PTc = {}

                    def qk_item(h, c, j):
                        def f():
                            sl = h % 2
                            r = j - 4 * c
                            st = max(0, r) * 128
                            b0 = 4 + 2 * (st_["sb"] % 2)
                            st_["sb"] += 1
                            qkeys = [("qkT", sl, j)] + [("qkT", sl, q) for q in range(4 * c + st // 128, 4 * c + 4)]
                            for m in range(2):
                                MM(ps[:, b0 + m, st:512], qkT[sl][64 * m:64 * m + 64, 1, j * 128:(j + 1) * 128],
                                   qkT[sl][64 * m:64 * m + 64, 0, 512 * c + st:512 * c + 512], True, True,
                                   qkeys, [PB(b0 + m)])
                            pi = st_["pt"] % NP
                            st_["pt"] += 1
                            PTc[(h, c, j)] = pi
                            Pt = PT[pi]
                            rb = [PB(b0), PB(b0 + 1)]
                            if r >= -1:
                                if r >= 0:
                                    nb = min(2, 4 - r)
                                    btsl = BT[:, h, 0:128 * nb]
                                else:
                                    nb = 1
                                    btsl = BT[:, h, 128:256]
                                wdt = 128 * nb
                                ei = st_["sb"] % 2
                                TT("dve", etmp[ei][:, :, 0:wdt], ps[:, b0:b0 + 2, st:st + wdt],
                                   V(btsl, [[0, 2], [1, wdt]]), ALU.add, rb + [("BT", h)], [("etmp", ei)])
                                ACT(Pt[:, :, st:st + wdt], etmp[ei][:, :, 0:wdt], AF.Exp, [("etmp", ei)], [("PT", pi)])
                                if st + wdt < 512:
                                    ACT(Pt[:, :, st + wdt:512], ps[:, b0:b0 + 2, st + wdt:512], AF.Exp,
                                        rb + ["b31"], [("PT", pi)], bias=b31[:, h:h + 1])
                            else:
                                ACT(Pt[:, :, :], ps[:, b0:b0 + 2, :], AF.Exp, rb + ["b31"], [("PT", pi)],
                                    bias=b31[:, h:h + 1])
                            tick()
                        return f

                    def av_item(h, c, i, m):
                        def f():
                            sl = h % 2
                            ob = i % 2
                            i2 = i % 2
                            for j in range(i + 1):
                                pi = PTc[(h, c, j)]
                                MM(ps[:, ob, 256 * m:256 * m + 129],
                                   PT[pi][:, m, (i - 4 * c) * 128:(i - 4 * c + 1) * 128],
                                   vaug[sl][:, j, 0:129], j == 0, j == i,
                                   [("PT", pi), ("vaug", sl)], [PB(ob)])
                            if m == 1:
                                RECIP(rl[:, i, :], V(ps[:, ob, 128:129], [[256, 2]]), [PB(ob)], [("rl", i)])
                                TS("dve", nrl[:, i:i + 1], rl[:, i, 1:2], neglam[:, 0:1], ALU.mult,
                                   [("rl", i), "neglam"], [("nrl", i)])
                                TS("dve", o1[i2][:], ps[:, ob, 0:128], rl[:, i, 0:1], ALU.mult, [PB(ob), ("rl", i)], [("o1", i2)])
                                STT(oo[i2][:], ps[:, ob, 256:384], nrl[:, i:i + 1], o1[i2][:], ALU.mult, ALU.add,
                                    [PB(ob), ("nrl", i), ("o1", i2)], [("oo", i2)])
                                ACT(junk2[:], oo[i2][:], AF.Square, [("oo", i2)], ["junk2", ("sso", i)], accum=sso[:, i:i + 1])
                                ACT(rso[:, i:i + 1], sso[:, i:i + 1], AF.Ln, [("sso", i)], [("rso", i)],
                                    bias=1e-5, scale=1.0 / 128)
                                ACT(rso[:, i:i + 1], rso[:, i:i + 1], AF.Exp, [("rso", i)], [("rso", i)], scale=-0.5)
                                yi = st_["yn"] % 6
                                st_["yn"] += 1
                                STT(yn[yi][:], oo[i2][:], rso[:, i:i + 1], subln_bc[:], ALU.mult, ALU.mult,
                                    [("oo", i2), ("rso", i), "subln"], [("yn", yi)])

                                def g():
                                    hf = c % 2
                                    pv = psbf(3)
                                    col = hf * 512 + (i % 4) * 128
                                    TR(pv[:, col:col + 128], yn[yi][:], ident[:], [("yn", yi), "ident"], [PB(3)])
                                    if i % 4 == 3:
                                        yc = st_["yst"] % 2
                                        st_["yst"] += 1
                                        ys = yst[yc]
                                        CP("act", ys[:], V(pv[:, hf * 512:hf * 512 + 512], [[128, 4], [1, 128]]),
                                           [PB(3)], [("yst", yc)])
                                        dst = bass.AP(tensor=ya_d.tensor,
                                                      offset=ya_d.offset + ((b * NT + 4 * c) * 128 * 8 + h) * 128,
                                                      ap=[[8 * 128, 128], [128 * 8 * 128, 4], [1, 128]])
                                        o_ = S.dma("sp", dst, ys[:], reads=[("yst", yc)], writes=[("ya_d", b, c, h)])
                                        if dbg:
                                            finals.append(o_)
                                defer(YDEF, g, "y")
                            tick()
                        return f

                    def run(items):
                        for it in items:
                            it()

                    def merge(A, B):
                        nb_done = 0
                        for idx, a in enumerate(A):
                            a()
                            tgt = (idx + 1) * len(B) // len(A)
                            while nb_done < tgt:
                                B[nb_done]()
                                nb_done += 1
                        while nb_done < len(B):
                            B[nb_done]()
                            nb_done += 1

                    def qk_items(h, c):
                        return [qk_item(h, c, j) for j in range(4 * c + 4)]

                    def av_items(h, c):
                        return [av_item(h, c, i, m) for i in range(4 * c, 4 * c + 4) for m in range(2)]

                    def proj_items(h):
                        return [proj_item(h, t) for t in range(NT)]

                    load_w(0)
                    run(proj_items(0))
                    for h in range(8):
                        if h < 7:
                            load_w(h + 1)
                        run(qk_items(h, 0))
                        for c in range(3):
                            merge(av_items(h, c), qk_items(h, c + 1))
                        merge(av_items(h, 3), proj_items(h + 1) if h < 7 else [])
                    flush()
                    S.barrier()
                if stage <= 1:
                    continue

                with contextlib.ExitStack() as P:
                    def sbp(name, shape, dt):
                        return P.enter_context(nc.sbuf_tensor(f"{name}_{b}", list(shape), dt))
                    ssmg_bc = sbp("p2_ssmg", [128, 2048], F32)
                    dt_all = sbp("p2_dt", [128, NT, 32], F32)
                    adt_all = sbp("p2_adt", [128, NT, 32], F32)
                    wdt_ = sbp("p2_wdt", [128, 8, 32], BF16)
                    dtt = [sbp(f"p2_dtt{i}", [128, 32], F32) for i in range(2)]
                    wx = [sbp(f"p2_wx{i}", [128, 8, 128], BF16) for i in range(3)]
                    wzs = [sbp(f"p2_wz{i}", [128, 8, 512], BF16) for i in range(2)]
                    acc = [sbp(f"p2_acc{i}", [128, 1024], F32) for i in range(2)]
                    accb = [sbp(f"p2_accb{i}", [128, 1024], F32) for i in range(2)]
                    zs_all = sbp("p2_zsall", [128, NT, 512], BF16)
                    halo = sbp("p2_halo", [128, 4], F32)
                    fmx = [sbp(f"p2_fmx{i}", [128, S_LEN], BF16) for i in range(2)]
                    BTg = sbp("p2_BTg", [128, S_LEN], BF16)
                    CTg = sbp("p2_CTg", [128, S_LEN], BF16)
                    xs_tok = sbp("p2_xs", [128, NT, 512], BF16)
                    B_tok = sbp("p2_Btok", [128, NT, 128], BF16)
                    state = sbp("p2_state", [128, 512], F32)
                    state_bf = sbp("p2_statebf", [128, 512], BF16)
                    s1 = sbp("p2_s1", [128, 512], F32)
                    cumtot = [sbp(f"p2_ct{i}", [128, 16], F32) for i in range(2)]
                    ecum = [sbp(f"p2_ec{i}", [128, 16], F32) for i in range(2)]
                    w8 = [sbp(f"p2_w8{i}", [128, 8], F32) for i in range(2)]
                    adtU = [sbp(f"p2_adtU{i}", [128, 8, 128], F32) for i in range(2)]
                    eseg = [sbp(f"p2_eseg{i}", [128, 8, 128], BF16) for i in range(2)]
                    cbTm = [sbp(f"p2_cbT{i}", [128, 128], BF16) for i in range(2)]
                    MT = [sbp(f"p2_MT{i}", [128, 8, 128], BF16) for i in range(2)]
                    xdt = [sbp(f"p2_xdt{i}", [128, 8, 64], BF16) for i in range(2)]
                    xw = [sbp(f"p2_xw{i}", [128, 8, 64], BF16) for i in range(2)]
                    t1 = [sbp(f"p2_t1{i}", [128, 512], F32) for i in range(2)]
                    t3 = [sbp(f"p2_t3{i}", [128, 512], F32) for i in range(2)]
                    junk3 = sbp("p2_junk3", [128, 512], BF16)
                    ssy = sbp("p2_ssy", [128, NT], F32)
                    rsy = sbp("p2_rsy", [128, NT], F32)
                    ynb = [sbp(f"p2_ynb{i}", [128, 512], BF16) for i in range(2)]
                    ysT = [sbp(f"p2_ysT{i}", [128, 4, 128], BF16) for i in range(2)]

                    S.dma("sp", ssmg_bc[:], bc(ssm_norm_g), writes=["ssmg"])
                    S.dma("pool", wdt_[:], w_in_v[:, :, C_DT:C_DT + 32], writes=["wdt"])
                    for t in range(NT):
                        i2 = t % 2
                        for kc in range(8):
                            MM(ps[:, 5, 0:32], hT[:, kc, t * 128:(t + 1) * 128], wdt_[:, kc, :], kc == 0, kc == 7,
                               [("hT", t), "wdt"], [PB(5)])
                        TT("dve", dtt[i2][:], ps[:, 5, 0:32], dtb_bc[:], ALU.add, [PB(5), "dtb"], [("dtt", i2)])
                        ACT(dtt[i2][:], dtt[i2][:], AF.Exp, [("dtt", i2)], [("dtt", i2)])
                        ACT(dt_all[:, t, :], dtt[i2][:], AF.Ln, [("dtt", i2)], [("dt", t)], bias=1.0, scale=1.0)
                        TT("pool", adt_all[:, t, :], dt_all[:, t, :], a_bc[:], ALU.mult, [("dt", t), "a_bc"], [("adt", t)])
                    if dbg and b == 0:
                        finals.append(S.dma("sp", dt_d, dt_all[:], reads=[("dt", t) for t in range(NT)]))

                    st2 = {"pp": 0, "ab": 0, "zb": 0}
                    pend2 = []

                    def chunk_desc(n):
                        g_, ci_ = n // 6, n % 6
                        if ci_ < 4:
                            return g_, ci_, C_XS + g_ * 512 + ci_ * 128, g_ * 4 + ci_, fmx[ci_ % 2], ("fmx", ci_ % 2)
                        if ci_ == 4:
                            return g_, ci_, C_B + g_ * 128, 16 + g_, BTg, "BTg"
                        return g_, ci_, C_C + g_ * 128, 20 + g_, CTg, "CTg"

                    def load_wx(n):
                        if n >= 24:
                            return
                        col0 = chunk_desc(n)[2]
                        S.dma("pool", wx[n % 3][:], w_in_v[:, :, col0:col0 + 128], writes=[("wx", n % 3)])

                    def load_wz(g_):
                        S.dma("pool", wzs[g_ % 2][:], w_in_v[:, :, C_Z + g_ * 512:C_Z + (g_ + 1) * 512], writes=[("wz", g_ % 2)])

                    def emit_pend2():
                        while pend2:
                            pend2.pop(0)()

                    load_wx(0)
                    load_wx(1)
                    load_wz(0)
                    for g in range(4):
                        for ci in range(6):
                            n = g * 6 + ci
                            _, _, col0, cch, dstT, dkey = chunk_desc(n)
                            wi = n % 3
                            load_wx(n + 2)
                            for half in range(2):
                                bp = 2 * (st2["pp"] % 2)
                                st2["pp"] += 1
                                ai = st2["ab"] % 2
                                st2["ab"] += 1
                                A, Bv = acc[ai], accb[ai]
                                ak, bk = ("acc", ai), ("accb", ai)
                                for tt in range(2):
                                    tok0 = half * 1024 + tt * 512
                                    for kc in range(8):
                                        MM(ps[:, bp + tt, :], wx[wi][:, kc, :], hT[:, kc, tok0:tok0 + 512], kc == 0, kc == 7,
                                           [("wx", wi)] + [("hT", tok0 // 128 + q) for q in range(4)], [PB(bp + tt)])
                                emit_pend2()
                                pin = V(ps[:, bp, :], [[1, 1024]])
                                pin1 = V(ps[:, bp, :], [[1, 1023]])
                                rb = [PB(bp), PB(bp + 1)]
                                if not CONV_SPLIT:
                                    ACT(A[:], pin, AF.Identity, rb + ["convw", "convb"], [ak],
                                        bias=convb[:, cch:cch + 1], scale=convw[:, cch, 3:4])
                                    for k in (2, 1, 0):
                                        d_ = 3 - k
                                        STT(A[:, d_:1024], V(ps[:, bp, :], [[1, 1024 - d_]]), convw[:, cch, k:k + 1], A[:, d_:1024],
                                            ALU.mult, ALU.add, rb + ["convw", ak], [ak])
                                        if half == 1:
                                            STT(A[:, 0:d_], halo[:, 3 - d_:3], convw[:, cch, k:k + 1], A[:, 0:d_],
                                                ALU.mult, ALU.add, ["halo", "convw", ak], [ak])
                                    if half == 0:
                                        CP("act", halo[:, 0:3], ps[:, bp + 1, 509:512], [PB(bp + 1)], ["halo"])
                                else:
                                    ACT(A[:], pin, AF.Identity, rb + ["convw", "convb"], [ak],
                                        bias=convb[:, cch:cch + 1], scale=convw[:, cch, 3:4])
                                    if "8" in os.environ.get("KF", ""):
                                        TS("dve", Bv[:], pin, convw[:, cch, 1:2], ALU.mult, rb + ["convw"], [bk])
                                    else:
                                        ACT(Bv[:], pin, AF.Identity, rb + ["convw"], [bk], scale=convw[:, cch, 1:2])
                                    STT(A[:, 1:1024], pin1, convw[:, cch, 2:3], A[:, 1:1024], ALU.mult, ALU.add, rb + ["convw", ak], [ak])
                                    STT(Bv[:, 1:1024], pin1, convw[:, cch, 0:1], Bv[:, 1:1024], ALU.mult, ALU.add, rb + ["convw", bk], [bk])
                                    if half == 1 and "7" not in os.environ.get("KF", ""):
                                        STT(A[:, 0:1], halo[:, 2:3], convw[:, cch, 2:3], A[:, 0:1], ALU.mult, ALU.add, ["halo", "convw", ak], [ak])
                                        STT(A[:, 0:2], halo[:, 1:3], convw[:, cch, 1:2], A[:, 0:2], ALU.mult, ALU.add, ["halo", "convw", ak], [ak])
                                        STT(A[:, 0:2], halo[:, 0:2], convw[:, cch, 0:1], A[:, 0:2], ALU.mult, ALU.add, ["halo", "convw", ak], [ak])
                                        STT(Bv[:, 0:1], halo[:, 2:3], convw[:, cch, 0:1], Bv[:, 0:1], ALU.mult, ALU.add, ["halo", "convw", bk], [bk])
                                    if half == 0:
                                        CP("act", halo[:, 0:3], ps[:, bp + 1, 509:512], [PB(bp + 1)], ["halo"])
                                    TT("dve" if "5" in os.environ.get("KF", "") else "pool", A[:, 2:1024], A[:, 2:1024], Bv[:, 0:1022], ALU.add, [ak, bk], [ak])
                                ACT(dstT[:, half * 1024:(half + 1) * 1024], A[:], AF.Silu, [ak], [dkey])
                            if ci < 5:
                                def trans(ci=ci, dstT=dstT, dkey=dkey):
                                    for tb in range(2):
                                        pv = psbf(4)
                                        for q in range(8):
                                            t = tb * 8 + q
                                            TR(pv[:, q * 128:(q + 1) * 128], dstT[:, t * 128:(t + 1) * 128], ident[:],
                                               [dkey, "ident"], [PB(4)])
                                        if ci < 4:
                                            CP("act", xs_tok[:, tb * 8:(tb + 1) * 8, ci * 128:(ci + 1) * 128],
                                               V(pv, [[128, 8], [1, 128]]), [PB(4)], ["xs_tok"])
                                        else:
                                            CP("act", B_tok[:, tb * 8:(tb + 1) * 8, :], V(pv, [[128, 8], [1, 128]]), [PB(4)], ["B_tok"])
                                pend2.append(trans)
                                if "1" in os.environ.get("KF", ""):
                                    emit_pend2()
                        if "a" in os.environ.get("KSTOP", ""):
                            break
                        wz = wzs[g % 2]
                        for t in range(NT):
                            zb = 6 + (st2["zb"] % 2)
                            st2["zb"] += 1
                            for kc in range(8):
                                MM(ps[:, zb, :], hT[:, kc, t * 128:(t + 1) * 128], wz[:, kc, :], kc == 0, kc == 7,
                                   [("hT", t), ("wz", g % 2)], [PB(zb)])
                            if t == 1:
                                emit_pend2()
                            ACT(zs_all[:, t, :], ps[:, zb, :], AF.Silu, [PB(zb)], [("zs", t)])
                        if "z" in os.environ.get("KSTOP", ""):
                            break
                        if g < 3:
                            load_wz(g + 1)
                        MEMSET("pool", state[:], 0.0, ["state"])
                        MEMSET("pool", state_bf[:], 0.0, ["state_bf"])

                        def stageA(c):
                            i2 = c % 2
                            cs = slice(c * 128, (c + 1) * 128)
                            adt_c = adt_all[:, c, g * 8:(g + 1) * 8]
                            dt_c = dt_all[:, c, g * 8:(g + 1) * 8]
                            MM(ps[:, 5, 0:8], U[:], adt_c, True, True, ["U", ("adt", c)], [PB(5)])
                            MM(ps[:, 5, 8:16], ones_f[:], adt_c, True, True, ["ones_f", ("adt", c)], [PB(5)])
                            CP("act", cumtot[i2][:], ps[:, 5, 0:16], [PB(5)], [("cumtot", i2)])
                            ACT(ecum[i2][:], cumtot[i2][:], AF.Exp, [("cumtot", i2)], [("ecum", i2)])
                            TT("dve", w8[i2][:], cumtot[i2][:, 8:16], cumtot[i2][:, 0:8], ALU.subtract, [("cumtot", i2)], [("w8", i2)])
                            ACT(w8[i2][:], w8[i2][:], AF.Exp, [("w8", i2)], [("w8", i2)])
                            TT("pool", adtU[i2][:], V(adt_c, [[1, 8], [0, 128]]), V(U[:], [[0, 8], [1, 128]]), ALU.mult,
                               [("adt", c), "U"], [("adtU", i2)])
                            for q in range(2):
                                MM(ps[:, q, :], Lst[:], adtU[i2][:, 4 * q:4 * q + 4, :], True, True, ["Lst", ("adtU", i2)], [PB(q)])
                            ACT(V(eseg[i2][:], [[512, 2], [1, 512]]), ps[:, 0:2, :], AF.Exp, [PB(0), PB(1)], [("eseg", i2)])
                            MM(ps[:, 4, 0:128], BTg[:, cs], CTg[:, cs], True, True, ["BTg", "CTg"], [PB(4)])
                            TT("dve", cbTm[i2][:], ps[:, 4, 0:128], U[:], ALU.mult, [PB(4), "U"], [("cbTm", i2)])
                            TT("dve", MT[i2][:], eseg[i2][:], V(cbTm[i2][:], [[0, 8], [1, 128]]), ALU.mult,
                               [("eseg", i2), ("cbTm", i2)], [("MT", i2)])
                            TT("pool", xdt[i2][:], V(xs_tok[:, c, :], [[64, 8], [1, 64]]), V(dt_c, [[1, 8], [0, 64]]), ALU.mult,
                               ["xs_tok", ("dt", c)], [("xdt", i2)])
                            TT("pool", xw[i2][:], xdt[i2][:], V(w8[i2][:], [[1, 8], [0, 64]]), ALU.mult,
                               [("xdt", i2), ("w8", i2)], [("xw", i2)])

                        def stageB(c):
                            i2 = c % 2
                            cs = slice(c * 128, (c + 1) * 128)
                            for hh in range(8):
                                MM(ps[:, 2, hh * 64:(hh + 1) * 64], MT[i2][:, hh, :], xdt[i2][:, hh, :], True, True,
                                   [("MT", i2), ("xdt", i2)], [PB(2)])
                            MM(ps[:, 3, :], CTg[:, cs], state_bf[:], True, True, ["CTg", "state_bf"], [PB(3)])
                            if c < NT - 1:
                                MM(ps[:, 7, :], B_tok[:, c, :], V(xw[i2][:], [[1, 512]]), True, True, ["B_tok", ("xw", i2)], [PB(7)])
                            emit_pend2()
                            TT("pool", V(t3[i2][:], [[64, 8], [1, 64]]), V(xs_tok[:, c, :], [[64, 8], [1, 64]]),
                               V(dsk_bc[:, g * 8:(g + 1) * 8], [[1, 8], [0, 64]]), ALU.mult, ["xs_tok", "dsk"], [("t3", i2)])
                            TT("dve", V(t1[i2][:], [[64, 8], [1, 64]]), V(ps[:, 3, :], [[64, 8], [1, 64]]),
                               V(ecum[i2][:, 0:8], [[1, 8], [0, 64]]), ALU.mult, [PB(3), ("ecum", i2)], [("t1", i2)])
                            TT("dve", t1[i2][:], ps[:, 2, :], t1[i2][:], ALU.add, [PB(2), ("t1", i2)], [("t1", i2)])
                            TT("dve", t1[i2][:], t1[i2][:], t3[i2][:], ALU.add, [("t1", i2), ("t3", i2)], [("t1", i2)])
                            TT("pool", t3[i2][:], t1[i2][:], zs_all[:, c, :], ALU.mult, [("t1", i2), ("zs", c)], [("t3", i2)])
                            ACT(junk3[:], t3[i2][:], AF.Square, [("t3", i2)], ["junk3", ("ssy", c)], accum=ssy[:, c:c + 1])
                            ACT(rsy[:, c:c + 1], ssy[:, c:c + 1], AF.Ln, [("ssy", c)], [("rsy", c)], bias=1e-5, scale=1.0 / 512)
                            ACT(rsy[:, c:c + 1], rsy[:, c:c + 1], AF.Exp, [("rsy", c)], [("rsy", c)], scale=-0.5)

                            def trans_y(c=c, i2=i2, g=g):
                                STT(ynb[i2][:], t3[i2][:], rsy[:, c:c + 1], ssmg_bc[:, g * 512:(g + 1) * 512], ALU.mult, ALU.mult,
                                    [("t3", i2), ("rsy", c), "ssmg"], [("ynb", i2)])
                                pv = psbf(6)
                                for q in range(4):
                                    TR(pv[:, q * 128:(q + 1) * 128], ynb[i2][:, q * 128:(q + 1) * 128], ident[:],
                                       [("ynb", i2), "ident"], [PB(6)])
                                CP("act", ysT[i2][:], V(pv[:, 0:512], [[128, 4], [1, 128]]), [PB(6)], [("ysT", i2)])
                                dst = bass.AP(tensor=yss_d.tensor,
                                              offset=yss_d.offset + ((b * NT + c) * 128 * 16 + g * 4) * 128,
                                              ap=[[16 * 128, 128], [128, 4], [1, 128]])
                                o_ = S.dma("sp", dst, ysT[i2][:], reads=[("ysT", i2)], writes=[("yss_d", b, c, g)])
                                if dbg:
                                    finals.append(o_)
                            pend2.append(trans_y)
                            if "3" in os.environ.get("KF", ""):
                                emit_pend2()
                            if c < NT - 1:
                                TT("dve", V(s1[:], [[64, 8], [1, 64]]), V(state[:], [[64, 8], [1, 64]]),
                                   V(ecum[i2][:, 8:16], [[1, 8], [0, 64]]), ALU.mult, ["state", ("ecum", i2)], ["s1"])
                                TT("dve", state[:], ps[:, 7, :], s1[:], ALU.add, [PB(7), "s1"], ["state"])
                                CP("act", state_bf[:], state[:], ["state"], ["state_bf"])

                        if "2" in os.environ.get("KF", ""):
                            for c in range(NT):
                                stageA(c)
                                stageB(c)
                        else:
                            stageA(0)
                            for c in range(NT):
                                if c + 1 < NT:
                                    stageA(c + 1)
                                stageB(c)
                    emit_pend2()
                    S.barrier()
                if stage <= 2:
                    continue

                with contextlib.ExitStack() as P:
                    def sbp(name, shape, dt):
                        return P.enter_context(nc.sbuf_tensor(f"{name}_{b}", list(shape), dt))
                    wpa = sbp("p3_wpa", [128, 8, 1024], BF16)
                    wps = sbp("p3_wps", [128, 16, 1024], BF16)
                    wg = sbp("p3_wg", [128, 8, 2048], BF16)
                    wo = sbp("p3_wo", [128, 8, 1024], BF16)
                    yat = [sbp(f"p3_yat{i}", [128, 8, 128], BF16) for i in range(2)]
                    ysst = [sbp(f"p3_ysst{i}", [128, 16, 128], BF16) for i in range(2)]
                    xt3 = [sbp(f"p3_xt{i}", [128, D], F32) for i in range(2)]
                    sa = [sbp(f"p3_sa{i}", [128, 512], F32) for i in range(2)]
                    sg_ = [sbp(f"p3_sg{i}", [128, 512], F32) for i in range(2)]
                    mixed = [sbp(f"p3_mixed{i}", [128, D], BF16) for i in range(2)]
                    mixT = [sbp(f"p3_mixT{i}", [128, 8, 128], BF16) for i in range(2)]
                    x1 = [sbp(f"p3_x1{i}", [128, D], F32) for i in range(2)]
                    hn3 = [sbp(f"p3_hn{i}", [128, D], BF16) for i in range(2)]
                    junk4 = sbp("p3_junk", [128, D], BF16)
                    ss3 = sbp("p3_ss", [128, NT], F32)
                    rs3 = sbp("p3_rs", [128, NT], F32)
                    def lw(dst, src, q, key):
                        S.dma("pool", dst[:, :, q * 512:(q + 1) * 512], src[:, :, q * 512:(q + 1) * 512], writes=[(key, q)])
                    w_g_v = w_in_v[:, :, C_G:C_G + 2048]
                    for j in range(2):
                        lw(wg, w_g_v, j, "wg")
                        lw(wg, w_g_v, 2 + j, "wg")
                        lw(wpa, w_pa_v, j, "wpa")
                        lw(wps, w_ps_v, j, "wps")
                    for j in range(2):
                        lw(wo, w_out_v, j, "wo")
                    hcs = {"hc": 0}

                    def M3(t):
                        i2 = t % 2
                        S.dma("sp", yat[i2][:], ya_d[b, t], reads=[("ya_d", b, t // 4, hh) for hh in range(8)], writes=[("yat", i2)])
                        S.dma("sp", ysst[i2][:], yss_d[b, t], reads=[("yss_d", b, t, g) for g in range(4)], writes=[("ysst", i2)])
                        S.dma("sp", xt3[i2][:], x[b, t * 128:(t + 1) * 128, :], writes=[("xt3", i2)])
                        for j in range(2):
                            h2 = hcs["hc"] % 2
                            hcs["hc"] += 1
                            cj = slice(j * 512, (j + 1) * 512)
                            for kc in range(8):
                                MM(ps[:, 2, :], hT[:, kc, t * 128:(t + 1) * 128], wg[:, kc, cj], kc == 0, kc == 7,
                                   [("hT", t), ("wg", j)], [PB(2)])
                            ACT(sa[h2][:], ps[:, 2, :], AF.Sigmoid, [PB(2)], [("sa", h2)])
                            for kc in range(8):
                                MM(ps[:, 3, :], hT[:, kc, t * 128:(t + 1) * 128], wg[:, kc, 1024 + j * 512:1024 + (j + 1) * 512],
                                   kc == 0, kc == 7, [("hT", t), ("wg", 2 + j)], [PB(3)])
                            ACT(sg_[h2][:], ps[:, 3, :], AF.Sigmoid, [PB(3)], [("sg", h2)])
                            for c in range(8):
                                MM(ps[:, 0, :], yat[i2][:, c, :], wpa[:, c, cj], c == 0, c == 7, [("yat", i2), ("wpa", j)], [PB(0)])
                            TT("dve", sa[h2][:], ps[:, 0, :], sa[h2][:], ALU.mult, [PB(0), ("sa", h2)], [("sa", h2)])
                            for c in range(16):
                                MM(ps[:, 1, :], ysst[i2][:, c, :], wps[:, c, cj], c == 0, c == 15, [("ysst", i2), ("wps", j)], [PB(1)])
                            TT("dve", sg_[h2][:], ps[:, 1, :], sg_[h2][:], ALU.mult, [PB(1), ("sg", h2)], [("sg", h2)])
                            TT("pool", mixed[i2][:, cj], sa[h2][:], sg_[h2][:], ALU.add, [("sa", h2), ("sg", h2)], [("mixed", i2)])

                    def T31(t):
                        i2 = t % 2
                        pv = psbf(4)
                        for c in range(8):
                            TR(pv[:, c * 128:(c + 1) * 128], mixed[i2][:, c * 128:(c + 1) * 128], ident[:], [("mixed", i2), "ident"], [PB(4)])
                        CP("act", mixT[i2][:], V(pv, [[128, 8], [1, 128]]), [PB(4)], [("mixT", i2)])
                        for j in range(2):
                            for c in range(8):
                                MM(ps[:, 5 + j, :], mixT[i2][:, c, :], wo[:, c, j * 512:(j + 1) * 512], c == 0, c == 7,
                                   [("mixT", i2), ("wo", j)], [PB(5 + j)])
                        TT("dve", x1[i2][:], V(ps[:, 5, :], [[1, 1024]]), xt3[i2][:], ALU.add, [PB(5), PB(6), ("xt3", i2)], [("x1", i2)])
                        o_ = S.dma("sp", out[b, t * 128:(t + 1) * 128, :], x1[i2][:], reads=[("x1", i2)], writes=[("x1d", b, t)])
                        if stage <= 3:
                            finals.append(o_)
                        ACT(junk4[:], x1[i2][:], AF.Square, [("x1", i2)], ["junk4", ("ss3", t)], accum=ss3[:, t:t + 1])
                        ACT(rs3[:, t:t + 1], ss3[:, t:t + 1], AF.Sqrt, [("ss3", t)], [("rs3", t)], bias=1e-6, scale=1.0 / D)
                        RECIP(rs3[:, t:t + 1], rs3[:, t:t + 1], [("rs3", t)], [("rs3", t)])
                        STT(hn3[i2][:], x1[i2][:], rs3[:, t:t + 1], gffn_bc[:], ALU.mult, ALU.mult,
                            [("x1", i2), ("rs3", t), "gffn"], [("hn3", i2)])

                    def T32(t):
                        i2 = t % 2
                        pv = psbf(7)
                        for c in range(8):
                            TR(pv[:, c * 128:(c + 1) * 128], hn3[i2][:, c * 128:(c + 1) * 128], ident[:], [("hn3", i2), "ident"], [PB(7)])
                        CP("act", hT[:, :, t * 128:(t + 1) * 128], V(pv, [[128, 8], [1, 128]]), [PB(7)], [("hT", t)])

                    for t in range(NT + 2):
                        if t < NT:
                            M3(t)
                        if 1 <= t <= NT:
                            T31(t - 1)
                        if t >= 2:
                            T32(t - 2)
                    S.barrier()
                if stage <= 3:
                    continue

                with contextlib.ExitStack() as P:
                    def sbp(name, shape, dt):
                        return P.enter_context(nc.sbuf_tensor(f"{name}_{b}", list(shape), dt))
                    wdn = sbp("p4_wdn", [128, NFC, 1024], BF16)
                    aT = sbp("p4_aT", [128, NFC, 1024], BF16)
                    wup = [sbp(f"p4_wup{i}", [128, 8, 256], BF16) for i in range(3)]
                    accg = [sbp(f"p4_accg{i}", [128, 1024], F32) for i in range(2)]
                    accb = [sbp(f"p4_accb{i}", [128, 1024], F32) for i in range(2)]
                    accv = [sbp(f"p4_accv{i}", [128, 1024], F32) for i in range(2)]
                    fhalo = sbp("p4_halo", [128, 2 * NFC, 2], F32)
                    x1t = [sbp(f"p4_x1t{i}", [128, D], F32) for i in range(2)]
                    ot = [sbp(f"p4_ot{i}", [128, D], F32) for i in range(2)]

                    def load_wup(n):
                        fc_ = n % NFC
                        wi_ = n % 3
                        S.dma("pool", wup[wi_][:, :, 0:128], w_up_v[:, :, fc_ * 128:(fc_ + 1) * 128], writes=[("wup", wi_, 0)])
                        S.dma("pool", wup[wi_][:, :, 128:256], w_up_v[:, :, D_FF + fc_ * 128:D_FF + (fc_ + 1) * 128],
                              writes=[("wup", wi_, 1)])

                    load_wup(0)
                    load_wup(1)
                    for q in range(2):
                        S.dma("pool", wdn[:, 0:11, q * 512:(q + 1) * 512], w_dn_v[:, 0:11, q * 512:(q + 1) * 512], writes=[("wdn", q, 0)])
                        S.dma("pool", wdn[:, 11:22, q * 512:(q + 1) * 512], w_dn_v[:, 11:22, q * 512:(q + 1) * 512], writes=[("wdn", q, 1)])
                    WDN = [("wdn", q, r_) for q in range(2) for r_ in range(2)]
                    wctr = 0
                    for blk in range(2):
                        for fc in range(NFC):
                            wi = wctr % 3
                            i2 = wctr % 2
                            if wctr + 2 < 2 * NFC:
                                load_wup(wctr + 2)
                            wctr += 1
                            for part in range(2):
                                base = 4 * i2 + 2 * part
                                cch = part * NFC + fc
                                A = (accg if part == 0 else accv)[i2]
                                ak = ("accg" if part == 0 else "accv", i2)
                                for tt in range(2):
                                    tok0 = blk * 1024 + tt * 512
                                    for kc in range(8):
                                        MM(ps[:, base + tt, :], wup[wi][:, kc, part * 128:(part + 1) * 128], hT[:, kc, tok0:tok0 + 512],
                                           kc == 0, kc == 7, [("wup", wi, part)] + [("hT", tok0 // 128 + q) for q in range(4)], [PB(base + tt)])
                                rb = [PB(base), PB(base + 1)]
                                pin = V(ps[:, base, :], [[1, 1024]])
                                if not CONV_SPLIT:
                                    ACT(A[:], pin, AF.Identity, rb + ["fconvw", "fconvb"], [ak],
                                        bias=fconvb[:, cch:cch + 1], scale=fconvw[:, cch, 2:3])
                                    for k in (1, 0):
                                        d_ = 2 - k
                                        STT(A[:, d_:1024], V(ps[:, base, :], [[1, 1024 - d_]]), fconvw[:, cch, k:k + 1], A[:, d_:1024],
                                            ALU.mult, ALU.add, rb + ["fconvw", ak], [ak])
                                        if blk == 1:
                                            STT(A[:, 0:d_], fhalo[:, cch, 2 - d_:2], fconvw[:, cch, k:k + 1], A[:, 0:d_],
                                                ALU.mult, ALU.add, [("fhalo", cch), "fconvw", ak], [ak])
                                    if blk == 0:
                                        CP("act", fhalo[:, cch, :], ps[:, base + 1, 510:512], [PB(base + 1)], [("fhalo", cch)])
                                else:
                                    ACT(A[:], pin, AF.Identity, rb + ["fconvw", "fconvb"], [ak],
                                        bias=fconvb[:, cch:cch + 1], scale=fconvw[:, cch, 2:3])
                                    if part == 0:
                                        Bv = accb[i2]
                                        bk = ("accb", i2)
                                        ACT(Bv[:], pin, AF.Identity, rb + ["fconvw"], [bk], scale=fconvw[:, cch, 0:1])
                                    STT(A[:, 1:1024], V(ps[:, base, :], [[1, 1023]]), fconvw[:, cch, 1:2], A[:, 1:1024],
                                        ALU.mult, ALU.add, rb + ["fconvw", ak], [ak])
                                    if part == 1:
                                        STT(A[:, 2:1024], V(ps[:, base, :], [[1, 1022]]), fconvw[:, cch, 0:1], A[:, 2:1024],
                                            ALU.mult, ALU.add, rb + ["fconvw", ak], [ak])
                                    if blk == 1:
                                        STT(A[:, 0:1], fhalo[:, cch, 1:2], fconvw[:, cch, 1:2], A[:, 0:1],
                                            ALU.mult, ALU.add, [("fhalo", cch), "fconvw", ak], [ak])
                                        STT(A[:, 0:2], fhalo[:, cch, 0:2], fconvw[:, cch, 0:1], A[:, 0:2],
                                            ALU.mult, ALU.add, [("fhalo", cch), "fconvw", ak], [ak])
                                    if blk == 0:
                                        CP("act", fhalo[:, cch, :], ps[:, base + 1, 510:512], [PB(base + 1)], [("fhalo", cch)])
                                    if part == 0:
                                        TT("pool", A[:, 2:1024], A[:, 2:1024], Bv[:, 0:1022], ALU.add, [ak, bk], [ak])
                            ACT(accg[i2][:], accg[i2][:], AF.Silu, [("accg", i2)], [("accg", i2)])
                            TT("pool", aT[:, fc, :], accg[i2][:], accv[i2][:], ALU.mult, [("accg", i2), ("accv", i2)], [("aT", fc)])
                        for tt in range(8):
                            t = blk * 8 + tt
                            i2 = t % 2
                            S.dma("sp", x1t[i2][:], out[b, t * 128:(t + 1) * 128, :], reads=[("x1d", b, t)], writes=[("x1t", i2)])
                            for j in range(2):
                                pb = 2 * i2 + j
                                for fc in range(NFC):
                                    MM(ps[:, pb, :], aT[:, fc, tt * 128:(tt + 1) * 128], wdn[:, fc, j * 512:(j + 1) * 512],
                                       fc == 0, fc == NFC - 1, [("aT", fc)] + WDN, [PB(pb)])
                            TT("dve", ot[i2][:], V(ps[:, 2 * i2, :], [[1, 1024]]), x1t[i2][:], ALU.add,
                               [PB(2 * i2), PB(2 * i2 + 1), ("x1t", i2)], [("ot", i2)])
                            finals.append(S.dma("sp", out[b, t * 128:(t + 1) * 128, :], ot[i2][:], reads=[("ot", i2)],
                                                writes=[("x1d", b, t)]))
                    S.barrier()
        S.emit(final_wait_ops=finals)
    return nc


_NC_CACHE = {}
PARAM_NAMES = ["rel_bias", "norm_mix_g", "w_in", "q_norm_g", "k_norm_g", "lambda_q1", "lambda_k1", "lambda_q2",
               "lambda_k2", "attn_subln_g", "conv_ssm_w", "conv_ssm_b", "dt_bias", "a_log", "d_skip", "ssm_norm_g",
               "w_proj_attn", "w_proj_ssm", "w_out", "norm_ffn_g", "w_up", "conv_ffn_w", "conv_ffn_b", "w_down"]


def make_in_maps(inputs, n_cores=8, nseq=2):
    x = np.ascontiguousarray(np.asarray(inputs["x"], dtype=np.float32))
    shared = {}
    for k in PARAM_NAMES:
        a = np.asarray(inputs[k], dtype=np.float32)
        if k == "rel_bias":
            shared[k] = np.ascontiguousarray(a)
        elif a.ndim == 2:
            shared[k] = np.ascontiguousarray(a[0:1])
        else:
            shared[k] = np.ascontiguousarray(a[0])
    in_maps = []
    for i in range(n_cores):
        m = dict(shared)
        m["x"] = np.ascontiguousarray(x[i * nseq:(i + 1) * nseq])
        in_maps.append(m)
    return in_maps


def kernel(**inputs):
    n_cores, nseq = 8, 2
    if "nc" not in _NC_CACHE:
        _NC_CACHE["nc"] = build(nseq=nseq)
    nc = _NC_CACHE["nc"]
    in_maps = make_in_maps(inputs, n_cores, nseq)
    res = run_bass_kernel_spmd(nc, in_maps, core_ids=list(range(n_cores)))
    return np.concatenate([np.asarray(r["out"], dtype=np.float32) for r in res.results], axis=0)
```

```python
import contextlib
import os
import numpy as np
import ml_dtypes
import concourse.bass as bass
import concourse.mybir as mybir
from concourse.bass_utils import run_bass_kernel_spmd

F32 = mybir.dt.float32
BF16 = mybir.dt.bfloat16
AF = mybir.ActivationFunctionType
ALU = mybir.AluOpType
AX = mybir.AxisListType

ENGS = ("pe", "act", "dve", "pool", "sp")

S_LEN = 2048
D = 1024
NT = 16
IN_COLS = 10272
C_Q, C_K, C_V, C_Z, C_XS, C_B, C_C, C_DT, C_G = 0, 1024, 2048, 3072, 5120, 7168, 7680, 8192, 8224
D_FF = 2816
NFC = 22


class Op:
    __slots__ = ("eng", "fn", "deps", "is_dma", "sig", "need_sig")

    def __init__(self, eng, fn, is_dma):
        self.eng = eng
        self.fn = fn
        self.is_dma = is_dma
        self.deps = []
        self.sig = None
        self.need_sig = False


class Sched:
    SEM_LIMIT = 30000

    def __init__(self, nc, n_dma_sems=12):
        self.nc = nc
        self.streams = {e: [] for e in ENGS}
        self.last_w = {}
        self.readers = {}
        self.n_dma_sems = n_dma_sems
        self.live_dmas = []

    def op(self, eng, fn, reads=(), writes=(), dma=False):
        o = Op(eng, fn, dma)
        deps = []
        for k in reads:
            w = self.last_w.get(k)
            if w is not None:
                deps.append(w)
            if isinstance(k, tuple) and k[0] == "ps":
                for r in self.readers.get(k, ()):
                    if r.eng != eng:
                        deps.append(r)
        for k in writes:
            w = self.last_w.get(k)
            if w is not None:
                deps.append(w)
            deps.extend(self.readers.get(k, ()))
        seen = set()
        for d in deps:
            if d is o or id(d) in seen:
                continue
            seen.add(id(d))
            if (not d.is_dma) and d.eng == eng and eng == "pe":
                continue
            o.deps.append(d)
            d.need_sig = True
        for k in reads:
            self.readers.setdefault(k, []).append(o)
        for k in writes:
            self.last_w[k] = o
            self.readers[k] = []
        self.streams[eng].append(o)
        if dma:
            self.live_dmas.append(o)
        return o

    def dma(self, q, out, in_, reads=(), writes=(), **kw):
        return self.op(q, lambda e: e.dma_start(out=out, in_=in_, **kw), reads, writes, dma=True)

    def barrier(self):
        lasts = []
        for e in ENGS:
            for o in reversed(self.streams[e]):
                if o.fn is not None and not o.is_dma:
                    lasts.append(o)
                    break
        dmas = list(self.live_dmas)
        for e in ENGS:
            b = Op(e, None, False)
            for d in lasts:
                if d.eng != e:
                    b.deps.append(d)
                    d.need_sig = True
            b.deps.extend(dmas)
            self.streams[e].append(b)
        self.live_dmas = []
        self.last_w = {}
        self.readers = {}

    def emit(self, final_wait_ops=()):
        nc = self.nc
        with contextlib.ExitStack() as es:
            for e in ENGS:
                sigs = [o for o in self.streams[e] if o.need_sig and not o.is_dma]
                n_sems = max(1, (len(sigs) + self.SEM_LIMIT - 1) // self.SEM_LIMIT)
                sems = [es.enter_context(nc.semaphore(f"s_{e}_{i}")) for i in range(n_sems)]
                for cnt, o in enumerate(sigs):
                    o.sig = (sems[cnt // self.SEM_LIMIT], cnt % self.SEM_LIMIT + 1)
            dma_prev = {}
            for e in ENGS:
                dmas = [o for o in self.streams[e] if o.is_dma]
                if not dmas:
                    continue
                k = min(self.n_dma_sems, len(dmas))
                sems = [es.enter_context(nc.semaphore(f"d_{e}_{i}")) for i in range(k)]
                uses = [0] * k
                for cnt, o in enumerate(dmas):
                    s = cnt % k
                    uses[s] += 1
                    o.sig = (sems[s], 16 * uses[s])
                    dma_prev[id(o)] = (sems[s], 16 * (uses[s] - 1)) if uses[s] > 1 else None
            blk = es.enter_context(nc.Block())
            handles = {"pe": blk.tensor, "act": blk.scalar, "dve": blk.vector,
                       "pool": blk.gpsimd, "sp": blk.sync}
            for e in ENGS:
                ops = self.streams[e]

                def body(h, ops=ops, e=e):
                    waited = {}

                    def wait(sem, val):
                        if waited.get(id(sem), 0) >= val:
                            return
                        waited[id(sem)] = val
                        h.wait_ge(sem, val)

                    for o in ops:
                        for d in o.deps:
                            wait(*d.sig)
                        if o.fn is None:
                            continue
                        if o.is_dma:
                            p = dma_prev.get(id(o))
                            if p is not None:
                                wait(*p)
                        inst = o.fn(h)
                        if o.is_dma:
                            inst.then_inc(o.sig[0], 16)
                        elif o.need_sig:
                            inst.then_inc(o.sig[0], 1)
                    if e == "sp":
                        for o in final_wait_ops:
                            wait(*o.sig)

                handles[e](body)


def V(ap, dims, off=0):
    return bass.AP(tensor=ap.tensor, offset=ap.offset + off,
                   ap=[list(ap.ap[0])] + [list(d) for d in dims])


def t5_bucket_table(nmax=256):
    n = np.arange(nmax)
    nf = np.maximum(n, 1).astype(np.float32)
    large = 16 + (np.log(nf / np.float32(16)) / np.float32(np.log(128 / 16)) * np.float32(16)).astype(np.int32)
    large = np.minimum(large, 31)
    return np.where(n < 16, n, large)


YDEF = 6
CONV_SPLIT = True


def build(nseq=2, stage=99, dbg=False):
    nc = bass.Bass("TRN2", target_bir_lowering=False)

    def din(name, shape):
        return nc.dram_tensor(name, list(shape), F32, kind="ExternalInput").ap()

    x = din("x", [nseq, S_LEN, D])
    rel_bias = din("rel_bias", [32, 8])
    norm_mix_g = din("norm_mix_g", [1, D])
    w_in = din("w_in", [D, IN_COLS])
    q_norm_g = din("q_norm_g", [1, 64])
    k_norm_g = din("k_norm_g", [1, 64])
    lam_q1 = din("lambda_q1", [1, 64])
    lam_k1 = din("lambda_k1", [1, 64])
    lam_q2 = din("lambda_q2", [1, 64])
    lam_k2 = din("lambda_k2", [1, 64])
    subln_g = din("attn_subln_g", [1, 128])
    conv_ssm_w = din("conv_ssm_w", [4, 3072])
    conv_ssm_b = din("conv_ssm_b", [1, 3072])
    dt_bias = din("dt_bias", [1, 32])
    a_log = din("a_log", [1, 32])
    d_skip = din("d_skip", [1, 32])
    ssm_norm_g = din("ssm_norm_g", [1, 2048])
    w_pa = din("w_proj_attn", [1024, 1024])
    w_ps = din("w_proj_ssm", [2048, 1024])
    w_out = din("w_out", [1024, 1024])
    norm_ffn_g = din("norm_ffn_g", [1, D])
    w_up = din("w_up", [D, 2 * D_FF])
    conv_ffn_w = din("conv_ffn_w", [3, 2 * D_FF])
    conv_ffn_b = din("conv_ffn_b", [1, 2 * D_FF])
    w_down = din("w_down", [D_FF, D])
    out = nc.dram_tensor("out", [nseq, S_LEN, D], F32, kind="ExternalOutput").ap()

    skind = "ExternalOutput" if dbg else "Internal"
    ext_d = nc.dram_tensor("ext_d", [8, 384], F32, kind="Internal").ap()
    Zd = nc.dram_tensor("Zd", [8, 128, 384], F32, kind="Internal").ap()
    ya_d = nc.dram_tensor("ya_d", [nseq, NT, 128, 8, 128], BF16, kind=skind).ap()
    yss_d = nc.dram_tensor("yss_d", [nseq, NT, 128, 16, 128], BF16, kind=skind).ap()
    if dbg:
        hT_d = nc.dram_tensor("hT_d", [128, 8, S_LEN], BF16, kind="ExternalOutput").ap()
        dt_d = nc.dram_tensor("dt_d", [128, NT, 32], F32, kind="ExternalOutput").ap()

    w_in_v = w_in.rearrange("(kc p) c -> p kc c", p=128)
    w_up_v = w_up.rearrange("(kc p) c -> p kc c", p=128)
    w_pa_v = w_pa.rearrange("(kc p) c -> p kc c", p=128)
    w_ps_v = w_ps.rearrange("(kc p) c -> p kc c", p=128)
    w_out_v = w_out.rearrange("(kc p) c -> p kc c", p=128)
    w_dn_v = w_down.rearrange("(kc p) c -> p kc c", p=128)

    S = Sched(nc)
    finals = []

    def MM(o, lhsT, rhs, start, stop, r, w):
        S.op("pe", lambda e: e.matmul(out=o, lhsT=lhsT, rhs=rhs, start=start, stop=stop), r, w)

    def TR(o, in_, ident, r, w):
        S.op("pe", lambda e: e.transpose(out=o, in_=in_, identity=ident), r, w)

    def ACT(o, in_, func, r, w, bias=None, scale=None, accum=None):
        kw = {}
        if bias is not None:
            kw["bias"] = bias
        if scale is not None:
            kw["scale"] = scale
        if accum is not None:
            kw["accum_out"] = accum
        S.op("act", lambda e: e.activation(out=o, in_=in_, func=func, **kw), r, w)

    def TT(eng, o, in0, in1, op, r, w):
        S.op(eng, lambda e: e.tensor_tensor(out=o, in0=in0, in1=in1, op=op), r, w)

    def TS(eng, o, in0, s1, op0, r, w, s2=None, op1=None):
        if op1 is None:
            S.op(eng, lambda e: e.tensor_scalar(out=o, in0=in0, scalar1=s1, scalar2=None, op0=op0), r, w)
        else:
            S.op(eng, lambda e: e.tensor_scalar(out=o, in0=in0, scalar1=s1, scalar2=s2, op0=op0, op1=op1), r, w)

    def STT(o, in0, scalar, in1, op0, op1, r, w):
        S.op("dve", lambda e: e.scalar_tensor_tensor(out=o, in0=in0, scalar=scalar, in1=in1, op0=op0, op1=op1), r, w)

    def CP(eng, o, in_, r, w):
        if eng == "act":
            S.op("act", lambda e: e.copy(out=o, in_=in_), r, w)
        else:
            S.op(eng, lambda e: e.tensor_copy(out=o, in_=in_), r, w)

    def RECIP(o, in_, r, w):
        S.op("dve", lambda e: e.reciprocal(out=o, in_=in_), r, w)

    def MEMSET(eng, ap, val, w):
        S.op(eng, lambda e: e.memset(ap, val), (), w)

    def bc(ap, n=128):
        return ap.broadcast_to([n, ap.shape[-1]])

    with contextlib.ExitStack() as G:
        def sbg(name, shape, dt):
            return G.enter_context(nc.sbuf_tensor(name, list(shape), dt))

        ps = G.enter_context(nc.psum_tensor("ps", [128, 8, 512], F32))

        def PB(b):
            return ("ps", b)

        def psbf(b):
            return ps[:, b, :].bitcast(BF16)

        ident = sbg("ident", [128, 128], BF16)
        identf = sbg("identf", [128, 128], F32)
        U = sbg("U", [128, 128], F32)
        Lst = sbg("Lst", [128, 128], F32)
        ones_f = sbg("ones_f", [128, 128], F32)
        gmix_bc = sbg("gmix_bc", [128, D], F32)
        gffn_bc = sbg("gffn_bc", [128, D], F32)
        gqk_bc = sbg("gqk_bc", [128, 256], F32)
        subln_bc = sbg("subln_bc", [128, 128], F32)
        neglam = sbg("neglam", [128, 1], F32)
        lamt = sbg("lamt", [128, 4, 64], F32)
        lamp = sbg("lamp", [128, 2, 64], F32)
        lame = sbg("lame", [128, 2], F32)
        b31 = sbg("b31", [128, 8], F32)
        convw = sbg("convw", [128, 24, 4], F32)
        convb = sbg("convb", [128, 24], F32)
        fconvw = sbg("fconvw", [128, 44, 3], F32)
        fconvb = sbg("fconvb", [128, 44], F32)
        dtb_bc = sbg("dtb_bc", [128, 32], F32)
        a_bc = sbg("a_bc", [128, 32], F32)
        dsk_bc = sbg("dsk_bc", [128, 32], F32)
        RB = sbg("RB", [8, 32], F32)
        ext_sb = sbg("ext_sb", [8, 384], F32)

        MEMSET("pool", ones_f[:], 1.0, ["ones_f"])
        MEMSET("pool", identf[:], 0.0, ["identf"])
        S.op("pool", lambda e: e.affine_select(out=identf[:], in_=identf[:], pattern=[[-1, 128]],
                                               compare_op=ALU.not_equal, fill=1.0, base=0, channel_multiplier=1),
             ["identf"], ["identf"])
        CP("dve", ident[:], identf[:], ["identf"], ["ident"])
        S.op("pool", lambda e: e.affine_select(out=U[:], in_=ones_f[:], pattern=[[1, 128]],
                                               compare_op=ALU.is_ge, fill=0.0, base=0, channel_multiplier=-1),
             ["ones_f"], ["U"])
        S.op("pool", lambda e: e.affine_select(out=Lst[:], in_=ones_f[:], pattern=[[-1, 128]],
                                               compare_op=ALU.is_ge, fill=0.0, base=-1, channel_multiplier=1),
             ["ones_f"], ["Lst"])
        S.dma("sp", gmix_bc[:], bc(norm_mix_g), writes=["gmix"])
        S.dma("sp", gffn_bc[:], bc(norm_ffn_g), writes=["gffn"])
        S.dma("sp", gqk_bc[:, 0:64], bc(q_norm_g), writes=["gqk"])
        S.dma("sp", gqk_bc[:, 64:128], bc(q_norm_g), writes=["gqk"])
        S.dma("sp", gqk_bc[:, 128:192], bc(k_norm_g), writes=["gqk"])
        S.dma("sp", gqk_bc[:, 192:256], bc(k_norm_g), writes=["gqk"])
        ACT(gqk_bc[:, 0:128], gqk_bc[:, 0:128], AF.Identity, ["gqk"], ["gqk"], scale=0.125)
        S.dma("sp", subln_bc[:], bc(subln_g), writes=["subln"])
        ACT(subln_bc[:], subln_bc[:], AF.Identity, ["subln"], ["subln"], scale=0.8)
        for i, a in enumerate((lam_q1, lam_q2, lam_k1, lam_k2)):
            S.dma("sp", lamt[:, i, :], bc(a), writes=["lamt"])
        TT("dve", lamp[:], lamt[:, 0:2, :], lamt[:, 2:4, :], ALU.mult, ["lamt"], ["lamp"])
        S.op("dve", lambda e: e.tensor_reduce(out=lame[:], in_=lamp[:], axis=AX.X, op=ALU.add), ["lamp"], ["lame"])
        ACT(lame[:], lame[:], AF.Exp, ["lame"], ["lame"])
        TT("dve", neglam[:], lame[:, 1:2], lame[:, 0:1], ALU.subtract, ["lame"], ["neglam"])
        TS("dve", neglam[:], neglam[:], -0.2, ALU.add, ["neglam"], ["neglam"])
        S.dma("sp", b31[:], bc(rel_bias[31:32, :]), writes=["b31"])
        S.dma("sp", dtb_bc[:], bc(dt_bias), writes=["dtb"])
        S.dma("sp", dsk_bc[:], bc(d_skip), writes=["dsk"])
        S.dma("sp", a_bc[:], bc(a_log), writes=["a_bc"])
        ACT(a_bc[:], a_bc[:], AF.Exp, ["a_bc"], ["a_bc"])
        ACT(a_bc[:], a_bc[:], AF.Identity, ["a_bc"], ["a_bc"], scale=-1.0)
        stg = sbg("stg", [64, 9, 128], F32)
        rbs = sbg("rbs", [32, 8], F32)
        MEMSET("pool", stg[:], 0.0, ["stg"])
        S.dma("sp", stg[0:24, 0:4, :], conv_ssm_w.rearrange("k (cc p) -> cc k p", p=128), writes=["stg"])
        S.dma("sp", stg[0:24, 4, :], conv_ssm_b[0, :].rearrange("(cc p) -> cc p", p=128), writes=["stg"])
        S.dma("sp", stg[0:44, 5:8, :], conv_ffn_w.rearrange("k (cc p) -> cc k p", p=128), writes=["stg"])
        S.dma("sp", stg[0:44, 8, :], conv_ffn_b[0, :].rearrange("(cc p) -> cc p", p=128), writes=["stg"])
        S.dma("sp", rbs[:], rel_bias, writes=["rbs"])
        def pcol(k):
            return k * 32 if k < 5 else 160 + (k - 5) * 64
        for k in range(9):
            n = 32 if k < 5 else 64
            TR(ps[:, 0, pcol(k):pcol(k) + n], stg[0:n, k, :], identf[0:n, 0:n], ["stg", "identf"], [PB(0)])
        for k in range(4):
            CP("dve", convw[:, :, k], ps[:, 0, pcol(k):pcol(k) + 24], [PB(0)], ["convw"])
        CP("dve", convb[:], ps[:, 0, pcol(4):pcol(4) + 24], [PB(0)], ["convb"])
        for k in range(3):
            CP("dve", fconvw[:, :, k], ps[:, 0, pcol(5 + k):pcol(5 + k) + 44], [PB(0)], ["fconvw"])
        CP("dve", fconvb[:], ps[:, 0, pcol(8):pcol(8) + 44], [PB(0)], ["fconvb"])
        TR(ps[0:8, 1, 0:32], rbs[:], identf[0:32, 0:32], ["rbs", "identf"], [PB(1)])
        CP("dve", RB[:], ps[0:8, 1, 0:32], [PB(1)], ["RB"])
        MEMSET("pool", ext_sb[:], -30000.0, ["ext_sb"])
        CP("dve", ext_sb[:, 127:143], RB[:, 0:16], ["RB", "ext_sb"], ["ext_sb"])
        bt = t5_bucket_table(256)
        assert (bt[113:] == 31).all()
        for bk in range(16, 32):
            idx = np.nonzero(bt == bk)[0]
            if len(idx) == 0:
                continue
            n0, n1 = int(idx[0]), int(idx[-1]) + 1
            assert n1 - n0 == len(idx)
            CP("dve", ext_sb[:, 127 + n0:127 + n1], V(RB[:, bk:bk + 1], [[0, n1 - n0]]), ["RB", "ext_sb"], ["ext_sb"])
        S.dma("sp", ext_d, ext_sb[:, :], reads=["ext_sb"], writes=["ext_d"])
        S.dma("sp", Zd, bass.AP(tensor=ext_d.tensor, offset=ext_d.offset, ap=[[384, 8], [0, 128], [1, 384]]),
              reads=["ext_d"], writes=["Zd"])
        S.barrier()

        for b in range(nseq):
            with contextlib.ExitStack() as Q:
                def sbq(name, shape, dt):
                    return Q.enter_context(nc.sbuf_tensor(f"{name}_{b}", list(shape), dt))

                hT = sbq("hT", [128, 8, S_LEN], BF16)
                HT_ALL = [("hT", t) for t in range(NT)]

                with contextlib.ExitStack() as P:
                    def sbp(name, shape, dt):
                        return P.enter_context(nc.sbuf_tensor(f"{name}_{b}", list(shape), dt))
                    xts = [sbp(f"p0_xt{i}", [128, D], F32) for i in range(2)]
                    hns = [sbp(f"p0_hn{i}", [128, D], BF16) for i in range(2)]
                    junk = sbp("p0_junk", [128, D], BF16)
                    ssq = sbp("p0_ssq", [128, NT], F32)
                    rs = sbp("p0_rs", [128, NT], F32)
                    for t in range(NT):
                        i2 = t % 2
                        xt, hn = xts[i2], hns[i2]
                        S.dma("sp", xt[:], x[b, t * 128:(t + 1) * 128, :], writes=[("xt", i2)])
                        ACT(junk[:], xt[:], AF.Square, [("xt", i2)], ["junk", ("ssq", t)], accum=ssq[:, t:t + 1])
                        ACT(rs[:, t:t + 1], ssq[:, t:t + 1], AF.Sqrt, [("ssq", t)], [("rs", t)], bias=1e-6, scale=1.0 / D)
                        RECIP(rs[:, t:t + 1], rs[:, t:t + 1], [("rs", t)], [("rs", t)])
                        STT(hn[:], xt[:], rs[:, t:t + 1], gmix_bc[:], ALU.mult, ALU.mult,
                            [("xt", i2), ("rs", t), "gmix"], [("hn", i2)])
                        pb = 2 * i2
                        pv = psbf(pb)
                        for c in range(8):
                            TR(pv[:, c * 128:(c + 1) * 128], hn[:, c * 128:(c + 1) * 128], ident[:],
                               [("hn", i2), "ident"], [PB(pb)])
                        CP("act" if i2 else "dve", hT[:, :, t * 128:(t + 1) * 128],
                           V(pv, [[128, 8], [1, 128]]), [PB(pb)], [("hT", t)])
                    if dbg and b == 0:
                        finals.append(S.dma("sp", hT_d, hT[:], reads=HT_ALL))
                    S.barrier()
                if stage <= 0:
                    continue

                with contextlib.ExitStack() as P:
                    def sbp(name, shape, dt):
                        return P.enter_context(nc.sbuf_tensor(f"{name}_{b}", list(shape), dt))
                    NP = 28
                    BT = sbp("p1_BT", [128, 8, 256], F32)
                    wqkv = [sbp(f"p1_w{i}", [128, 8, 384], BF16) for i in range(2)]
                    qkT = [sbp(f"p1_qkT{i}", [128, 2, S_LEN], BF16) for i in range(2)]
                    vaug = [sbp(f"p1_v{i}", [128, NT, 132], BF16) for i in range(2)]
                    PT = [sbp(f"p1_PT{i}", [128, 2, 512], BF16) for i in range(NP)]
                    sq = [sbp(f"p1_sq{i}", [128, 256], F32) for i in range(2)]
                    tmpn = [sbp(f"p1_tmpn{i}", [128, 256], F32) for i in range(2)]
                    qkn = [sbp(f"p1_qkn{i}", [128, 256], BF16) for i in range(6)]
                    ssq4 = sbp("p1_ssq4", [128, NT, 4], F32)
                    rs4 = sbp("p1_rs4", [128, NT, 4], F32)
                    etmp = [sbp(f"p1_et{i}", [128, 2, 256], F32) for i in range(2)]
                    rl = sbp("p1_rl", [128, NT, 2], F32)
                    nrl = sbp("p1_nrl", [128, NT], F32)
                    o1 = [sbp(f"p1_o1{i}", [128, 128], F32) for i in range(2)]
                    oo = [sbp(f"p1_oo{i}", [128, 128], F32) for i in range(2)]
                    junk2 = sbp("p1_junk2", [128, 128], BF16)
                    sso = sbp("p1_sso", [128, NT], F32)
                    rso = sbp("p1_rso", [128, NT], F32)
                    yn = [sbp(f"p1_yn{i}", [128, 128], BF16) for i in range(6)]
                    yst = [sbp(f"p1_yst{i}", [128, 4, 128], BF16) for i in range(2)]

                    for h in range(8):
                        S.dma("sp", BT[:, h, :],
                              bass.AP(tensor=Zd.tensor, offset=Zd.offset + h * 128 * 384 + 127, ap=[[383, 128], [1, 256]]),
                              writes=[("BT", h)])
                    for i in range(2):
                        MEMSET("pool", vaug[i][:, :, 128:129], 1.0, [("vaug", i)])

                    pends = {"q": [], "y": []}

                    def tick():
                        for pend in pends.values():
                            for it in pend:
                                it[0] -= 1
                            while pend and pend[0][0] <= 0:
                                pend.pop(0)[1]()

                    def defer(n, fn, q="q"):
                        pends[q].append([n, fn])

                    def flush():
                        for pend in pends.values():
                            while pend:
                                pend.pop(0)[1]()

                    st_ = {"pt": 0, "sb": 0, "yst": 0, "qkn": 0, "yn": 0}

                    def load_w(h):
                        sl = h % 2
                        for j3, c0 in enumerate((C_Q, C_K, C_V)):
                            S.dma("pool", wqkv[sl][:, :, j3 * 128:(j3 + 1) * 128],
                                  w_in_v[:, :, c0 + h * 128:c0 + (h + 1) * 128], writes=[("wqkv", sl, j3)])

                    def proj_item(h, t):
                        def f():
                            sl = h % 2
                            W = wqkv[sl]
                            i2 = t % 2
                            pb = 4 + 2 * i2
                            for kc in range(8):
                                MM(ps[:, pb, 0:384], hT[:, kc, t * 128:(t + 1) * 128], W[:, kc, :], kc == 0, kc == 7,
                                   [("hT", t)] + [("wqkv", sl, q) for q in range(3)], [PB(pb)])
                            ACT(sq[i2][:], ps[:, pb, 0:256], AF.Square, [PB(pb)], [("sq", i2)])
                            S.op("dve", lambda e: e.tensor_reduce(
                                out=ssq4[:, t, :], in_=V(sq[i2][:], [[64, 4], [1, 64]]), axis=AX.X, op=ALU.add),
                                [("sq", i2)], [("ssq4", t)])
                            ACT(rs4[:, t, :], ssq4[:, t, :], AF.Ln, [("ssq4", t)], [("rs4", t)], bias=1e-6, scale=1.0 / 64)
                            ACT(rs4[:, t, :], rs4[:, t, :], AF.Exp, [("rs4", t)], [("rs4", t)], scale=-0.5)
                            TT("dve", V(tmpn[i2][:], [[64, 4], [1, 64]]), V(ps[:, pb, 0:256], [[64, 4], [1, 64]]),
                               V(rs4[:, t, :], [[1, 4], [0, 64]]), ALU.mult, [PB(pb), ("rs4", t)], [("tmpn", i2)])
                            qi = st_["qkn"] % 6
                            st_["qkn"] += 1
                            TT("pool", qkn[qi][:], tmpn[i2][:], gqk_bc[:], ALU.mult, [("tmpn", i2), "gqk"], [("qkn", qi)])
                            CP("dve", vaug[sl][:, t, 0:128], ps[:, pb, 256:384], [PB(pb)], [("vaug", sl)])

                            def g():
                                pv = psbf(2)
                                hf = i2 * 512
                                for m2 in range(2):
                                    TR(pv[:, hf + m2 * 128:hf + (m2 + 1) * 128], qkn[qi][:, m2 * 128:(m2 + 1) * 128], ident[:],
                                       [("qkn", qi), "ident"], [PB(2)])
                                CP("dve", qkT[sl][:, :, t * 128:(t + 1) * 128],
                                   V(pv[:, hf:hf + 256], [[128, 2], [1, 128]]), [PB(2)], [("qkT", sl, t)])
                            defer(4, g)
                            tick()
                        return f

                    PTc = {}

                    def qk_item(h, c, j):
                        def f():
                            sl = h % 2
                            r = j - 4 * c
                            st = max(0, r) * 128
                            b0 = 4 + 2 * (st_["sb"] % 2)
                            st_["sb"] += 1
                            qkeys = [("qkT", sl, j)] + [("qkT", sl, q) for q in range(4 * c + st // 128, 4 * c + 4)]
                            for m in range(2):
                                MM(ps[:, b0 + m, st:512], qkT[sl][64 * m:64 * m + 64, 1, j * 128:(j + 1) * 128],
                                   qkT[sl][64 * m:64 * m + 64, 0, 512 * c + st:512 * c + 512], True, True,
                                   qkeys, [PB(b0 + m)])
                            pi = st_["pt"] % NP
                            st_["pt"] += 1
                            PTc[(h, c, j)] = pi
                            Pt = PT[pi]
                            rb = [PB(b0), PB(b0 + 1)]
                            if r >= -1:
                                if r >= 0:
                                    nb = min(2, 4 - r)
                                    btsl = BT[:, h, 0:128 * nb]
                                else:
                                    nb = 1
                                    btsl = BT[:, h, 128:256]
                                wdt = 128 * nb
                                ei = st_["sb"] % 2
                                TT("dve", etmp[ei][:, :, 0:wdt], ps[:, b0:b0 + 2, st:st + wdt],
                                   V(btsl, [[0, 2], [1, wdt]]), ALU.add, rb + [("BT", h)], [("etmp", ei)])
                                ACT(Pt[:, :, st:st + wdt], etmp[ei][:, :, 0:wdt], AF.Exp, [("etmp", ei)], [("PT", pi)])
                                if st + wdt < 512:
                                    ACT(Pt[:, :, st + wdt:512], ps[:, b0:b0 + 2, st + wdt:512], AF.Exp,
                                        rb + ["b31"], [("PT", pi)], bias=b31[:, h:h + 1])
                            else:
                                ACT(Pt[:, :, :], ps[:, b0:b0 + 2, :], AF.Exp, rb + ["b31"], [("PT", pi)],
                                    bias=b31[:, h:h + 1])
                            tick()
                        return f

                    def av_item(h, c, i, m):
                        def f():
                            sl = h % 2
                            ob = i % 2
                            i2 = i % 2
                            for j in range(i + 1):
                                pi = PTc[(h, c, j)]
                                MM(ps[:, ob, 256 * m:256 * m + 129],
                                   PT[pi][:, m, (i - 4 * c) * 128:(i - 4 * c + 1) * 128],
                                   vaug[sl][:, j, 0:129], j == 0, j == i,
                                   [("PT", pi), ("vaug", sl)], [PB(ob)])
                            if m == 1:
                                RECIP(rl[:, i, :], V(ps[:, ob, 128:129], [[256, 2]]), [PB(ob)], [("rl", i)])
                                TS("dve", nrl[:, i:i + 1], rl[:, i, 1:2], neglam[:, 0:1], ALU.mult,
                                   [("rl", i), "neglam"], [("nrl", i)])
                                TS("dve", o1[i2][:], ps[:, ob, 0:128], rl[:, i, 0:1], ALU.mult, [PB(ob), ("rl", i)], [("o1", i2)])
                                STT(oo[i2][:], ps[:, ob, 256:384], nrl[:, i:i + 1], o1[i2][:], ALU.mult, ALU.add,
                                    [PB(ob), ("nrl", i), ("o1", i2)], [("oo", i2)])
                                ACT(junk2[:], oo[i2][:], AF.Square, [("oo", i2)], ["junk2", ("sso", i)], accum=sso[:, i:i + 1])
                                ACT(rso[:, i:i + 1], sso[:, i:i + 1], AF.Ln, [("sso", i)], [("rso", i)],
                                    bias=1e-5, scale=1.0 / 128)
                                ACT(rso[:, i:i + 1], rso[:, i:i + 1], AF.Exp, [("rso", i)], [("rso", i)], scale=-0.5)
                                yi = st_["yn"] % 6
                                st_["yn"] += 1
                                STT(yn[yi][:], oo[i2][:], rso[:, i:i + 1], subln_bc[:], ALU.mult, ALU.mult,
                                    [("oo", i2), ("rso", i), "subln"], [("yn", yi)])

                                def g():
                                    hf = c % 2
                                    pv = psbf(3)
                                    col = hf * 512 + (i % 4) * 128
                                    TR(pv[:, col:col + 128], yn[yi][:], ident[:], [("yn", yi), "ident"], [PB(3)])
                                    if i % 4 == 3:
                                        yc = st_["yst"] % 2
                                        st_["yst"] += 1
                                        ys = yst[yc]
                                        CP("act", ys[:], V(pv[:, hf * 512:hf * 512 + 512], [[128, 4], [1, 128]]),
                                           [PB(3)], [("yst", yc)])
                                        dst = bass.AP(tensor=ya_d.tensor,
                                                      offset=ya_d.offset + ((b * NT + 4 * c) * 128 * 8 + h) * 128,
                                                      ap=[[8 * 128, 128], [128 * 8 * 128, 4], [1, 128]])
                                        o_ = S.dma("sp", dst, ys[:], reads=[("yst", yc)], writes=[("ya_d", b, c, h)])
                                        if dbg:
                                            finals.append(o_)
                                defer(YDEF, g, "y")
                            tick()
                        return f

                    def run(items):
                        for it in items:
                            it()

                    def merge(A, B):
                        nb_done = 0
                        for idx, a in enumerate(A):
                            a()
                            tgt = (idx + 1) * len(B) // len(A)
                            while nb_done < tgt:
                                B[nb_done]()
                                nb_done += 1
                        while nb_done < len(B):
                            B[nb_done]()
                            nb_done += 1

                    def qk_items(h, c):
                        return [qk_item(h, c, j) for j in range(4 * c + 4)]

                    def av_items(h, c):
                        return [av_item(h, c, i, m) for i in range(4 * c, 4 * c + 4) for m in range(2)]

                    def proj_items(h):
                        return [proj_item(h, t) for t in range(NT)]

                    load_w(0)
                    run(proj_items(0))
                    for h in range(8):
                        if h < 7:
                            load_w(h + 1)
                        run(qk_items(h, 0))
                        for c in range(3):
                            merge(av_items(h, c), qk_items(h, c + 1))
                        merge(av_items(h, 3), proj_items(h + 1) if h < 7 else [])
                    flush()
                    S.barrier()
                if stage <= 1:
                    continue

                with contextlib.ExitStack() as P:
                    def sbp(name, shape, dt):
                        return P.enter_context(nc.sbuf_tensor(f"{name}_{b}", list(shape), dt))
                    ssmg_bc = sbp("p2_ssmg", [128, 2048], F32)
                    dt_all = sbp("p2_dt", [128, NT, 32], F32)
                    adt_all = sbp("p2_adt", [128, NT, 32], F32)
                    wdt_ = sbp("p2_wdt", [128, 8, 32], BF16)
                    dtt = [sbp(f"p2_dtt{i}", [128, 32], F32) for i in range(2)]
                    wx = [sbp(f"p2_wx{i}", [128, 8, 128], BF16) for i in range(3)]
                    wzs = [sbp(f"p2_wz{i}", [128, 8, 512], BF16) for i in range(2)]
                    acc = [sbp(f"p2_acc{i}", [128, 1024], F32) for i in range(2)]
                    accb = [sbp(f"p2_accb{i}", [128, 1024], F32) for i in range(2)]
                    zs_all = sbp("p2_zsall", [128, NT, 512], BF16)
                    halo = sbp("p2_halo", [128, 4], F32)
                    fmx = [sbp(f"p2_fmx{i}", [128, S_LEN], BF16) for i in range(2)]
                    BTg = sbp("p2_BTg", [128, S_LEN], BF16)
                    CTg = sbp("p2_CTg", [128, S_LEN], BF16)
                    xs_tok = sbp("p2_xs", [128, NT, 512], BF16)
                    B_tok = sbp("p2_Btok", [128, NT, 128], BF16)
                    state = sbp("p2_state", [128, 512], F32)
                    state_bf = sbp("p2_statebf", [128, 512], BF16)
                    s1 = sbp("p2_s1", [128, 512], F32)
                    cumtot = [sbp(f"p2_ct{i}", [128, 16], F32) for i in range(2)]
                    ecum = [sbp(f"p2_ec{i}", [128, 16], F32) for i in range(2)]
                    w8 = [sbp(f"p2_w8{i}", [128, 8], F32) for i in range(2)]
                    adtU = [sbp(f"p2_adtU{i}", [128, 8, 128], F32) for i in range(2)]
                    eseg = [sbp(f"p2_eseg{i}", [128, 8, 128], BF16) for i in range(2)]
                    cbTm = [sbp(f"p2_cbT{i}", [128, 128], BF16) for i in range(2)]
                    MT = [sbp(f"p2_MT{i}", [128, 8, 128], BF16) for i in range(2)]
                    xdt = [sbp(f"p2_xdt{i}", [128, 8, 64], BF16) for i in range(2)]
                    xw = [sbp(f"p2_xw{i}", [128, 8, 64], BF16) for i in range(2)]
                    t1 = [sbp(f"p2_t1{i}", [128, 512], F32) for i in range(2)]
                    t3 = [sbp(f"p2_t3{i}", [128, 512], F32) for i in range(2)]
                    junk3 = sbp("p2_junk3", [128, 512], BF16)
                    ssy = sbp("p2_ssy", [128, NT], F32)
                    rsy = sbp("p2_rsy", [128, NT], F32)
                    ynb = [sbp(f"p2_ynb{i}", [128, 512], BF16) for i in range(2)]
                    ysT = [sbp(f"p2_ysT{i}", [128, 4, 128], BF16) for i in range(2)]

                    S.dma("sp", ssmg_bc[:], bc(ssm_norm_g), writes=["ssmg"])
                    S.dma("pool", wdt_[:], w_in_v[:, :, C_DT:C_DT + 32], writes=["wdt"])
                    for t in range(NT):
                        i2 = t % 2
                        for kc in range(8):
                            MM(ps[:, 5, 0:32], hT[:, kc, t * 128:(t + 1) * 128], wdt_[:, kc, :], kc == 0, kc == 7,
                               [("hT", t), "wdt"], [PB(5)])
                        TT("dve", dtt[i2][:], ps[:, 5, 0:32], dtb_bc[:], ALU.add, [PB(5), "dtb"], [("dtt", i2)])
                        ACT(dtt[i2][:], dtt[i2][:], AF.Exp, [("dtt", i2)], [("dtt", i2)])
                        ACT(dt_all[:, t, :], dtt[i2][:], AF.Ln, [("dtt", i2)], [("dt", t)], bias=1.0, scale=1.0)
                        TT("pool", adt_all[:, t, :], dt_all[:, t, :], a_bc[:], ALU.mult, [("dt", t), "a_bc"], [("adt", t)])
                    if dbg and b == 0:
                        finals.append(S.dma("sp", dt_d, dt_all[:], reads=[("dt", t) for t in range(NT)]))

                    st2 = {"pp": 0, "ab": 0, "zb": 0}
                    pend2 = []

                    def chunk_desc(n):
                        g_, ci_ = n // 6, n % 6
                        if ci_ < 4:
                            return g_, ci_, C_XS + g_ * 512 + ci_ * 128, g_ * 4 + ci_, fmx[ci_ % 2], ("fmx", ci_ % 2)
                        if ci_ == 4:
                            return g_, ci_, C_B + g_ * 128, 16 + g_, BTg, "BTg"
                        return g_, ci_, C_C + g_ * 128, 20 + g_, CTg, "CTg"

                    def load_wx(n):
                        if n >= 24:
                            return
                        col0 = chunk_desc(n)[2]
                        S.dma("pool", wx[n % 3][:], w_in_v[:, :, col0:col0 + 128], writes=[("wx", n % 3)])

                    def load_wz(g_):
                        S.dma("pool", wzs[g_ % 2][:], w_in_v[:, :, C_Z + g_ * 512:C_Z + (g_ + 1) * 512], writes=[("wz", g_ % 2)])

                    def emit_pend2():
                        while pend2:
                            pend2.pop(0)()

                    load_wx(0)
                    load_wx(1)
                    load_wz(0)
                    for g in range(4):
                        for ci in range(6):
                            n = g * 6 + ci
                            _, _, col0, cch, dstT, dkey = chunk_desc(n)
                            wi = n % 3
                            load_wx(n + 2)
                            for half in range(2):
                                bp = 2 * (st2["pp"] % 2)
                                st2["pp"] += 1
                                ai = st2["ab"] % 2
                                st2["ab"] += 1
                                A, Bv = acc[ai], accb[ai]
                                ak, bk = ("acc", ai), ("accb", ai)
                                for tt in range(2):
                                    tok0 = half * 1024 + tt * 512
                                    for kc in range(8):
                                        MM(ps[:, bp + tt, :], wx[wi][:, kc, :], hT[:, kc, tok0:tok0 + 512], kc == 0, kc == 7,
                                           [("wx", wi)] + [("hT", tok0 // 128 + q) for q in range(4)], [PB(bp + tt)])
                                emit_pend2()
                                pin = V(ps[:, bp, :], [[1, 1024]])
                                pin1 = V(ps[:, bp, :], [[1, 1023]])
                                rb = [PB(bp), PB(bp + 1)]
                                if not CONV_SPLIT:
                                    ACT(A[:], pin, AF.Identity, rb + ["convw", "convb"], [ak],
                                        bias=convb[:, cch:cch + 1], scale=convw[:, cch, 3:4])
                                    for k in (2, 1, 0):
                                        d_ = 3 - k
                                        STT(A[:, d_:1024], V(ps[:, bp, :], [[1, 1024 - d_]]), convw[:, cch, k:k + 1], A[:, d_:1024],
                                            ALU.mult, ALU.add, rb + ["convw", ak], [ak])
                                        if half == 1:
                                            STT(A[:, 0:d_], halo[:, 3 - d_:3], convw[:, cch, k:k + 1], A[:, 0:d_],
                                                ALU.mult, ALU.add, ["halo", "convw", ak], [ak])
                                    if half == 0:
                                        CP("act", halo[:, 0:3], ps[:, bp + 1, 509:512], [PB(bp + 1)], ["halo"])
                                else:
                                    ACT(A[:], pin, AF.Identity, rb + ["convw", "convb"], [ak],
                                        bias=convb[:, cch:cch + 1], scale=convw[:, cch, 3:4])
                                    if "8" in os.environ.get("KF", ""):
                                        TS("dve", Bv[:], pin, convw[:, cch, 1:2], ALU.mult, rb + ["convw"], [bk])
                                    else:
                                        ACT(Bv[:], pin, AF.Identity, rb + ["convw"], [bk], scale=convw[:, cch, 1:2])
                                    STT(A[:, 1:1024], pin1, convw[:, cch, 2:3], A[:, 1:1024], ALU.mult, ALU.add, rb + ["convw", ak], [ak])
                                    STT(Bv[:, 1:1024], pin1, convw[:, cch, 0:1], Bv[:, 1:1024], ALU.mult, ALU.add, rb + ["convw", bk], [bk])
                                    if half == 1 and "7" not in os.environ.get("KF", ""):
                                        STT(A[:, 0:1], halo[:, 2:3], convw[:, cch, 2:3], A[:, 0:1], ALU.mult, ALU.add, ["halo", "convw", ak], [ak])
                                        STT(A[:, 0:2], halo[:, 1:3], convw[:, cch, 1:2], A[:, 0:2], ALU.mult, ALU.add, ["halo", "convw", ak], [ak])
                                        STT(A[:, 0:2], halo[:, 0:2], convw[:, cch, 0:1], A[:, 0:2], ALU.mult, ALU.add, ["halo", "convw", ak], [ak])
                                        STT(Bv[:, 0:1], halo[:, 2:3], convw[:, cch, 0:1], Bv[:, 0:1], ALU.mult, ALU.add, ["halo", "convw", bk], [bk])
                                    if half == 0:
                                        CP("act", halo[:, 0:3], ps[:, bp + 1, 509:512], [PB(bp + 1)], ["halo"])
                                    TT("dve" if "5" in os.environ.get("KF", "") else "pool", A[:, 2:1024], A[:, 2:1024], Bv[:, 0:1022], ALU.add, [ak, bk], [ak])
                                ACT(dstT[:, half * 1024:(half + 1) * 1024], A[:], AF.Silu, [ak], [dkey])
                            if ci < 5:
                                def trans(ci=ci, dstT=dstT, dkey=dkey):
                                    for tb in range(2):
                                        pv = psbf(4)
                                        for q in range(8):
                                            t = tb * 8 + q
                                            TR(pv[:, q * 128:(q + 1) * 128], dstT[:, t * 128:(t + 1) * 128], ident[:],
                                               [dkey, "ident"], [PB(4)])
                                        if ci < 4:
                                            CP("act", xs_tok[:, tb * 8:(tb + 1) * 8, ci * 128:(ci + 1) * 128],
                                               V(pv, [[128, 8], [1, 128]]), [PB(4)], ["xs_tok"])
                                        else:
                                            CP("act", B_tok[:, tb * 8:(tb + 1) * 8, :], V(pv, [[128, 8], [1, 128]]), [PB(4)], ["B_tok"])
                                pend2.append(trans)
                                if "1" in os.environ.get("KF", ""):
                                    emit_pend2()
                        if "a" in os.environ.get("KSTOP", ""):
                            break
                        wz = wzs[g % 2]
                        for t in range(NT):
                            zb = 6 + (st2["zb"] % 2)
                            st2["zb"] += 1
                            for kc in range(8):
                                MM(ps[:, zb, :], hT[:, kc, t * 128:(t + 1) * 128], wz[:, kc, :], kc == 0, kc == 7,
                                   [("hT", t), ("wz", g % 2)], [PB(zb)])
                            if t == 1:
                                emit_pend2()
                            ACT(zs_all[:, t, :], ps[:, zb, :], AF.Silu, [PB(zb)], [("zs", t)])
                        if "z" in os.environ.get("KSTOP", ""):
                            break
                        if g < 3:
                            load_wz(g + 1)
                        MEMSET("pool", state[:], 0.0, ["state"])
                        MEMSET("pool", state_bf[:], 0.0, ["state_bf"])

                        def stageA(c):
                            i2 = c % 2
                            cs = slice(c * 128, (c + 1) * 128)
                            adt_c = adt_all[:, c, g * 8:(g + 1) * 8]
                            dt_c = dt_all[:, c, g * 8:(g + 1) * 8]
                            MM(ps[:, 5, 0:8], U[:], adt_c, True, True, ["U", ("adt", c)], [PB(5)])
                            MM(ps[:, 5, 8:16], ones_f[:], adt_c, True, True, ["ones_f", ("adt", c)], [PB(5)])
                            CP("act", cumtot[i2][:], ps[:, 5, 0:16], [PB(5)], [("cumtot", i2)])
                            ACT(ecum[i2][:], cumtot[i2][:], AF.Exp, [("cumtot", i2)], [("ecum", i2)])
                            TT("dve", w8[i2][:], cumtot[i2][:, 8:16], cumtot[i2][:, 0:8], ALU.subtract, [("cumtot", i2)], [("w8", i2)])
                            ACT(w8[i2][:], w8[i2][:], AF.Exp, [("w8", i2)], [("w8", i2)])
                            TT("pool", adtU[i2][:], V(adt_c, [[1, 8], [0, 128]]), V(U[:], [[0, 8], [1, 128]]), ALU.mult,
                               [("adt", c), "U"], [("adtU", i2)])
                            for q in range(2):
                                MM(ps[:, q, :], Lst[:], adtU[i2][:, 4 * q:4 * q + 4, :], True, True, ["Lst", ("adtU", i2)], [PB(q)])
                            ACT(V(eseg[i2][:], [[512, 2], [1, 512]]), ps[:, 0:2, :], AF.Exp, [PB(0), PB(1)], [("eseg", i2)])
                            MM(ps[:, 4, 0:128], BTg[:, cs], CTg[:, cs], True, True, ["BTg", "CTg"], [PB(4)])
                            TT("dve", cbTm[i2][:], ps[:, 4, 0:128], U[:], ALU.mult, [PB(4), "U"], [("cbTm", i2)])
                            TT("dve", MT[i2][:], eseg[i2][:], V(cbTm[i2][:], [[0, 8], [1, 128]]), ALU.mult,
                               [("eseg", i2), ("cbTm", i2)], [("MT", i2)])
                            TT("pool", xdt[i2][:], V(xs_tok[:, c, :], [[64, 8], [1, 64]]), V(dt_c, [[1, 8], [0, 64]]), ALU.mult,
                               ["xs_tok", ("dt", c)], [("xdt", i2)])
                            TT("pool", xw[i2][:], xdt[i2][:], V(w8[i2][:], [[1, 8], [0, 64]]), ALU.mult,
                               [("xdt", i2), ("w8", i2)], [("xw", i2)])

                        def stageB(c):
                            i2 = c % 2
                            cs = slice(c * 128, (c + 1) * 128)
                            for hh in range(8):
                                MM(ps[:, 2, hh * 64:(hh + 1) * 64], MT[i2][:, hh, :], xdt[i2][:, hh, :], True, True,
                                   [("MT", i2), ("xdt", i2)], [PB(2)])
                            MM(ps[:, 3, :], CTg[:, cs], state_bf[:], True, True, ["CTg", "state_bf"], [PB(3)])
                            if c < NT - 1:
                                MM(ps[:, 7, :], B_tok[:, c, :], V(xw[i2][:], [[1, 512]]), True, True, ["B_tok", ("xw", i2)], [PB(7)])
                            emit_pend2()
                            TT("pool", V(t3[i2][:], [[64, 8], [1, 64]]), V(xs_tok[:, c, :], [[64, 8], [1, 64]]),
                               V(dsk_bc[:, g * 8:(g + 1) * 8], [[1, 8], [0, 64]]), ALU.mult, ["xs_tok", "dsk"], [("t3", i2)])
                            TT("dve", V(t1[i2][:], [[64, 8], [1, 64]]), V(ps[:, 3, :], [[64, 8], [1, 64]]),
                               V(ecum[i2][:, 0:8], [[1, 8], [0, 64]]), ALU.mult, [PB(3), ("ecum", i2)], [("t1", i2)])
                            TT("dve", t1[i2][:], ps[:, 2, :], t1[i2][:], ALU.add, [PB(2), ("t1", i2)], [("t1", i2)])
                            TT("dve", t1[i2][:], t1[i2][:], t3[i2][:], ALU.add, [("t1", i2), ("t3", i2)], [("t1", i2)])
                            TT("pool", t3[i2][:], t1[i2][:], zs_all[:, c, :], ALU.mult, [("t1", i2), ("zs", c)], [("t3", i2)])
                            ACT(junk3[:], t3[i2][:], AF.Square, [("t3", i2)], ["junk3", ("ssy", c)], accum=ssy[:, c:c + 1])
                            ACT(rsy[:, c:c + 1], ssy[:, c:c + 1], AF.Ln, [("ssy", c)], [("rsy", c)], bias=1e-5, scale=1.0 / 512)
                            ACT(rsy[:, c:c + 1], rsy[:, c:c + 1], AF.Exp, [("rsy", c)], [("rsy", c)], scale=-0.5)

                            def trans_y(c=c, i2=i2, g=g):
                                STT(ynb[i2][:], t3[i2][:], rsy[:, c:c + 1], ssmg_bc[:, g * 512:(g + 1) * 512], ALU.mult, ALU.mult,
                                    [("t3", i2), ("rsy", c), "ssmg"], [("ynb", i2)])
                                pv = psbf(6)
                                for q in range(4):
                                    TR(pv[:, q * 128:(q + 1) * 128], ynb[i2][:, q * 128:(q + 1) * 128], ident[:],
                                       [("ynb", i2), "ident"], [PB(6)])
                                CP("act", ysT[i2][:], V(pv[:, 0:512], [[128, 4], [1, 128]]), [PB(6)], [("ysT", i2)])
                                dst = bass.AP(tensor=yss_d.tensor,
                                              offset=yss_d.offset + ((b * NT + c) * 128 * 16 + g * 4) * 128,
                                              ap=[[16 * 128, 128], [128, 4], [1, 128]])
                                o_ = S.dma("sp", dst, ysT[i2][:], reads=[("ysT", i2)], writes=[("yss_d", b, c, g)])
                                if dbg:
                                    finals.append(o_)
                            pend2.append(trans_y)
                            if "3" in os.environ.get("KF", ""):
                                emit_pend2()
                            if c < NT - 1:
                                TT("dve", V(s1[:], [[64, 8], [1, 64]]), V(state[:], [[64, 8], [1, 64]]),
                                   V(ecum[i2][:, 8:16], [[1, 8], [0, 64]]), ALU.mult, ["state", ("ecum", i2)], ["s1"])
                                TT("dve", state[:], ps[:, 7, :], s1[:], ALU.add, [PB(7), "s1"], ["state"])
                                CP("act", state_bf[:], state[:], ["state"], ["state_bf"])

                        if "2" in os.environ.get("KF", ""):
                            for c in range(NT):
                                stageA(c)
                                stageB(c)
                        else:
                            stageA(0)
                            for c in range(NT):
                                if c + 1 < NT:
                                    stageA(c + 1)
                                stageB(c)
                    emit_pend2()
                    S.barrier()
                if stage <= 2:
                    continue

                with contextlib.ExitStack() as P:
                    def sbp(name, shape, dt):
                        return P.enter_context(nc.sbuf_tensor(f"{name}_{b}", list(shape), dt))
                    wpa = sbp("p3_wpa", [128, 8, 1024], BF16)
                    wps = sbp("p3_wps", [128, 16, 1024], BF16)
                    wg = sbp("p3_wg", [128, 8, 2048], BF16)
                    wo = sbp("p3_wo", [128, 8, 1024], BF16)
                    yat = [sbp(f"p3_yat{i}", [128, 8, 128], BF16) for i in range(2)]
                    ysst = [sbp(f"p3_ysst{i}", [128, 16, 128], BF16) for i in range(2)]
                    xt3 = [sbp(f"p3_xt{i}", [128, D], F32) for i in range(2)]
                    sa = [sbp(f"p3_sa{i}", [128, 512], F32) for i in range(2)]
                    sg_ = [sbp(f"p3_sg{i}", [128, 512], F32) for i in range(2)]
                    mixed = [sbp(f"p3_mixed{i}", [128, D], BF16) for i in range(2)]
                    mixT = [sbp(f"p3_mixT{i}", [128, 8, 128], BF16) for i in range(2)]
                    x1 = [sbp(f"p3_x1{i}", [128, D], F32) for i in range(2)]
                    hn3 = [sbp(f"p3_hn{i}", [128, D], BF16) for i in range(2)]
                    junk4 = sbp("p3_junk", [128, D], BF16)
                    ss3 = sbp("p3_ss", [128, NT], F32)
                    rs3 = sbp("p3_rs", [128, NT], F32)
                    def lw(dst, src, q, key):
                        S.dma("pool", dst[:, :, q * 512:(q + 1) * 512], src[:, :, q * 512:(q + 1) * 512], writes=[(key, q)])
                    w_g_v = w_in_v[:, :, C_G:C_G + 2048]
                    for j in range(2):
                        lw(wg, w_g_v, j, "wg")
                        lw(wg, w_g_v, 2 + j, "wg")
                        lw(wpa, w_pa_v, j, "wpa")
                        lw(wps, w_ps_v, j, "wps")
                    for j in range(2):
                        lw(wo, w_out_v, j, "wo")
                    hcs = {"hc": 0}

                    def M3(t):
                        i2 = t % 2
                        S.dma("sp", yat[i2][:], ya_d[b, t], reads=[("ya_d", b, t // 4, hh) for hh in range(8)], writes=[("yat", i2)])
                        S.dma("sp", ysst[i2][:], yss_d[b, t], reads=[("yss_d", b, t, g) for g in range(4)], writes=[("ysst", i2)])
                        S.dma("sp", xt3[i2][:], x[b, t * 128:(t + 1) * 128, :], writes=[("xt3", i2)])
                        for j in range(2):
                            h2 = hcs["hc"] % 2
                            hcs["hc"] += 1
                            cj = slice(j * 512, (j + 1) * 512)
                            for kc in range(8):
                                MM(ps[:, 2, :], hT[:, kc, t * 128:(t + 1) * 128], wg[:, kc, cj], kc == 0, kc == 7,
                                   [("hT", t), ("wg", j)], [PB(2)])
                            ACT(sa[h2][:], ps[:, 2, :], AF.Sigmoid, [PB(2)], [("sa", h2)])
                            for kc in range(8):
                                MM(ps[:, 3, :], hT[:, kc, t * 128:(t + 1) * 128], wg[:, kc, 1024 + j * 512:1024 + (j + 1) * 512],
                                   kc == 0, kc == 7, [("hT", t), ("wg", 2 + j)], [PB(3)])
                            ACT(sg_[h2][:], ps[:, 3, :], AF.Sigmoid, [PB(3)], [("sg", h2)])
                            for c in range(8):
                                MM(ps[:, 0, :], yat[i2][:, c, :], wpa[:, c, cj], c == 0, c == 7, [("yat", i2), ("wpa", j)], [PB(0)])
                            TT("dve", sa[h2][:], ps[:, 0, :], sa[h2][:], ALU.mult, [PB(0), ("sa", h2)], [("sa", h2)])
                            for c in range(16):
                                MM(ps[:, 1, :], ysst[i2][:, c, :], wps[:, c, cj], c == 0, c == 15, [("ysst", i2), ("wps", j)], [PB(1)])
                            TT("dve", sg_[h2][:], ps[:, 1, :], sg_[h2][:], ALU.mult, [PB(1), ("sg", h2)], [("sg", h2)])
                            TT("pool", mixed[i2][:, cj], sa[h2][:], sg_[h2][:], ALU.add, [("sa", h2), ("sg", h2)], [("mixed", i2)])

                    def T31(t):
                        i2 = t % 2
                        pv = psbf(4)
                        for c in range(8):
                            TR(pv[:, c * 128:(c + 1) * 128], mixed[i2][:, c * 128:(c + 1) * 128], ident[:], [("mixed", i2), "ident"], [PB(4)])
                        CP("act", mixT[i2][:], V(pv, [[128, 8], [1, 128]]), [PB(4)], [("mixT", i2)])
                        for j in range(2):
                            for c in range(8):
                                MM(ps[:, 5 + j, :], mixT[i2][:, c, :], wo[:, c, j * 512:(j + 1) * 512], c == 0, c == 7,
                                   [("mixT", i2), ("wo", j)], [PB(5 + j)])
                        TT("dve", x1[i2][:], V(ps[:, 5, :], [[1, 1024]]), xt3[i2][:], ALU.add, [PB(5), PB(6), ("xt3", i2)], [("x1", i2)])
                        o_ = S.dma("sp", out[b, t * 128:(t + 1) * 128, :], x1[i2][:], reads=[("x1", i2)], writes=[("x1d", b, t)])
                        if stage <= 3:
                            finals.append(o_)
                        ACT(junk4[:], x1[i2][:], AF.Square, [("x1", i2)], ["junk4", ("ss3", t)], accum=ss3[:, t:t + 1])
                        ACT(rs3[:, t:t + 1], ss3[:, t:t + 1], AF.Sqrt, [("ss3", t)], [("rs3", t)], bias=1e-6, scale=1.0 / D)
                        RECIP(rs3[:, t:t + 1], rs3[:, t:t + 1], [("rs3", t)], [("rs3", t)])
                        STT(hn3[i2][:], x1[i2][:], rs3[:, t:t + 1], gffn_bc[:], ALU.mult, ALU.mult,
                            [("x1", i2), ("rs3", t), "gffn"], [("hn3", i2)])

                    def T32(t):
                        i2 = t % 2
                        pv = psbf(7)
                        for c in range(8):
                            TR(pv[:, c * 128:(c + 1) * 128], hn3[i2][:, c * 128:(c + 1) * 128], ident[:], [("hn3", i2), "ident"], [PB(7)])
                        CP("act", hT[:, :, t * 128:(t + 1) * 128], V(pv, [[128, 8], [1, 128]]), [PB(7)], [("hT", t)])

                    for t in range(NT + 2):
                        if t < NT:
                            M3(t)
                        if 1 <= t <= NT:
                            T31(t - 1)
                        if t >= 2:
                            T32(t - 2)
                    S.barrier()
                if stage <= 3:
                    continue

                with contextlib.ExitStack() as P:
                    def sbp(name, shape, dt):
                        return P.enter_context(nc.sbuf_tensor(f"{name}_{b}", list(shape), dt))
                    wdn = sbp("p4_wdn", [128, NFC, 1024], BF16)
                    aT = sbp("p4_aT", [128, NFC, 1024], BF16)
                    wup = [sbp(f"p4_wup{i}", [128, 8, 256], BF16) for i in range(3)]
                    accg = [sbp(f"p4_accg{i}", [128, 1024], F32) for i in range(2)]
                    accb = [sbp(f"p4_accb{i}", [128, 1024], F32) for i in range(2)]
                    accv = [sbp(f"p4_accv{i}", [128, 1024], F32) for i in range(2)]
                    fhalo = sbp("p4_halo", [128, 2 * NFC, 2], F32)
                    x1t = [sbp(f"p4_x1t{i}", [128, D], F32) for i in range(2)]
                    ot = [sbp(f"p4_ot{i}", [128, D], F32) for i in range(2)]

                    def load_wup(n):
                        fc_ = n % NFC
                        wi_ = n % 3
                        S.dma("pool", wup[wi_][:, :, 0:128], w_up_v[:, :, fc_ * 128:(fc_ + 1) * 128], writes=[("wup", wi_, 0)])
                        S.dma("pool", wup[wi_][:, :, 128:256], w_up_v[:, :, D_FF + fc_ * 128:D_FF + (fc_ + 1) * 128],
                              writes=[("wup", wi_, 1)])

                    load_wup(0)
                    load_wup(1)
                    for q in range(2):
                        S.dma("pool", wdn[:, 0:11, q * 512:(q + 1) * 512], w_dn_v[:, 0:11, q * 512:(q + 1) * 512], writes=[("wdn", q, 0)])
                        S.dma("pool", wdn[:, 11:22, q * 512:(q + 1) * 512], w_dn_v[:, 11:22, q * 512:(q + 1) * 512], writes=[("wdn", q, 1)])
                    WDN = [("wdn", q, r_) for q in range(2) for r_ in range(2)]
                    wctr = 0
                    for blk in range(2):
                        for fc in range(NFC):
                            wi = wctr % 3
                            i2 = wctr % 2
                            if wctr + 2 < 2 * NFC:
                                load_wup(wctr + 2)
                            wctr += 1
                            for part in range(2):
                                base = 4 * i2 + 2 * part
                                cch = part * NFC + fc
                                A = (accg if part == 0 else accv)[i2]
                                ak = ("accg" if part == 0 else "accv", i2)
                                for tt in range(2):
                                    tok0 = blk * 1024 + tt * 512
                                    for kc in range(8):
                                        MM(ps[:, base + tt, :], wup[wi][:, kc, part * 128:(part + 1) * 128], hT[:, kc, tok0:tok0 + 512],
                                           kc == 0, kc == 7, [("wup", wi, part)] + [("hT", tok0 // 128 + q) for q in range(4)], [PB(base + tt)])
                                rb = [PB(base), PB(base + 1)]
                                pin = V(ps[:, base, :], [[1, 1024]])
                                if not CONV_SPLIT:
                                    ACT(A[:], pin, AF.Identity, rb + ["fconvw", "fconvb"], [ak],
                                        bias=fconvb[:, cch:cch + 1], scale=fconvw[:, cch, 2:3])
                                    for k in (1, 0):
                                        d_ = 2 - k
                                        STT(A[:, d_:1024], V(ps[:, base, :], [[1, 1024 - d_]]), fconvw[:, cch, k:k + 1], A[:, d_:1024],
                                            ALU.mult, ALU.add, rb + ["fconvw", ak], [ak])
                                        if blk == 1:
                                            STT(A[:, 0:d_], fhalo[:, cch, 2 - d_:2], fconvw[:, cch, k:k + 1], A[:, 0:d_],
                                                ALU.mult, ALU.add, [("fhalo", cch), "fconvw", ak], [ak])
                                    if blk == 0:
                                        CP("act", fhalo[:, cch, :], ps[:, base + 1, 510:512], [PB(base + 1)], [("fhalo", cch)])
                                else:
                                    ACT(A[:], pin, AF.Identity, rb + ["fconvw", "fconvb"], [ak],
                                        bias=fconvb[:, cch:cch + 1], scale=fconvw[:, cch, 2:3])
                                    if part == 0:
                                        Bv = accb[i2]
                                        bk = ("accb", i2)
                                        ACT(Bv[:], pin, AF.Identity, rb + ["fconvw"], [bk], scale=fconvw[:, cch, 0:1])
                                    STT(A[:, 1:1024], V(ps[:, base, :], [[1, 1023]]), fconvw[:, cch, 1:2], A[:, 1:1024],
                                        ALU.mult, ALU.add, rb + ["fconvw", ak], [ak])
                                    if part == 1:
                                        STT(A[:, 2:1024], V(ps[:, base, :], [[1, 1022]]), fconvw[:, cch, 0:1], A[:, 2:1024],
                                            ALU.mult, ALU.add, rb + ["fconvw", ak], [ak])
                                    if blk == 1:
                                        STT(A[:, 0:1], fhalo[:, cch, 1:2], fconvw[:, cch, 1:2], A[:, 0:1],
                                            ALU.mult, ALU.add, [("fhalo", cch), "fconvw", ak], [ak])
                                        STT(A[:, 0:2], fhalo[:, cch, 0:2], fconvw[:, cch, 0:1], A[:, 0:2],
                                            ALU.mult, ALU.add, [("fhalo", cch), "fconvw", ak], [ak])
                                    if blk == 0:
                                        CP("act", fhalo[:, cch, :], ps[:, base + 1, 510:512], [PB(base + 1)], [("fhalo", cch)])
                                    if part == 0:
                                        TT("pool", A[:, 2:1024], A[:, 2:1024], Bv[:, 0:1022], ALU.add, [ak, bk], [ak])
                            ACT(accg[i2][:], accg[i2][:], AF.Silu, [("accg", i2)], [("accg", i2)])
                            TT("pool", aT[:, fc, :], accg[i2][:], accv[i2][:], ALU.mult, [("accg", i2), ("accv", i2)], [("aT", fc)])
                        for tt in range(8):
                            t = blk * 8 + tt
                            i2 = t % 2
                            S.dma("sp", x1t[i2][:], out[b, t * 128:(t + 1) * 128, :], reads=[("x1d", b, t)], writes=[("x1t", i2)])
                            for j in range(2):
                                pb = 2 * i2 + j
                                for fc in range(NFC):
                                    MM(ps[:, pb, :], aT[:, fc, tt * 128:(tt + 1) * 128], wdn[:, fc, j * 512:(j + 1) * 512],
                                       fc == 0, fc == NFC - 1, [("aT", fc)] + WDN, [PB(pb)])
                            TT("dve", ot[i2][:], V(ps[:, 2 * i2, :], [[1, 1024]]), x1t[i2][:], ALU.add,
                               [PB(2 * i2), PB(2 * i2 + 1), ("x1t", i2)], [("ot", i2)])
                            finals.append(S.dma("sp", out[b, t * 128:(t + 1) * 128, :], ot[i2][:], reads=[("ot", i2)],
                                                writes=[("x1d", b, t)]))
                    S.barrier()
        S.emit(final_wait_ops=finals)
    return nc


_NC_CACHE = {}
PARAM_NAMES = ["rel_bias", "norm_mix_g", "w_in", "q_norm_g", "k_norm_g", "lambda_q1", "lambda_k1", "lambda_q2",
               "lambda_k2", "attn_subln_g", "conv_ssm_w", "conv_ssm_b", "dt_bias", "a_log", "d_skip", "ssm_norm_g",
               "w_proj_attn", "w_proj_ssm", "w_out", "norm_ffn_g", "w_up", "conv_ffn_w", "conv_ffn_b", "w_down"]


def make_in_maps(inputs, n_cores=8, nseq=2):
    x = np.ascontiguousarray(np.asarray(inputs["x"], dtype=np.float32))
    shared = {}
    for k in PARAM_NAMES:
        a = np.asarray(inputs[k], dtype=np.float32)
        if k == "rel_bias":
            shared[k] = np.ascontiguousarray(a)
        elif a.ndim == 2:
            shared[k] = np.ascontiguousarray(a[0:1])
        else:
            shared[k] = np.ascontiguousarray(a[0])
    in_maps = []
    for i in range(n_cores):
        m = dict(shared)
        m["x"] = np.ascontiguousarray(x[i * nseq:(i + 1) * nseq])
        in_maps.append(m)
    return in_maps


def kernel(**inputs):
    n_cores, nseq = 8, 2
    if "nc" not in _NC_CACHE:
        _NC_CACHE["nc"] = build(nseq=nseq)
    nc = _NC_CACHE["nc"]
    in_maps = make_in_maps(inputs, n_cores, nseq)
    res = run_bass_kernel_spmd(nc, in_maps, core_ids=list(range(n_cores)))
    return np.concatenate([np.asarray(r["out"], dtype=np.float32) for r in res.results], axis=0)
```
